# Optimizing a Trainium2 kernel written in Bass

```python
import math
import jax, jax.numpy as jnp
from jax import lax
import numpy as np

D_MODEL = 1024
BATCH = 4
SEQ = 4096
DEPTH = 1

N_MEM = 256
HA = 8
DA = 64
A_WIDTH = HA * 2 * DA
Q_BLOCK = 128
NUM_BUCKETS = 32
MAX_DISTANCE = 128
HB = 8
KB = 128
VB = 128
B_WIDTH = HB * VB
CHUNK = 64
HC = 4
DC = 256
C_WIDTH = HC * DC
N_BRANCH = 3
IN_SIZES = (
    2 * HA * DA, 2 * HA * DA, A_WIDTH, A_WIDTH,
    HB * KB, HB * KB, HB * KB, B_WIDTH, B_WIDTH,
    C_WIDTH, C_WIDTH,
    N_BRANCH * D_MODEL,
)
IN_COLS = sum(IN_SIZES)
EPS = 1e-6

kernel_name = "hybrid_diffattn_hgrn2_memxattn_block"


def rmsnorm(x, w):
    xf = x.astype(jnp.float32)
    y = xf * lax.rsqrt(jnp.mean(xf * xf, axis=-1, keepdims=True) + EPS)
    return (y * w.astype(jnp.float32)).astype(x.dtype)


def t5_bucket(rel):
    nb = NUM_BUCKETS // 2
    max_exact = nb // 2
    ret = jnp.where(rel > 0, nb, 0)
    n = jnp.abs(rel)
    nf = jnp.maximum(n, 1).astype(jnp.float32)
    large = max_exact + (jnp.log(nf / max_exact) / math.log(MAX_DISTANCE / max_exact)
                         * (nb - max_exact)).astype(jnp.int32)
    large = jnp.minimum(large, nb - 1)
    return ret + jnp.where(n < max_exact, n, large)


def diff_attention(q, k, v, lam, rel_bias):
    B, L = q.shape[0], q.shape[1]
    nblk = L // Q_BLOCK
    qb = q.reshape(B, nblk, Q_BLOCK, 2, HA, DA).transpose(1, 0, 3, 4, 2, 5)
    kt = k.transpose(0, 2, 3, 1, 4)
    vt = v.transpose(0, 2, 1, 3)
    kpos = jnp.arange(L, dtype=jnp.int32)
    scale = DA ** -0.5

    def block(args):
        qblk, start = args
        qpos = start + jnp.arange(Q_BLOCK, dtype=jnp.int32)
        bucket = t5_bucket(kpos[None, :] - qpos[:, None])
        bias = jnp.transpose(rel_bias[bucket], (2, 0, 1)).astype(jnp.float32)
        s = jnp.einsum('bmhqd,bmhkd->bmhqk', qblk, kt).astype(jnp.float32) * scale + bias
        p = jax.nn.softmax(s, axis=-1)
        w = p[:, 0] - lam * p[:, 1]
        return jnp.einsum('bhqk,bhkv->bhqv', w.astype(vt.dtype), vt)

    starts = jnp.arange(nblk, dtype=jnp.int32) * Q_BLOCK
    out = lax.map(block, (qb, starts))
    return out.transpose(1, 0, 3, 2, 4).reshape(B, L, HA, 2 * DA)


def gla_chunk(q, k, v, g):
    B, H, L, K = q.shape
    V = v.shape[-1]
    N = L // CHUNK
    qc = q.reshape(B, H, N, CHUNK, K).astype(jnp.float32)
    kc = k.reshape(B, H, N, CHUNK, K).astype(jnp.float32)
    vc = v.reshape(B, H, N, CHUNK, V).astype(jnp.float32)
    gc = g.reshape(B, H, N, CHUNK, K).astype(jnp.float32)
    b = jnp.cumsum(gc, axis=3)
    b_mid = b[:, :, :, CHUNK // 2 - 1:CHUNK // 2, :]
    b_last = b[:, :, :, CHUNK - 1:, :]
    a = jnp.einsum('bhnck,bhnsk->bhncs', qc * jnp.exp(b - b_mid), kc * jnp.exp(b_mid - b))
    mask = jnp.arange(CHUNK)[None, :] <= jnp.arange(CHUNK)[:, None]
    a = jnp.where(mask, a, 0.0)
    o_intra = jnp.einsum('bhncs,bhnsv->bhncv', a, vc)
    u = jnp.einsum('bhnsk,bhnsv->bhnkv', kc * jnp.exp(b_last - b), vc)
    decay = jnp.exp(b_last[:, :, :, 0, :])

    def step(s, inp):
        d, un = inp
        return d[..., None] * s + un, s

    s0 = jnp.zeros((B, H, K, V), jnp.float32)
    _, s_prev = lax.scan(step, s0, (jnp.moveaxis(decay, 2, 0), jnp.moveaxis(u, 2, 0)))
    s_prev = jnp.moveaxis(s_prev, 0, 2)
    o_inter = jnp.einsum('bhnck,bhnkv->bhncv', qc * jnp.exp(b), s_prev)
    return (o_intra + o_inter).reshape(B, H, L, V)


def hgrn2_gates(f_logit, lb):
    f32 = f_logit.astype(jnp.float32)
    g = jnp.log(lb + (1.0 - lb) * jax.nn.sigmoid(f32))
    kk = (1.0 - lb) * jax.nn.sigmoid(-f32)
    return g, kk


def to_heads(t, h, d):
    B, L = t.shape[0], t.shape[1]
    return t.reshape(B, L, h, d).transpose(0, 2, 1, 3)


def split_columns(p):
    idx = []
    acc = 0
    for s in IN_SIZES[:-1]:
        acc += s
        idx.append(acc)
    return jnp.split(p, idx, axis=-1)


def setup_inputs(seed: int = 0) -> dict:
    key = jax.random.key(seed)
    ks = jax.random.split(key, 20)
    f32 = jnp.float32
    nrm = lambda k, shape, s: (jax.random.normal(k, shape, f32) * s)
    return {
        "x": nrm(ks[0], (BATCH, SEQ, D_MODEL), 1.0),
        "mem": nrm(ks[1], (BATCH, N_MEM, D_MODEL), 1.0),
        "pre_norm": 1.0 + nrm(ks[2], (DEPTH, D_MODEL), 0.02),
        "post_norm": 1.0 + nrm(ks[3], (DEPTH, D_MODEL), 0.02),
        "w_in": nrm(ks[4], (DEPTH, D_MODEL, IN_COLS), D_MODEL ** -0.5),
        "lambda_q1": nrm(ks[5], (DEPTH, DA), 0.1),
        "lambda_k1": nrm(ks[6], (DEPTH, DA), 0.1),
        "lambda_q2": nrm(ks[7], (DEPTH, DA), 0.1),
        "lambda_k2": nrm(ks[8], (DEPTH, DA), 0.1),
        "diff_subln": 1.0 + nrm(ks[9], (DEPTH, 2 * DA), 0.02),
        "rel_bias": nrm(ks[10], (NUM_BUCKETS, HA), 0.5),
        "lb_logits": nrm(ks[11], (2, DEPTH + 1, HB * KB), 0.1),
        "hgrn_norm": 1.0 + nrm(ks[12], (DEPTH, VB), 0.02),
        "mem_norm": 1.0 + nrm(ks[13], (DEPTH, D_MODEL), 0.02),
        "w_mem_kv": nrm(ks[14], (DEPTH, D_MODEL, 2 * C_WIDTH), D_MODEL ** -0.5),
        "w_branch": nrm(ks[15], (DEPTH, N_BRANCH, A_WIDTH, D_MODEL), A_WIDTH ** -0.5),
        "w_out": nrm(ks[16], (DEPTH, D_MODEL, D_MODEL), D_MODEL ** -0.5),
    }


def reference(x, mem, pre_norm, post_norm, w_in, lambda_q1, lambda_k1, lambda_q2, lambda_k2,
              diff_subln, rel_bias, lb_logits, hgrn_norm, mem_norm, w_mem_kv, w_branch, w_out):
    B, L, _ = x.shape
    lb_all = jnp.cumsum(jax.nn.softmax(lb_logits.astype(jnp.float32), axis=1), axis=1)
    for l in range(DEPTH):
        h = rmsnorm(x, pre_norm[l])
        proj = h @ w_in[l]
        aq, ak, av, az, bq, bff, bfb, bi, bz, cq, cz, gates = split_columns(proj)

        lam_init = 0.8 - 0.6 * math.exp(-0.3 * l)
        lam = (jnp.exp(jnp.sum(lambda_q1[l].astype(jnp.float32) * lambda_k1[l].astype(jnp.float32)))
               - jnp.exp(jnp.sum(lambda_q2[l].astype(jnp.float32) * lambda_k2[l].astype(jnp.float32)))
               + lam_init)
        oa = diff_attention(aq.reshape(B, L, 2, HA, DA), ak.reshape(B, L, 2, HA, DA),
                            av.reshape(B, L, HA, 2 * DA), lam, rel_bias)
        oa = rmsnorm(oa, diff_subln[l]) * (1.0 - lam_init)
        ya = (oa.reshape(B, L, A_WIDTH) * jax.nn.silu(az)) @ w_branch[l, 0]

        g_f, k_f = hgrn2_gates(bff, lb_all[0, l])
        g_b, k_b = hgrn2_gates(bfb, lb_all[1, l])
        qh = to_heads(bq, HB, KB)
        vh = to_heads(bi, HB, VB)
        o_fwd = gla_chunk(qh, to_heads(k_f, HB, KB), vh, to_heads(g_f, HB, KB))
        flip = lambda t: jnp.flip(t, axis=2)
        o_bwd = flip(gla_chunk(flip(qh), flip(to_heads(k_b, HB, KB)), flip(vh),
                               flip(to_heads(g_b, HB, KB))))
        ob = (o_fwd + o_bwd).transpose(0, 2, 1, 3)
        ob = rmsnorm(ob, hgrn_norm[l]).astype(x.dtype)
        yb = (ob.reshape(B, L, B_WIDTH) * jax.nn.silu(bz)) @ w_branch[l, 1]

        m = rmsnorm(mem, mem_norm[l])
        mk, mv = jnp.split(m @ w_mem_kv[l], 2, axis=-1)
        mk = mk.reshape(B, N_MEM, HC, DC)
        mv = mv.reshape(B, N_MEM, HC, DC)
        s = jnp.einsum('blhd,bmhd->bhlm', cq.reshape(B, L, HC, DC), mk).astype(jnp.float32) * (DC ** -0.5)
        p = jax.nn.softmax(s, axis=-1).astype(mv.dtype)
        oc = jnp.einsum('bhlm,bmhd->blhd', p, mv).reshape(B, L, C_WIDTH)
        yc = (oc * jax.nn.silu(cz)) @ w_branch[l, 2]

        ga, gb, gc = jnp.split(jax.nn.sigmoid(gates), N_BRANCH, axis=-1)
        y = (ga * ya + gb * yb + gc * yc) @ w_out[l]
        x = x + rmsnorm(y, post_norm[l])
    return x
```

```python
import math
import os
import numpy as np
import concourse.bass as bass
import concourse.mybir as mybir
from concourse.ap import AP
from concourse.bass_utils import run_bass_kernel_spmd

F32 = mybir.dt.float32
BF16 = mybir.dt.bfloat16
AF = mybir.ActivationFunctionType
ALU = mybir.AluOpType
AX = mybir.AxisListType

ENGS = ("pe", "act", "dve", "pool", "sp")
NT = 2048
D = 1024
EPS = 1e-6
LAM_INIT = 0.2
WX = 1280
WW = 1152


class T:
    __slots__ = ("name", "w", "r", "excl")

    def __init__(self, name, excl=False):
        self.name = name
        self.w = None
        self.r = {}
        self.excl = excl


class Sched:
    def __init__(self, nc):
        self.nc = nc
        self.streams = {e: [] for e in ENGS}
        self.count = {e: 0 for e in ENGS}
        self.seen = {e: {} for e in ENGS}
        self.sems = {}
        self.dma_cnt = {}
        self._ctx = []

    def sem(self, key):
        if key not in self.sems:
            cm = self.nc.semaphore("s_" + str(key))
            self.sems[key] = cm.__enter__()
            self._ctx.append(cm)
        return self.sems[key]

    def _waits(self, eng, reads, writes):
        need = {}

        def add(tok):
            if tok is None:
                return
            k, v = tok
            if need.get(k, 0) < v:
                need[k] = v
        for t in reads:
            add(t.w)
            if t.excl:
                for k, v in t.r.items():
                    if k != eng:
                        add((k, v))
        for t in writes:
            add(t.w)
            for k, v in t.r.items():
                add((k, v))
        if eng == "pe":
            need.pop("pe", None)
        out = []
        seen = self.seen[eng]
        for k, v in need.items():
            if seen.get(k, 0) < v:
                seen[k] = v
                out.append((k, v))
        return out

    def _commit(self, tok, reads, writes):
        for t in writes:
            t.w = tok
            t.r = {}
        for t in reads:
            if t.r.get(tok[0], 0) < tok[1]:
                t.r[tok[0]] = tok[1]

    def op(self, eng, fn, reads=(), writes=()):
        waits = self._waits(eng, reads, writes)
        self.count[eng] += 1
        tok = (eng, self.count[eng])
        self.streams[eng].append((waits, fn, tok))
        self._commit(tok, reads, writes)
        return tok

    def dma(self, fn, semkey, reads=(), writes=(), q="sp"):
        if semkey == "cst":
            self._ncst = getattr(self, "_ncst", 0) + 1
            semkey = "cst%d" % self._ncst
        waits = self._waits(q, reads, writes)
        self.dma_cnt[semkey] = self.dma_cnt.get(semkey, 0) + 1
        tok = (semkey, 16 * self.dma_cnt[semkey])
        self.sem(semkey)
        self.streams[q].append((waits, fn, tok))
        self._commit(tok, reads, writes)
        return tok

    def barrier(self):
        for e in ENGS:
            for k in list(self.sems.keys()) + [x for x in ENGS if x not in self.sems]:
                v = self.count[k] if k in self.count else 16 * self.dma_cnt.get(k, 0)
                if v > 0 and k != e:
                    self.wait_tok(e, (k, v))

    def wait_tok(self, eng, tok):
        if self.seen[eng].get(tok[0], 0) < tok[1]:
            self.seen[eng][tok[0]] = tok[1]
            self.streams[eng].append(([tok], None, None))

    def emit(self, block):
        engmap = {"pe": "tensor", "act": "scalar", "dve": "vector", "pool": "gpsimd", "sp": "sync"}
        for e in ENGS:
            self.sem(e)
        sched = self

        def make(e):
            items = sched.streams[e]
            sched.streams[e] = []

            def body(engine):
                for waits, fn, tok in items:
                    for k, v in waits:
                        engine.wait_ge(sched.sems[k], v)
                    if fn is None:
                        continue
                    ins = fn(engine)
                    if tok[0] == e:
                        ins.then_inc(sched.sems[e], 1)
                    else:
                        ins.then_inc(sched.sems[tok[0]], 16)
            return body
        for e in ENGS:
            getattr(block, engmap[e])(make(e))

    def close(self):
        for cm in reversed(self._ctx):
            cm.__exit__(None, None, None)


def _t5_bucket_np(rel):
    rel = np.asarray(rel, dtype=np.int64)
    nb = 16
    max_exact = 8
    ret = np.where(rel > 0, nb, 0)
    n = np.abs(rel)
    nf = np.maximum(n, 1).astype(np.float32)
    large = max_exact + (np.log(nf / np.float32(max_exact)) / np.float32(math.log(128 / max_exact))
                         * np.float32(nb - max_exact)).astype(np.int32)
    large = np.minimum(large, nb - 1)
    return ret + np.where(n < max_exact, n, large)


C_ID, C_AJ, C_ONE = 0, 128, 256
C_TRI = {"f": 384, "b": 512}
C_SREV = {"f": 640, "b": 768}
C_MASK = {"f": 896, "b": 1024}
C_CIND = 1152
C_N = 1160


def _consts():
    c = np.zeros((128, C_N), np.float32)
    i = np.arange(128)
    c[:, C_ID:C_ID + 128] = np.eye(128)
    c[:, C_AJ:C_AJ + 128] = np.eye(128)[::-1]
    c[:, C_ONE:C_ONE + 128] = 1.0
    s = i[:, None]
    cc = i[None, :]
    same = (s // 64) == (cc // 64)
    c[:, C_TRI["f"]:C_TRI["f"] + 128] = (same & (s <= cc))
    c[:, C_TRI["b"]:C_TRI["b"] + 128] = (same & (s >= cc))
    c[:, C_SREV["f"]:C_SREV["f"] + 128] = (same & (s > cc))
    c[:, C_SREV["b"]:C_SREV["b"] + 128] = (same & (s < cc))
    c[:, C_MASK["f"]:C_MASK["f"] + 128] = (same & (s <= cc))
    c[:, C_MASK["b"]:C_MASK["b"] + 128] = (same & (s >= cc))
    c[:, C_CIND] = (i < 64)
    c[:, C_CIND + 1] = (i >= 64)
    return c


def _oh_tables(mirror):
    x = np.arange(WX)
    delta = 639 - x
    if mirror:
        delta = -delta
    bk = _t5_bucket_np(delta)
    oh = np.zeros((32, WX), np.float32)
    oh[bk, x] = 1.0
    ohfar = np.zeros((32, 256), np.float32)
    bm = int(_t5_bucket_np(np.array([1000 if mirror else -1000]))[0])
    bp = int(_t5_bucket_np(np.array([-1000 if mirror else 1000]))[0])
    ohfar[bm, 0:128] = 1.0
    ohfar[bp, 128:256] = 1.0
    return oh, ohfar


class Prog:
    def __init__(self, phases="0ABCF", debug=False):
        self.phases = phases
        self.debug = debug
        nc = bass.Bass("TRN2", target_bir_lowering=False)
        self.nc = nc
        self.S = Sched(nc)
        self._glob = []
        self._ph = []
        dt = nc.dram_tensor
        self.x_own = dt("x_own", [NT, D], F32, kind="ExternalInput").ap()
        self.x_oth = dt("x_oth", [NT, D], F32, kind="ExternalInput").ap()
        self.mem = dt("mem", [256, D], F32, kind="ExternalInput").ap()
        self.w_in = dt("w_in", [D, 14336], F32, kind="ExternalInput").ap()
        self.w_kv = dt("w_kv", [D, 2048], F32, kind="ExternalInput").ap()
        self.w_br = dt("w_br", [3, D, D], F32, kind="ExternalInput").ap()
        self.w_out = dt("w_out", [D, D], F32, kind="ExternalInput").ap()
        self.vecs = dt("vecs", [8, D], F32, kind="ExternalInput").ap()
        self.small = dt("small", [1, 512], F32, kind="ExternalInput").ap()
        self.relb = dt("relb", [32, 8], F32, kind="ExternalInput").ap()
        self.cst = dt("cst", [128, C_N], F32, kind="ExternalInput").ap()
        self.oh = dt("oh", [32, WX], F32, kind="ExternalInput").ap()
        self.ohfar = dt("ohfar", [32, 256], F32, kind="ExternalInput").ap()
        self.y = dt("y", [NT, D], F32, kind="ExternalOutput").ap()
        self.wscr_h = dt("wscr", [8, WX], F32)
        self.wscr = self.wscr_h.ap()
        if debug:
            self.dbg = dt("dbg", [3, 128, 4 * NT], F32, kind="ExternalOutput").ap()
        self.nwl = 0
        self.out_toks = []

    def _alloc(self, lst, kind, name, shape, dtype):
        self._nalloc = getattr(self, "_nalloc", 0) + 1
        name = "%s_%d" % (name, self._nalloc)
        cm = (self.nc.sbuf_tensor if kind == "sb" else self.nc.psum_tensor)(name, shape, dtype)
        h = cm.__enter__()
        lst.append(cm)
        return h

    def gsb(self, name, shape, dtype):
        return self._alloc(self._glob, "sb", name, shape, dtype)

    def sb(self, name, shape, dtype):
        return self._alloc(self._ph, "sb", name, shape, dtype)

    def end_phase(self):
        if not os.environ.get("K_NOBAR"):
            self.S.barrier()
        with self.nc.Block() as block:
            self.S.emit(block)
        for cm in reversed(self._ph):
            cm.__exit__(None, None, None)
        self._ph = []

    def act(self, out, in_, func, reads, writes, bias=0.0, scale=1.0, accum_out=None):
        kw = {}
        if accum_out is not None:
            kw["accum_out"] = accum_out
        return self.S.op("act", lambda e: e.activation(out=out, in_=in_, func=func, bias=bias, scale=scale, **kw),
                         reads=reads, writes=writes)

    def tt(self, eng, out, in0, in1, op, reads, writes):
        return self.S.op(eng, lambda e: e.tensor_tensor(out=out, in0=in0, in1=in1, op=op), reads=reads, writes=writes)

    def ts(self, eng, out, in0, s1, s2, op0, op1, reads, writes):
        if s2 is None:
            return self.S.op(eng, lambda e: e.tensor_scalar(out=out, in0=in0, scalar1=s1, scalar2=None, op0=op0),
                             reads=reads, writes=writes)
        return self.S.op(eng, lambda e: e.tensor_scalar(out=out, in0=in0, scalar1=s1, scalar2=s2, op0=op0, op1=op1),
                         reads=reads, writes=writes)

    def stt(self, eng, out, in0, scalar, in1, op0, op1, reads, writes):
        return self.S.op(eng, lambda e: e.scalar_tensor_tensor(out=out, in0=in0, scalar=scalar, in1=in1, op0=op0, op1=op1),
                         reads=reads, writes=writes)

    def copy(self, eng, out, in_, reads, writes):
        if eng == "act":
            return self.S.op("act", lambda e: e.copy(out=out, in_=in_), reads=reads, writes=writes)
        return self.S.op(eng, lambda e: e.tensor_copy(out=out, in_=in_), reads=reads, writes=writes)

    def mm(self, mms, reads, writes):
        def fn(e):
            ins = None
            for (o, l, r, st, sp) in mms:
                ins = e.matmul(o, lhsT=l, rhs=r, start=st, stop=sp)
            return ins
        return self.S.op("pe", fn, reads=reads, writes=writes)

    def setup_globals(self):
        nc, S = self.nc, self.S
        self.ps = self._alloc(self._glob, "ps", "psall", [128, 8, 512], F32)
        self.Tps = [T("ps%d" % i, excl=True) for i in range(8)]
        self.cf = self.gsb("cf", [128, C_N], F32)
        self.Tcf = T("cf")
        self.cb = self.gsb("cb", [128, 384], BF16)
        self.Tcb = T("cb")
        self.ones128 = self.gsb("ones128", [128, 128], BF16)
        self.hT = self.gsb("hT", [128, 8, NT], BF16)
        self.ThT = [T("hT%d" % j) for j in range(4)]
        self.merged = self.gsb("merged", [128, 8, NT], BF16)
        self.hTo = self.merged
        self.ThTo = [T("hTo%d" % j) for j in range(4)]
        self.Tmg = [[T("mg%d_%d" % (dc, j)) for j in range(4)] for dc in range(8)]
        self.wst = [self.gsb("wst%d" % i, [128, 8, 128], F32) for i in range(2)]
        self.Twst = [T("wst%d" % i) for i in range(2)]
        self.wbf = [self.gsb("wbf%d" % i, [128, 8, 128], BF16) for i in range(3)]
        self.Twbf = [T("wbf%d" % i) for i in range(3)]
        self.stat = self.gsb("stat", [128, 256], F32)
        self.Tstat = T("stat")
        self.sm = self.gsb("sm", [1, 512], F32)
        self.Tsm = T("sm")
        self.cols = self.gsb("cols", [128, 8], F32)
        self.Tcols = T("cols")
        self.cfar = self.gsb("cfar", [128, 16], F32)
        self.Tcfar = T("cfar")

        S.dma(lambda e: e.dma_start(out=self.cf[:], in_=self.cst[:, :]), "cst", writes=[self.Tcf])
        S.dma(lambda e: e.dma_start(out=self.sm[:], in_=self.small[:, :]), "cst", writes=[self.Tsm])
        self.copy("dve", self.cb[:], self.cf[:, 0:384], [self.Tcf], [self.Tcb])
        self.ts("dve", self.ones128[:], self.cf[:, C_ONE:C_ONE + 128], 1.0 / 128, None, ALU.mult, None,
                [self.Tcf], [self.Tcb])
        S.op("pool", lambda e: e.memset(self.stat[:], 0.0), writes=[self.Tstat])
        self.idb = self.cb[:, 0:128]
        self.ajb = self.cb[:, 128:256]
        self.oneb = self.cb[:, 256:384]

    def psb(self, i):
        return self.ps[:, i, :]

    def load_w(self, src2d, ncol=128, dst=None, Tdst=None):
        S = self.S
        k = self.nwl
        self.nwl += 1
        st, Tst = self.wst[k % 2], self.Twst[k % 2]
        if dst is None:
            self.nwb = getattr(self, "nwb", 0) + 1
            bf, Tbf = self.wbf[self.nwb % 3], self.Twbf[self.nwb % 3]
        else:
            bf, Tbf = dst, Tdst
        src = src2d.rearrange("(c p) n -> p c n", p=128)
        q = "sp"
        S.dma(lambda e: e.dma_start(out=st[:, :, 0:ncol], in_=src), "wst%d" % (k % 2), writes=[Tst], q=q)
        if dst is None:
            S.op("pool", lambda e: e.tensor_copy(out=bf[:, :, 0:ncol], in_=st[:, :, 0:ncol]), reads=[Tst], writes=[Tbf])
        else:
            S.op("pool", lambda e: e.tensor_copy(out=bf, in_=st[:, :, 0:ncol]), reads=[Tst], writes=[Tbf])
        return bf, Tbf

    def norm_transpose(self, src_rows, wbc, Twbc, dst, Tdst, dst_cols, idx, xst, Txst, xbb, Txbb):
        S = self.S
        sl = idx % 2
        xs, Txs = xst[sl], Txst[sl]
        xb, Txb = xbb[sl], Txbb[sl]
        c0 = (idx % 64) * 4
        stat = self.stat
        S.dma(lambda e: e.dma_start(out=xs[:], in_=src_rows), "xld%d" % sl, writes=[Txs])
        self.act(xb[:], xs[:], AF.Square, [Txs], [Txb, self.Tstat], accum_out=stat[:, c0:c0 + 1])
        self.act(stat[:, c0 + 1:c0 + 2], stat[:, c0:c0 + 1], AF.Ln, [self.Tstat], [self.Tstat], bias=EPS, scale=1.0 / D)
        self.act(stat[:, c0 + 2:c0 + 3], stat[:, c0 + 1:c0 + 2], AF.Exp, [self.Tstat], [self.Tstat], scale=-0.5)
        self.stt("dve", xb[:], xs[:], stat[:, c0 + 2:c0 + 3], wbc[:], ALU.mult, ALU.mult,
                 [Txs, self.Tstat, Twbc], [Txb])
        pb = 6 + sl
        pT = self.ps[:, pb, :].bitcast(BF16)

        def tr(e):
            ins = None
            for c in range(8):
                ins = e.transpose(out=pT[:, c * 128:(c + 1) * 128], in_=xb[:, c * 128:(c + 1) * 128], identity=self.idb)
            return ins
        S.op("pe", tr, reads=[Txb, self.Tcb], writes=[self.Tps[pb]])
        eng = "act" if idx % 2 == 0 else "dve"
        self.copy(eng, dst[:, :, dst_cols], pT.rearrange("p (c t) -> p c t", c=8), [self.Tps[pb]], [Tdst])

    def phase0(self):
        S = self.S
        xst = [self.sb("xst%d" % i, [128, D], F32) for i in range(2)]
        Txst = [T("xst%d" % i) for i in range(2)]
        xbb = [self.sb("xbb%d" % i, [128, D], BF16) for i in range(2)]
        Txbb = [T("xbb%d" % i) for i in range(2)]
        pn = self.sb("pn_bc", [128, D], F32)
        Tpn = T("pn")
        S.dma(lambda e: e.dma_start(out=pn[:], in_=self.vecs[0, :].partition_broadcast(128)), "cst", writes=[Tpn])
        for i in range(32):
            own = i < 16
            j = i % 16
            src = (self.x_own if own else self.x_oth)[j * 128:(j + 1) * 128, :]
            dst = self.hT if own else self.hTo
            Td = (self.ThT if own else self.ThTo)[j // 4]
            self.norm_transpose(src, pn, Tpn, dst, Td, slice(j * 128, (j + 1) * 128), i, xst, Txst, xbb, Txbb)

        sm, cols = self.sm, self.cols
        tmp = self.sb("lamtmp", [1, 256], F32)
        Ttmp = T("lamtmp")
        self.tt("dve", tmp[:, 0:64], sm[:, 0:64], sm[:, 64:128], ALU.mult, [self.Tsm], [Ttmp])
        self.tt("dve", tmp[:, 64:128], sm[:, 128:192], sm[:, 192:256], ALU.mult, [self.Tsm], [Ttmp])
        S.op("dve", lambda e: e.reduce_sum(out=tmp[:, 128:130], in_=tmp[:, 0:128].rearrange("p (a b) -> p a b", a=2),
                                           axis=AX.X), reads=[Ttmp], writes=[Ttmp])
        self.act(tmp[:, 130:132], tmp[:, 128:130], AF.Exp, [Ttmp], [Ttmp])
        self.tt("dve", tmp[:, 132:133], tmp[:, 131:132], tmp[:, 130:131], ALU.subtract, [Ttmp], [Ttmp])
        self.ts("dve", tmp[:, 133:134], tmp[:, 132:133], -LAM_INIT, None, ALU.add, None, [Ttmp], [Ttmp])
        onesrow = self.cf[0:1, C_ONE:C_ONE + 128]
        self.mm([(self.ps[:, 0, 0:1], onesrow, tmp[:, 133:134], True, True),
                 (self.ps[:, 0, 1:2], sm[:, 256:384], self.cf[0:1, C_ONE:C_ONE + 1], True, True),
                 (self.ps[:, 0, 2:3], sm[:, 384:512], self.cf[0:1, C_ONE:C_ONE + 1], True, True)],
                [Ttmp, self.Tsm, self.Tcf], [self.Tps[0]])
        self.copy("dve", cols[:, 0:3], self.ps[:, 0, 0:3], [self.Tps[0]], [self.Tcols])

        rb = self.sb("rb", [32, 8], F32)
        ohs = self.sb("ohs", [32, WX], F32)
        ohf = self.sb("ohf", [32, 256], F32)
        Trb = T("rb")
        S.dma(lambda e: e.dma_start(out=rb[:], in_=self.relb[:, :]), "cst", writes=[Trb])
        S.dma(lambda e: e.dma_start(out=ohs[:], in_=self.oh[:, :]), "cst", writes=[Trb])
        S.dma(lambda e: e.dma_start(out=ohf[:], in_=self.ohfar[:, :]), "cst", writes=[Trb])
        wv = self.sb("wv", [8, WX], F32)
        Twv = T("wv")
        self.mm([(self.ps[0:8, 1, 0:512], rb[:], ohs[:, 0:512], True, True),
                 (self.ps[0:8, 2, 0:512], rb[:], ohs[:, 512:1024], True, True),
                 (self.ps[0:8, 3, 0:256], rb[:], ohs[:, 1024:1280], True, True),
                 (self.ps[:, 4, 0:8], ohf[:, 0:128], rb[:], True, True),
                 (self.ps[:, 4, 8:16], ohf[:, 128:256], rb[:], True, True)],
                [Trb], [self.Tps[1], self.Tps[2], self.Tps[3], self.Tps[4]])
        self.ts("dve", wv[:, 0:512], self.ps[0:8, 1, 0:512], 8.0, None, ALU.mult, None, [self.Tps[1]], [Twv])
        self.ts("dve", wv[:, 512:1024], self.ps[0:8, 2, 0:512], 8.0, None, ALU.mult, None, [self.Tps[2]], [Twv])
        self.ts("dve", wv[:, 1024:1280], self.ps[0:8, 3, 0:256], 8.0, None, ALU.mult, None, [self.Tps[3]], [Twv])
        self.copy("dve", self.cfar[:], self.ps[:, 4, 0:16], [self.Tps[4]], [self.Tcfar])
        self.Twscr = T("wscr")
        S.dma(lambda e: e.dma_start(out=self.wscr[:, :], in_=wv[:]), "cst", reads=[Twv], writes=[self.Twscr])
        self.end_phase()

    def sbA(self, name, shape, dtype):
        if not hasattr(self, "_mid"):
            self._mid = []
        assert not self._ph, "allocate mid tensors first"
        return self._alloc(self._mid, "sb", name, shape, dtype)

    def end_phaseA_keep(self):
        self.end_phase()

    def free_mid(self):
        for cm in reversed(self._mid):
            cm.__exit__(None, None, None)
        self._mid = []

    def proj_fm(self, wbf, Tw, c0, src, Tsrc_blk, tok0, ntok, pbank, extra_reads=()):
        mms = []
        for c in range(8):
            mms.append((self.ps[:, pbank, 0:ntok], wbf[:, c, c0:c0 + 128], src[:, c, tok0:tok0 + ntok], c == 0, c == 7))
        return self.mm(mms, [Tw, Tsrc_blk] + list(extra_reads), [self.Tps[pbank]])

    def proj_tm(self, wbf, Tw, c0, ncol, src, Tsrc_blk, tok0, pout, Tpout):
        mms = []
        for c in range(8):
            mms.append((pout, src[:, c, tok0:tok0 + 128], wbf[:, c, c0:c0 + ncol], c == 0, c == 7))
        return self.mm(mms, [Tw, Tsrc_blk], Tpout)

    def branch_merge(self, X, g, Tg, first):
        S = self.S
        gate0 = 11264 + X * 1024
        tmpf = [self.sb("bm_t%d" % i, [128, 512], F32) for i in range(2)]
        Ttmpf = [T("bm_t%d" % i) for i in range(2)]
        tmpb = [self.sb("bm_b%d" % i, [128, 512], BF16) for i in range(2)]
        Ttmpb = [T("bm_b%d" % i) for i in range(2)]
        it = 0
        for dc in range(8):
            wb, Twb = self.load_w(self.w_br[X, :, dc * 128:(dc + 1) * 128])
            wg, Twg = self.load_w(self.w_in[:, gate0 + dc * 128:gate0 + (dc + 1) * 128])
            for J in range(4):
                pby, pbg = (0, 1) if it % 2 == 0 else (2, 3)
                sl = it % 2
                it += 1
                mms = []
                for fc in range(8):
                    mms.append((self.ps[:, pby, :], wb[:, fc, 0:128], g[:, fc, J * 512:(J + 1) * 512], fc == 0, fc == 7))
                self.mm(mms, [Twb, Tg[J]], [self.Tps[pby]])
                self.proj_fm(wg, Twg, 0, self.hT, self.ThT[J], J * 512, 512, pbg)
                tf, Ttf = tmpf[sl], Ttmpf[sl]
                tb, Ttb = tmpb[sl], Ttmpb[sl]
                self.act(tf[:], self.ps[:, pbg, :], AF.Exp, [self.Tps[pbg]], [Ttf], scale=-1.0)
                self.act(tf[:], tf[:], AF.Ln, [Ttf], [Ttf], bias=1.0)
                self.act(tf[:], tf[:], AF.Exp, [Ttf], [Ttf], scale=-1.0)
                mg = self.merged[:, dc, J * 512:(J + 1) * 512]
                if first:
                    self.tt("dve", mg, self.ps[:, pby, :], tf[:], ALU.mult, [self.Tps[pby], Ttf], [self.Tmg[dc][J]])
                else:
                    self.tt("dve", tb[:], self.ps[:, pby, :], tf[:], ALU.mult, [self.Tps[pby], Ttf], [Ttb])
                    self.tt("pool", mg, mg, tb[:], ALU.add, [Ttb, self.Tmg[dc][J]], [self.Tmg[dc][J]])

    def dump_dbg(self, X, g, Tg):
        if not self.debug:
            return
        self.S.dma(lambda e: e.dma_start(out=self.dbg[X], in_=g[:].rearrange("p c t -> p (c t)").bitcast(F32)),
                   "dbg", reads=list(Tg))

    def phaseA(self):
        S = self.S
        gA, TgA = self.gA, self.TgA
        KT = self.sb("KT", [128, 4096], BF16)
        TKT = T("KT")
        QT = self.sb("QT", [128, NT], BF16)
        TQT = T("QT")
        V = self.sb("V", [128, 32, 128], BF16)
        TV = T("V")
        Wf = self.sb("Wf", [128, WW], F32)
        TWf = T("Wf")
        Wb = self.sb("Wb", [128, WW], BF16)
        TWb = T("Wb")
        NE = 3
        E = [self.sb("E%d" % i, [128, 2, 512], BF16) for i in range(NE)]
        TE = [T("E%d" % i) for i in range(NE)]
        acc = self.sb("acc", [128, 4, 512], F32)
        Tacc = [T("acc%d" % i) for i in range(4)]
        Zs = self.sb("Zs", [128, 2, 512], F32)
        TZs = [T("Zs0"), T("Zs1")]
        sqb = self.sb("sqb", [128, 512], BF16)
        Tsqb = T("sqb")
        wzA = [self.sb("wzA%d" % i, [128, 8, 128], BF16) for i in range(2)]
        TwzA = [T("wzA%d" % i) for i in range(2)]
        fin_q = []
        zero_c = self.sb("zero_c", [128, 1], F32)
        Tz = T("zero_c")
        S.op("pool", lambda e: e.memset(zero_c[:], 0.0), writes=[Tz])
        ln08 = math.log(1.0 - LAM_INIT)
        unit = 0
        for h in range(int(os.environ.get('K_AHEADS', '8'))):
            base = h * 512
            wq_, Twq_ = self.load_w(self.w_in[:, base:base + 128])
            for J in range(4):
                pb = J % 2
                self.proj_fm(wq_, Twq_, 0, self.hT, self.ThT[J], J * 512, 512, pb)
                self.copy("act" if J % 2 == 0 else "dve", QT[:, J * 512:(J + 1) * 512], self.ps[:, pb, :], [self.Tps[pb]], [TQT])
            wk_, Twk_ = self.load_w(self.w_in[:, base + 128:base + 256])
            for Jk in range(8):
                pb = Jk % 2
                src, Ts = (self.hT, self.ThT[Jk]) if Jk < 4 else (self.hTo, self.ThTo[Jk - 4])
                self.proj_fm(wk_, Twk_, 0, src, Ts, (Jk % 4) * 512, 512, pb)
                self.copy("act" if Jk % 2 == 0 else "dve", KT[:, Jk * 512:(Jk + 1) * 512], self.ps[:, pb, :], [self.Tps[pb]], [TKT])
            w23, Tw23 = self.load_w(self.w_in[:, base + 256:base + 384])
            self.load_w(self.w_in[:, base + 384:base + 512], dst=wzA[h % 2][:], Tdst=TwzA[h % 2])
            for i4 in range(8):
                pb = i4 % 2
                mms = []
                for ii in range(4):
                    i = i4 * 4 + ii
                    src, Ts = (self.hT, self.ThT[i // 4]) if i < 16 else (self.hTo, self.ThTo[(i - 16) // 4])
                    for c in range(8):
                        mms.append((self.ps[:, pb, ii * 128:(ii + 1) * 128], src[:, c, (i % 16) * 128:(i % 16 + 1) * 128],
                                    w23[:, c, 0:128], c == 0, c == 7))
                Ts_all = self.ThT[i4] if i4 < 4 else self.ThTo[i4 - 4]
                self.mm(mms, [Tw23, Ts_all], [self.Tps[pb]])
                self.copy("act" if i4 % 2 == 0 else "dve", V[:, i4 * 4:(i4 + 1) * 4, :],
                          self.ps[:, pb, :].rearrange("p (a b) -> p a b", a=4), [self.Tps[pb]], [TV])
            hsrc = AP(tensor=self.wscr_h, offset=h * WX, ap=[[1, 128], [1, WW]])
            S.dma(lambda e, hsrc=hsrc: e.dma_start(out=Wf[:], in_=hsrc), "wf", reads=[self.Twscr], writes=[TWf])
            self.copy("pool", Wb[:], Wf[:], [TWf], [TWb])

            for J in range(4):
                def qk(i, slot):
                    b0 = slot * 2
                    mms = [(self.ps[:, b0, :], KT[0:64, i * 128:(i + 1) * 128], QT[0:64, J * 512:(J + 1) * 512], True, False),
                           (self.ps[:, b0 + 1, :], KT[64:128, i * 128:(i + 1) * 128], QT[64:128, J * 512:(J + 1) * 512], True, False)]
                    near = (4 * J - 1) <= i <= (4 * J + 4)
                    rd = [TKT, TQT]
                    if near:
                        off = 639 - ((i - 4 * J) * 128 + 127)
                        mms.append((self.ps[:, b0, :], self.ajb, Wb[:, off:off + 512], False, True))
                        mms.append((self.ps[:, b0 + 1, :], self.ajb, Wb[:, off:off + 512], False, True))
                        rd += [TWb, self.Tcb]
                    else:
                        mms[0] = mms[0][:4] + (True,)
                        mms[1] = mms[1][:4] + (True,)
                    self.mm(mms, rd, [self.Tps[b0], self.Tps[b0 + 1]])
                    return near

                def ex(i, slot, near, u):
                    b0 = slot * 2
                    e_, Te_ = E[u % NE], TE[u % NE]
                    if near:
                        bias = zero_c[:, 0:1]
                        rd = [Tz]
                    else:
                        side = 0 if i < 4 * J else 1
                        bias = self.cfar[:, side * 8 + h:side * 8 + h + 1]
                        rd = [self.Tcfar]
                    self.act(e_[:].rearrange("p a b -> p (a b)"), self.ps[:, b0:b0 + 2, :].rearrange("p a b -> p (a b)"),
                             AF.Exp, [self.Tps[b0], self.Tps[b0 + 1]] + rd, [Te_], bias=bias, scale=0.125)

                def pv(i, u):
                    e_, Te_ = E[u % NE], TE[u % NE]
                    st, sp = (i == 0), (i == 31)
                    mms = [(self.ps[:, 6, :], V[:, i, :], e_[:, 0, :], st, sp),
                           (self.ps[:, 7, :], V[:, i, :], e_[:, 1, :], st, sp)]
                    self.mm(mms, [TV, Te_], [self.Tps[6], self.Tps[7]])
                    if i == 0:
                        self.copy("dve", Zs[:], e_[:], [Te_], [TZs[0], TZs[1]])
                    else:
                        self.tt("dve", Zs[:], Zs[:], e_[:], ALU.add, [Te_, TZs[0], TZs[1]], [TZs[0], TZs[1]])

                nears = {}
                nears[0] = qk(0, 0)
                for i in range(32):
                    if i + 1 < 32:
                        nears[i + 1] = qk(i + 1, (i + 1) % 2)
                    ex(i, i % 2, nears[i], unit + i)
                    pv(i, unit + i)
                    if fin_q and i >= 1:
                        fin_q.pop(0)()
                assert not fin_q
                unit += 32
                onesf = self.cf[:, C_ONE:C_ONE + 128]
                self.mm([(self.ps[:, 4, :], onesf, Zs[:, 0, :], True, True),
                         (self.ps[:, 5, :], onesf, Zs[:, 1, :], True, True)], [TZs[0], TZs[1], self.Tcf],
                        [self.Tps[4], self.Tps[5]])
                A0, A1, A2, A3 = acc[:, 0, :], acc[:, 1, :], acc[:, 2, :], acc[:, 3, :]
                self.copy("act", A0, self.ps[:, 6, :], [self.Tps[6]], [Tacc[0]])
                self.copy("dve", A1, self.ps[:, 7, :], [self.Tps[7]], [Tacc[1]])
                self.copy("act", A2, self.ps[:, 4, :], [self.Tps[4]], [Tacc[2]])
                self.copy("dve", A3, self.ps[:, 5, :], [self.Tps[5]], [Tacc[3]])
                self.act(A2, A2, AF.Ln, [Tacc[2]], [Tacc[2]])
                self.act(A3, A3, AF.Ln, [Tacc[3]], [Tacc[3]])
                wz_, Twz_ = wzA[h % 2], TwzA[h % 2]

                def fin_ops(h=h, J=J, wz_=wz_, Twz_=Twz_):
                    ops = []
                    if os.environ.get("K_AZFIRST"):
                        ops.append(lambda: self.proj_fm(wz_, Twz_, 0, self.hT, self.ThT[J], J * 512, 512, 5))
                    ops.append(lambda: self.act(A2, A2, AF.Exp, [Tacc[2]], [Tacc[2]], scale=-1.0))
                    ops.append(lambda: self.act(A3, A3, AF.Exp, [Tacc[3]], [Tacc[3]], scale=-1.0))
                    ops.append(lambda: self.tt("dve", A0, A0, A2, ALU.mult, [Tacc[0], Tacc[2]], [Tacc[0]]))
                    ops.append(lambda: self.tt("pool", A1, A1, A3, ALU.mult, [Tacc[1], Tacc[3]], [Tacc[1]]))
                    ops.append(lambda: self.stt("dve", A2, A1, self.cols[:, 0:1], A0, ALU.mult, ALU.add,
                                                [Tacc[0], Tacc[1], self.Tcols], [Tacc[2]]))
                    ops.append(lambda: self.act(sqb[:], A2, AF.Square, [Tacc[2]], [Tsqb]))
                    ops.append(lambda: self.mm([(self.ps[:, 4, :], self.ones128[:], sqb[:], True, True)], [Tsqb, self.Tcb], [self.Tps[4]]))
                    ops.append(lambda: self.act(A0, self.ps[:, 4, :], AF.Ln, [self.Tps[4]], [Tacc[0]], bias=EPS))
                    if not os.environ.get("K_AZFIRST"):
                        ops.append(lambda: self.proj_fm(wz_, Twz_, 0, self.hT, self.ThT[J], J * 512, 512, 5))
                    ops.append(lambda: self.act(A1, self.ps[:, 5, :], AF.Exp, [self.Tps[5]], [Tacc[1]], scale=-1.0))
                    ops.append(lambda: self.act(A1, A1, AF.Ln, [Tacc[1]], [Tacc[1]], bias=1.0))
                    ops.append(lambda: self.stt("dve", A0, A0, 0.5, A1, ALU.mult, ALU.add, [Tacc[0], Tacc[1]], [Tacc[0]]))
                    ops.append(lambda: self.act(A0, A0, AF.Exp, [Tacc[0]], [Tacc[0]], scale=-1.0, bias=ln08))
                    ops.append(lambda: self.stt("dve", A3, A2, self.cols[:, 1:2], self.ps[:, 5, :], ALU.mult, ALU.mult,
                                                [Tacc[2], self.Tcols, self.Tps[5]], [Tacc[3]]))
                    ops.append(lambda: self.tt("dve", gA[:, h, J * 512:(J + 1) * 512], A3, A0, ALU.mult, [Tacc[3], Tacc[0]], [TgA[J]]))
                    return ops
                fin_q.extend(fin_ops())
                if os.environ.get("K_NODEFER"):
                    while fin_q:
                        fin_q.pop(0)()
        while fin_q:
            fin_q.pop(0)()
        self.dump_dbg(0, gA, TgA)

    def phaseC(self):
        S = self.S
        gC, TgC = self.gC, self.TgC
        mT = self.sb("mT", [128, 8, 256], BF16)
        TmT = T("mT")
        mn = self.sb("mn_bc", [128, D], F32)
        Tmn = T("mn")
        xst = [self.sb("cxst%d" % i, [128, D], F32) for i in range(2)]
        Txst = [T("cxst%d" % i) for i in range(2)]
        xbb = [self.sb("cxbb%d" % i, [128, D], BF16) for i in range(2)]
        Txbb = [T("cxbb%d" % i) for i in range(2)]
        S.dma(lambda e: e.dma_start(out=mn[:], in_=self.vecs[2, :].partition_broadcast(128)), "cst", writes=[Tmn])
        for i in range(2):
            self.norm_transpose(self.mem[i * 128:(i + 1) * 128, :], mn, Tmn, mT, TmT, slice(i * 128, (i + 1) * 128),
                                48 + i, xst, Txst, xbb, Txbb)
        mkT = self.sb("mkT", [128, 8, 256], BF16)
        TmkT = T("mkT")
        mv = self.sb("mv", [128, 2, 1024], BF16)
        Tmv = T("mv")
        for blk in range(16):
            w, Tw = self.load_w(self.w_kv[:, blk * 128:(blk + 1) * 128])
            pb = blk % 2
            if blk < 8:
                self.proj_fm(w, Tw, 0, mT, TmT, 0, 256, pb)
                self.copy("dve", mkT[:, blk, :], self.ps[:, pb, 0:256], [self.Tps[pb]], [TmkT])
            else:
                for mt in range(2):
                    self.proj_tm(w, Tw, 0, 128, mT, TmT, mt * 128, self.ps[:, pb, mt * 128:(mt + 1) * 128], [self.Tps[pb]])
                self.copy("dve", mv[:, :, (blk - 8) * 128:(blk - 7) * 128],
                          self.ps[:, pb, 0:256].rearrange("p (a b) -> p a b", a=2), [self.Tps[pb]], [Tmv])
        wq = self.sb("cwq", [128, 8, 256], BF16)
        Twq = T("cwq")
        wz = self.sb("cwz", [128, 8, 256], BF16)
        Twz = T("cwz")
        cqT = [self.sb("cqT%d" % i, [128, 2, 512], BF16) for i in range(2)]
        TcqT = [T("cqT%d" % i) for i in range(2)]
        Ec = [self.sb("Ec%d" % i, [128, 2, 512], BF16) for i in range(2)]
        TEc = [T("Ec%d" % i) for i in range(2)]
        fL = [self.sb("fL%d" % i, [128, 512], F32) for i in range(2)]
        TfL = [T("fL%d" % i) for i in range(2)]
        f1 = [self.sb("cf1%d" % i, [128, 512], F32) for i in range(2)]
        Tf1 = [T("cf1%d" % i) for i in range(2)]
        f2 = [self.sb("cf2%d" % i, [128, 512], F32) for i in range(2)]
        Tf2 = [T("cf2%d" % i) for i in range(2)]

        def unit_gen(hc, J, si):
            ba, bb, bc, bd = [4 * si + x for x in range(4)]
            oc = (ba, bb)
            for ch in range(2):
                self.proj_fm(wq, Twq, ch * 128, self.hT, self.ThT[J], J * 512, 512, oc[ch])
                yield
                self.copy("act" if ch == 0 else "dve", cqT[si][:, ch, :], self.ps[:, oc[ch], :], [self.Tps[oc[ch]]], [TcqT[si]])
                yield
            mms = []
            for mt in range(2):
                for ch in range(2):
                    mms.append((self.ps[:, bc + mt, :], mkT[:, hc * 2 + ch, mt * 128:(mt + 1) * 128], cqT[si][:, ch, :], ch == 0, ch == 1))
            self.mm(mms, [TmkT, TcqT[si]], [self.Tps[bc], self.Tps[bd]])
            yield
            self.act(Ec[si][:].rearrange("p a b -> p (a b)"), self.ps[:, bc:bc + 2, :].rearrange("p a b -> p (a b)"), AF.Exp,
                     [self.Tps[bc], self.Tps[bd]], [TEc[si]], scale=1.0 / 16)
            yield
            mms = []
            for ch in range(2):
                for mt in range(2):
                    mms.append((self.ps[:, oc[ch], :], mv[:, mt, hc * 256 + ch * 128:hc * 256 + (ch + 1) * 128], Ec[si][:, mt, :], mt == 0, mt == 1))
            for mt in range(2):
                mms.append((self.ps[:, bc, :], self.oneb, Ec[si][:, mt, :], mt == 0, mt == 1))
            self.mm(mms, [Tmv, TEc[si], self.Tcb], [self.Tps[ba], self.Tps[bb], self.Tps[bc]])
            yield
            self.act(fL[si][:], self.ps[:, bc, :], AF.Ln, [self.Tps[bc]], [TfL[si]])
            yield
            for ch in range(2):
                self.proj_fm(wz, Twz, ch * 128, self.hT, self.ThT[J], J * 512, 512, bd)
                yield
                self.act(f1[si][:], self.ps[:, bd, :], AF.Exp, [self.Tps[bd]], [Tf1[si]], scale=-1.0)
                yield
                self.act(f1[si][:], f1[si][:], AF.Ln, [Tf1[si]], [Tf1[si]], bias=1.0)
                yield
                self.tt("pool", f1[si][:], f1[si][:], fL[si][:], ALU.add, [Tf1[si], TfL[si]], [Tf1[si]])
                yield
                self.act(f1[si][:], f1[si][:], AF.Exp, [Tf1[si]], [Tf1[si]], scale=-1.0)
                yield
                self.tt("dve", f2[si][:], self.ps[:, bd, :], f1[si][:], ALU.mult, [self.Tps[bd], Tf1[si]], [Tf2[si]])
                yield
                self.tt("dve", gC[:, hc * 2 + ch, J * 512:(J + 1) * 512], self.ps[:, oc[ch], :], f2[si][:], ALU.mult,
                        [self.Tps[oc[ch]], Tf2[si]], [TgC[J]])
                yield

        for hc in range(4):
            for k in range(2):
                self.load_w(self.w_in[:, 9216 + hc * 256 + k * 128:9216 + hc * 256 + (k + 1) * 128],
                            dst=wq[:, :, k * 128:(k + 1) * 128], Tdst=Twq)
                self.load_w(self.w_in[:, 10240 + hc * 256 + k * 128:10240 + hc * 256 + (k + 1) * 128],
                            dst=wz[:, :, k * 128:(k + 1) * 128], Tdst=Twz)
            pending = list(range(4))
            active = []
            while pending or active:
                while pending and len(active) < 2:
                    used = [e[1] for e in active]
                    si = 0 if 0 not in used else 1
                    active.append([unit_gen(hc, pending.pop(0), si), si])
                for ent in list(active):
                    try:
                        next(ent[0])
                    except StopIteration:
                        active.remove(ent)
        self.dump_dbg(2, gC, TgC)

    def phaseB(self):
        S = self.S
        gB, TgB = self.gB, self.TgB
        cf = self.cf
        if self.debug:
            for j in range(4):
                S.op("pool", lambda e, j=j: e.memset(gB[:, :, j * 512:(j + 1) * 512], 0.0), writes=[TgB[j]])
        lrows = self.sb("lrows", [8, 4, 128], F32)
        Tlrows = T("lrows")
        for r in range(4):
            S.dma(lambda e, r=r: e.dma_start(out=lrows[:, r, :], in_=self.vecs[3 + r, :].rearrange("(h k) -> h k", k=128)),
                  "cst", writes=[Tlrows])
        lbc = self.sb("lbc", [128, 32], F32)
        Tlbc = T("lbc")
        self.mm([(self.ps[:, 0, r * 8:(r + 1) * 8], lrows[:, r, :], cf[0:8, C_ID:C_ID + 8], True, True) for r in range(4)],
                [Tlrows, self.Tcf], [self.Tps[0]])
        p4 = self.ps[:, 0, 0:32].rearrange("p (r h) -> p r h", r=4)
        l4 = lbc[:, 0:32].rearrange("p (r h) -> p r h", r=4)
        self.copy("dve", lbc[:, 0:32], self.ps[:, 0, 0:32], [self.Tps[0]], [Tlbc])
        for d in range(2):
            self.tt("dve", lbc[:, d * 16:d * 16 + 8], lbc[:, d * 16 + 8:d * 16 + 16], lbc[:, d * 16:d * 16 + 8], ALU.subtract,
                    [Tlbc], [Tlbc])
            self.act(lbc[:, d * 16:d * 16 + 8], lbc[:, d * 16:d * 16 + 8], AF.Exp, [Tlbc], [Tlbc])
            self.act(lbc[:, d * 16:d * 16 + 8], lbc[:, d * 16:d * 16 + 8], AF.Ln, [Tlbc], [Tlbc], bias=1.0)
            self.act(lbc[:, d * 16:d * 16 + 8], lbc[:, d * 16:d * 16 + 8], AF.Exp, [Tlbc], [Tlbc], scale=-1.0)
        cmask = self.sb("cmask", [128, 512], F32)
        Tcmask = T("cmask")
        S.op("pool", lambda e: e.memset(cmask[:], 1.0), writes=[Tcmask])
        S.op("pool", lambda e: e.memset(cmask[:].rearrange("p (c s) -> p c s", s=64)[:, :, 0:1], 0.0), writes=[Tcmask])
        wH = self.sb("wH", [128, 8, 640], BF16)
        TwH = T("wH")
        OF = self.sb("bOF", [128, NT], F32)
        TOF = [T("bOF%d" % j) for j in range(4)]

        class WS:
            pass
        sets = []
        for i in range(2):
            w = WS()
            def mk(nm, shape, dt, i=i, w=w):
                setattr(w, nm, self.sb("b%s%d" % (nm, i), shape, dt))
                setattr(w, "T" + nm, T("b%s%d" % (nm, i)))
            mk("A1", [128, 512], F32); mk("A2", [128, 512], F32); mk("G", [128, 512], F32); mk("P", [128, 512], F32)
            mk("CR", [128, 512], F32); mk("EI", [128, 512], F32); mk("KKT", [128, 512], BF16); mk("KBT", [128, 512], BF16)
            mk("Vb", [128, 4, 128], BF16); mk("DEC", [128, 8], F32)
            mk("KB", [128, 4, 128], BF16); mk("QB", [128, 512], BF16); mk("KI", [128, 512], BF16)
            mk("AM", [128, 4, 128], BF16); mk("SN", [128, 8, 128], F32); mk("SNb", [128, 8, 128], BF16)
            sets.append(w)
        OS = self.sb("bOS", [128, 512], F32); TOS = T("bOS")
        SQ = self.sb("bSQ", [128, 512], BF16); TSQ = T("bSQ")
        F1 = self.sb("bF1", [128, 512], F32); TF1 = T("bF1")
        F2 = self.sb("bF2", [128, 512], F32); TF2 = T("bF2")
        Sp = [self.sb("bSp%d" % d, [128, 128], F32) for d in range(2)]
        TSp = [T("bSp%d" % d) for d in range(2)]
        dn = {0: "f", 1: "b"}
        nblk = 0
        for hb in range(int(os.environ.get('K_BHEADS', '8'))):
            base = 4096 + hb * 640
            for k in range(5):
                self.load_w(self.w_in[:, base + k * 128:base + (k + 1) * 128], dst=wH[:, :, k * 128:(k + 1) * 128], Tdst=TwH)
            for d in range(2):
                S.op("pool", lambda e, d=d: e.memset(Sp[d][:], 0.0), writes=[TSp[d]])
            fw = [(0, jb, True, self.hT, self.ThT) for jb in range(4)]
            bo = [(1, jb, False, self.hTo, self.ThTo) for jb in (3, 2, 1, 0)]
            bw = [(1, jb, True, self.hT, self.ThT) for jb in (3, 2, 1, 0)]
            sweeps = []
            for i in range(4):
                sweeps += [fw[i], bo[i]]
            sweeps += bw
            def block_gen(hb, d, jb, wout, src, Tsrc, si):
                w = sets[si]
                ba, bb, bc, bd = [4 * si + x for x in range(4)]
                dk = dn[d]
                tok0 = jb * 512
                fcol = 128 * (1 + d)
                self.proj_fm(wH, TwH, fcol, src, Tsrc[jb], tok0, 512, ba)
                yield
                mV = []
                for t in range(4):
                    for c in range(8):
                        mV.append((self.ps[:, bb, t * 128:(t + 1) * 128], src[:, c, tok0 + t * 128:tok0 + (t + 1) * 128],
                                   wH[:, c, 384:512], c == 0, c == 7))
                self.mm(mV, [TwH, Tsrc[jb]], [self.Tps[bb]])
                yield
                lbcol = lbc[:, d * 16 + hb:d * 16 + hb + 1]
                self.act(w.A1[:], self.ps[:, ba, :], AF.Exp, [self.Tps[ba]], [w.TA1], scale=-1.0)
                yield
                self.copy("dve", w.Vb[:], self.ps[:, bb, :].rearrange("p (a b) -> p a b", a=4), [self.Tps[bb]], [w.TVb])
                yield
                self.act(w.A2[:], w.A1[:], AF.Ln, [w.TA1], [w.TA2], bias=1.0)
                yield
                self.act(w.A1[:], w.A1[:], AF.Ln, [w.TA1, Tlbc], [w.TA1], bias=1.0, scale=lbcol)
                yield
                self.tt("dve", w.G[:], w.A1[:], w.A2[:], ALU.subtract, [w.TA1, w.TA2], [w.TG])
                yield
                S.op("dve", lambda e, w=w: e.tensor_tensor_scan(out=w.P[:], data0=cmask[:], data1=w.G[:], initial=0.0,
                                                                op0=ALU.mult, op1=ALU.add),
                     reads=[w.TG, Tcmask], writes=[w.TP])
                yield
                self.act(w.A2[:], w.G[:], AF.Exp, [w.TG], [w.TA2])
                yield
                self.ts("pool", w.KKT[:], w.A2[:], -1.0, 1.0, ALU.mult, ALU.add, [w.TA2], [w.TKKT])
                yield
                P3 = w.P[:].rearrange("p (c s) -> p c s", s=64)
                tot = P3[:, :, 63:64]
                totb = tot.to_broadcast([128, 8, 64])
                CR3 = w.CR[:].rearrange("p (c s) -> p c s", s=64)
                if d == 0:
                    self.tt("dve", CR3, totb, P3, ALU.subtract, [w.TP], [w.TCR])
                    yield
                    bsrc, Tb = w.P, w.TP
                else:
                    self.tt("dve", w.CR[:], w.P[:], w.G[:], ALU.subtract, [w.TP, w.TG], [w.TCR])
                    yield
                    self.tt("dve", w.A1[:].rearrange("p (c s) -> p c s", s=64), totb, CR3, ALU.subtract, [w.TP, w.TCR], [w.TA1])
                    yield
                    bsrc, Tb = w.A1, w.TA1
                self.act(w.DEC[:], P3[:, :, 63], AF.Exp, [w.TP], [w.TDEC])
                yield
                if wout:
                    self.act(w.EI[:], bsrc[:], AF.Exp, [Tb], [w.TEI], scale=-1.0)
                    yield
                    self.act(w.A2[:], bsrc[:], AF.Exp, [Tb], [w.TA2])
                    yield
                self.act(w.CR[:], w.CR[:], AF.Exp, [w.TCR], [w.TCR])
                yield
                self.tt("pool", w.KBT[:], w.KKT[:], w.CR[:], ALU.mult, [w.TKKT, w.TCR], [w.TKBT])
                yield
                pT = self.ps[:, bd, :].bitcast(BF16)

                def tr(e, pT=pT, w=w):
                    ins = None
                    for t in range(4):
                        ins = e.transpose(out=pT[:, t * 128:(t + 1) * 128], in_=w.KBT[:, t * 128:(t + 1) * 128], identity=self.idb)
                    return ins
                S.op("pe", tr, reads=[w.TKBT, self.Tcb], writes=[self.Tps[bd]])
                yield
                self.copy("act", w.KB[:], pT[:, 0:512].rearrange("p (a b) -> p a b", a=4), [self.Tps[bd]], [w.TKB])
                yield
                if wout:
                    self.proj_fm(wH, TwH, 0, src, Tsrc[jb], tok0, 512, bc)
                    yield
                    self.tt("dve", w.QB[:], self.ps[:, bc, :], w.A2[:], ALU.mult, [self.Tps[bc], w.TA2], [w.TQB])
                    yield
                    self.tt("pool", w.KI[:], w.KKT[:], w.EI[:], ALU.mult, [w.TKKT, w.TEI], [w.TKI])
                    yield
                torder = range(4) if d == 0 else range(3, -1, -1)
                corder = (0, 1) if d == 0 else (1, 0)
                chunks = [(t, ch) for t in torder for ch in corder]
                mU = []
                for n, (t, ch) in enumerate(chunks):
                    pr = slice(ch * 64, (ch + 1) * 64)
                    ub = ba if ch == 0 else bb
                    mU.append((self.ps[:, ub, t * 128:(t + 1) * 128], w.KB[pr, t, :], w.Vb[pr, t, :], True, True))
                self.mm(mU, [w.TKB, w.TVb], [self.Tps[ba], self.Tps[bb]])
                yield
                yield ("state", d)
                self.copy("pool", w.SN[:, 0, :], Sp[d][:], [TSp[d]], [w.TSN])
                yield
                for n, (t, ch) in enumerate(chunks):
                    ub = ba if ch == 0 else bb
                    ups = self.ps[:, ub, t * 128:(t + 1) * 128]
                    if n < 7:
                        self.stt("dve", w.SN[:, n + 1, :], w.SN[:, n, :], w.DEC[:, t * 2 + ch:t * 2 + ch + 1], ups, ALU.mult, ALU.add,
                                 [w.TSN, w.TDEC, self.Tps[ub]], [w.TSN])
                        yield
                    else:
                        self.stt("dve", Sp[d][:], w.SN[:, n, :], w.DEC[:, t * 2 + ch:t * 2 + ch + 1], ups, ALU.mult, ALU.add,
                                 [w.TSN, w.TDEC, self.Tps[ub]], [TSp[d]])
                        yield
                if wout:
                    self.copy("pool", w.SNb[:], w.SN[:], [w.TSN], [w.TSNb])
                    yield
                    mA = []
                    for t in range(4):
                        mA.append((self.ps[:, bc, t * 128:(t + 1) * 128], w.KI[:, t * 128:(t + 1) * 128], w.QB[:, t * 128:(t + 1) * 128], True, True))
                    self.mm(mA, [w.TKI, w.TQB], [self.Tps[bc]])
                    yield
                    mk_ = cf[:, C_MASK[dk]:C_MASK[dk] + 128].unsqueeze(1).to_broadcast([128, 4, 128])
                    self.tt("dve", w.AM[:], self.ps[:, bc, :].rearrange("p (a b) -> p a b", a=4), mk_, ALU.mult,
                            [self.Tps[bc], self.Tcf], [w.TAM])
                    yield
                    mI = []
                    for t in range(4):
                        mI.append((self.ps[:, bd, t * 128:(t + 1) * 128], w.Vb[:, t, :], w.AM[:, t, :], True, True))
                    self.mm(mI, [w.TVb, w.TAM], [self.Tps[bd]])
                    yield
                    mN = []
                    for n, (t, ch) in enumerate(chunks):
                        c0 = t * 128 + ch * 64
                        mN.append((self.ps[:, bc, c0:c0 + 64], w.SNb[:, n, :], w.QB[:, c0:c0 + 64], True, True))
                    self.mm(mN, [w.TSNb, w.TQB], [self.Tps[bc]])
                    yield
                if wout and d == 0:
                    self.copy("act", w.A1[:], self.ps[:, bc, :], [self.Tps[bc]], [w.TA1])
                    yield
                    self.tt("dve", OF[:, tok0:tok0 + 512], self.ps[:, bd, :], w.A1[:], ALU.add, [self.Tps[bd], w.TA1], [TOF[jb]])
                    yield
                elif wout:
                    self.copy("act", w.A1[:], self.ps[:, bc, :], [self.Tps[bc]], [w.TA1])
                    yield
                    self.tt("dve", w.P[:], self.ps[:, bd, :], OF[:, tok0:tok0 + 512], ALU.add, [self.Tps[bd], TOF[jb]], [w.TP])
                    yield
                    self.tt("pool", w.P[:], w.P[:], w.A1[:], ALU.add, [w.TP, w.TA1], [w.TP])
                    yield
                    self.act(w.KKT[:], w.P[:], AF.Square, [w.TP], [w.TKKT])
                    yield
                    self.mm([(self.ps[:, ba, :], self.ones128[:], w.KKT[:], True, True)], [w.TKKT, self.Tcb], [self.Tps[ba]])
                    yield
                    self.proj_fm(wH, TwH, 512, self.hT, self.ThT[jb], tok0, 512, bb)
                    yield
                    self.act(w.A1[:], self.ps[:, ba, :], AF.Ln, [self.Tps[ba]], [w.TA1], bias=EPS)
                    yield
                    self.act(w.G[:], self.ps[:, bb, :], AF.Exp, [self.Tps[bb]], [w.TG], scale=-1.0)
                    yield
                    self.act(w.G[:], w.G[:], AF.Ln, [w.TG], [w.TG], bias=1.0)
                    yield
                    self.stt("dve", w.A1[:], w.A1[:], 0.5, w.G[:], ALU.mult, ALU.add, [w.TA1, w.TG], [w.TA1])
                    yield
                    self.act(w.A1[:], w.A1[:], AF.Exp, [w.TA1], [w.TA1], scale=-1.0)
                    yield
                    self.stt("dve", w.G[:], w.P[:], self.cols[:, 2:3], self.ps[:, bb, :], ALU.mult, ALU.mult,
                             [w.TP, self.Tcols, self.Tps[bb]], [w.TG])
                    yield
                    self.tt("dve", gB[:, hb, tok0:tok0 + 512], w.G[:], w.A1[:], ALU.mult, [w.TG, w.TA1], [TgB[jb]])
                    yield

            active = []
            pending = [(d, jb, wout, src, Tsrc) for (d, jb, wout, src, Tsrc) in sweeps]
            done_dirs = {0: 0, 1: 0}
            started_dirs = {0: 0, 1: 0}
            nstart = 0
            while pending or active:
                while pending and len(active) < int(os.environ.get("K_BWIN", "2")):
                    (d, jb, wout, src, Tsrc) = pending.pop(0)
                    used = [e[4] for e in active]
                    si = 0 if 0 not in used else 1
                    g = block_gen(hb, d, jb, wout, src, Tsrc, si)
                    active.append([g, d, started_dirs[d], False, si])
                    started_dirs[d] += 1
                    nstart += 1
                progressed = False
                for ent in list(active):
                    g, d, order, blocked, _si = ent
                    if blocked:
                        if done_dirs[d] < order:
                            continue
                        ent[3] = False
                    try:
                        r = next(g)
                        progressed = True
                        if isinstance(r, tuple) and r[0] == "state" and done_dirs[d] < order:
                            ent[3] = True
                    except StopIteration:
                        active.remove(ent)
                        done_dirs[d] += 1
                        progressed = True
                assert progressed
        self.dump_dbg(1, gB, TgB)

    def phaseF(self):
        S = self.S
        Wo = self.sb("Wo", [128, 8, D], BF16)
        TWo = T("Wo")
        for k in range(8):
            self.load_w(self.w_out[:, k * 128:(k + 1) * 128], dst=Wo[:, :, k * 128:(k + 1) * 128], Tdst=TWo)
        pn = self.sb("post_bc", [128, D], F32)
        Tpn = T("postn")
        S.dma(lambda e: e.dma_start(out=pn[:], in_=self.vecs[1, :].partition_broadcast(128)), "cst", writes=[Tpn])
        xs = [self.sb("xres%d" % i, [128, D], F32) for i in range(2)]
        Txs = [T("xres%d" % i) for i in range(2)]
        ob = [self.sb("ob%d" % i, [128, D], F32) for i in range(2)]
        Tob = [T("ob%d" % i) for i in range(2)]
        junk = self.sb("junk", [128, D], BF16)
        Tj = T("junk")
        stat = self.stat
        for t in range(16):
            sl = t % 2
            b0 = 0 if sl == 0 else 2
            S.dma(lambda e, t=t, sl=sl: e.dma_start(out=xs[sl][:], in_=self.x_own[t * 128:(t + 1) * 128, :]),
                  "xres%d" % sl, writes=[Txs[sl]])
            mms = []
            for half in range(2):
                for dc in range(8):
                    mms.append((self.ps[:, b0 + half, :], self.merged[:, dc, t * 128:(t + 1) * 128],
                                Wo[:, dc, half * 512:(half + 1) * 512], dc == 0, dc == 7))
            self.mm(mms, [TWo] + [self.Tmg[dc][t // 4] for dc in range(8)], [self.Tps[b0], self.Tps[b0 + 1]])
            c0 = 128 + t * 4
            yps = self.ps[:, b0:b0 + 2, :].rearrange("p a b -> p (a b)")
            self.act(junk[:], yps, AF.Square, [self.Tps[b0], self.Tps[b0 + 1]], [Tj, self.Tstat], accum_out=stat[:, c0:c0 + 1])
            self.act(stat[:, c0 + 1:c0 + 2], stat[:, c0:c0 + 1], AF.Ln, [self.Tstat], [self.Tstat], bias=EPS, scale=1.0 / D)
            self.act(stat[:, c0 + 2:c0 + 3], stat[:, c0 + 1:c0 + 2], AF.Exp, [self.Tstat], [self.Tstat], scale=-0.5)
            self.stt("dve", ob[sl][:], yps, stat[:, c0 + 2:c0 + 3], pn[:], ALU.mult, ALU.mult,
                     [self.Tps[b0], self.Tps[b0 + 1], self.Tstat, Tpn], [Tob[sl]])
            self.tt("pool", ob[sl][:], ob[sl][:], xs[sl][:], ALU.add, [Tob[sl], Txs[sl]], [Tob[sl]])
            tok = S.dma(lambda e, t=t, sl=sl: e.dma_start(out=self.y[t * 128:(t + 1) * 128, :], in_=ob[sl][:]),
                        "yst%d" % sl, reads=[Tob[sl]])
            self.out_toks.append(tok)
        for tok in self.out_toks[-2:]:
            S.wait_tok("sp", tok)
        if self.debug and S.dma_cnt.get("dbg", 0):
            S.wait_tok("sp", ("dbg", 16 * S.dma_cnt["dbg"]))

    def zero_merged(self):
        for dc in range(8):
            self.S.op("pool", lambda e, dc=dc: e.memset(self.merged[:, dc, :], 0.0), writes=self.Tmg[dc])

    def build(self):
        self.setup_globals()
        self.phase0()
        self._mid = []
        self.gB = self._alloc(self._mid, "sb", "gB", [128, 8, NT], BF16)
        self.TgB = [T("gB%d" % j) for j in range(4)]
        if "B" in self.phases:
            self.phaseB()
            self.end_phase()
        self.gA = self._alloc(self._mid, "sb", "gA", [128, 8, NT], BF16)
        self.TgA = [T("gA%d" % j) for j in range(4)]
        if "A" in self.phases:
            self.phaseA()
            self.end_phase()
        self.S.barrier()
        first = True
        for X, ph, g, Tg in ((0, "A", self.gA, self.TgA), (1, "B", self.gB, self.TgB)):
            if ph in self.phases:
                self.branch_merge(X, g, Tg, first)
                first = False
                self.end_phase()
        self.free_mid()
        if "C" in self.phases:
            self.gC = self.sb("gC", [128, 8, NT], BF16)
            self.TgC = [T("gC%d" % j) for j in range(4)]
            self.phaseC()
            self.branch_merge(2, self.gC, self.TgC, first)
            first = False
            self.end_phase()
        if "F" in self.phases:
            if first:
                self.zero_merged()
            self.phaseF()
        elif self.debug and self.S.dma_cnt.get("dbg", 0):
            self.S.wait_tok("sp", ("dbg", 16 * self.S.dma_cnt["dbg"]))
        self.end_phase()
        return self.nc


def _prep_shared(inputs):
    f = lambda a: np.ascontiguousarray(np.asarray(a, dtype=np.float32))
    w_in = f(inputs["w_in"])[0]
    aq, ak, av, az = w_in[:, 0:1024], w_in[:, 1024:2048], w_in[:, 2048:3072], w_in[:, 3072:4096]
    rest = w_in[:, 4096:]
    blocks = []
    for h in range(8):
        q = np.concatenate([aq[:, h * 64:(h + 1) * 64], aq[:, 512 + h * 64:512 + (h + 1) * 64]], axis=1)
        k = np.concatenate([ak[:, h * 64:(h + 1) * 64], ak[:, 512 + h * 64:512 + (h + 1) * 64]], axis=1)
        blocks += [q, k, av[:, h * 128:(h + 1) * 128], az[:, h * 128:(h + 1) * 128]]
    wA = np.concatenate(blocks, axis=1)
    def wB(swap):
        fo, bo = (2048, 1024) if swap else (1024, 2048)
        bl = []
        for hb in range(8):
            sl = slice(hb * 128, (hb + 1) * 128)
            bl += [rest[:, 0:1024][:, sl], rest[:, fo:fo + 1024][:, sl], rest[:, bo:bo + 1024][:, sl],
                   rest[:, 3072:4096][:, sl], rest[:, 4096:5120][:, sl]]
        return np.concatenate(bl, axis=1)
    tail = rest[:, 5120:]
    w_even = np.ascontiguousarray(np.concatenate([wA, wB(False), tail], axis=1))
    w_odd = np.ascontiguousarray(np.concatenate([wA, wB(True), tail], axis=1))
    lbl = f(inputs["lb_logits"])
    vec_even = np.zeros((8, 1024), np.float32)
    vec_even[0] = f(inputs["pre_norm"])[0]
    vec_even[1] = f(inputs["post_norm"])[0]
    vec_even[2] = f(inputs["mem_norm"])[0]
    vec_even[3], vec_even[4] = lbl[0, 0], lbl[0, 1]
    vec_even[5], vec_even[6] = lbl[1, 0], lbl[1, 1]
    vec_odd = vec_even.copy()
    vec_odd[3], vec_odd[4] = lbl[1, 0], lbl[1, 1]
    vec_odd[5], vec_odd[6] = lbl[0, 0], lbl[0, 1]
    small = np.zeros((1, 512), np.float32)
    small[0, 0:64] = f(inputs["lambda_q1"])[0]
    small[0, 64:128] = f(inputs["lambda_k1"])[0]
    small[0, 128:192] = f(inputs["lambda_q2"])[0]
    small[0, 192:256] = f(inputs["lambda_k2"])[0]
    small[0, 256:384] = f(inputs["diff_subln"])[0]
    small[0, 384:512] = f(inputs["hgrn_norm"])[0]
    return dict(w_even=w_even, w_odd=w_odd, vec_even=vec_even, vec_odd=vec_odd, small=small,
                w_kv=f(inputs["w_mem_kv"])[0], w_br=f(inputs["w_branch"])[0], w_out=f(inputs["w_out"])[0],
                relb=f(inputs["rel_bias"]))


def _in_maps(inputs):
    sh = _prep_shared(inputs)
    x = np.asarray(inputs["x"], dtype=np.float32)
    mem = np.asarray(inputs["mem"], dtype=np.float32)
    cst = _consts()
    tabs = [_oh_tables(False), _oh_tables(True)]
    maps = []
    for c in range(8):
        b, half = c // 2, c % 2
        xb = x[b] if half == 0 else x[b, ::-1]
        maps.append({
            "x_own": np.ascontiguousarray(xb[0:NT]), "x_oth": np.ascontiguousarray(xb[NT:2 * NT]),
            "mem": np.ascontiguousarray(mem[b]),
            "w_in": sh["w_even"] if half == 0 else sh["w_odd"],
            "w_kv": sh["w_kv"], "w_br": sh["w_br"], "w_out": sh["w_out"],
            "vecs": sh["vec_even"] if half == 0 else sh["vec_odd"],
            "small": sh["small"], "relb": sh["relb"], "cst": cst,
            "oh": tabs[half][0], "ohfar": tabs[half][1],
        })
    return maps


def _run(inputs, phases="0ABCF", debug=False, cores=None):
    prog = Prog(phases=phases, debug=debug)
    nc = prog.build()
    maps = _in_maps(inputs)
    ids = list(range(8)) if cores is None else cores
    res = run_bass_kernel_spmd(nc, [maps[c] for c in ids], core_ids=list(range(len(ids))))
    return res, ids


def kernel(**inputs):
    res, ids = _run(inputs, phases=os.environ.get("K_PHASES", "0ABCF"))
    out = np.zeros((4, 4096, D), np.float32)
    for r, c in zip(res.results, ids):
        b, half = c // 2, c % 2
        y = np.asarray(r["y"], dtype=np.float32)
        if half == 0:
            out[b, 0:NT] = y
        else:
            out[b, NT:2 * NT] = y[::-1]
    return out
```

```python
import math
import os
import numpy as np
import concourse.bass as bass
import concourse.mybir as mybir
from concourse.ap import AP
from concourse.bass_utils import run_bass_kernel_spmd

F32 = mybir.dt.float32
BF16 = mybir.dt.bfloat16
AF = mybir.ActivationFunctionType
ALU = mybir.AluOpType
AX = mybir.AxisListType

ENGS = ("pe", "act", "dve", "pool", "sp")
NT = 2048
D = 1024
EPS = 1e-6
LAM_INIT = 0.2
WX = 1280
WW = 1152


class T:
    __slots__ = ("name", "w", "r", "excl")

    def __init__(self, name, excl=False):
        self.name = name
        self.w = None
        self.r = {}
        self.excl = excl


class Sched:
    def __init__(self, nc):
        self.nc = nc
        self.streams = {e: [] for e in ENGS}
        self.count = {e: 0 for e in ENGS}
        self.seen = {e: {} for e in ENGS}
        self.sems = {}
        self.dma_cnt = {}
        self._ctx = []

    def sem(self, key):
        if key not in self.sems:
            cm = self.nc.semaphore("s_" + str(key))
            self.sems[key] = cm.__enter__()
            self._ctx.append(cm)
        return self.sems[key]

    def _waits(self, eng, reads, writes):
        need = {}

        def add(tok):
            if tok is None:
                return
            k, v = tok
            if need.get(k, 0) < v:
                need[k] = v
        for t in reads:
            add(t.w)
            if t.excl:
                for k, v in t.r.items():
                    if k != eng:
                        add((k, v))
        for t in writes:
            add(t.w)
            for k, v in t.r.items():
                add((k, v))
        if eng == "pe":
            need.pop("pe", None)
        out = []
        seen = self.seen[eng]
        for k, v in need.items():
            if seen.get(k, 0) < v:
                seen[k] = v
                out.append((k, v))
        return out

    def _commit(self, tok, reads, writes):
        for t in writes:
            t.w = tok
            t.r = {}
        for t in reads:
            if t.r.get(tok[0], 0) < tok[1]:
                t.r[tok[0]] = tok[1]

    def op(self, eng, fn, reads=(), writes=(), single=False):
        waits = self._waits(eng, reads, writes)
        self.count[eng] += 1
        tok = (eng, self.count[eng])
        self.streams[eng].append((waits, fn, tok, single))
        self._commit(tok, reads, writes)
        return tok

    def dma(self, fn, semkey, reads=(), writes=(), q="sp"):
        if semkey == "cst":
            self._ncst = getattr(self, "_ncst", 0) + 1
            semkey = "cst%d" % self._ncst
        waits = self._waits(q, reads, writes)
        self.dma_cnt[semkey] = self.dma_cnt.get(semkey, 0) + 1
        tok = (semkey, 16 * self.dma_cnt[semkey])
        self.sem(semkey)
        self.streams[q].append((waits, fn, tok, False))
        self._commit(tok, reads, writes)
        return tok

    def barrier(self):
        for e in ENGS:
            for k in list(self.sems.keys()) + [x for x in ENGS if x not in self.sems]:
                v = self.count[k] if k in self.count else 16 * self.dma_cnt.get(k, 0)
                if v > 0 and k != e:
                    self.wait_tok(e, (k, v))

    def wait_tok(self, eng, tok):
        if self.seen[eng].get(tok[0], 0) < tok[1]:
            self.seen[eng][tok[0]] = tok[1]
            self.streams[eng].append(([tok], None, None, False))

    def emit(self, block):
        engmap = {"pe": "tensor", "act": "scalar", "dve": "vector", "pool": "gpsimd", "sp": "sync"}
        for e in ENGS:
            self.sem(e)
        sched = self

        def make(e):
            items = sched.streams[e]
            sched.streams[e] = []

            def body(engine):
                embed = not os.environ.get("K_NOEMBED")
                for waits, fn, tok, single in items:
                    emb = None
                    if single and waits and embed:
                        emb = waits[-1]
                        waits = waits[:-1]
                    for k, v in waits:
                        engine.wait_ge(sched.sems[k], v)
                    if fn is None:
                        continue
                    ins = fn(engine)
                    if emb is not None:
                        ins._wait_ge(sched.sems[emb[0]], emb[1])
                    if tok[0] == e:
                        ins.then_inc(sched.sems[e], 1)
                    else:
                        ins.then_inc(sched.sems[tok[0]], 16)
            return body
        for e in ENGS:
            getattr(block, engmap[e])(make(e))

    def close(self):
        for cm in reversed(self._ctx):
            cm.__exit__(None, None, None)


def _t5_bucket_np(rel):
    rel = np.asarray(rel, dtype=np.int64)
    nb = 16
    max_exact = 8
    ret = np.where(rel > 0, nb, 0)
    n = np.abs(rel)
    nf = np.maximum(n, 1).astype(np.float32)
    large = max_exact + (np.log(nf / np.float32(max_exact)) / np.float32(math.log(128 / max_exact))
                         * np.float32(nb - max_exact)).astype(np.int32)
    large = np.minimum(large, nb - 1)
    return ret + np.where(n < max_exact, n, large)


C_ID, C_AJ, C_ONE = 0, 128, 256
C_TRI = {"f": 384, "b": 512}
C_SREV = {"f": 640, "b": 768}
C_MASK = {"f": 896, "b": 1024}
C_CIND = 1152
C_N = 1160


def _consts():
    c = np.zeros((128, C_N), np.float32)
    i = np.arange(128)
    c[:, C_ID:C_ID + 128] = np.eye(128)
    c[:, C_AJ:C_AJ + 128] = np.eye(128)[::-1]
    c[:, C_ONE:C_ONE + 128] = 1.0
    s = i[:, None]
    cc = i[None, :]
    same = (s // 64) == (cc // 64)
    c[:, C_TRI["f"]:C_TRI["f"] + 128] = (same & (s <= cc))
    c[:, C_TRI["b"]:C_TRI["b"] + 128] = (same & (s >= cc))
    c[:, C_SREV["f"]:C_SREV["f"] + 128] = (same & (s > cc))
    c[:, C_SREV["b"]:C_SREV["b"] + 128] = (same & (s < cc))
    c[:, C_MASK["f"]:C_MASK["f"] + 128] = (same & (s <= cc))
    c[:, C_MASK["b"]:C_MASK["b"] + 128] = (same & (s >= cc))
    c[:, C_CIND] = (i < 64)
    c[:, C_CIND + 1] = (i >= 64)
    return c


def _oh_tables(mirror):
    x = np.arange(WX)
    delta = 639 - x
    if mirror:
        delta = -delta
    bk = _t5_bucket_np(delta)
    oh = np.zeros((32, WX), np.float32)
    oh[bk, x] = 1.0
    ohfar = np.zeros((32, 256), np.float32)
    bm = int(_t5_bucket_np(np.array([1000 if mirror else -1000]))[0])
    bp = int(_t5_bucket_np(np.array([-1000 if mirror else 1000]))[0])
    ohfar[bm, 0:128] = 1.0
    ohfar[bp, 128:256] = 1.0
    return oh, ohfar


class Prog:
    def __init__(self, phases="0ABCF", debug=False):
        self.phases = phases
        self.debug = debug
        nc = bass.Bass("TRN2", target_bir_lowering=False)
        self.nc = nc
        self.S = Sched(nc)
        self._glob = []
        self._ph = []
        dt = nc.dram_tensor
        self.x_own = dt("x_own", [NT, D], F32, kind="ExternalInput").ap()
        self.x_oth = dt("x_oth", [NT, D], F32, kind="ExternalInput").ap()
        self.mem = dt("mem", [256, D], F32, kind="ExternalInput").ap()
        self.w_in = dt("w_in", [D, 14336], F32, kind="ExternalInput").ap()
        self.w_kv = dt("w_kv", [D, 2048], F32, kind="ExternalInput").ap()
        self.w_br = dt("w_br", [3, D, D], F32, kind="ExternalInput").ap()
        self.w_out = dt("w_out", [D, D], F32, kind="ExternalInput").ap()
        self.vecs = dt("vecs", [8, D], F32, kind="ExternalInput").ap()
        self.small = dt("small", [1, 512], F32, kind="ExternalInput").ap()
        self.relb = dt("relb", [32, 8], F32, kind="ExternalInput").ap()
        self.cst = dt("cst", [128, C_N], F32, kind="ExternalInput").ap()
        self.oh = dt("oh", [32, WX], F32, kind="ExternalInput").ap()
        self.ohfar = dt("ohfar", [32, 256], F32, kind="ExternalInput").ap()
        self.y = dt("y", [NT, D], F32, kind="ExternalOutput").ap()
        self.wscr_h = dt("wscr", [8, WX], F32)
        self.wscr = self.wscr_h.ap()
        if debug:
            self.dbg = dt("dbg", [3, 128, 4 * NT], F32, kind="ExternalOutput").ap()
        self.nwl = 0
        self.out_toks = []

    def _alloc(self, lst, kind, name, shape, dtype):
        self._nalloc = getattr(self, "_nalloc", 0) + 1
        name = "%s_%d" % (name, self._nalloc)
        cm = (self.nc.sbuf_tensor if kind == "sb" else self.nc.psum_tensor)(name, shape, dtype)
        h = cm.__enter__()
        lst.append(cm)
        return h

    def gsb(self, name, shape, dtype):
        return self._alloc(self._glob, "sb", name, shape, dtype)

    def sb(self, name, shape, dtype):
        return self._alloc(self._ph, "sb", name, shape, dtype)

    def end_phase(self):
        if not os.environ.get("K_NOBAR"):
            self.S.barrier()
        with self.nc.Block() as block:
            self.S.emit(block)
        for cm in reversed(self._ph):
            cm.__exit__(None, None, None)
        self._ph = []

    def act(self, out, in_, func, reads, writes, bias=0.0, scale=1.0, accum_out=None):
        kw = {}
        if accum_out is not None:
            kw["accum_out"] = accum_out
        return self.S.op("act", lambda e: e.activation(out=out, in_=in_, func=func, bias=bias, scale=scale, **kw),
                         reads=reads, writes=writes, single=True)

    def tt(self, eng, out, in0, in1, op, reads, writes):
        return self.S.op(eng, lambda e: e.tensor_tensor(out=out, in0=in0, in1=in1, op=op), reads=reads, writes=writes, single=True)

    def ts(self, eng, out, in0, s1, s2, op0, op1, reads, writes):
        if s2 is None:
            return self.S.op(eng, lambda e: e.tensor_scalar(out=out, in0=in0, scalar1=s1, scalar2=None, op0=op0),
                             reads=reads, writes=writes, single=True)
        return self.S.op(eng, lambda e: e.tensor_scalar(out=out, in0=in0, scalar1=s1, scalar2=s2, op0=op0, op1=op1),
                         reads=reads, writes=writes, single=True)

    def stt(self, eng, out, in0, scalar, in1, op0, op1, reads, writes):
        return self.S.op(eng, lambda e: e.scalar_tensor_tensor(out=out, in0=in0, scalar=scalar, in1=in1, op0=op0, op1=op1),
                         reads=reads, writes=writes, single=True)

    def copy(self, eng, out, in_, reads, writes):
        if eng == "act":
            return self.S.op("act", lambda e: e.copy(out=out, in_=in_), reads=reads, writes=writes, single=True)
        return self.S.op(eng, lambda e: e.tensor_copy(out=out, in_=in_), reads=reads, writes=writes, single=True)

    def mm(self, mms, reads, writes):
        def fn(e):
            ins = None
            for (o, l, r, st, sp) in mms:
                ins = e.matmul(o, lhsT=l, rhs=r, start=st, stop=sp)
            return ins
        return self.S.op("pe", fn, reads=reads, writes=writes)

    def setup_globals(self):
        nc, S = self.nc, self.S
        self.ps = self._alloc(self._glob, "ps", "psall", [128, 8, 512], F32)
        self.Tps = [T("ps%d" % i, excl=True) for i in range(8)]
        self.cf = self.gsb("cf", [128, C_N], F32)
        self.Tcf = T("cf")
        self.cb = self.gsb("cb", [128, 384], BF16)
        self.Tcb = T("cb")
        self.ones128 = self.gsb("ones128", [128, 128], BF16)
        self.hT = self.gsb("hT", [128, 8, NT], BF16)
        self.ThT = [T("hT%d" % j) for j in range(4)]
        self.merged = self.gsb("merged", [128, 8, NT], BF16)
        self.hTo = self.merged
        self.ThTo = [T("hTo%d" % j) for j in range(4)]
        self.Tmg = [[T("mg%d_%d" % (dc, j)) for j in range(4)] for dc in range(8)]
        self.wst = [self.gsb("wst%d" % i, [128, 8, 128], F32) for i in range(2)]
        self.Twst = [T("wst%d" % i) for i in range(2)]
        self.wbf = [self.gsb("wbf%d" % i, [128, 8, 128], BF16) for i in range(3)]
        self.Twbf = [T("wbf%d" % i) for i in range(3)]
        self.stat = self.gsb("stat", [128, 256], F32)
        self.Tstat = T("stat")
        self.sm = self.gsb("sm", [1, 512], F32)
        self.Tsm = T("sm")
        self.cols = self.gsb("cols", [128, 8], F32)
        self.Tcols = T("cols")
        self.cfar = self.gsb("cfar", [128, 16], F32)
        self.Tcfar = T("cfar")

        S.dma(lambda e: e.dma_start(out=self.cf[:], in_=self.cst[:, :]), "cst", writes=[self.Tcf])
        S.dma(lambda e: e.dma_start(out=self.sm[:], in_=self.small[:, :]), "cst", writes=[self.Tsm])
        self.copy("dve", self.cb[:], self.cf[:, 0:384], [self.Tcf], [self.Tcb])
        self.ts("dve", self.ones128[:], self.cf[:, C_ONE:C_ONE + 128], 1.0 / 128, None, ALU.mult, None,
                [self.Tcf], [self.Tcb])
        S.op("pool", lambda e: e.memset(self.stat[:], 0.0), writes=[self.Tstat])
        self.idb = self.cb[:, 0:128]
        self.ajb = self.cb[:, 128:256]
        self.oneb = self.cb[:, 256:384]

    def psb(self, i):
        return self.ps[:, i, :]

    def load_w(self, src2d, ncol=128, dst=None, Tdst=None):
        S = self.S
        k = self.nwl
        self.nwl += 1
        st, Tst = self.wst[k % 2], self.Twst[k % 2]
        if dst is None:
            self.nwb = getattr(self, "nwb", 0) + 1
            bf, Tbf = self.wbf[self.nwb % 3], self.Twbf[self.nwb % 3]
        else:
            bf, Tbf = dst, Tdst
        src = src2d.rearrange("(c p) n -> p c n", p=128)
        q = "sp"
        S.dma(lambda e: e.dma_start(out=st[:, :, 0:ncol], in_=src), "wst%d" % (k % 2), writes=[Tst], q=q)
        if dst is None:
            S.op("pool", lambda e: e.tensor_copy(out=bf[:, :, 0:ncol], in_=st[:, :, 0:ncol]), reads=[Tst], writes=[Tbf])
        else:
            S.op("pool", lambda e: e.tensor_copy(out=bf, in_=st[:, :, 0:ncol]), reads=[Tst], writes=[Tbf])
        return bf, Tbf

    def norm_transpose(self, src_rows, wbc, Twbc, dst, Tdst, dst_cols, idx, xst, Txst, xbb, Txbb):
        S = self.S
        sl = idx % 2
        xs, Txs = xst[sl], Txst[sl]
        xb, Txb = xbb[sl], Txbb[sl]
        c0 = (idx % 64) * 4
        stat = self.stat
        S.dma(lambda e: e.dma_start(out=xs[:], in_=src_rows), "xld%d" % sl, writes=[Txs])
        self.act(xb[:], xs[:], AF.Square, [Txs], [Txb, self.Tstat], accum_out=stat[:, c0:c0 + 1])
        self.act(stat[:, c0 + 1:c0 + 2], stat[:, c0:c0 + 1], AF.Ln, [self.Tstat], [self.Tstat], bias=EPS, scale=1.0 / D)
        self.act(stat[:, c0 + 2:c0 + 3], stat[:, c0 + 1:c0 + 2], AF.Exp, [self.Tstat], [self.Tstat], scale=-0.5)
        self.stt("dve", xb[:], xs[:], stat[:, c0 + 2:c0 + 3], wbc[:], ALU.mult, ALU.mult,
                 [Txs, self.Tstat, Twbc], [Txb])
        pb = 6 + sl
        pT = self.ps[:, pb, :].bitcast(BF16)

        def tr(e):
            ins = None
            for c in range(8):
                ins = e.transpose(out=pT[:, c * 128:(c + 1) * 128], in_=xb[:, c * 128:(c + 1) * 128], identity=self.idb)
            return ins
        S.op("pe", tr, reads=[Txb, self.Tcb], writes=[self.Tps[pb]])
        eng = "act" if idx % 2 == 0 else "dve"
        self.copy(eng, dst[:, :, dst_cols], pT.rearrange("p (c t) -> p c t", c=8), [self.Tps[pb]], [Tdst])

    def phase0(self):
        S = self.S
        xst = [self.sb("xst%d" % i, [128, D], F32) for i in range(2)]
        Txst = [T("xst%d" % i) for i in range(2)]
        xbb = [self.sb("xbb%d" % i, [128, D], BF16) for i in range(2)]
        Txbb = [T("xbb%d" % i) for i in range(2)]
        pn = self.sb("pn_bc", [128, D], F32)
        Tpn = T("pn")
        S.dma(lambda e: e.dma_start(out=pn[:], in_=self.vecs[0, :].partition_broadcast(128)), "cst", writes=[Tpn])
        for i in range(32):
            own = i < 16
            j = i % 16
            src = (self.x_own if own else self.x_oth)[j * 128:(j + 1) * 128, :]
            dst = self.hT if own else self.hTo
            Td = (self.ThT if own else self.ThTo)[j // 4]
            self.norm_transpose(src, pn, Tpn, dst, Td, slice(j * 128, (j + 1) * 128), i, xst, Txst, xbb, Txbb)

        sm, cols = self.sm, self.cols
        tmp = self.sb("lamtmp", [1, 256], F32)
        Ttmp = T("lamtmp")
        self.tt("dve", tmp[:, 0:64], sm[:, 0:64], sm[:, 64:128], ALU.mult, [self.Tsm], [Ttmp])
        self.tt("dve", tmp[:, 64:128], sm[:, 128:192], sm[:, 192:256], ALU.mult, [self.Tsm], [Ttmp])
        S.op("dve", lambda e: e.reduce_sum(out=tmp[:, 128:130], in_=tmp[:, 0:128].rearrange("p (a b) -> p a b", a=2),
                                           axis=AX.X), reads=[Ttmp], writes=[Ttmp])
        self.act(tmp[:, 130:132], tmp[:, 128:130], AF.Exp, [Ttmp], [Ttmp])
        self.tt("dve", tmp[:, 132:133], tmp[:, 131:132], tmp[:, 130:131], ALU.subtract, [Ttmp], [Ttmp])
        self.ts("dve", tmp[:, 133:134], tmp[:, 132:133], -LAM_INIT, None, ALU.add, None, [Ttmp], [Ttmp])
        onesrow = self.cf[0:1, C_ONE:C_ONE + 128]
        self.mm([(self.ps[:, 0, 0:1], onesrow, tmp[:, 133:134], True, True),
                 (self.ps[:, 0, 1:2], sm[:, 256:384], self.cf[0:1, C_ONE:C_ONE + 1], True, True),
                 (self.ps[:, 0, 2:3], sm[:, 384:512], self.cf[0:1, C_ONE:C_ONE + 1], True, True)],
                [Ttmp, self.Tsm, self.Tcf], [self.Tps[0]])
        self.copy("dve", cols[:, 0:3], self.ps[:, 0, 0:3], [self.Tps[0]], [self.Tcols])

        rb = self.sb("rb", [32, 8], F32)
        ohs = self.sb("ohs", [32, WX], F32)
        ohf = self.sb("ohf", [32, 256], F32)
        Trb = T("rb")
        S.dma(lambda e: e.dma_start(out=rb[:], in_=self.relb[:, :]), "cst", writes=[Trb])
        S.dma(lambda e: e.dma_start(out=ohs[:], in_=self.oh[:, :]), "cst", writes=[Trb])
        S.dma(lambda e: e.dma_start(out=ohf[:], in_=self.ohfar[:, :]), "cst", writes=[Trb])
        wv = self.sb("wv", [8, WX], F32)
        Twv = T("wv")
        self.mm([(self.ps[0:8, 1, 0:512], rb[:], ohs[:, 0:512], True, True),
                 (self.ps[0:8, 2, 0:512], rb[:], ohs[:, 512:1024], True, True),
                 (self.ps[0:8, 3, 0:256], rb[:], ohs[:, 1024:1280], True, True),
                 (self.ps[:, 4, 0:8], ohf[:, 0:128], rb[:], True, True),
                 (self.ps[:, 4, 8:16], ohf[:, 128:256], rb[:], True, True)],
                [Trb], [self.Tps[1], self.Tps[2], self.Tps[3], self.Tps[4]])
        self.ts("dve", wv[:, 0:512], self.ps[0:8, 1, 0:512], 8.0, None, ALU.mult, None, [self.Tps[1]], [Twv])
        self.ts("dve", wv[:, 512:1024], self.ps[0:8, 2, 0:512], 8.0, None, ALU.mult, None, [self.Tps[2]], [Twv])
        self.ts("dve", wv[:, 1024:1280], self.ps[0:8, 3, 0:256], 8.0, None, ALU.mult, None, [self.Tps[3]], [Twv])
        self.copy("dve", self.cfar[:], self.ps[:, 4, 0:16], [self.Tps[4]], [self.Tcfar])
        self.Twscr = T("wscr")
        S.dma(lambda e: e.dma_start(out=self.wscr[:, :], in_=wv[:]), "cst", reads=[Twv], writes=[self.Twscr])
        self.end_phase()

    def sbA(self, name, shape, dtype):
        if not hasattr(self, "_mid"):
            self._mid = []
        assert not self._ph, "allocate mid tensors first"
        return self._alloc(self._mid, "sb", name, shape, dtype)

    def end_phaseA_keep(self):
        self.end_phase()

    def free_mid(self):
        for cm in reversed(self._mid):
            cm.__exit__(None, None, None)
        self._mid = []

    def proj_fm(self, wbf, Tw, c0, src, Tsrc_blk, tok0, ntok, pbank, extra_reads=()):
        mms = []
        for c in range(8):
            mms.append((self.ps[:, pbank, 0:ntok], wbf[:, c, c0:c0 + 128], src[:, c, tok0:tok0 + ntok], c == 0, c == 7))
        return self.mm(mms, [Tw, Tsrc_blk] + list(extra_reads), [self.Tps[pbank]])

    def proj_tm(self, wbf, Tw, c0, ncol, src, Tsrc_blk, tok0, pout, Tpout):
        mms = []
        for c in range(8):
            mms.append((pout, src[:, c, tok0:tok0 + 128], wbf[:, c, c0:c0 + ncol], c == 0, c == 7))
        return self.mm(mms, [Tw, Tsrc_blk], Tpout)

    def branch_merge(self, X, g, Tg, first):
        S = self.S
        gate0 = 11264 + X * 1024
        tmpf = [self.sb("bm_t%d" % i, [128, 512], F32) for i in range(2)]
        Ttmpf = [T("bm_t%d" % i) for i in range(2)]
        tmpb = [self.sb("bm_b%d" % i, [128, 512], BF16) for i in range(2)]
        Ttmpb = [T("bm_b%d" % i) for i in range(2)]
        it = 0
        for dc in range(8):
            wb, Twb = self.load_w(self.w_br[X, :, dc * 128:(dc + 1) * 128])
            wg, Twg = self.load_w(self.w_in[:, gate0 + dc * 128:gate0 + (dc + 1) * 128])
            for J in range(4):
                pby, pbg = (0, 1) if it % 2 == 0 else (2, 3)
                sl = it % 2
                it += 1
                mms = []
                for fc in range(8):
                    mms.append((self.ps[:, pby, :], wb[:, fc, 0:128], g[:, fc, J * 512:(J + 1) * 512], fc == 0, fc == 7))
                self.mm(mms, [Twb, Tg[J]], [self.Tps[pby]])
                self.proj_fm(wg, Twg, 0, self.hT, self.ThT[J], J * 512, 512, pbg)
                tf, Ttf = tmpf[sl], Ttmpf[sl]
                tb, Ttb = tmpb[sl], Ttmpb[sl]
                self.act(tf[:], self.ps[:, pbg, :], AF.Exp, [self.Tps[pbg]], [Ttf], scale=-1.0)
                self.act(tf[:], tf[:], AF.Ln, [Ttf], [Ttf], bias=1.0)
                self.act(tf[:], tf[:], AF.Exp, [Ttf], [Ttf], scale=-1.0)
                mg = self.merged[:, dc, J * 512:(J + 1) * 512]
                if first:
                    self.tt("dve", mg, self.ps[:, pby, :], tf[:], ALU.mult, [self.Tps[pby], Ttf], [self.Tmg[dc][J]])
                else:
                    self.tt("dve", tb[:], self.ps[:, pby, :], tf[:], ALU.mult, [self.Tps[pby], Ttf], [Ttb])
                    self.tt("pool", mg, mg, tb[:], ALU.add, [Ttb, self.Tmg[dc][J]], [self.Tmg[dc][J]])

    def dump_dbg(self, X, g, Tg):
        if not self.debug:
            return
        self.S.dma(lambda e: e.dma_start(out=self.dbg[X], in_=g[:].rearrange("p c t -> p (c t)").bitcast(F32)),
                   "dbg", reads=list(Tg))

    def phaseA(self):
        S = self.S
        gA, TgA = self.gA, self.TgA
        KT = self.sb("KT", [128, 4096], BF16)
        TKT = T("KT")
        QT = self.sb("QT", [128, NT], BF16)
        TQT = T("QT")
        V = self.sb("V", [128, 32, 128], BF16)
        TV = T("V")
        Wf = self.sb("Wf", [128, WW], F32)
        TWf = T("Wf")
        Wb = self.sb("Wb", [128, WW], BF16)
        TWb = T("Wb")
        NE = 3
        E = [self.sb("E%d" % i, [128, 2, 512], BF16) for i in range(NE)]
        TE = [T("E%d" % i) for i in range(NE)]
        acc = self.sb("acc", [128, 4, 512], F32)
        Tacc = [T("acc%d" % i) for i in range(4)]
        Zs2 = [self.sb("Zs%d" % i, [128, 2, 512], F32) for i in range(2)]
        TZs2 = [[T("Zs%d_0" % i), T("Zs%d_1" % i)] for i in range(2)]
        sqb = self.sb("sqb", [128, 512], BF16)
        Tsqb = T("sqb")
        wzA = [self.sb("wzA%d" % i, [128, 8, 128], BF16) for i in range(2)]
        TwzA = [T("wzA%d" % i) for i in range(2)]
        fin_q = []
        pend = []
        zero_c = self.sb("zero_c", [128, 1], F32)
        Tz = T("zero_c")
        S.op("pool", lambda e: e.memset(zero_c[:], 0.0), writes=[Tz])
        ln08 = math.log(1.0 - LAM_INIT)
        unit = 0
        for h in range(int(os.environ.get('K_AHEADS', '8'))):
            base = h * 512
            wq_, Twq_ = self.load_w(self.w_in[:, base:base + 128])
            for J in range(4):
                pb = J % 2
                self.proj_fm(wq_, Twq_, 0, self.hT, self.ThT[J], J * 512, 512, pb)
                self.copy("act" if J % 2 == 0 else "dve", QT[:, J * 512:(J + 1) * 512], self.ps[:, pb, :], [self.Tps[pb]], [TQT])
            wk_, Twk_ = self.load_w(self.w_in[:, base + 128:base + 256])
            for Jk in range(8):
                pb = Jk % 2
                src, Ts = (self.hT, self.ThT[Jk]) if Jk < 4 else (self.hTo, self.ThTo[Jk - 4])
                self.proj_fm(wk_, Twk_, 0, src, Ts, (Jk % 4) * 512, 512, pb)
                self.copy("act" if Jk % 2 == 0 else "dve", KT[:, Jk * 512:(Jk + 1) * 512], self.ps[:, pb, :], [self.Tps[pb]], [TKT])
            w23, Tw23 = self.load_w(self.w_in[:, base + 256:base + 384])
            self.load_w(self.w_in[:, base + 384:base + 512], dst=wzA[h % 2][:], Tdst=TwzA[h % 2])
            for i4 in range(8):
                pb = i4 % 2
                mms = []
                for ii in range(4):
                    i = i4 * 4 + ii
                    src, Ts = (self.hT, self.ThT[i // 4]) if i < 16 else (self.hTo, self.ThTo[(i - 16) // 4])
                    for c in range(8):
                        mms.append((self.ps[:, pb, ii * 128:(ii + 1) * 128], src[:, c, (i % 16) * 128:(i % 16 + 1) * 128],
                                    w23[:, c, 0:128], c == 0, c == 7))
                Ts_all = self.ThT[i4] if i4 < 4 else self.ThTo[i4 - 4]
                self.mm(mms, [Tw23, Ts_all], [self.Tps[pb]])
                self.copy("act" if i4 % 2 == 0 else "dve", V[:, i4 * 4:(i4 + 1) * 4, :],
                          self.ps[:, pb, :].rearrange("p (a b) -> p a b", a=4), [self.Tps[pb]], [TV])
            hsrc = AP(tensor=self.wscr_h, offset=h * WX, ap=[[1, 128], [1, WW]])
            S.dma(lambda e, hsrc=hsrc: e.dma_start(out=Wf[:], in_=hsrc), "wf", reads=[self.Twscr], writes=[TWf])
            self.copy("pool", Wb[:], Wf[:], [TWf], [TWb])

            for J in range(4):
                def qk(i, slot):
                    b0 = slot * 2
                    mms = [(self.ps[:, b0, :], KT[0:64, i * 128:(i + 1) * 128], QT[0:64, J * 512:(J + 1) * 512], True, False),
                           (self.ps[:, b0 + 1, :], KT[64:128, i * 128:(i + 1) * 128], QT[64:128, J * 512:(J + 1) * 512], True, False)]
                    near = (4 * J - 1) <= i <= (4 * J + 4)
                    rd = [TKT, TQT]
                    if near:
                        off = 639 - ((i - 4 * J) * 128 + 127)
                        mms.append((self.ps[:, b0, :], self.ajb, Wb[:, off:off + 512], False, True))
                        mms.append((self.ps[:, b0 + 1, :], self.ajb, Wb[:, off:off + 512], False, True))
                        rd += [TWb, self.Tcb]
                    else:
                        mms[0] = mms[0][:4] + (True,)
                        mms[1] = mms[1][:4] + (True,)
                    self.mm(mms, rd, [self.Tps[b0], self.Tps[b0 + 1]])
                    return near

                def ex(i, slot, near, u):
                    b0 = slot * 2
                    e_, Te_ = E[u % NE], TE[u % NE]
                    if near:
                        bias = zero_c[:, 0:1]
                        rd = [Tz]
                    else:
                        side = 0 if i < 4 * J else 1
                        bias = self.cfar[:, side * 8 + h:side * 8 + h + 1]
                        rd = [self.Tcfar]
                    self.act(e_[:].rearrange("p a b -> p (a b)"), self.ps[:, b0:b0 + 2, :].rearrange("p a b -> p (a b)"),
                             AF.Exp, [self.Tps[b0], self.Tps[b0 + 1]] + rd, [Te_], bias=bias, scale=0.125)

                def pv(i, u):
                    e_, Te_ = E[u % NE], TE[u % NE]
                    st, sp = (i == 0), (i == 31)
                    mms = [(self.ps[:, 6, :], V[:, i, :], e_[:, 0, :], st, sp),
                           (self.ps[:, 7, :], V[:, i, :], e_[:, 1, :], st, sp)]
                    self.mm(mms, [TV, Te_], [self.Tps[6], self.Tps[7]])
                    zf = Zs[:].rearrange("p a b -> p (a b)")
                    ef = e_[:].rearrange("p a b -> p (a b)")
                    SPL = 896
                    if i == 0:
                        self.copy("dve", zf[:, 0:SPL], ef[:, 0:SPL], [Te_], [TZs[0]])
                        self.copy("pool", zf[:, SPL:1024], ef[:, SPL:1024], [Te_], [TZs[1]])
                    else:
                        self.tt("dve", zf[:, 0:SPL], zf[:, 0:SPL], ef[:, 0:SPL], ALU.add, [Te_, TZs[0]], [TZs[0]])
                        self.tt("pool", zf[:, SPL:1024], zf[:, SPL:1024], ef[:, SPL:1024], ALU.add, [Te_, TZs[1]], [TZs[1]])

                Zs, TZs = Zs2[(unit // 32) % 2], TZs2[(unit // 32) % 2]
                nears = {}
                nears[0] = qk(0, 0)
                for i in range(32):
                    if i + 1 < 32:
                        nears[i + 1] = qk(i + 1, (i + 1) % 2)
                    ex(i, i % 2, nears[i], unit + i)
                    if i == 0 and pend:
                        imm_, fo_ = pend.pop(0)
                        imm_()
                        fin_q.extend(fo_())
                    pv(i, unit + i)
                    if fin_q and i >= 1:
                        f_ = fin_q.pop(0)
                        if f_ is not None:
                            f_()
                assert not fin_q
                unit += 32
                onesf = self.cf[:, C_ONE:C_ONE + 128]
                A0, A1, A2, A3 = acc[:, 0, :], acc[:, 1, :], acc[:, 2, :], acc[:, 3, :]
                wz_, Twz_ = wzA[h % 2], TwzA[h % 2]

                def imm(Zs=Zs, TZs=TZs):
                    self.copy("dve", A0, self.ps[:, 6, :], [self.Tps[6]], [Tacc[0]])
                    self.copy("dve", A1, self.ps[:, 7, :], [self.Tps[7]], [Tacc[1]])
                    self.mm([(self.ps[:, 4, :], onesf, Zs[:, 0, :], True, True),
                             (self.ps[:, 5, :], onesf, Zs[:, 1, :], True, True)], [TZs[0], TZs[1], self.Tcf],
                            [self.Tps[4], self.Tps[5]])

                def fin_ops(h=h, J=J, wz_=wz_, Twz_=Twz_):
                    ops = []
                    ops.append(lambda: self.act(A2, self.ps[:, 4, :], AF.Ln, [self.Tps[4]], [Tacc[2]]))
                    ops.append(lambda: self.act(A3, self.ps[:, 5, :], AF.Ln, [self.Tps[5]], [Tacc[3]]))
                    ops.append(lambda: self.act(A2, A2, AF.Exp, [Tacc[2]], [Tacc[2]], scale=-1.0))
                    ops.append(lambda: self.act(A3, A3, AF.Exp, [Tacc[3]], [Tacc[3]], scale=-1.0))
                    ops.append(lambda: self.tt("dve", A0, A0, A2, ALU.mult, [Tacc[0], Tacc[2]], [Tacc[0]]))
                    ops.append(lambda: self.tt("dve", A1, A1, A3, ALU.mult, [Tacc[1], Tacc[3]], [Tacc[1]]))
                    ops.append(lambda: self.stt("dve", A2, A1, self.cols[:, 0:1], A0, ALU.mult, ALU.add,
                                                [Tacc[0], Tacc[1], self.Tcols], [Tacc[2]]))
                    ops.append(None)
                    ops.append(lambda: self.act(sqb[:], A2, AF.Square, [Tacc[2]], [Tsqb]))
                    ops.append(lambda: self.mm([(self.ps[:, 4, :], self.ones128[:], sqb[:], True, True)], [Tsqb, self.Tcb], [self.Tps[4]]))
                    ops.append(lambda: self.proj_fm(wz_, Twz_, 0, self.hT, self.ThT[J], J * 512, 512, 5))
                    ops.append(None)
                    ops.append(lambda: self.act(A0, self.ps[:, 4, :], AF.Ln, [self.Tps[4]], [Tacc[0]], bias=EPS))
                    ops.append(lambda: self.act(A1, self.ps[:, 5, :], AF.Exp, [self.Tps[5]], [Tacc[1]], scale=-1.0))
                    ops.append(lambda: self.act(A1, A1, AF.Ln, [Tacc[1]], [Tacc[1]], bias=1.0))
                    ops.append(lambda: self.stt("dve", A0, A0, 0.5, A1, ALU.mult, ALU.add, [Tacc[0], Tacc[1]], [Tacc[0]]))
                    ops.append(None)
                    ops.append(lambda: self.act(A0, A0, AF.Exp, [Tacc[0]], [Tacc[0]], scale=-1.0, bias=ln08))
                    ops.append(lambda: self.stt("dve", A3, A2, self.cols[:, 1:2], self.ps[:, 5, :], ALU.mult, ALU.mult,
                                                [Tacc[2], self.Tcols, self.Tps[5]], [Tacc[3]]))
                    ops.append(lambda: self.tt("dve", gA[:, h, J * 512:(J + 1) * 512], A3, A0, ALU.mult, [Tacc[3], Tacc[0]], [TgA[J]]))
                    return ops
                pend.append((imm, fin_ops))
        while pend:
            imm_, fo_ = pend.pop(0)
            imm_()
            fin_q.extend(fo_())
        while fin_q:
            f_ = fin_q.pop(0)
            if f_ is not None:
                f_()
        self.dump_dbg(0, gA, TgA)

    def phaseC(self):
        S = self.S
        gC, TgC = self.gC, self.TgC
        mT = self.sb("mT", [128, 8, 256], BF16)
        TmT = T("mT")
        mn = self.sb("mn_bc", [128, D], F32)
        Tmn = T("mn")
        xst = [self.sb("cxst%d" % i, [128, D], F32) for i in range(2)]
        Txst = [T("cxst%d" % i) for i in range(2)]
        xbb = [self.sb("cxbb%d" % i, [128, D], BF16) for i in range(2)]
        Txbb = [T("cxbb%d" % i) for i in range(2)]
        S.dma(lambda e: e.dma_start(out=mn[:], in_=self.vecs[2, :].partition_broadcast(128)), "cst", writes=[Tmn])
        for i in range(2):
            self.norm_transpose(self.mem[i * 128:(i + 1) * 128, :], mn, Tmn, mT, TmT, slice(i * 128, (i + 1) * 128),
                                48 + i, xst, Txst, xbb, Txbb)
        mkT = self.sb("mkT", [128, 8, 256], BF16)
        TmkT = T("mkT")
        mv = self.sb("mv", [128, 2, 1024], BF16)
        Tmv = T("mv")
        for blk in range(16):
            w, Tw = self.load_w(self.w_kv[:, blk * 128:(blk + 1) * 128])
            pb = blk % 2
            if blk < 8:
                self.proj_fm(w, Tw, 0, mT, TmT, 0, 256, pb)
                self.copy("dve", mkT[:, blk, :], self.ps[:, pb, 0:256], [self.Tps[pb]], [TmkT])
            else:
                for mt in range(2):
                    self.proj_tm(w, Tw, 0, 128, mT, TmT, mt * 128, self.ps[:, pb, mt * 128:(mt + 1) * 128], [self.Tps[pb]])
                self.copy("dve", mv[:, :, (blk - 8) * 128:(blk - 7) * 128],
                          self.ps[:, pb, 0:256].rearrange("p (a b) -> p a b", a=2), [self.Tps[pb]], [Tmv])
        wq = self.sb("cwq", [128, 8, 256], BF16)
        Twq = T("cwq")
        wz = self.sb("cwz", [128, 8, 256], BF16)
        Twz = T("cwz")
        cqT = [self.sb("cqT%d" % i, [128, 2, 512], BF16) for i in range(2)]
        TcqT = [T("cqT%d" % i) for i in range(2)]
        Ec = [self.sb("Ec%d" % i, [128, 2, 512], BF16) for i in range(2)]
        TEc = [T("Ec%d" % i) for i in range(2)]
        fL = [self.sb("fL%d" % i, [128, 512], F32) for i in range(2)]
        TfL = [T("fL%d" % i) for i in range(2)]
        f1 = [self.sb("cf1%d" % i, [128, 512], F32) for i in range(2)]
        Tf1 = [T("cf1%d" % i) for i in range(2)]
        f2 = [self.sb("cf2%d" % i, [128, 512], F32) for i in range(2)]
        Tf2 = [T("cf2%d" % i) for i in range(2)]

        def unit_gen(hc, J, si):
            ba, bb, bc, bd = [4 * si + x for x in range(4)]
            oc = (ba, bb)
            for ch in range(2):
                self.proj_fm(wq, Twq, ch * 128, self.hT, self.ThT[J], J * 512, 512, oc[ch])
                yield
                self.copy("act" if ch == 0 else "dve", cqT[si][:, ch, :], self.ps[:, oc[ch], :], [self.Tps[oc[ch]]], [TcqT[si]])
                yield
            mms = []
            for mt in range(2):
                for ch in range(2):
                    mms.append((self.ps[:, bc + mt, :], mkT[:, hc * 2 + ch, mt * 128:(mt + 1) * 128], cqT[si][:, ch, :], ch == 0, ch == 1))
            self.mm(mms, [TmkT, TcqT[si]], [self.Tps[bc], self.Tps[bd]])
            yield
            self.act(Ec[si][:].rearrange("p a b -> p (a b)"), self.ps[:, bc:bc + 2, :].rearrange("p a b -> p (a b)"), AF.Exp,
                     [self.Tps[bc], self.Tps[bd]], [TEc[si]], scale=1.0 / 16)
            yield
            mms = []
            for ch in range(2):
                for mt in range(2):
                    mms.append((self.ps[:, oc[ch], :], mv[:, mt, hc * 256 + ch * 128:hc * 256 + (ch + 1) * 128], Ec[si][:, mt, :], mt == 0, mt == 1))
            for mt in range(2):
                mms.append((self.ps[:, bc, :], self.oneb, Ec[si][:, mt, :], mt == 0, mt == 1))
            self.mm(mms, [Tmv, TEc[si], self.Tcb], [self.Tps[ba], self.Tps[bb], self.Tps[bc]])
            yield
            self.act(fL[si][:], self.ps[:, bc, :], AF.Ln, [self.Tps[bc]], [TfL[si]])
            yield
            for ch in range(2):
                self.proj_fm(wz, Twz, ch * 128, self.hT, self.ThT[J], J * 512, 512, bd)
                yield
                self.act(f1[si][:], self.ps[:, bd, :], AF.Exp, [self.Tps[bd]], [Tf1[si]], scale=-1.0)
                yield
                self.act(f1[si][:], f1[si][:], AF.Ln, [Tf1[si]], [Tf1[si]], bias=1.0)
                yield
                self.tt("pool", f1[si][:], f1[si][:], fL[si][:], ALU.add, [Tf1[si], TfL[si]], [Tf1[si]])
                yield
                self.act(f1[si][:], f1[si][:], AF.Exp, [Tf1[si]], [Tf1[si]], scale=-1.0)
                yield
                self.tt("dve", f2[si][:], self.ps[:, bd, :], f1[si][:], ALU.mult, [self.Tps[bd], Tf1[si]], [Tf2[si]])
                yield
                self.tt("dve", gC[:, hc * 2 + ch, J * 512:(J + 1) * 512], self.ps[:, oc[ch], :], f2[si][:], ALU.mult,
                        [self.Tps[oc[ch]], Tf2[si]], [TgC[J]])
                yield

        for hc in range(4):
            for k in range(2):
                self.load_w(self.w_in[:, 9216 + hc * 256 + k * 128:9216 + hc * 256 + (k + 1) * 128],
                            dst=wq[:, :, k * 128:(k + 1) * 128], Tdst=Twq)
                self.load_w(self.w_in[:, 10240 + hc * 256 + k * 128:10240 + hc * 256 + (k + 1) * 128],
                            dst=wz[:, :, k * 128:(k + 1) * 128], Tdst=Twz)
            pending = list(range(4))
            active = []
            while pending or active:
                while pending and len(active) < 2:
                    used = [e[1] for e in active]
                    si = 0 if 0 not in used else 1
                    active.append([unit_gen(hc, pending.pop(0), si), si])
                for ent in list(active):
                    try:
                        next(ent[0])
                    except StopIteration:
                        active.remove(ent)
        self.dump_dbg(2, gC, TgC)

    def phaseB(self):
        S = self.S
        gB, TgB = self.gB, self.TgB
        cf = self.cf
        if self.debug:
            for j in range(4):
                S.op("pool", lambda e, j=j: e.memset(gB[:, :, j * 512:(j + 1) * 512], 0.0), writes=[TgB[j]])
        lrows = self.sb("lrows", [8, 4, 128], F32)
        Tlrows = T("lrows")
        for r in range(4):
            S.dma(lambda e, r=r: e.dma_start(out=lrows[:, r, :], in_=self.vecs[3 + r, :].rearrange("(h k) -> h k", k=128)),
                  "cst", writes=[Tlrows])
        lbc = self.sb("lbc", [128, 32], F32)
        Tlbc = T("lbc")
        self.mm([(self.ps[:, 0, r * 8:(r + 1) * 8], lrows[:, r, :], cf[0:8, C_ID:C_ID + 8], True, True) for r in range(4)],
                [Tlrows, self.Tcf], [self.Tps[0]])
        p4 = self.ps[:, 0, 0:32].rearrange("p (r h) -> p r h", r=4)
        l4 = lbc[:, 0:32].rearrange("p (r h) -> p r h", r=4)
        self.copy("dve", lbc[:, 0:32], self.ps[:, 0, 0:32], [self.Tps[0]], [Tlbc])
        for d in range(2):
            self.tt("dve", lbc[:, d * 16:d * 16 + 8], lbc[:, d * 16 + 8:d * 16 + 16], lbc[:, d * 16:d * 16 + 8], ALU.subtract,
                    [Tlbc], [Tlbc])
            self.act(lbc[:, d * 16:d * 16 + 8], lbc[:, d * 16:d * 16 + 8], AF.Exp, [Tlbc], [Tlbc])
            self.act(lbc[:, d * 16:d * 16 + 8], lbc[:, d * 16:d * 16 + 8], AF.Ln, [Tlbc], [Tlbc], bias=1.0)
            self.act(lbc[:, d * 16:d * 16 + 8], lbc[:, d * 16:d * 16 + 8], AF.Exp, [Tlbc], [Tlbc], scale=-1.0)
        cmask = self.sb("cmask", [128, 512], F32)
        Tcmask = T("cmask")
        S.op("pool", lambda e: e.memset(cmask[:], 1.0), writes=[Tcmask])
        S.op("pool", lambda e: e.memset(cmask[:].rearrange("p (c s) -> p c s", s=64)[:, :, 0:1], 0.0), writes=[Tcmask])
        wH = self.sb("wH", [128, 8, 640], BF16)
        TwH = T("wH")
        OF = self.sb("bOF", [128, NT], F32)
        TOF = [T("bOF%d" % j) for j in range(4)]

        class WS:
            pass
        sets = []
        for i in range(2):
            w = WS()
            def mk(nm, shape, dt, i=i, w=w):
                setattr(w, nm, self.sb("b%s%d" % (nm, i), shape, dt))
                setattr(w, "T" + nm, T("b%s%d" % (nm, i)))
            mk("A1", [128, 512], F32); mk("A2", [128, 512], F32); mk("G", [128, 512], F32); mk("P", [128, 512], F32)
            mk("CR", [128, 512], F32); mk("EI", [128, 512], F32); mk("KKT", [128, 512], BF16); mk("KBT", [128, 512], BF16)
            mk("Vb", [128, 4, 128], BF16); mk("DEC", [128, 8], F32)
            mk("KB", [128, 4, 128], BF16); mk("QB", [128, 512], BF16); mk("KI", [128, 512], BF16)
            mk("AM", [128, 4, 128], BF16); mk("SN", [128, 8, 128], F32); mk("SNb", [128, 8, 128], BF16)
            sets.append(w)
        OS = self.sb("bOS", [128, 512], F32); TOS = T("bOS")
        SQ = self.sb("bSQ", [128, 512], BF16); TSQ = T("bSQ")
        F1 = self.sb("bF1", [128, 512], F32); TF1 = T("bF1")
        F2 = self.sb("bF2", [128, 512], F32); TF2 = T("bF2")
        Sp = [self.sb("bSp%d" % d, [128, 128], F32) for d in range(2)]
        TSp = [T("bSp%d" % d) for d in range(2)]
        dn = {0: "f", 1: "b"}
        nblk = 0
        for hb in range(int(os.environ.get('K_BHEADS', '8'))):
            base = 4096 + hb * 640
            for k in range(5):
                self.load_w(self.w_in[:, base + k * 128:base + (k + 1) * 128], dst=wH[:, :, k * 128:(k + 1) * 128], Tdst=TwH)
            for d in range(2):
                S.op("pool", lambda e, d=d: e.memset(Sp[d][:], 0.0), writes=[TSp[d]])
            fw = [(0, jb, True, self.hT, self.ThT) for jb in range(4)]
            bo = [(1, jb, False, self.hTo, self.ThTo) for jb in (3, 2, 1, 0)]
            bw = [(1, jb, True, self.hT, self.ThT) for jb in (3, 2, 1, 0)]
            sweeps = []
            for i in range(4):
                sweeps += [fw[i], bo[i]]
            sweeps += bw
            def block_gen(hb, d, jb, wout, src, Tsrc, si):
                w = sets[si]
                ba, bb, bc, bd = [4 * si + x for x in range(4)]
                dk = dn[d]
                tok0 = jb * 512
                fcol = 128 * (1 + d)
                self.proj_fm(wH, TwH, fcol, src, Tsrc[jb], tok0, 512, ba)
                yield
                mV = []
                for t in range(4):
                    for c in range(8):
                        mV.append((self.ps[:, bb, t * 128:(t + 1) * 128], src[:, c, tok0 + t * 128:tok0 + (t + 1) * 128],
                                   wH[:, c, 384:512], c == 0, c == 7))
                self.mm(mV, [TwH, Tsrc[jb]], [self.Tps[bb]])
                yield
                lbcol = lbc[:, d * 16 + hb:d * 16 + hb + 1]
                self.act(w.A1[:], self.ps[:, ba, :], AF.Exp, [self.Tps[ba]], [w.TA1], scale=-1.0)
                yield
                self.copy("dve", w.Vb[:], self.ps[:, bb, :].rearrange("p (a b) -> p a b", a=4), [self.Tps[bb]], [w.TVb])
                yield
                self.act(w.A2[:], w.A1[:], AF.Ln, [w.TA1], [w.TA2], bias=1.0)
                yield
                self.act(w.A1[:], w.A1[:], AF.Ln, [w.TA1, Tlbc], [w.TA1], bias=1.0, scale=lbcol)
                yield
                self.tt("dve", w.G[:], w.A1[:], w.A2[:], ALU.subtract, [w.TA1, w.TA2], [w.TG])
                yield
                S.op("dve", lambda e, w=w: e.tensor_tensor_scan(out=w.P[:], data0=cmask[:], data1=w.G[:], initial=0.0,
                                                                op0=ALU.mult, op1=ALU.add),
                     reads=[w.TG, Tcmask], writes=[w.TP])
                yield
                self.act(w.A2[:], w.G[:], AF.Exp, [w.TG], [w.TA2])
                yield
                self.ts("pool", w.KKT[:], w.A2[:], -1.0, 1.0, ALU.mult, ALU.add, [w.TA2], [w.TKKT])
                yield
                P3 = w.P[:].rearrange("p (c s) -> p c s", s=64)
                tot = P3[:, :, 63:64]
                totb = tot.to_broadcast([128, 8, 64])
                CR3 = w.CR[:].rearrange("p (c s) -> p c s", s=64)
                if d == 0:
                    self.tt("dve", CR3, totb, P3, ALU.subtract, [w.TP], [w.TCR])
                    yield
                    bsrc, Tb = w.P, w.TP
                else:
                    self.tt("dve", w.CR[:], w.P[:], w.G[:], ALU.subtract, [w.TP, w.TG], [w.TCR])
                    yield
                    self.tt("dve", w.A1[:].rearrange("p (c s) -> p c s", s=64), totb, CR3, ALU.subtract, [w.TP, w.TCR], [w.TA1])
                    yield
                    bsrc, Tb = w.A1, w.TA1
                self.act(w.DEC[:], P3[:, :, 63], AF.Exp, [w.TP], [w.TDEC])
                yield
                if wout:
                    self.act(w.EI[:], bsrc[:], AF.Exp, [Tb], [w.TEI], scale=-1.0)
                    yield
                    self.act(w.A2[:], bsrc[:], AF.Exp, [Tb], [w.TA2])
                    yield
                self.act(w.CR[:], w.CR[:], AF.Exp, [w.TCR], [w.TCR])
                yield
                self.tt("pool", w.KBT[:], w.KKT[:], w.CR[:], ALU.mult, [w.TKKT, w.TCR], [w.TKBT])
                yield
                pT = self.ps[:, bd, :].bitcast(BF16)

                def tr(e, pT=pT, w=w):
                    ins = None
                    for t in range(4):
                        ins = e.transpose(out=pT[:, t * 128:(t + 1) * 128], in_=w.KBT[:, t * 128:(t + 1) * 128], identity=self.idb)
                    return ins
                S.op("pe", tr, reads=[w.TKBT, self.Tcb], writes=[self.Tps[bd]])
                yield
                self.copy("act", w.KB[:], pT[:, 0:512].rearrange("p (a b) -> p a b", a=4), [self.Tps[bd]], [w.TKB])
                yield
                if wout:
                    self.proj_fm(wH, TwH, 0, src, Tsrc[jb], tok0, 512, bc)
                    yield
                    self.tt("dve", w.QB[:], self.ps[:, bc, :], w.A2[:], ALU.mult, [self.Tps[bc], w.TA2], [w.TQB])
                    yield
                    self.tt("pool", w.KI[:], w.KKT[:], w.EI[:], ALU.mult, [w.TKKT, w.TEI], [w.TKI])
                    yield
                torder = range(4) if d == 0 else range(3, -1, -1)
                corder = (0, 1) if d == 0 else (1, 0)
                chunks = [(t, ch) for t in torder for ch in corder]
                mU = []
                for n, (t, ch) in enumerate(chunks):
                    pr = slice(ch * 64, (ch + 1) * 64)
                    ub = ba if ch == 0 else bb
                    mU.append((self.ps[:, ub, t * 128:(t + 1) * 128], w.KB[pr, t, :], w.Vb[pr, t, :], True, True))
                self.mm(mU, [w.TKB, w.TVb], [self.Tps[ba], self.Tps[bb]])
                yield
                yield ("state", d)
                self.copy("pool", w.SN[:, 0, :], Sp[d][:], [TSp[d]], [w.TSN])
                yield
                for n, (t, ch) in enumerate(chunks):
                    ub = ba if ch == 0 else bb
                    ups = self.ps[:, ub, t * 128:(t + 1) * 128]
                    if n < 7:
                        self.stt("dve", w.SN[:, n + 1, :], w.SN[:, n, :], w.DEC[:, t * 2 + ch:t * 2 + ch + 1], ups, ALU.mult, ALU.add,
                                 [w.TSN, w.TDEC, self.Tps[ub]], [w.TSN])
                        yield
                    else:
                        self.stt("dve", Sp[d][:], w.SN[:, n, :], w.DEC[:, t * 2 + ch:t * 2 + ch + 1], ups, ALU.mult, ALU.add,
                                 [w.TSN, w.TDEC, self.Tps[ub]], [TSp[d]])
                        yield
                if wout:
                    self.copy("pool", w.SNb[:], w.SN[:], [w.TSN], [w.TSNb])
                    yield
                    mA = []
                    for t in range(4):
                        mA.append((self.ps[:, bc, t * 128:(t + 1) * 128], w.KI[:, t * 128:(t + 1) * 128], w.QB[:, t * 128:(t + 1) * 128], True, True))
                    self.mm(mA, [w.TKI, w.TQB], [self.Tps[bc]])
                    yield
                    mk_ = cf[:, C_MASK[dk]:C_MASK[dk] + 128].unsqueeze(1).to_broadcast([128, 4, 128])
                    self.tt("dve", w.AM[:], self.ps[:, bc, :].rearrange("p (a b) -> p a b", a=4), mk_, ALU.mult,
                            [self.Tps[bc], self.Tcf], [w.TAM])
                    yield
                    mI = []
                    for t in range(4):
                        mI.append((self.ps[:, bd, t * 128:(t + 1) * 128], w.Vb[:, t, :], w.AM[:, t, :], True, True))
                    self.mm(mI, [w.TVb, w.TAM], [self.Tps[bd]])
                    yield
                    mN = []
                    for n, (t, ch) in enumerate(chunks):
                        c0 = t * 128 + ch * 64
                        mN.append((self.ps[:, bc, c0:c0 + 64], w.SNb[:, n, :], w.QB[:, c0:c0 + 64], True, True))
                    self.mm(mN, [w.TSNb, w.TQB], [self.Tps[bc]])
                    yield
                if wout and d == 0:
                    self.copy("act", w.A1[:], self.ps[:, bc, :], [self.Tps[bc]], [w.TA1])
                    yield
                    self.tt("dve", OF[:, tok0:tok0 + 512], self.ps[:, bd, :], w.A1[:], ALU.add, [self.Tps[bd], w.TA1], [TOF[jb]])
                    yield
                elif wout:
                    self.copy("act", w.A1[:], self.ps[:, bc, :], [self.Tps[bc]], [w.TA1])
                    yield
                    self.tt("dve", w.P[:], self.ps[:, bd, :], OF[:, tok0:tok0 + 512], ALU.add, [self.Tps[bd], TOF[jb]], [w.TP])
                    yield
                    self.tt("pool", w.P[:], w.P[:], w.A1[:], ALU.add, [w.TP, w.TA1], [w.TP])
                    yield
                    self.act(w.KKT[:], w.P[:], AF.Square, [w.TP], [w.TKKT])
                    yield
                    self.mm([(self.ps[:, ba, :], self.ones128[:], w.KKT[:], True, True)], [w.TKKT, self.Tcb], [self.Tps[ba]])
                    yield
                    self.proj_fm(wH, TwH, 512, self.hT, self.ThT[jb], tok0, 512, bb)
                    yield
                    self.act(w.A1[:], self.ps[:, ba, :], AF.Ln, [self.Tps[ba]], [w.TA1], bias=EPS)
                    yield
                    self.act(w.G[:], self.ps[:, bb, :], AF.Exp, [self.Tps[bb]], [w.TG], scale=-1.0)
                    yield
                    self.act(w.G[:], w.G[:], AF.Ln, [w.TG], [w.TG], bias=1.0)
                    yield
                    self.stt("dve", w.A1[:], w.A1[:], 0.5, w.G[:], ALU.mult, ALU.add, [w.TA1, w.TG], [w.TA1])
                    yield
                    self.act(w.A1[:], w.A1[:], AF.Exp, [w.TA1], [w.TA1], scale=-1.0)
                    yield
                    self.stt("dve", w.G[:], w.P[:], self.cols[:, 2:3], self.ps[:, bb, :], ALU.mult, ALU.mult,
                             [w.TP, self.Tcols, self.Tps[bb]], [w.TG])
                    yield
                    self.tt("dve", gB[:, hb, tok0:tok0 + 512], w.G[:], w.A1[:], ALU.mult, [w.TG, w.TA1], [TgB[jb]])
                    yield

            active = []
            pending = [(d, jb, wout, src, Tsrc) for (d, jb, wout, src, Tsrc) in sweeps]
            done_dirs = {0: 0, 1: 0}
            started_dirs = {0: 0, 1: 0}
            nstart = 0
            while pending or active:
                while pending and len(active) < int(os.environ.get("K_BWIN", "2")):
                    (d, jb, wout, src, Tsrc) = pending.pop(0)
                    used = [e[4] for e in active]
                    si = 0 if 0 not in used else 1
                    g = block_gen(hb, d, jb, wout, src, Tsrc, si)
                    active.append([g, d, started_dirs[d], False, si])
                    started_dirs[d] += 1
                    nstart += 1
                progressed = False
                for ent in list(active):
                    g, d, order, blocked, _si = ent
                    if blocked:
                        if done_dirs[d] < order:
                            continue
                        ent[3] = False
                    try:
                        r = next(g)
                        progressed = True
                        if isinstance(r, tuple) and r[0] == "state" and done_dirs[d] < order:
                            ent[3] = True
                    except StopIteration:
                        active.remove(ent)
                        done_dirs[d] += 1
                        progressed = True
                assert progressed
        self.dump_dbg(1, gB, TgB)

    def phaseF(self):
        S = self.S
        Wo = self.sb("Wo", [128, 8, D], BF16)
        TWo = T("Wo")
        for k in range(8):
            self.load_w(self.w_out[:, k * 128:(k + 1) * 128], dst=Wo[:, :, k * 128:(k + 1) * 128], Tdst=TWo)
        pn = self.sb("post_bc", [128, D], F32)
        Tpn = T("postn")
        S.dma(lambda e: e.dma_start(out=pn[:], in_=self.vecs[1, :].partition_broadcast(128)), "cst", writes=[Tpn])
        xs = [self.sb("xres%d" % i, [128, D], F32) for i in range(2)]
        Txs = [T("xres%d" % i) for i in range(2)]
        ob = [self.sb("ob%d" % i, [128, D], F32) for i in range(2)]
        Tob = [T("ob%d" % i) for i in range(2)]
        junk = self.sb("junk", [128, D], BF16)
        Tj = T("junk")
        stat = self.stat
        for t in range(16):
            sl = t % 2
            b0 = 0 if sl == 0 else 2
            S.dma(lambda e, t=t, sl=sl: e.dma_start(out=xs[sl][:], in_=self.x_own[t * 128:(t + 1) * 128, :]),
                  "xres%d" % sl, writes=[Txs[sl]])
            mms = []
            for half in range(2):
                for dc in range(8):
                    mms.append((self.ps[:, b0 + half, :], self.merged[:, dc, t * 128:(t + 1) * 128],
                                Wo[:, dc, half * 512:(half + 1) * 512], dc == 0, dc == 7))
            self.mm(mms, [TWo] + [self.Tmg[dc][t // 4] for dc in range(8)], [self.Tps[b0], self.Tps[b0 + 1]])
            c0 = 128 + t * 4
            yps = self.ps[:, b0:b0 + 2, :].rearrange("p a b -> p (a b)")
            self.act(junk[:], yps, AF.Square, [self.Tps[b0], self.Tps[b0 + 1]], [Tj, self.Tstat], accum_out=stat[:, c0:c0 + 1])
            self.act(stat[:, c0 + 1:c0 + 2], stat[:, c0:c0 + 1], AF.Ln, [self.Tstat], [self.Tstat], bias=EPS, scale=1.0 / D)
            self.act(stat[:, c0 + 2:c0 + 3], stat[:, c0 + 1:c0 + 2], AF.Exp, [self.Tstat], [self.Tstat], scale=-0.5)
            self.stt("dve", ob[sl][:], yps, stat[:, c0 + 2:c0 + 3], pn[:], ALU.mult, ALU.mult,
                     [self.Tps[b0], self.Tps[b0 + 1], self.Tstat, Tpn], [Tob[sl]])
            self.tt("pool", ob[sl][:], ob[sl][:], xs[sl][:], ALU.add, [Tob[sl], Txs[sl]], [Tob[sl]])
            tok = S.dma(lambda e, t=t, sl=sl: e.dma_start(out=self.y[t * 128:(t + 1) * 128, :], in_=ob[sl][:]),
                        "yst%d" % sl, reads=[Tob[sl]])
            self.out_toks.append(tok)
        for tok in self.out_toks[-2:]:
            S.wait_tok("sp", tok)
        if self.debug and S.dma_cnt.get("dbg", 0):
            S.wait_tok("sp", ("dbg", 16 * S.dma_cnt["dbg"]))

    def zero_merged(self):
        for dc in range(8):
            self.S.op("pool", lambda e, dc=dc: e.memset(self.merged[:, dc, :], 0.0), writes=self.Tmg[dc])

    def build(self):
        self.setup_globals()
        self.phase0()
        self._mid = []
        self.gB = self._alloc(self._mid, "sb", "gB", [128, 8, NT], BF16)
        self.TgB = [T("gB%d" % j) for j in range(4)]
        if "B" in self.phases:
            self.phaseB()
            self.end_phase()
        self.gA = self._alloc(self._mid, "sb", "gA", [128, 8, NT], BF16)
        self.TgA = [T("gA%d" % j) for j in range(4)]
        if "A" in self.phases:
            self.phaseA()
            self.end_phase()
        self.S.barrier()
        first = True
        for X, ph, g, Tg in ((0, "A", self.gA, self.TgA), (1, "B", self.gB, self.TgB)):
            if ph in self.phases:
                self.branch_merge(X, g, Tg, first)
                first = False
                self.end_phase()
        self.free_mid()
        if "C" in self.phases:
            self.gC = self.sb("gC", [128, 8, NT], BF16)
            self.TgC = [T("gC%d" % j) for j in range(4)]
            self.phaseC()
            self.branch_merge(2, self.gC, self.TgC, first)
            first = False
            self.end_phase()
        if "F" in self.phases:
            if first:
                self.zero_merged()
            self.phaseF()
        elif self.debug and self.S.dma_cnt.get("dbg", 0):
            self.S.wait_tok("sp", ("dbg", 16 * self.S.dma_cnt["dbg"]))
        self.end_phase()
        return self.nc


def _prep_shared(inputs):
    f = lambda a: np.ascontiguousarray(np.asarray(a, dtype=np.float32))
    w_in = f(inputs["w_in"])[0]
    aq, ak, av, az = w_in[:, 0:1024], w_in[:, 1024:2048], w_in[:, 2048:3072], w_in[:, 3072:4096]
    rest = w_in[:, 4096:]
    blocks = []
    for h in range(8):
        q = np.concatenate([aq[:, h * 64:(h + 1) * 64], aq[:, 512 + h * 64:512 + (h + 1) * 64]], axis=1)
        k = np.concatenate([ak[:, h * 64:(h + 1) * 64], ak[:, 512 + h * 64:512 + (h + 1) * 64]], axis=1)
        blocks += [q, k, av[:, h * 128:(h + 1) * 128], az[:, h * 128:(h + 1) * 128]]
    wA = np.concatenate(blocks, axis=1)
    def wB(swap):
        fo, bo = (2048, 1024) if swap else (1024, 2048)
        bl = []
        for hb in range(8):
            sl = slice(hb * 128, (hb + 1) * 128)
            bl += [rest[:, 0:1024][:, sl], rest[:, fo:fo + 1024][:, sl], rest[:, bo:bo + 1024][:, sl],
                   rest[:, 3072:4096][:, sl], rest[:, 4096:5120][:, sl]]
        return np.concatenate(bl, axis=1)
    tail = rest[:, 5120:]
    w_even = np.ascontiguousarray(np.concatenate([wA, wB(False), tail], axis=1))
    w_odd = np.ascontiguousarray(np.concatenate([wA, wB(True), tail], axis=1))
    lbl = f(inputs["lb_logits"])
    vec_even = np.zeros((8, 1024), np.float32)
    vec_even[0] = f(inputs["pre_norm"])[0]
    vec_even[1] = f(inputs["post_norm"])[0]
    vec_even[2] = f(inputs["mem_norm"])[0]
    vec_even[3], vec_even[4] = lbl[0, 0], lbl[0, 1]
    vec_even[5], vec_even[6] = lbl[1, 0], lbl[1, 1]
    vec_odd = vec_even.copy()
    vec_odd[3], vec_odd[4] = lbl[1, 0], lbl[1, 1]
    vec_odd[5], vec_odd[6] = lbl[0, 0], lbl[0, 1]
    small = np.zeros((1, 512), np.float32)
    small[0, 0:64] = f(inputs["lambda_q1"])[0]
    small[0, 64:128] = f(inputs["lambda_k1"])[0]
    small[0, 128:192] = f(inputs["lambda_q2"])[0]
    small[0, 192:256] = f(inputs["lambda_k2"])[0]
    small[0, 256:384] = f(inputs["diff_subln"])[0]
    small[0, 384:512] = f(inputs["hgrn_norm"])[0]
    return dict(w_even=w_even, w_odd=w_odd, vec_even=vec_even, vec_odd=vec_odd, small=small,
                w_kv=f(inputs["w_mem_kv"])[0], w_br=f(inputs["w_branch"])[0], w_out=f(inputs["w_out"])[0],
                relb=f(inputs["rel_bias"]))


def _in_maps(inputs):
    sh = _prep_shared(inputs)
    x = np.asarray(inputs["x"], dtype=np.float32)
    mem = np.asarray(inputs["mem"], dtype=np.float32)
    cst = _consts()
    tabs = [_oh_tables(False), _oh_tables(True)]
    maps = []
    for c in range(8):
        b, half = c // 2, c % 2
        xb = x[b] if half == 0 else x[b, ::-1]
        maps.append({
            "x_own": np.ascontiguousarray(xb[0:NT]), "x_oth": np.ascontiguousarray(xb[NT:2 * NT]),
            "mem": np.ascontiguousarray(mem[b]),
            "w_in": sh["w_even"] if half == 0 else sh["w_odd"],
            "w_kv": sh["w_kv"], "w_br": sh["w_br"], "w_out": sh["w_out"],
            "vecs": sh["vec_even"] if half == 0 else sh["vec_odd"],
            "small": sh["small"], "relb": sh["relb"], "cst": cst,
            "oh": tabs[half][0], "ohfar": tabs[half][1],
        })
    return maps


def _run(inputs, phases="0ABCF", debug=False, cores=None):
    prog = Prog(phases=phases, debug=debug)
    nc = prog.build()
    maps = _in_maps(inputs)
    ids = list(range(8)) if cores is None else cores
    res = run_bass_kernel_spmd(nc, [maps[c] for c in ids], core_ids=list(range(len(ids))))
    return res, ids


def kernel(**inputs):
    res, ids = _run(inputs, phases=os.environ.get("K_PHASES", "0ABCF"))
    out = np.zeros((4, 4096, D), np.float32)
    for r, c in zip(res.results, ids):
        b, half = c // 2, c % 2
        y = np.asarray(r["y"], dtype=np.float32)
        if half == 0:
            out[b, 0:NT] = y
        else:
            out[b, NT:2 * NT] = y[::-1]
    return out
```

```python
import math
import os
import numpy as np
import concourse.bass as bass
import concourse.mybir as mybir
from concourse.ap import AP
from concourse.bass_utils import run_bass_kernel_spmd

F32 = mybir.dt.float32
BF16 = mybir.dt.bfloat16
AF = mybir.ActivationFunctionType
ALU = mybir.AluOpType
AX = mybir.AxisListType

ENGS = ("pe", "act", "dve", "pool", "sp")
NT = 2048
D = 1024
EPS = 1e-6
LAM_INIT = 0.2
WX = 1280
WW = 1152


class T:
    __slots__ = ("name", "w", "r", "excl")

    def __init__(self, name, excl=False):
        self.name = name
        self.w = None
        self.r = {}
        self.excl = excl


class Sched:
    def __init__(self, nc):
        self.nc = nc
        self.streams = {e: [] for e in ENGS}
        self.count = {e: 0 for e in ENGS}
        self.seen = {e: {} for e in ENGS}
        self.sems = {}
        self.dma_cnt = {}
        self._ctx = []

    def sem(self, key):
        if key not in self.sems:
            cm = self.nc.semaphore("s_" + str(key))
            self.sems[key] = cm.__enter__()
            self._ctx.append(cm)
        return self.sems[key]

    def _waits(self, eng, reads, writes):
        need = {}

        def add(tok):
            if tok is None:
                return
            k, v = tok
            if need.get(k, 0) < v:
                need[k] = v
        for t in reads:
            add(t.w)
            if t.excl:
                for k, v in t.r.items():
                    if k != eng:
                        add((k, v))
        for t in writes:
            add(t.w)
            for k, v in t.r.items():
                add((k, v))
        if eng == "pe":
            need.pop("pe", None)
        out = []
        seen = self.seen[eng]
        for k, v in need.items():
            if seen.get(k, 0) < v:
                seen[k] = v
                out.append((k, v))
        return out

    def _commit(self, tok, reads, writes):
        for t in writes:
            t.w = tok
            t.r = {}
        for t in reads:
            if t.r.get(tok[0], 0) < tok[1]:
                t.r[tok[0]] = tok[1]

    def op(self, eng, fn, reads=(), writes=(), single=False):
        waits = self._waits(eng, reads, writes)
        self.count[eng] += 1
        tok = (eng, self.count[eng])
        self.streams[eng].append((waits, fn, tok, single))
        self._commit(tok, reads, writes)
        return tok

    def dma(self, fn, semkey, reads=(), writes=(), q="sp"):
        if semkey == "cst":
            self._ncst = getattr(self, "_ncst", 0) + 1
            semkey = "cst%d" % self._ncst
        waits = self._waits(q, reads, writes)
        self.dma_cnt[semkey] = self.dma_cnt.get(semkey, 0) + 1
        tok = (semkey, 16 * self.dma_cnt[semkey])
        self.sem(semkey)
        self.streams[q].append((waits, fn, tok, False))
        self._commit(tok, reads, writes)
        return tok

    def barrier(self):
        for e in ENGS:
            for k in list(self.sems.keys()) + [x for x in ENGS if x not in self.sems]:
                v = self.count[k] if k in self.count else 16 * self.dma_cnt.get(k, 0)
                if v > 0 and k != e:
                    self.wait_tok(e, (k, v))

    def wait_tok(self, eng, tok):
        if self.seen[eng].get(tok[0], 0) < tok[1]:
            self.seen[eng][tok[0]] = tok[1]
            self.streams[eng].append(([tok], None, None, False))

    def emit(self, block):
        engmap = {"pe": "tensor", "act": "scalar", "dve": "vector", "pool": "gpsimd", "sp": "sync"}
        for e in ENGS:
            self.sem(e)
        sched = self

        def make(e):
            items = sched.streams[e]
            sched.streams[e] = []

            def body(engine):
                embed = not os.environ.get("K_NOEMBED")
                for waits, fn, tok, single in items:
                    emb = None
                    if single and waits and embed:
                        emb = waits[-1]
                        waits = waits[:-1]
                    for k, v in waits:
                        engine.wait_ge(sched.sems[k], v)
                    if fn is None:
                        continue
                    ins = fn(engine)
                    first = ins
                    if isinstance(ins, tuple):
                        first, ins = ins
                    if emb is not None:
                        first._wait_ge(sched.sems[emb[0]], emb[1])
                    if tok[0] == e:
                        ins.then_inc(sched.sems[e], 1)
                    else:
                        ins.then_inc(sched.sems[tok[0]], 16)
            return body
        for e in ENGS:
            getattr(block, engmap[e])(make(e))

    def close(self):
        for cm in reversed(self._ctx):
            cm.__exit__(None, None, None)


def _t5_bucket_np(rel):
    rel = np.asarray(rel, dtype=np.int64)
    nb = 16
    max_exact = 8
    ret = np.where(rel > 0, nb, 0)
    n = np.abs(rel)
    nf = np.maximum(n, 1).astype(np.float32)
    large = max_exact + (np.log(nf / np.float32(max_exact)) / np.float32(math.log(128 / max_exact))
                         * np.float32(nb - max_exact)).astype(np.int32)
    large = np.minimum(large, nb - 1)
    return ret + np.where(n < max_exact, n, large)


C_ID, C_AJ, C_ONE = 0, 128, 256
C_TRI = {"f": 384, "b": 512}
C_SREV = {"f": 640, "b": 768}
C_MASK = {"f": 896, "b": 1024}
C_CIND = 1152
C_N = 1160


def _consts():
    c = np.zeros((128, C_N), np.float32)
    i = np.arange(128)
    c[:, C_ID:C_ID + 128] = np.eye(128)
    c[:, C_AJ:C_AJ + 128] = np.eye(128)[::-1]
    c[:, C_ONE:C_ONE + 128] = 1.0
    s = i[:, None]
    cc = i[None, :]
    same = (s // 64) == (cc // 64)
    c[:, C_TRI["f"]:C_TRI["f"] + 128] = (same & (s <= cc))
    c[:, C_TRI["b"]:C_TRI["b"] + 128] = (same & (s >= cc))
    c[:, C_SREV["f"]:C_SREV["f"] + 128] = (same & (s > cc))
    c[:, C_SREV["b"]:C_SREV["b"] + 128] = (same & (s < cc))
    c[:, C_MASK["f"]:C_MASK["f"] + 128] = (same & (s <= cc))
    c[:, C_MASK["b"]:C_MASK["b"] + 128] = (same & (s >= cc))
    c[:, C_CIND] = (i < 64)
    c[:, C_CIND + 1] = (i >= 64)
    return c


def _oh_tables(mirror):
    x = np.arange(WX)
    delta = 639 - x
    if mirror:
        delta = -delta
    bk = _t5_bucket_np(delta)
    oh = np.zeros((32, WX), np.float32)
    oh[bk, x] = 1.0
    ohfar = np.zeros((32, 256), np.float32)
    bm = int(_t5_bucket_np(np.array([1000 if mirror else -1000]))[0])
    bp = int(_t5_bucket_np(np.array([-1000 if mirror else 1000]))[0])
    ohfar[bm, 0:128] = 1.0
    ohfar[bp, 128:256] = 1.0
    return oh, ohfar


class Prog:
    def __init__(self, phases="0ABCF", debug=False):
        self.phases = phases
        self.debug = debug
        nc = bass.Bass("TRN2", target_bir_lowering=False)
        self.nc = nc
        self.S = Sched(nc)
        self._glob = []
        self._ph = []
        dt = nc.dram_tensor
        self.x_own = dt("x_own", [NT, D], F32, kind="ExternalInput").ap()
        self.x_oth = dt("x_oth", [NT, D], F32, kind="ExternalInput").ap()
        self.mem = dt("mem", [256, D], F32, kind="ExternalInput").ap()
        self.w_in = dt("w_in", [D, 14336], F32, kind="ExternalInput").ap()
        self.w_kv = dt("w_kv", [D, 2048], F32, kind="ExternalInput").ap()
        self.w_br = dt("w_br", [3, D, D], F32, kind="ExternalInput").ap()
        self.w_out = dt("w_out", [D, D], F32, kind="ExternalInput").ap()
        self.vecs = dt("vecs", [8, D], F32, kind="ExternalInput").ap()
        self.small = dt("small", [1, 512], F32, kind="ExternalInput").ap()
        self.relb = dt("relb", [32, 8], F32, kind="ExternalInput").ap()
        self.cst = dt("cst", [128, C_N], F32, kind="ExternalInput").ap()
        self.oh = dt("oh", [32, WX], F32, kind="ExternalInput").ap()
        self.ohfar = dt("ohfar", [32, 256], F32, kind="ExternalInput").ap()
        self.y = dt("y", [NT, D], F32, kind="ExternalOutput").ap()
        self.wscr_h = dt("wscr", [8, WX], F32)
        self.wscr = self.wscr_h.ap()
        if debug:
            self.dbg = dt("dbg", [3, 128, 4 * NT], F32, kind="ExternalOutput").ap()
        self.nwl = 0
        self.out_toks = []

    def _alloc(self, lst, kind, name, shape, dtype):
        self._nalloc = getattr(self, "_nalloc", 0) + 1
        name = "%s_%d" % (name, self._nalloc)
        cm = (self.nc.sbuf_tensor if kind == "sb" else self.nc.psum_tensor)(name, shape, dtype)
        h = cm.__enter__()
        lst.append(cm)
        return h

    def gsb(self, name, shape, dtype):
        return self._alloc(self._glob, "sb", name, shape, dtype)

    def sb(self, name, shape, dtype):
        return self._alloc(self._ph, "sb", name, shape, dtype)

    def end_phase(self):
        if not os.environ.get("K_NOBAR"):
            self.S.barrier()
        with self.nc.Block() as block:
            self.S.emit(block)
        for cm in reversed(self._ph):
            cm.__exit__(None, None, None)
        self._ph = []

    def act(self, out, in_, func, reads, writes, bias=0.0, scale=1.0, accum_out=None):
        kw = {}
        if accum_out is not None:
            kw["accum_out"] = accum_out
        return self.S.op("act", lambda e: e.activation(out=out, in_=in_, func=func, bias=bias, scale=scale, **kw),
                         reads=reads, writes=writes, single=True)

    def tt(self, eng, out, in0, in1, op, reads, writes):
        return self.S.op(eng, lambda e: e.tensor_tensor(out=out, in0=in0, in1=in1, op=op), reads=reads, writes=writes, single=True)

    def ts(self, eng, out, in0, s1, s2, op0, op1, reads, writes):
        if s2 is None:
            return self.S.op(eng, lambda e: e.tensor_scalar(out=out, in0=in0, scalar1=s1, scalar2=None, op0=op0),
                             reads=reads, writes=writes, single=True)
        return self.S.op(eng, lambda e: e.tensor_scalar(out=out, in0=in0, scalar1=s1, scalar2=s2, op0=op0, op1=op1),
                         reads=reads, writes=writes, single=True)

    def stt(self, eng, out, in0, scalar, in1, op0, op1, reads, writes):
        return self.S.op(eng, lambda e: e.scalar_tensor_tensor(out=out, in0=in0, scalar=scalar, in1=in1, op0=op0, op1=op1),
                         reads=reads, writes=writes, single=True)

    def copy(self, eng, out, in_, reads, writes):
        if eng == "act":
            return self.S.op("act", lambda e: e.copy(out=out, in_=in_), reads=reads, writes=writes, single=True)
        return self.S.op(eng, lambda e: e.tensor_copy(out=out, in_=in_), reads=reads, writes=writes, single=True)

    def mm(self, mms, reads, writes):
        def fn(e):
            ins = None
            first = None
            for (o, l, r, st, sp) in mms:
                ins = e.matmul(o, lhsT=l, rhs=r, start=st, stop=sp)
                if first is None:
                    first = ins
            return (first, ins)
        return self.S.op("pe", fn, reads=reads, writes=writes, single=True)

    def setup_globals(self):
        nc, S = self.nc, self.S
        self.ps = self._alloc(self._glob, "ps", "psall", [128, 8, 512], F32)
        self.Tps = [T("ps%d" % i, excl=True) for i in range(8)]
        self.cf = self.gsb("cf", [128, C_N], F32)
        self.Tcf = T("cf")
        self.cb = self.gsb("cb", [128, 384], BF16)
        self.Tcb = T("cb")
        self.ones128 = self.gsb("ones128", [128, 128], BF16)
        self.hT = self.gsb("hT", [128, 8, NT], BF16)
        self.ThT = [T("hT%d" % j) for j in range(4)]
        self.merged = self.gsb("merged", [128, 8, NT], BF16)
        self.hTo = self.merged
        self.ThTo = [T("hTo%d" % j) for j in range(4)]
        self.Tmg = [[T("mg%d_%d" % (dc, j)) for j in range(4)] for dc in range(8)]
        self.wst = [self.gsb("wst%d" % i, [128, 8, 128], F32) for i in range(2)]
        self.Twst = [T("wst%d" % i) for i in range(2)]
        self.wbf = [self.gsb("wbf%d" % i, [128, 8, 128], BF16) for i in range(3)]
        self.Twbf = [T("wbf%d" % i) for i in range(3)]
        self.stat = self.gsb("stat", [128, 256], F32)
        self.Tstat = T("stat")
        self.sm = self.gsb("sm", [1, 512], F32)
        self.Tsm = T("sm")
        self.cols = self.gsb("cols", [128, 8], F32)
        self.Tcols = T("cols")
        self.cfar = self.gsb("cfar", [128, 16], F32)
        self.Tcfar = T("cfar")

        S.dma(lambda e: e.dma_start(out=self.cf[:], in_=self.cst[:, :]), "cst", writes=[self.Tcf])
        S.dma(lambda e: e.dma_start(out=self.sm[:], in_=self.small[:, :]), "cst", writes=[self.Tsm])
        self.copy("dve", self.cb[:], self.cf[:, 0:384], [self.Tcf], [self.Tcb])
        self.ts("dve", self.ones128[:], self.cf[:, C_ONE:C_ONE + 128], 1.0 / 128, None, ALU.mult, None,
                [self.Tcf], [self.Tcb])
        S.op("pool", lambda e: e.memset(self.stat[:], 0.0), writes=[self.Tstat])
        self.idb = self.cb[:, 0:128]
        self.ajb = self.cb[:, 128:256]
        self.oneb = self.cb[:, 256:384]

    def psb(self, i):
        return self.ps[:, i, :]

    def load_w(self, src2d, ncol=128, dst=None, Tdst=None):
        S = self.S
        k = self.nwl
        self.nwl += 1
        st, Tst = self.wst[k % 2], self.Twst[k % 2]
        if dst is None:
            self.nwb = getattr(self, "nwb", 0) + 1
            bf, Tbf = self.wbf[self.nwb % 3], self.Twbf[self.nwb % 3]
        else:
            bf, Tbf = dst, Tdst
        src = src2d.rearrange("(c p) n -> p c n", p=128)
        q = "sp"
        S.dma(lambda e: e.dma_start(out=st[:, :, 0:ncol], in_=src), "wst%d" % (k % 2), writes=[Tst], q=q)
        if dst is None:
            S.op("pool", lambda e: e.tensor_copy(out=bf[:, :, 0:ncol], in_=st[:, :, 0:ncol]), reads=[Tst], writes=[Tbf])
        else:
            S.op("pool", lambda e: e.tensor_copy(out=bf, in_=st[:, :, 0:ncol]), reads=[Tst], writes=[Tbf])
        return bf, Tbf

    def norm_transpose(self, src_rows, wbc, Twbc, dst, Tdst, dst_cols, idx, xst, Txst, xbb, Txbb):
        S = self.S
        sl = idx % 2
        xs, Txs = xst[sl], Txst[sl]
        xb, Txb = xbb[sl], Txbb[sl]
        c0 = (idx % 64) * 4
        stat = self.stat
        S.dma(lambda e: e.dma_start(out=xs[:], in_=src_rows), "xld%d" % sl, writes=[Txs])
        self.act(xb[:], xs[:], AF.Square, [Txs], [Txb, self.Tstat], accum_out=stat[:, c0:c0 + 1])
        self.act(stat[:, c0 + 1:c0 + 2], stat[:, c0:c0 + 1], AF.Ln, [self.Tstat], [self.Tstat], bias=EPS, scale=1.0 / D)
        self.act(stat[:, c0 + 2:c0 + 3], stat[:, c0 + 1:c0 + 2], AF.Exp, [self.Tstat], [self.Tstat], scale=-0.5)
        self.stt("dve", xb[:], xs[:], stat[:, c0 + 2:c0 + 3], wbc[:], ALU.mult, ALU.mult,
                 [Txs, self.Tstat, Twbc], [Txb])
        pb = 6 + sl
        pT = self.ps[:, pb, :].bitcast(BF16)

        def tr(e):
            ins = None
            for c in range(8):
                ins = e.transpose(out=pT[:, c * 128:(c + 1) * 128], in_=xb[:, c * 128:(c + 1) * 128], identity=self.idb)
            return ins
        S.op("pe", tr, reads=[Txb, self.Tcb], writes=[self.Tps[pb]])
        eng = "act" if idx % 2 == 0 else "dve"
        self.copy(eng, dst[:, :, dst_cols], pT.rearrange("p (c t) -> p c t", c=8), [self.Tps[pb]], [Tdst])

    def phase0(self):
        S = self.S
        xst = [self.sb("xst%d" % i, [128, D], F32) for i in range(2)]
        Txst = [T("xst%d" % i) for i in range(2)]
        xbb = [self.sb("xbb%d" % i, [128, D], BF16) for i in range(2)]
        Txbb = [T("xbb%d" % i) for i in range(2)]
        pn = self.sb("pn_bc", [128, D], F32)
        Tpn = T("pn")
        S.dma(lambda e: e.dma_start(out=pn[:], in_=self.vecs[0, :].partition_broadcast(128)), "cst", writes=[Tpn])
        for i in range(32):
            own = i < 16
            j = i % 16
            src = (self.x_own if own else self.x_oth)[j * 128:(j + 1) * 128, :]
            dst = self.hT if own else self.hTo
            Td = (self.ThT if own else self.ThTo)[j // 4]
            self.norm_transpose(src, pn, Tpn, dst, Td, slice(j * 128, (j + 1) * 128), i, xst, Txst, xbb, Txbb)

        sm, cols = self.sm, self.cols
        tmp = self.sb("lamtmp", [1, 256], F32)
        Ttmp = T("lamtmp")
        self.tt("dve", tmp[:, 0:64], sm[:, 0:64], sm[:, 64:128], ALU.mult, [self.Tsm], [Ttmp])
        self.tt("dve", tmp[:, 64:128], sm[:, 128:192], sm[:, 192:256], ALU.mult, [self.Tsm], [Ttmp])
        S.op("dve", lambda e: e.reduce_sum(out=tmp[:, 128:130], in_=tmp[:, 0:128].rearrange("p (a b) -> p a b", a=2),
                                           axis=AX.X), reads=[Ttmp], writes=[Ttmp])
        self.act(tmp[:, 130:132], tmp[:, 128:130], AF.Exp, [Ttmp], [Ttmp])
        self.tt("dve", tmp[:, 132:133], tmp[:, 131:132], tmp[:, 130:131], ALU.subtract, [Ttmp], [Ttmp])
        self.ts("dve", tmp[:, 133:134], tmp[:, 132:133], -LAM_INIT, None, ALU.add, None, [Ttmp], [Ttmp])
        onesrow = self.cf[0:1, C_ONE:C_ONE + 128]
        self.mm([(self.ps[:, 0, 0:1], onesrow, tmp[:, 133:134], True, True),
                 (self.ps[:, 0, 1:2], sm[:, 256:384], self.cf[0:1, C_ONE:C_ONE + 1], True, True),
                 (self.ps[:, 0, 2:3], sm[:, 384:512], self.cf[0:1, C_ONE:C_ONE + 1], True, True)],
                [Ttmp, self.Tsm, self.Tcf], [self.Tps[0]])
        self.copy("dve", cols[:, 0:3], self.ps[:, 0, 0:3], [self.Tps[0]], [self.Tcols])

        rb = self.sb("rb", [32, 8], F32)
        ohs = self.sb("ohs", [32, WX], F32)
        ohf = self.sb("ohf", [32, 256], F32)
        Trb = T("rb")
        S.dma(lambda e: e.dma_start(out=rb[:], in_=self.relb[:, :]), "cst", writes=[Trb])
        S.dma(lambda e: e.dma_start(out=ohs[:], in_=self.oh[:, :]), "cst", writes=[Trb])
        S.dma(lambda e: e.dma_start(out=ohf[:], in_=self.ohfar[:, :]), "cst", writes=[Trb])
        wv = self.sb("wv", [8, WX], F32)
        Twv = T("wv")
        self.mm([(self.ps[0:8, 1, 0:512], rb[:], ohs[:, 0:512], True, True),
                 (self.ps[0:8, 2, 0:512], rb[:], ohs[:, 512:1024], True, True),
                 (self.ps[0:8, 3, 0:256], rb[:], ohs[:, 1024:1280], True, True),
                 (self.ps[:, 4, 0:8], ohf[:, 0:128], rb[:], True, True),
                 (self.ps[:, 4, 8:16], ohf[:, 128:256], rb[:], True, True)],
                [Trb], [self.Tps[1], self.Tps[2], self.Tps[3], self.Tps[4]])
        self.ts("dve", wv[:, 0:512], self.ps[0:8, 1, 0:512], 8.0, None, ALU.mult, None, [self.Tps[1]], [Twv])
        self.ts("dve", wv[:, 512:1024], self.ps[0:8, 2, 0:512], 8.0, None, ALU.mult, None, [self.Tps[2]], [Twv])
        self.ts("dve", wv[:, 1024:1280], self.ps[0:8, 3, 0:256], 8.0, None, ALU.mult, None, [self.Tps[3]], [Twv])
        self.copy("dve", self.cfar[:], self.ps[:, 4, 0:16], [self.Tps[4]], [self.Tcfar])
        self.Twscr = T("wscr")
        S.dma(lambda e: e.dma_start(out=self.wscr[:, :], in_=wv[:]), "cst", reads=[Twv], writes=[self.Twscr])
        self.end_phase()

    def sbA(self, name, shape, dtype):
        if not hasattr(self, "_mid"):
            self._mid = []
        assert not self._ph, "allocate mid tensors first"
        return self._alloc(self._mid, "sb", name, shape, dtype)

    def end_phaseA_keep(self):
        self.end_phase()

    def free_mid(self):
        for cm in reversed(self._mid):
            cm.__exit__(None, None, None)
        self._mid = []

    def proj_fm(self, wbf, Tw, c0, src, Tsrc_blk, tok0, ntok, pbank, extra_reads=()):
        mms = []
        for c in range(8):
            mms.append((self.ps[:, pbank, 0:ntok], wbf[:, c, c0:c0 + 128], src[:, c, tok0:tok0 + ntok], c == 0, c == 7))
        return self.mm(mms, [Tw, Tsrc_blk] + list(extra_reads), [self.Tps[pbank]])

    def proj_tm(self, wbf, Tw, c0, ncol, src, Tsrc_blk, tok0, pout, Tpout):
        mms = []
        for c in range(8):
            mms.append((pout, src[:, c, tok0:tok0 + 128], wbf[:, c, c0:c0 + ncol], c == 0, c == 7))
        return self.mm(mms, [Tw, Tsrc_blk], Tpout)

    def branch_merge(self, X, g, Tg, first):
        S = self.S
        gate0 = 11264 + X * 1024
        tmpf = [self.sb("bm_t%d" % i, [128, 512], F32) for i in range(2)]
        Ttmpf = [T("bm_t%d" % i) for i in range(2)]
        tmpb = [self.sb("bm_b%d" % i, [128, 512], BF16) for i in range(2)]
        Ttmpb = [T("bm_b%d" % i) for i in range(2)]
        it = 0
        for dc in range(8):
            wb, Twb = self.load_w(self.w_br[X, :, dc * 128:(dc + 1) * 128])
            wg, Twg = self.load_w(self.w_in[:, gate0 + dc * 128:gate0 + (dc + 1) * 128])
            for J in range(4):
                pby, pbg = (0, 1) if it % 2 == 0 else (2, 3)
                sl = it % 2
                it += 1
                mms = []
                for fc in range(8):
                    mms.append((self.ps[:, pby, :], wb[:, fc, 0:128], g[:, fc, J * 512:(J + 1) * 512], fc == 0, fc == 7))
                self.mm(mms, [Twb, Tg[J]], [self.Tps[pby]])
                self.proj_fm(wg, Twg, 0, self.hT, self.ThT[J], J * 512, 512, pbg)
                tf, Ttf = tmpf[sl], Ttmpf[sl]
                tb, Ttb = tmpb[sl], Ttmpb[sl]
                self.act(tf[:], self.ps[:, pbg, :], AF.Exp, [self.Tps[pbg]], [Ttf], scale=-1.0)
                self.act(tf[:], tf[:], AF.Ln, [Ttf], [Ttf], bias=1.0)
                self.act(tf[:], tf[:], AF.Exp, [Ttf], [Ttf], scale=-1.0)
                mg = self.merged[:, dc, J * 512:(J + 1) * 512]
                if first:
                    self.tt("dve", mg, self.ps[:, pby, :], tf[:], ALU.mult, [self.Tps[pby], Ttf], [self.Tmg[dc][J]])
                else:
                    self.tt("dve", tb[:], self.ps[:, pby, :], tf[:], ALU.mult, [self.Tps[pby], Ttf], [Ttb])
                    self.tt("pool", mg, mg, tb[:], ALU.add, [Ttb, self.Tmg[dc][J]], [self.Tmg[dc][J]])

    def dump_dbg(self, X, g, Tg):
        if not self.debug:
            return
        self.S.dma(lambda e: e.dma_start(out=self.dbg[X], in_=g[:].rearrange("p c t -> p (c t)").bitcast(F32)),
                   "dbg", reads=list(Tg))

    def phaseA(self):
        S = self.S
        gA, TgA = self.gA, self.TgA
        KT = self.sb("KT", [128, 4096], BF16)
        TKT = T("KT")
        QT = self.sb("QT", [128, NT], BF16)
        TQT = T("QT")
        V = self.sb("V", [128, 32, 128], BF16)
        TV = T("V")
        Wf = self.sb("Wf", [128, WW], F32)
        TWf = T("Wf")
        Wb = self.sb("Wb", [128, WW], BF16)
        TWb = T("Wb")
        NE = 3
        E = [self.sb("E%d" % i, [128, 2, 512], BF16) for i in range(NE)]
        TE = [T("E%d" % i) for i in range(NE)]
        acc = self.sb("acc", [128, 4, 512], F32)
        Tacc = [T("acc%d" % i) for i in range(4)]
        Zs2 = [self.sb("Zs%d" % i, [128, 2, 512], F32) for i in range(2)]
        TZs2 = [[T("Zs%d_0" % i), T("Zs%d_1" % i)] for i in range(2)]
        sqb = self.sb("sqb", [128, 512], BF16)
        Tsqb = T("sqb")
        wzA = [self.sb("wzA%d" % i, [128, 8, 128], BF16) for i in range(2)]
        TwzA = [T("wzA%d" % i) for i in range(2)]
        fin_q = []
        pend = []
        zero_c = self.sb("zero_c", [128, 1], F32)
        Tz = T("zero_c")
        S.op("pool", lambda e: e.memset(zero_c[:], 0.0), writes=[Tz])
        ln08 = math.log(1.0 - LAM_INIT)
        unit = 0
        for h in range(int(os.environ.get('K_AHEADS', '8'))):
            base = h * 512

            def load_head(hh):
                b_ = hh * 512
                r = [self.load_w(self.w_in[:, b_ + k * 128:b_ + (k + 1) * 128]) for k in range(3)]
                self.load_w(self.w_in[:, b_ + 384:b_ + 512], dst=wzA[hh % 2][:], Tdst=TwzA[hh % 2])
                return r
            pre_w = load_head(h)
            (wq_, Twq_), (wk_, Twk_), (w23, Tw23) = pre_w
            for J in range(4):
                pb = J % 2
                self.proj_fm(wq_, Twq_, 0, self.hT, self.ThT[J], J * 512, 512, pb)
                self.copy("act" if J % 2 == 0 else "dve", QT[:, J * 512:(J + 1) * 512], self.ps[:, pb, :], [self.Tps[pb]], [TQT])
            for Jk in range(8):
                pb = Jk % 2
                src, Ts = (self.hT, self.ThT[Jk]) if Jk < 4 else (self.hTo, self.ThTo[Jk - 4])
                self.proj_fm(wk_, Twk_, 0, src, Ts, (Jk % 4) * 512, 512, pb)
                self.copy("act" if Jk % 2 == 0 else "dve", KT[:, Jk * 512:(Jk + 1) * 512], self.ps[:, pb, :], [self.Tps[pb]], [TKT])
            for i4 in range(8):
                pb = i4 % 2
                mms = []
                for ii in range(4):
                    i = i4 * 4 + ii
                    src, Ts = (self.hT, self.ThT[i // 4]) if i < 16 else (self.hTo, self.ThTo[(i - 16) // 4])
                    for c in range(8):
                        mms.append((self.ps[:, pb, ii * 128:(ii + 1) * 128], src[:, c, (i % 16) * 128:(i % 16 + 1) * 128],
                                    w23[:, c, 0:128], c == 0, c == 7))
                Ts_all = self.ThT[i4] if i4 < 4 else self.ThTo[i4 - 4]
                self.mm(mms, [Tw23, Ts_all], [self.Tps[pb]])
                self.copy("act" if i4 % 2 == 0 else "dve", V[:, i4 * 4:(i4 + 1) * 4, :],
                          self.ps[:, pb, :].rearrange("p (a b) -> p a b", a=4), [self.Tps[pb]], [TV])
            hsrc = AP(tensor=self.wscr_h, offset=h * WX, ap=[[1, 128], [1, WW]])
            S.dma(lambda e, hsrc=hsrc: e.dma_start(out=Wf[:], in_=hsrc), "wf", reads=[self.Twscr], writes=[TWf])
            self.copy("pool", Wb[:], Wf[:], [TWf], [TWb])

            for J in range(4):
                def qk(i, slot):
                    b0 = slot * 2
                    mms = [(self.ps[:, b0, :], KT[0:64, i * 128:(i + 1) * 128], QT[0:64, J * 512:(J + 1) * 512], True, False),
                           (self.ps[:, b0 + 1, :], KT[64:128, i * 128:(i + 1) * 128], QT[64:128, J * 512:(J + 1) * 512], True, False)]
                    near = (4 * J - 1) <= i <= (4 * J + 4)
                    rd = [TKT, TQT]
                    if near:
                        off = 639 - ((i - 4 * J) * 128 + 127)
                        mms.append((self.ps[:, b0, :], self.ajb, Wb[:, off:off + 512], False, True))
                        mms.append((self.ps[:, b0 + 1, :], self.ajb, Wb[:, off:off + 512], False, True))
                        rd += [TWb, self.Tcb]
                    else:
                        mms[0] = mms[0][:4] + (True,)
                        mms[1] = mms[1][:4] + (True,)
                    self.mm(mms, rd, [self.Tps[b0], self.Tps[b0 + 1]])
                    return near

                def ex(i, slot, near, u):
                    b0 = slot * 2
                    e_, Te_ = E[u % NE], TE[u % NE]
                    if near:
                        bias = zero_c[:, 0:1]
                        rd = [Tz]
                    else:
                        side = 0 if i < 4 * J else 1
                        bias = self.cfar[:, side * 8 + h:side * 8 + h + 1]
                        rd = [self.Tcfar]
                    self.act(e_[:].rearrange("p a b -> p (a b)"), self.ps[:, b0:b0 + 2, :].rearrange("p a b -> p (a b)"),
                             AF.Exp, [self.Tps[b0], self.Tps[b0 + 1]] + rd, [Te_], bias=bias, scale=0.125)

                def pv(i, u):
                    e_, Te_ = E[u % NE], TE[u % NE]
                    st, sp = (i == 0), (i == 31)
                    mms = [(self.ps[:, 6, :], V[:, i, :], e_[:, 0, :], st, sp),
                           (self.ps[:, 7, :], V[:, i, :], e_[:, 1, :], st, sp)]
                    self.mm(mms, [TV, Te_], [self.Tps[6], self.Tps[7]])
                    zf = Zs[:].rearrange("p a b -> p (a b)")
                    ef = e_[:].rearrange("p a b -> p (a b)")
                    SPL = 896
                    if i == 0:
                        self.copy("dve", zf[:, 0:SPL], ef[:, 0:SPL], [Te_], [TZs[0]])
                        self.copy("pool", zf[:, SPL:1024], ef[:, SPL:1024], [Te_], [TZs[1]])
                    else:
                        self.tt("dve", zf[:, 0:SPL], zf[:, 0:SPL], ef[:, 0:SPL], ALU.add, [Te_, TZs[0]], [TZs[0]])
                        self.tt("pool", zf[:, SPL:1024], zf[:, SPL:1024], ef[:, SPL:1024], ALU.add, [Te_, TZs[1]], [TZs[1]])

                Zs, TZs = Zs2[(unit // 32) % 2], TZs2[(unit // 32) % 2]
                nears = {}
                nears[0] = qk(0, 0)
                for i in range(32):
                    if i + 1 < 32:
                        nears[i + 1] = qk(i + 1, (i + 1) % 2)
                    ex(i, i % 2, nears[i], unit + i)
                    if i == 0 and pend:
                        imm_, fo_ = pend.pop(0)
                        imm_()
                        fin_q.extend(fo_())
                    pv(i, unit + i)
                    if fin_q and i >= 1:
                        f_ = fin_q.pop(0)
                        if f_ is not None:
                            f_()
                assert not fin_q
                unit += 32
                onesf = self.cf[:, C_ONE:C_ONE + 128]
                A0, A1, A2, A3 = acc[:, 0, :], acc[:, 1, :], acc[:, 2, :], acc[:, 3, :]
                wz_, Twz_ = wzA[h % 2], TwzA[h % 2]

                def imm(Zs=Zs, TZs=TZs):
                    self.copy("dve", A0, self.ps[:, 6, :], [self.Tps[6]], [Tacc[0]])
                    self.copy("dve", A1, self.ps[:, 7, :], [self.Tps[7]], [Tacc[1]])
                    self.mm([(self.ps[:, 4, :], onesf, Zs[:, 0, :], True, True),
                             (self.ps[:, 5, :], onesf, Zs[:, 1, :], True, True)], [TZs[0], TZs[1], self.Tcf],
                            [self.Tps[4], self.Tps[5]])

                def fin_ops(h=h, J=J, wz_=wz_, Twz_=Twz_):
                    ops = []
                    ops.append(lambda: self.act(A2, self.ps[:, 4, :], AF.Ln, [self.Tps[4]], [Tacc[2]]))
                    ops.append(lambda: self.act(A3, self.ps[:, 5, :], AF.Ln, [self.Tps[5]], [Tacc[3]]))
                    ops.append(lambda: self.act(A2, A2, AF.Exp, [Tacc[2]], [Tacc[2]], scale=-1.0))
                    ops.append(lambda: self.act(A3, A3, AF.Exp, [Tacc[3]], [Tacc[3]], scale=-1.0))
                    ops.append(lambda: self.tt("dve", A0, A0, A2, ALU.mult, [Tacc[0], Tacc[2]], [Tacc[0]]))
                    ops.append(lambda: self.tt("dve", A1, A1, A3, ALU.mult, [Tacc[1], Tacc[3]], [Tacc[1]]))
                    ops.append(lambda: self.stt("dve", A2, A1, self.cols[:, 0:1], A0, ALU.mult, ALU.add,
                                                [Tacc[0], Tacc[1], self.Tcols], [Tacc[2]]))
                    ops.append(None)
                    ops.append(lambda: self.act(sqb[:], A2, AF.Square, [Tacc[2]], [Tsqb]))
                    ops.append(lambda: self.mm([(self.ps[:, 4, :], self.ones128[:], sqb[:], True, True)], [Tsqb, self.Tcb], [self.Tps[4]]))
                    ops.append(lambda: self.proj_fm(wz_, Twz_, 0, self.hT, self.ThT[J], J * 512, 512, 5))
                    ops.append(None)
                    ops.append(lambda: self.act(A0, self.ps[:, 4, :], AF.Ln, [self.Tps[4]], [Tacc[0]], bias=EPS))
                    ops.append(lambda: self.act(A1, self.ps[:, 5, :], AF.Exp, [self.Tps[5]], [Tacc[1]], scale=-1.0))
                    ops.append(lambda: self.act(A1, A1, AF.Ln, [Tacc[1]], [Tacc[1]], bias=1.0))
                    ops.append(lambda: self.stt("dve", A0, A0, 0.5, A1, ALU.mult, ALU.add, [Tacc[0], Tacc[1]], [Tacc[0]]))
                    ops.append(None)
                    ops.append(lambda: self.act(A0, A0, AF.Exp, [Tacc[0]], [Tacc[0]], scale=-1.0, bias=ln08))
                    ops.append(lambda: self.stt("dve", A3, A2, self.cols[:, 1:2], self.ps[:, 5, :], ALU.mult, ALU.mult,
                                                [Tacc[2], self.Tcols, self.Tps[5]], [Tacc[3]]))
                    ops.append(lambda: self.tt("dve", gA[:, h, J * 512:(J + 1) * 512], A3, A0, ALU.mult, [Tacc[3], Tacc[0]], [TgA[J]]))
                    return ops
                pend.append((imm, fin_ops))
        while pend:
            imm_, fo_ = pend.pop(0)
            imm_()
            fin_q.extend(fo_())
        while fin_q:
            f_ = fin_q.pop(0)
            if f_ is not None:
                f_()
        self.dump_dbg(0, gA, TgA)

    def phaseC(self):
        S = self.S
        gC, TgC = self.gC, self.TgC
        mT = self.sb("mT", [128, 8, 256], BF16)
        TmT = T("mT")
        mn = self.sb("mn_bc", [128, D], F32)
        Tmn = T("mn")
        xst = [self.sb("cxst%d" % i, [128, D], F32) for i in range(2)]
        Txst = [T("cxst%d" % i) for i in range(2)]
        xbb = [self.sb("cxbb%d" % i, [128, D], BF16) for i in range(2)]
        Txbb = [T("cxbb%d" % i) for i in range(2)]
        S.dma(lambda e: e.dma_start(out=mn[:], in_=self.vecs[2, :].partition_broadcast(128)), "cst", writes=[Tmn])
        for i in range(2):
            self.norm_transpose(self.mem[i * 128:(i + 1) * 128, :], mn, Tmn, mT, TmT, slice(i * 128, (i + 1) * 128),
                                48 + i, xst, Txst, xbb, Txbb)
        mkT = self.sb("mkT", [128, 8, 256], BF16)
        TmkT = T("mkT")
        mv = self.sb("mv", [128, 2, 1024], BF16)
        Tmv = T("mv")
        for blk in range(16):
            w, Tw = self.load_w(self.w_kv[:, blk * 128:(blk + 1) * 128])
            pb = blk % 2
            if blk < 8:
                self.proj_fm(w, Tw, 0, mT, TmT, 0, 256, pb)
                self.copy("dve", mkT[:, blk, :], self.ps[:, pb, 0:256], [self.Tps[pb]], [TmkT])
            else:
                for mt in range(2):
                    self.proj_tm(w, Tw, 0, 128, mT, TmT, mt * 128, self.ps[:, pb, mt * 128:(mt + 1) * 128], [self.Tps[pb]])
                self.copy("dve", mv[:, :, (blk - 8) * 128:(blk - 7) * 128],
                          self.ps[:, pb, 0:256].rearrange("p (a b) -> p a b", a=2), [self.Tps[pb]], [Tmv])
        wq = self.sb("cwq", [128, 8, 256], BF16)
        Twq = T("cwq")
        wz = self.sb("cwz", [128, 8, 256], BF16)
        Twz = T("cwz")
        cqT = [self.sb("cqT%d" % i, [128, 2, 512], BF16) for i in range(2)]
        TcqT = [T("cqT%d" % i) for i in range(2)]
        Ec = [self.sb("Ec%d" % i, [128, 2, 512], BF16) for i in range(2)]
        TEc = [T("Ec%d" % i) for i in range(2)]
        fL = [self.sb("fL%d" % i, [128, 512], F32) for i in range(2)]
        TfL = [T("fL%d" % i) for i in range(2)]
        f1 = [self.sb("cf1%d" % i, [128, 512], F32) for i in range(2)]
        Tf1 = [T("cf1%d" % i) for i in range(2)]
        f2 = [self.sb("cf2%d" % i, [128, 512], F32) for i in range(2)]
        Tf2 = [T("cf2%d" % i) for i in range(2)]

        def unit_gen(hc, J, si):
            ba, bb, bc, bd = [4 * si + x for x in range(4)]
            oc = (ba, bb)
            for ch in range(2):
                self.proj_fm(wq, Twq, ch * 128, self.hT, self.ThT[J], J * 512, 512, oc[ch])
                yield
                self.copy("act" if ch == 0 else "dve", cqT[si][:, ch, :], self.ps[:, oc[ch], :], [self.Tps[oc[ch]]], [TcqT[si]])
                yield
            mms = []
            for mt in range(2):
                for ch in range(2):
                    mms.append((self.ps[:, bc + mt, :], mkT[:, hc * 2 + ch, mt * 128:(mt + 1) * 128], cqT[si][:, ch, :], ch == 0, ch == 1))
            self.mm(mms, [TmkT, TcqT[si]], [self.Tps[bc], self.Tps[bd]])
            yield
            self.act(Ec[si][:].rearrange("p a b -> p (a b)"), self.ps[:, bc:bc + 2, :].rearrange("p a b -> p (a b)"), AF.Exp,
                     [self.Tps[bc], self.Tps[bd]], [TEc[si]], scale=1.0 / 16)
            yield
            mms = []
            for ch in range(2):
                for mt in range(2):
                    mms.append((self.ps[:, oc[ch], :], mv[:, mt, hc * 256 + ch * 128:hc * 256 + (ch + 1) * 128], Ec[si][:, mt, :], mt == 0, mt == 1))
            for mt in range(2):
                mms.append((self.ps[:, bc, :], self.oneb, Ec[si][:, mt, :], mt == 0, mt == 1))
            self.mm(mms, [Tmv, TEc[si], self.Tcb], [self.Tps[ba], self.Tps[bb], self.Tps[bc]])
            yield
            self.act(fL[si][:], self.ps[:, bc, :], AF.Ln, [self.Tps[bc]], [TfL[si]])
            yield
            for ch in range(2):
                self.proj_fm(wz, Twz, ch * 128, self.hT, self.ThT[J], J * 512, 512, bd)
                yield
                self.act(f1[si][:], self.ps[:, bd, :], AF.Exp, [self.Tps[bd]], [Tf1[si]], scale=-1.0)
                yield
                self.act(f1[si][:], f1[si][:], AF.Ln, [Tf1[si]], [Tf1[si]], bias=1.0)
                yield
                self.tt("pool", f1[si][:], f1[si][:], fL[si][:], ALU.add, [Tf1[si], TfL[si]], [Tf1[si]])
                yield
                self.act(f1[si][:], f1[si][:], AF.Exp, [Tf1[si]], [Tf1[si]], scale=-1.0)
                yield
                self.tt("dve", f2[si][:], self.ps[:, bd, :], f1[si][:], ALU.mult, [self.Tps[bd], Tf1[si]], [Tf2[si]])
                yield
                self.tt("dve", gC[:, hc * 2 + ch, J * 512:(J + 1) * 512], self.ps[:, oc[ch], :], f2[si][:], ALU.mult,
                        [self.Tps[oc[ch]], Tf2[si]], [TgC[J]])
                yield

        for hc in range(4):
            for k in range(2):
                self.load_w(self.w_in[:, 9216 + hc * 256 + k * 128:9216 + hc * 256 + (k + 1) * 128],
                            dst=wq[:, :, k * 128:(k + 1) * 128], Tdst=Twq)
                self.load_w(self.w_in[:, 10240 + hc * 256 + k * 128:10240 + hc * 256 + (k + 1) * 128],
                            dst=wz[:, :, k * 128:(k + 1) * 128], Tdst=Twz)
            pending = list(range(4))
            active = []
            while pending or active:
                while pending and len(active) < 2:
                    used = [e[1] for e in active]
                    si = 0 if 0 not in used else 1
                    active.append([unit_gen(hc, pending.pop(0), si), si])
                for ent in list(active):
                    try:
                        next(ent[0])
                    except StopIteration:
                        active.remove(ent)
        self.dump_dbg(2, gC, TgC)

    def phaseB(self):
        S = self.S
        gB, TgB = self.gB, self.TgB
        cf = self.cf
        if self.debug:
            for j in range(4):
                S.op("pool", lambda e, j=j: e.memset(gB[:, :, j * 512:(j + 1) * 512], 0.0), writes=[TgB[j]])
        OF = self.sb("bOF", [128, NT], F32)
        TOF = [T("bOF%d" % j) for j in range(4)]
        lrows = OF[0:8, 0:512].rearrange("p (r k) -> p r k", r=4)
        Tlrows = TOF[0]
        for r in range(4):
            S.dma(lambda e, r=r: e.dma_start(out=lrows[:, r, :], in_=self.vecs[3 + r, :].rearrange("(h k) -> h k", k=128)),
                  "cst", writes=[Tlrows])
        lbc = self.sb("lbc", [128, 32], F32)
        Tlbc = T("lbc")
        self.mm([(self.ps[:, 0, r * 8:(r + 1) * 8], lrows[:, r, :], cf[0:8, C_ID:C_ID + 8], True, True) for r in range(4)],
                [Tlrows, self.Tcf], [self.Tps[0]])
        p4 = self.ps[:, 0, 0:32].rearrange("p (r h) -> p r h", r=4)
        l4 = lbc[:, 0:32].rearrange("p (r h) -> p r h", r=4)
        self.copy("dve", lbc[:, 0:32], self.ps[:, 0, 0:32], [self.Tps[0]], [Tlbc])
        for d in range(2):
            self.tt("dve", lbc[:, d * 16:d * 16 + 8], lbc[:, d * 16 + 8:d * 16 + 16], lbc[:, d * 16:d * 16 + 8], ALU.subtract,
                    [Tlbc], [Tlbc])
            self.act(lbc[:, d * 16:d * 16 + 8], lbc[:, d * 16:d * 16 + 8], AF.Exp, [Tlbc], [Tlbc])
            self.act(lbc[:, d * 16:d * 16 + 8], lbc[:, d * 16:d * 16 + 8], AF.Ln, [Tlbc], [Tlbc], bias=1.0)
            self.act(lbc[:, d * 16:d * 16 + 8], lbc[:, d * 16:d * 16 + 8], AF.Exp, [Tlbc], [Tlbc], scale=-1.0)
        cmask = self.sb("cmask", [128, 512], F32)
        Tcmask = T("cmask")
        S.op("pool", lambda e: e.memset(cmask[:], 1.0), writes=[Tcmask])
        S.op("pool", lambda e: e.memset(cmask[:].rearrange("p (c s) -> p c s", s=64)[:, :, 0:1], 0.0), writes=[Tcmask])
        wH2 = [self.sb("wH%d" % i, [128, 8, 640], BF16) for i in range(2)]
        TwH2 = [T("wH%d" % i) for i in range(2)]

        class WS:
            pass
        sets = []
        for i in range(2):
            w = WS()
            def mk(nm, shape, dt, i=i, w=w):
                setattr(w, nm, self.sb("b%s%d" % (nm, i), shape, dt))
                setattr(w, "T" + nm, T("b%s%d" % (nm, i)))
            mk("A1", [128, 512], F32); mk("A2", [128, 512], F32); mk("G", [128, 512], F32); mk("P", [128, 512], F32)
            mk("CR", [128, 512], F32); mk("EI", [128, 512], F32); mk("KKT", [128, 512], BF16); mk("KBT", [128, 512], BF16)
            mk("Vb", [128, 4, 128], BF16); mk("DEC", [128, 8], F32)
            mk("KB", [128, 4, 128], BF16); mk("QB", [128, 512], BF16); mk("KI", [128, 512], BF16)
            mk("AM", [128, 4, 128], BF16); mk("SN", [128, 8, 128], F32); mk("SNb", [128, 8, 128], BF16)
            sets.append(w)
        OS = self.sb("bOS", [128, 512], F32); TOS = T("bOS")
        SQ = self.sb("bSQ", [128, 512], BF16); TSQ = T("bSQ")
        F1 = self.sb("bF1", [128, 512], F32); TF1 = T("bF1")
        F2 = self.sb("bF2", [128, 512], F32); TF2 = T("bF2")
        Sp = [self.sb("bSp%d" % d, [128, 128], F32) for d in range(2)]
        TSp = [T("bSp%d" % d) for d in range(2)]
        dn = {0: "f", 1: "b"}
        nblk = 0
        for hb in range(int(os.environ.get('K_BHEADS', '8'))):
            wH, TwH = wH2[hb % 2], TwH2[hb % 2]
            nheads = int(os.environ.get('K_BHEADS', '8'))
            for hl in ([0, 1] if hb == 0 else [hb + 1]):
                if hl < nheads:
                    base = 4096 + hl * 640
                    for k in range(5):
                        self.load_w(self.w_in[:, base + k * 128:base + (k + 1) * 128],
                                    dst=wH2[hl % 2][:, :, k * 128:(k + 1) * 128], Tdst=TwH2[hl % 2])
            for d in range(2):
                S.op("pool", lambda e, d=d: e.memset(Sp[d][:], 0.0), writes=[TSp[d]])
            fw = [(0, jb, True, self.hT, self.ThT) for jb in range(4)]
            bo = [(1, jb, False, self.hTo, self.ThTo) for jb in (3, 2, 1, 0)]
            bw = [(1, jb, True, self.hT, self.ThT) for jb in (3, 2, 1, 0)]
            sweeps = []
            for i in range(4):
                sweeps += [fw[i], bo[i]]
            sweeps += bw
            def block_gen(hb, d, jb, wout, src, Tsrc, si, wH=wH, TwH=TwH):
                w = sets[si]
                ba, bb, bc, bd = [4 * si + x for x in range(4)]
                dk = dn[d]
                tok0 = jb * 512
                fcol = 128 * (1 + d)
                self.proj_fm(wH, TwH, fcol, src, Tsrc[jb], tok0, 512, ba)
                yield
                mV = []
                for t in range(4):
                    for c in range(8):
                        mV.append((self.ps[:, bb, t * 128:(t + 1) * 128], src[:, c, tok0 + t * 128:tok0 + (t + 1) * 128],
                                   wH[:, c, 384:512], c == 0, c == 7))
                self.mm(mV, [TwH, Tsrc[jb]], [self.Tps[bb]])
                yield
                lbcol = lbc[:, d * 16 + hb:d * 16 + hb + 1]
                self.act(w.A1[:], self.ps[:, ba, :], AF.Exp, [self.Tps[ba]], [w.TA1], scale=-1.0)
                yield
                self.copy("dve", w.Vb[:], self.ps[:, bb, :].rearrange("p (a b) -> p a b", a=4), [self.Tps[bb]], [w.TVb])
                yield
                self.act(w.A2[:], w.A1[:], AF.Ln, [w.TA1], [w.TA2], bias=1.0)
                yield
                self.act(w.A1[:], w.A1[:], AF.Ln, [w.TA1, Tlbc], [w.TA1], bias=1.0, scale=lbcol)
                yield
                self.tt("dve", w.G[:], w.A1[:], w.A2[:], ALU.subtract, [w.TA1, w.TA2], [w.TG])
                yield
                S.op("dve", lambda e, w=w: e.tensor_tensor_scan(out=w.P[:], data0=cmask[:], data1=w.G[:], initial=0.0,
                                                                op0=ALU.mult, op1=ALU.add),
                     reads=[w.TG, Tcmask], writes=[w.TP])
                yield
                self.act(w.A2[:], w.G[:], AF.Exp, [w.TG], [w.TA2])
                yield
                self.ts("pool", w.KKT[:], w.A2[:], -1.0, 1.0, ALU.mult, ALU.add, [w.TA2], [w.TKKT])
                yield
                P3 = w.P[:].rearrange("p (c s) -> p c s", s=64)
                tot = P3[:, :, 63:64]
                totb = tot.to_broadcast([128, 8, 64])
                CR3 = w.CR[:].rearrange("p (c s) -> p c s", s=64)
                if d == 0:
                    self.tt("dve", CR3, totb, P3, ALU.subtract, [w.TP], [w.TCR])
                    yield
                    bsrc, Tb = w.P, w.TP
                else:
                    self.tt("dve", w.CR[:], w.P[:], w.G[:], ALU.subtract, [w.TP, w.TG], [w.TCR])
                    yield
                    self.tt("dve", w.A1[:].rearrange("p (c s) -> p c s", s=64), totb, CR3, ALU.subtract, [w.TP, w.TCR], [w.TA1])
                    yield
                    bsrc, Tb = w.A1, w.TA1
                self.act(w.DEC[:], P3[:, :, 63], AF.Exp, [w.TP], [w.TDEC])
                yield
                if wout:
                    self.act(w.EI[:], bsrc[:], AF.Exp, [Tb], [w.TEI], scale=-1.0)
                    yield
                    self.act(w.A2[:], bsrc[:], AF.Exp, [Tb], [w.TA2])
                    yield
                self.act(w.CR[:], w.CR[:], AF.Exp, [w.TCR], [w.TCR])
                yield
                self.tt("pool", w.KBT[:], w.KKT[:], w.CR[:], ALU.mult, [w.TKKT, w.TCR], [w.TKBT])
                yield
                pT = self.ps[:, bd, :].bitcast(BF16)

                def tr(e, pT=pT, w=w):
                    ins = None
                    for t in range(4):
                        ins = e.transpose(out=pT[:, t * 128:(t + 1) * 128], in_=w.KBT[:, t * 128:(t + 1) * 128], identity=self.idb)
                    return ins
                S.op("pe", tr, reads=[w.TKBT, self.Tcb], writes=[self.Tps[bd]])
                yield
                self.copy("act", w.KB[:], pT[:, 0:512].rearrange("p (a b) -> p a b", a=4), [self.Tps[bd]], [w.TKB])
                yield
                if wout:
                    self.proj_fm(wH, TwH, 0, src, Tsrc[jb], tok0, 512, bc)
                    yield
                    self.tt("dve", w.QB[:], self.ps[:, bc, :], w.A2[:], ALU.mult, [self.Tps[bc], w.TA2], [w.TQB])
                    yield
                    self.tt("pool", w.KI[:], w.KKT[:], w.EI[:], ALU.mult, [w.TKKT, w.TEI], [w.TKI])
                    yield
                torder = range(4) if d == 0 else range(3, -1, -1)
                corder = (0, 1) if d == 0 else (1, 0)
                chunks = [(t, ch) for t in torder for ch in corder]
                mU = []
                for n, (t, ch) in enumerate(chunks):
                    pr = slice(ch * 64, (ch + 1) * 64)
                    ub = ba if ch == 0 else bb
                    mU.append((self.ps[:, ub, t * 128:(t + 1) * 128], w.KB[pr, t, :], w.Vb[pr, t, :], True, True))
                self.mm(mU, [w.TKB, w.TVb], [self.Tps[ba], self.Tps[bb]])
                yield
                yield ("state", d)
                self.copy("pool", w.SN[:, 0, :], Sp[d][:], [TSp[d]], [w.TSN])
                yield
                for n, (t, ch) in enumerate(chunks):
                    ub = ba if ch == 0 else bb
                    ups = self.ps[:, ub, t * 128:(t + 1) * 128]
                    if n < 7:
                        self.stt("dve", w.SN[:, n + 1, :], w.SN[:, n, :], w.DEC[:, t * 2 + ch:t * 2 + ch + 1], ups, ALU.mult, ALU.add,
                                 [w.TSN, w.TDEC, self.Tps[ub]], [w.TSN])
                        yield
                    else:
                        self.stt("dve", Sp[d][:], w.SN[:, n, :], w.DEC[:, t * 2 + ch:t * 2 + ch + 1], ups, ALU.mult, ALU.add,
                                 [w.TSN, w.TDEC, self.Tps[ub]], [TSp[d]])
                        yield
                if wout:
                    self.copy("act", w.SNb[:], w.SN[:], [w.TSN], [w.TSNb])
                    yield
                    mA = []
                    for t in range(4):
                        mA.append((self.ps[:, bc, t * 128:(t + 1) * 128], w.KI[:, t * 128:(t + 1) * 128], w.QB[:, t * 128:(t + 1) * 128], True, True))
                    self.mm(mA, [w.TKI, w.TQB], [self.Tps[bc]])
                    yield
                    mk_ = cf[:, C_MASK[dk]:C_MASK[dk] + 128].unsqueeze(1).to_broadcast([128, 4, 128])
                    self.tt("dve", w.AM[:], self.ps[:, bc, :].rearrange("p (a b) -> p a b", a=4), mk_, ALU.mult,
                            [self.Tps[bc], self.Tcf], [w.TAM])
                    yield
                    mI = []
                    for t in range(4):
                        mI.append((self.ps[:, bd, t * 128:(t + 1) * 128], w.Vb[:, t, :], w.AM[:, t, :], True, True))
                    self.mm(mI, [w.TVb, w.TAM], [self.Tps[bd]])
                    yield
                    mN = []
                    for n, (t, ch) in enumerate(chunks):
                        c0 = t * 128 + ch * 64
                        mN.append((self.ps[:, bc, c0:c0 + 64], w.SNb[:, n, :], w.QB[:, c0:c0 + 64], True, True))
                    self.mm(mN, [w.TSNb, w.TQB], [self.Tps[bc]])
                    yield
                if wout and d == 0:
                    self.copy("act", w.A1[:], self.ps[:, bc, :], [self.Tps[bc]], [w.TA1])
                    yield
                    self.tt("dve", OF[:, tok0:tok0 + 512], self.ps[:, bd, :], w.A1[:], ALU.add, [self.Tps[bd], w.TA1], [TOF[jb]])
                    yield
                elif wout:
                    self.copy("act", w.A1[:], self.ps[:, bc, :], [self.Tps[bc]], [w.TA1])
                    yield
                    self.tt("dve", w.P[:], self.ps[:, bd, :], OF[:, tok0:tok0 + 512], ALU.add, [self.Tps[bd], TOF[jb]], [w.TP])
                    yield
                    self.tt("pool", w.P[:], w.P[:], w.A1[:], ALU.add, [w.TP, w.TA1], [w.TP])
                    yield
                    self.act(w.KKT[:], w.P[:], AF.Square, [w.TP], [w.TKKT])
                    yield
                    self.mm([(self.ps[:, ba, :], self.ones128[:], w.KKT[:], True, True)], [w.TKKT, self.Tcb], [self.Tps[ba]])
                    yield
                    self.proj_fm(wH, TwH, 512, self.hT, self.ThT[jb], tok0, 512, bb)
                    yield
                    self.act(w.A1[:], self.ps[:, ba, :], AF.Ln, [self.Tps[ba]], [w.TA1], bias=EPS)
                    yield
                    self.act(w.G[:], self.ps[:, bb, :], AF.Exp, [self.Tps[bb]], [w.TG], scale=-1.0)
                    yield
                    self.act(w.G[:], w.G[:], AF.Ln, [w.TG], [w.TG], bias=1.0)
                    yield
                    self.stt("dve", w.A1[:], w.A1[:], 0.5, w.G[:], ALU.mult, ALU.add, [w.TA1, w.TG], [w.TA1])
                    yield
                    self.act(w.A1[:], w.A1[:], AF.Exp, [w.TA1], [w.TA1], scale=-1.0)
                    yield
                    self.stt("dve", w.G[:], w.P[:], self.cols[:, 2:3], self.ps[:, bb, :], ALU.mult, ALU.mult,
                             [w.TP, self.Tcols, self.Tps[bb]], [w.TG])
                    yield
                    self.tt("dve", gB[:, hb, tok0:tok0 + 512], w.G[:], w.A1[:], ALU.mult, [w.TG, w.TA1], [TgB[jb]])
                    yield

            active = []
            pending = [(d, jb, wout, src, Tsrc) for (d, jb, wout, src, Tsrc) in sweeps]
            done_dirs = {0: 0, 1: 0}
            started_dirs = {0: 0, 1: 0}
            nstart = 0
            while pending or active:
                while pending and len(active) < int(os.environ.get("K_BWIN", "2")):
                    (d, jb, wout, src, Tsrc) = pending.pop(0)
                    used = [e[4] for e in active]
                    si = 0 if 0 not in used else 1
                    g = block_gen(hb, d, jb, wout, src, Tsrc, si)
                    active.append([g, d, started_dirs[d], False, si])
                    started_dirs[d] += 1
                    nstart += 1
                progressed = False
                for ent in list(active):
                    g, d, order, blocked, _si = ent
                    if blocked:
                        if done_dirs[d] < order:
                            continue
                        ent[3] = False
                    try:
                        r = next(g)
                        progressed = True
                        if isinstance(r, tuple) and r[0] == "state" and done_dirs[d] < order:
                            ent[3] = True
                    except StopIteration:
                        active.remove(ent)
                        done_dirs[d] += 1
                        progressed = True
                assert progressed
        self.dump_dbg(1, gB, TgB)

    def phaseF(self):
        S = self.S
        Wo = self.sb("Wo", [128, 8, D], BF16)
        TWo = T("Wo")
        for k in range(8):
            self.load_w(self.w_out[:, k * 128:(k + 1) * 128], dst=Wo[:, :, k * 128:(k + 1) * 128], Tdst=TWo)
        pn = self.sb("post_bc", [128, D], F32)
        Tpn = T("postn")
        S.dma(lambda e: e.dma_start(out=pn[:], in_=self.vecs[1, :].partition_broadcast(128)), "cst", writes=[Tpn])
        xs = [self.sb("xres%d" % i, [128, D], F32) for i in range(2)]
        Txs = [T("xres%d" % i) for i in range(2)]
        ob = [self.sb("ob%d" % i, [128, D], F32) for i in range(2)]
        Tob = [T("ob%d" % i) for i in range(2)]
        junk = self.sb("junk", [128, D], BF16)
        Tj = T("junk")
        stat = self.stat
        for t in range(16):
            sl = t % 2
            b0 = 0 if sl == 0 else 2
            S.dma(lambda e, t=t, sl=sl: e.dma_start(out=xs[sl][:], in_=self.x_own[t * 128:(t + 1) * 128, :]),
                  "xres%d" % sl, writes=[Txs[sl]])
            mms = []
            for half in range(2):
                for dc in range(8):
                    mms.append((self.ps[:, b0 + half, :], self.merged[:, dc, t * 128:(t + 1) * 128],
                                Wo[:, dc, half * 512:(half + 1) * 512], dc == 0, dc == 7))
            self.mm(mms, [TWo] + [self.Tmg[dc][t // 4] for dc in range(8)], [self.Tps[b0], self.Tps[b0 + 1]])
            c0 = 128 + t * 4
            yps = self.ps[:, b0:b0 + 2, :].rearrange("p a b -> p (a b)")
            self.act(junk[:], yps, AF.Square, [self.Tps[b0], self.Tps[b0 + 1]], [Tj, self.Tstat], accum_out=stat[:, c0:c0 + 1])
            self.act(stat[:, c0 + 1:c0 + 2], stat[:, c0:c0 + 1], AF.Ln, [self.Tstat], [self.Tstat], bias=EPS, scale=1.0 / D)
            self.act(stat[:, c0 + 2:c0 + 3], stat[:, c0 + 1:c0 + 2], AF.Exp, [self.Tstat], [self.Tstat], scale=-0.5)
            self.stt("dve", ob[sl][:], yps, stat[:, c0 + 2:c0 + 3], pn[:], ALU.mult, ALU.mult,
                     [self.Tps[b0], self.Tps[b0 + 1], self.Tstat, Tpn], [Tob[sl]])
            self.tt("pool", ob[sl][:], ob[sl][:], xs[sl][:], ALU.add, [Tob[sl], Txs[sl]], [Tob[sl]])
            tok = S.dma(lambda e, t=t, sl=sl: e.dma_start(out=self.y[t * 128:(t + 1) * 128, :], in_=ob[sl][:]),
                        "yst%d" % sl, reads=[Tob[sl]])
            self.out_toks.append(tok)
        for tok in self.out_toks[-2:]:
            S.wait_tok("sp", tok)
        if self.debug and S.dma_cnt.get("dbg", 0):
            S.wait_tok("sp", ("dbg", 16 * S.dma_cnt["dbg"]))

    def zero_merged(self):
        for dc in range(8):
            self.S.op("pool", lambda e, dc=dc: e.memset(self.merged[:, dc, :], 0.0), writes=self.Tmg[dc])

    def build(self):
        self.setup_globals()
        self.phase0()
        self._mid = []
        self.gB = self._alloc(self._mid, "sb", "gB", [128, 8, NT], BF16)
        self.TgB = [T("gB%d" % j) for j in range(4)]
        if "B" in self.phases:
            self.phaseB()
            self.end_phase()
        self.gA = self._alloc(self._mid, "sb", "gA", [128, 8, NT], BF16)
        self.TgA = [T("gA%d" % j) for j in range(4)]
        if "A" in self.phases:
            self.phaseA()
            self.end_phase()
        self.S.barrier()
        first = True
        for X, ph, g, Tg in ((0, "A", self.gA, self.TgA), (1, "B", self.gB, self.TgB)):
            if ph in self.phases:
                self.branch_merge(X, g, Tg, first)
                first = False
                self.end_phase()
        self.free_mid()
        if "C" in self.phases:
            self.gC = self.sb("gC", [128, 8, NT], BF16)
            self.TgC = [T("gC%d" % j) for j in range(4)]
            self.phaseC()
            self.branch_merge(2, self.gC, self.TgC, first)
            first = False
            self.end_phase()
        if "F" in self.phases:
            if first:
                self.zero_merged()
            self.phaseF()
        elif self.debug and self.S.dma_cnt.get("dbg", 0):
            self.S.wait_tok("sp", ("dbg", 16 * self.S.dma_cnt["dbg"]))
        self.end_phase()
        return self.nc


def _prep_shared(inputs):
    f = lambda a: np.ascontiguousarray(np.asarray(a, dtype=np.float32))
    w_in = f(inputs["w_in"])[0]
    aq, ak, av, az = w_in[:, 0:1024], w_in[:, 1024:2048], w_in[:, 2048:3072], w_in[:, 3072:4096]
    rest = w_in[:, 4096:]
    blocks = []
    for h in range(8):
        q = np.concatenate([aq[:, h * 64:(h + 1) * 64], aq[:, 512 + h * 64:512 + (h + 1) * 64]], axis=1)
        k = np.concatenate([ak[:, h * 64:(h + 1) * 64], ak[:, 512 + h * 64:512 + (h + 1) * 64]], axis=1)
        blocks += [q, k, av[:, h * 128:(h + 1) * 128], az[:, h * 128:(h + 1) * 128]]
    wA = np.concatenate(blocks, axis=1)
    def wB(swap):
        fo, bo = (2048, 1024) if swap else (1024, 2048)
        bl = []
        for hb in range(8):
            sl = slice(hb * 128, (hb + 1) * 128)
            bl += [rest[:, 0:1024][:, sl], rest[:, fo:fo + 1024][:, sl], rest[:, bo:bo + 1024][:, sl],
                   rest[:, 3072:4096][:, sl], rest[:, 4096:5120][:, sl]]
        return np.concatenate(bl, axis=1)
    tail = rest[:, 5120:]
    w_even = np.ascontiguousarray(np.concatenate([wA, wB(False), tail], axis=1))
    w_odd = np.ascontiguousarray(np.concatenate([wA, wB(True), tail], axis=1))
    lbl = f(inputs["lb_logits"])
    vec_even = np.zeros((8, 1024), np.float32)
    vec_even[0] = f(inputs["pre_norm"])[0]
    vec_even[1] = f(inputs["post_norm"])[0]
    vec_even[2] = f(inputs["mem_norm"])[0]
    vec_even[3], vec_even[4] = lbl[0, 0], lbl[0, 1]
    vec_even[5], vec_even[6] = lbl[1, 0], lbl[1, 1]
    vec_odd = vec_even.copy()
    vec_odd[3], vec_odd[4] = lbl[1, 0], lbl[1, 1]
    vec_odd[5], vec_odd[6] = lbl[0, 0], lbl[0, 1]
    small = np.zeros((1, 512), np.float32)
    small[0, 0:64] = f(inputs["lambda_q1"])[0]
    small[0, 64:128] = f(inputs["lambda_k1"])[0]
    small[0, 128:192] = f(inputs["lambda_q2"])[0]
    small[0, 192:256] = f(inputs["lambda_k2"])[0]
    small[0, 256:384] = f(inputs["diff_subln"])[0]
    small[0, 384:512] = f(inputs["hgrn_norm"])[0]
    return dict(w_even=w_even, w_odd=w_odd, vec_even=vec_even, vec_odd=vec_odd, small=small,
                w_kv=f(inputs["w_mem_kv"])[0], w_br=f(inputs["w_branch"])[0], w_out=f(inputs["w_out"])[0],
                relb=f(inputs["rel_bias"]))


def _in_maps(inputs):
    sh = _prep_shared(inputs)
    x = np.asarray(inputs["x"], dtype=np.float32)
    mem = np.asarray(inputs["mem"], dtype=np.float32)
    cst = _consts()
    tabs = [_oh_tables(False), _oh_tables(True)]
    maps = []
    for c in range(8):
        b, half = c // 2, c % 2
        xb = x[b] if half == 0 else x[b, ::-1]
        maps.append({
            "x_own": np.ascontiguousarray(xb[0:NT]), "x_oth": np.ascontiguousarray(xb[NT:2 * NT]),
            "mem": np.ascontiguousarray(mem[b]),
            "w_in": sh["w_even"] if half == 0 else sh["w_odd"],
            "w_kv": sh["w_kv"], "w_br": sh["w_br"], "w_out": sh["w_out"],
            "vecs": sh["vec_even"] if half == 0 else sh["vec_odd"],
            "small": sh["small"], "relb": sh["relb"], "cst": cst,
            "oh": tabs[half][0], "ohfar": tabs[half][1],
        })
    return maps


def _run(inputs, phases="0ABCF", debug=False, cores=None):
    prog = Prog(phases=phases, debug=debug)
    nc = prog.build()
    maps = _in_maps(inputs)
    ids = list(range(8)) if cores is None else cores
    res = run_bass_kernel_spmd(nc, [maps[c] for c in ids], core_ids=list(range(len(ids))))
    return res, ids


def kernel(**inputs):
    res, ids = _run(inputs, phases=os.environ.get("K_PHASES", "0ABCF"))
    out = np.zeros((4, 4096, D), np.float32)
    for r, c in zip(res.results, ids):
        b, half = c // 2, c % 2
        y = np.asarray(r["y"], dtype=np.float32)
        if half == 0:
            out[b, 0:NT] = y
        else:
            out[b, NT:2 * NT] = y[::-1]
    return out
```

```python
import math
import os
import numpy as np
import concourse.bass as bass
import concourse.mybir as mybir
from concourse.ap import AP
from concourse.bass_utils import run_bass_kernel_spmd

F32 = mybir.dt.float32
BF16 = mybir.dt.bfloat16
AF = mybir.ActivationFunctionType
ALU = mybir.AluOpType
AX = mybir.AxisListType

ENGS = ("pe", "act", "dve", "pool", "sp")
NT = 2048
D = 1024
EPS = 1e-6
LAM_INIT = 0.2
WX = 1280
WW = 1152


class T:
    __slots__ = ("name", "w", "r", "excl")

    def __init__(self, name, excl=False):
        self.name = name
        self.w = None
        self.r = {}
        self.excl = excl


class Sched:
    def __init__(self, nc):
        self.nc = nc
        self.streams = {e: [] for e in ENGS}
        self.count = {e: 0 for e in ENGS}
        self.seen = {e: {} for e in ENGS}
        self.sems = {}
        self.dma_cnt = {}
        self._ctx = []

    def sem(self, key):
        if key not in self.sems:
            cm = self.nc.semaphore("s_" + str(key))
            self.sems[key] = cm.__enter__()
            self._ctx.append(cm)
        return self.sems[key]

    def _waits(self, eng, reads, writes):
        need = {}

        def add(tok):
            if tok is None:
                return
            k, v = tok
            if need.get(k, 0) < v:
                need[k] = v
        for t in reads:
            add(t.w)
            if t.excl:
                for k, v in t.r.items():
                    if k != eng:
                        add((k, v))
        for t in writes:
            add(t.w)
            for k, v in t.r.items():
                add((k, v))
        if eng == "pe":
            need.pop("pe", None)
        out = []
        seen = self.seen[eng]
        for k, v in need.items():
            if seen.get(k, 0) < v:
                seen[k] = v
                out.append((k, v))
        return out

    def _commit(self, tok, reads, writes):
        for t in writes:
            t.w = tok
            t.r = {}
        for t in reads:
            if t.r.get(tok[0], 0) < tok[1]:
                t.r[tok[0]] = tok[1]

    def op(self, eng, fn, reads=(), writes=(), single=False):
        waits = self._waits(eng, reads, writes)
        self.count[eng] += 1
        tok = (eng, self.count[eng])
        self.streams[eng].append((waits, fn, tok, single))
        self._commit(tok, reads, writes)
        return tok

    def dma(self, fn, semkey, reads=(), writes=(), q="sp"):
        if semkey == "cst":
            self._ncst = getattr(self, "_ncst", 0) + 1
            semkey = "cst%d" % self._ncst
        waits = self._waits(q, reads, writes)
        self.dma_cnt[semkey] = self.dma_cnt.get(semkey, 0) + 1
        tok = (semkey, 16 * self.dma_cnt[semkey])
        self.sem(semkey)
        self.streams[q].append((waits, fn, tok, False))
        self._commit(tok, reads, writes)
        return tok

    def barrier(self):
        for e in ENGS:
            for k in list(self.sems.keys()) + [x for x in ENGS if x not in self.sems]:
                v = self.count[k] if k in self.count else 16 * self.dma_cnt.get(k, 0)
                if v > 0 and k != e:
                    self.wait_tok(e, (k, v))

    def wait_tok(self, eng, tok):
        if self.seen[eng].get(tok[0], 0) < tok[1]:
            self.seen[eng][tok[0]] = tok[1]
            self.streams[eng].append(([tok], None, None, False))

    def emit(self, block):
        engmap = {"pe": "tensor", "act": "scalar", "dve": "vector", "pool": "gpsimd", "sp": "sync"}
        for e in ENGS:
            self.sem(e)
        sched = self

        def make(e):
            items = sched.streams[e]
            sched.streams[e] = []

            def body(engine):
                embed = not os.environ.get("K_NOEMBED")
                for waits, fn, tok, single in items:
                    emb = None
                    if single and waits and embed:
                        emb = waits[-1]
                        waits = waits[:-1]
                    for k, v in waits:
                        engine.wait_ge(sched.sems[k], v)
                    if fn is None:
                        continue
                    ins = fn(engine)
                    first = ins
                    if isinstance(ins, tuple):
                        first, ins = ins
                    if emb is not None:
                        first._wait_ge(sched.sems[emb[0]], emb[1])
                    if tok[0] == e:
                        ins.then_inc(sched.sems[e], 1)
                    else:
                        ins.then_inc(sched.sems[tok[0]], 16)
            return body
        for e in ENGS:
            getattr(block, engmap[e])(make(e))

    def close(self):
        for cm in reversed(self._ctx):
            cm.__exit__(None, None, None)


def _t5_bucket_np(rel):
    rel = np.asarray(rel, dtype=np.int64)
    nb = 16
    max_exact = 8
    ret = np.where(rel > 0, nb, 0)
    n = np.abs(rel)
    nf = np.maximum(n, 1).astype(np.float32)
    large = max_exact + (np.log(nf / np.float32(max_exact)) / np.float32(math.log(128 / max_exact))
                         * np.float32(nb - max_exact)).astype(np.int32)
    large = np.minimum(large, nb - 1)
    return ret + np.where(n < max_exact, n, large)


C_ID, C_AJ, C_ONE = 0, 128, 256
C_TRI = {"f": 384, "b": 512}
C_SREV = {"f": 640, "b": 768}
C_MASK = {"f": 896, "b": 1024}
C_CIND = 1152
C_N = 1160


def _consts():
    c = np.zeros((128, C_N), np.float32)
    i = np.arange(128)
    c[:, C_ID:C_ID + 128] = np.eye(128)
    c[:, C_AJ:C_AJ + 128] = np.eye(128)[::-1]
    c[:, C_ONE:C_ONE + 128] = 1.0
    s = i[:, None]
    cc = i[None, :]
    same = (s // 64) == (cc // 64)
    c[:, C_TRI["f"]:C_TRI["f"] + 128] = (same & (s <= cc))
    c[:, C_TRI["b"]:C_TRI["b"] + 128] = (same & (s >= cc))
    c[:, C_SREV["f"]:C_SREV["f"] + 128] = (same & (s > cc))
    c[:, C_SREV["b"]:C_SREV["b"] + 128] = (same & (s < cc))
    c[:, C_MASK["f"]:C_MASK["f"] + 128] = (same & (s <= cc))
    c[:, C_MASK["b"]:C_MASK["b"] + 128] = (same & (s >= cc))
    c[:, C_CIND] = (i < 64)
    c[:, C_CIND + 1] = (i >= 64)
    return c


def _oh_tables(mirror):
    x = np.arange(WX)
    delta = 639 - x
    if mirror:
        delta = -delta
    bk = _t5_bucket_np(delta)
    oh = np.zeros((32, WX), np.float32)
    oh[bk, x] = 1.0
    ohfar = np.zeros((32, 256), np.float32)
    bm = int(_t5_bucket_np(np.array([1000 if mirror else -1000]))[0])
    bp = int(_t5_bucket_np(np.array([-1000 if mirror else 1000]))[0])
    ohfar[bm, 0:128] = 1.0
    ohfar[bp, 128:256] = 1.0
    return oh, ohfar


class Prog:
    def __init__(self, phases="0ABCF", debug=False):
        self.phases = phases
        self.debug = debug
        nc = bass.Bass("TRN2", target_bir_lowering=False)
        self.nc = nc
        self.S = Sched(nc)
        self._glob = []
        self._ph = []
        dt = nc.dram_tensor
        self.x_own = dt("x_own", [NT, D], F32, kind="ExternalInput").ap()
        self.x_oth = dt("x_oth", [NT, D], F32, kind="ExternalInput").ap()
        self.mem = dt("mem", [256, D], F32, kind="ExternalInput").ap()
        self.w_in = dt("w_in", [D, 14336], F32, kind="ExternalInput").ap()
        self.w_kv = dt("w_kv", [D, 2048], F32, kind="ExternalInput").ap()
        self.w_br = dt("w_br", [3, D, D], F32, kind="ExternalInput").ap()
        self.w_out = dt("w_out", [D, D], F32, kind="ExternalInput").ap()
        self.vecs = dt("vecs", [8, D], F32, kind="ExternalInput").ap()
        self.small = dt("small", [1, 512], F32, kind="ExternalInput").ap()
        self.relb = dt("relb", [32, 8], F32, kind="ExternalInput").ap()
        self.cst = dt("cst", [128, C_N], F32, kind="ExternalInput").ap()
        self.oh = dt("oh", [32, WX], F32, kind="ExternalInput").ap()
        self.ohfar = dt("ohfar", [32, 256], F32, kind="ExternalInput").ap()
        self.y = dt("y", [NT, D], F32, kind="ExternalOutput").ap()
        self.wscr_h = dt("wscr", [8, WX], F32)
        self.wscr = self.wscr_h.ap()
        if debug:
            self.dbg = dt("dbg", [3, 128, 4 * NT], F32, kind="ExternalOutput").ap()
        self.nwl = 0
        self.out_toks = []

    def _alloc(self, lst, kind, name, shape, dtype):
        self._nalloc = getattr(self, "_nalloc", 0) + 1
        name = "%s_%d" % (name, self._nalloc)
        cm = (self.nc.sbuf_tensor if kind == "sb" else self.nc.psum_tensor)(name, shape, dtype)
        h = cm.__enter__()
        lst.append(cm)
        return h

    def gsb(self, name, shape, dtype):
        return self._alloc(self._glob, "sb", name, shape, dtype)

    def sb(self, name, shape, dtype):
        return self._alloc(self._ph, "sb", name, shape, dtype)

    def end_phase(self):
        if not os.environ.get("K_NOBAR"):
            self.S.barrier()
        with self.nc.Block() as block:
            self.S.emit(block)
        for cm in reversed(self._ph):
            cm.__exit__(None, None, None)
        self._ph = []

    def act(self, out, in_, func, reads, writes, bias=0.0, scale=1.0, accum_out=None):
        kw = {}
        if accum_out is not None:
            kw["accum_out"] = accum_out
        return self.S.op("act", lambda e: e.activation(out=out, in_=in_, func=func, bias=bias, scale=scale, **kw),
                         reads=reads, writes=writes, single=True)

    def tt(self, eng, out, in0, in1, op, reads, writes):
        return self.S.op(eng, lambda e: e.tensor_tensor(out=out, in0=in0, in1=in1, op=op), reads=reads, writes=writes, single=True)

    def ts(self, eng, out, in0, s1, s2, op0, op1, reads, writes):
        if s2 is None:
            return self.S.op(eng, lambda e: e.tensor_scalar(out=out, in0=in0, scalar1=s1, scalar2=None, op0=op0),
                             reads=reads, writes=writes, single=True)
        return self.S.op(eng, lambda e: e.tensor_scalar(out=out, in0=in0, scalar1=s1, scalar2=s2, op0=op0, op1=op1),
                         reads=reads, writes=writes, single=True)

    def stt(self, eng, out, in0, scalar, in1, op0, op1, reads, writes):
        return self.S.op(eng, lambda e: e.scalar_tensor_tensor(out=out, in0=in0, scalar=scalar, in1=in1, op0=op0, op1=op1),
                         reads=reads, writes=writes, single=True)

    def copy(self, eng, out, in_, reads, writes):
        if eng == "act":
            return self.S.op("act", lambda e: e.copy(out=out, in_=in_), reads=reads, writes=writes, single=True)
        return self.S.op(eng, lambda e: e.tensor_copy(out=out, in_=in_), reads=reads, writes=writes, single=True)

    def mm(self, mms, reads, writes):
        def fn(e):
            ins = None
            first = None
            for (o, l, r, st, sp) in mms:
                ins = e.matmul(o, lhsT=l, rhs=r, start=st, stop=sp)
                if first is None:
                    first = ins
            return (first, ins)
        return self.S.op("pe", fn, reads=reads, writes=writes, single=True)

    def setup_globals(self):
        nc, S = self.nc, self.S
        self.ps = self._alloc(self._glob, "ps", "psall", [128, 8, 512], F32)
        self.Tps = [T("ps%d" % i, excl=True) for i in range(8)]
        self.cf = self.gsb("cf", [128, C_N], F32)
        self.Tcf = T("cf")
        self.cb = self.gsb("cb", [128, 384], BF16)
        self.Tcb = T("cb")
        self.ones128 = self.gsb("ones128", [128, 128], BF16)
        self.hT = self.gsb("hT", [128, 8, NT], BF16)
        self.ThT = [T("hT%d" % j) for j in range(4)]
        self.merged = self.gsb("merged", [128, 8, NT], BF16)
        self.hTo = self.merged
        self.ThTo = [T("hTo%d" % j) for j in range(4)]
        self.Tmg = [[T("mg%d_%d" % (dc, j)) for j in range(4)] for dc in range(8)]
        self.wst = [self.gsb("wst%d" % i, [128, 8, 128], F32) for i in range(2)]
        self.Twst = [T("wst%d" % i) for i in range(2)]
        self.wbf = [self.gsb("wbf%d" % i, [128, 8, 128], BF16) for i in range(3)]
        self.Twbf = [T("wbf%d" % i) for i in range(3)]
        self.stat = self.gsb("stat", [128, 256], F32)
        self.Tstat = T("stat")
        self.sm = self.gsb("sm", [1, 512], F32)
        self.Tsm = T("sm")
        self.cols = self.gsb("cols", [128, 8], F32)
        self.Tcols = T("cols")
        self.cfar = self.gsb("cfar", [128, 16], F32)
        self.Tcfar = T("cfar")

        S.dma(lambda e: e.dma_start(out=self.cf[:], in_=self.cst[:, :]), "cst", writes=[self.Tcf])
        S.dma(lambda e: e.dma_start(out=self.sm[:], in_=self.small[:, :]), "cst", writes=[self.Tsm])
        self.copy("dve", self.cb[:], self.cf[:, 0:384], [self.Tcf], [self.Tcb])
        self.ts("dve", self.ones128[:], self.cf[:, C_ONE:C_ONE + 128], 1.0 / 128, None, ALU.mult, None,
                [self.Tcf], [self.Tcb])
        S.op("pool", lambda e: e.memset(self.stat[:], 0.0), writes=[self.Tstat])
        self.idb = self.cb[:, 0:128]
        self.ajb = self.cb[:, 128:256]
        self.oneb = self.cb[:, 256:384]

    def psb(self, i):
        return self.ps[:, i, :]

    def load_w(self, src2d, ncol=128, dst=None, Tdst=None):
        S = self.S
        k = self.nwl
        self.nwl += 1
        st, Tst = self.wst[k % 2], self.Twst[k % 2]
        if dst is None:
            self.nwb = getattr(self, "nwb", 0) + 1
            bf, Tbf = self.wbf[self.nwb % 3], self.Twbf[self.nwb % 3]
        else:
            bf, Tbf = dst, Tdst
        src = src2d.rearrange("(c p) n -> p c n", p=128)
        q = "sp"
        S.dma(lambda e: e.dma_start(out=st[:, :, 0:ncol], in_=src), "wst%d" % (k % 2), writes=[Tst], q=q)
        ce = getattr(self, "cast_eng", "pool")
        if dst is None:
            S.op(ce, lambda e: e.tensor_copy(out=bf[:, :, 0:ncol], in_=st[:, :, 0:ncol]), reads=[Tst], writes=[Tbf], single=True)
        else:
            S.op(ce, lambda e: e.tensor_copy(out=bf, in_=st[:, :, 0:ncol]), reads=[Tst], writes=[Tbf], single=True)
        return bf, Tbf

    def norm_transpose(self, src_rows, wbc, Twbc, dst, Tdst, dst_cols, idx, xst, Txst, xbb, Txbb):
        S = self.S
        sl = idx % 2
        xs, Txs = xst[sl], Txst[sl]
        xb, Txb = xbb[sl], Txbb[sl]
        c0 = (idx % 64) * 4
        stat = self.stat
        S.dma(lambda e: e.dma_start(out=xs[:], in_=src_rows), "xld%d" % sl, writes=[Txs])
        self.act(xb[:], xs[:], AF.Square, [Txs], [Txb, self.Tstat], accum_out=stat[:, c0:c0 + 1])
        self.act(stat[:, c0 + 1:c0 + 2], stat[:, c0:c0 + 1], AF.Ln, [self.Tstat], [self.Tstat], bias=EPS, scale=1.0 / D)
        self.act(stat[:, c0 + 2:c0 + 3], stat[:, c0 + 1:c0 + 2], AF.Exp, [self.Tstat], [self.Tstat], scale=-0.5)
        self.stt("dve", xb[:], xs[:], stat[:, c0 + 2:c0 + 3], wbc[:], ALU.mult, ALU.mult,
                 [Txs, self.Tstat, Twbc], [Txb])
        pb = 6 + sl
        pT = self.ps[:, pb, :].bitcast(BF16)

        def tr(e):
            ins = None
            for c in range(8):
                ins = e.transpose(out=pT[:, c * 128:(c + 1) * 128], in_=xb[:, c * 128:(c + 1) * 128], identity=self.idb)
            return ins
        S.op("pe", tr, reads=[Txb, self.Tcb], writes=[self.Tps[pb]])
        eng = "act" if idx % 2 == 0 else "dve"
        self.copy(eng, dst[:, :, dst_cols], pT.rearrange("p (c t) -> p c t", c=8), [self.Tps[pb]], [Tdst])

    def phase0(self):
        S = self.S
        xst = [self.sb("xst%d" % i, [128, D], F32) for i in range(2)]
        Txst = [T("xst%d" % i) for i in range(2)]
        xbb = [self.sb("xbb%d" % i, [128, D], BF16) for i in range(2)]
        Txbb = [T("xbb%d" % i) for i in range(2)]
        pn = self.sb("pn_bc", [128, D], F32)
        Tpn = T("pn")
        S.dma(lambda e: e.dma_start(out=pn[:], in_=self.vecs[0, :].partition_broadcast(128)), "cst", writes=[Tpn])
        for i in range(32):
            own = i < 16
            j = i % 16
            src = (self.x_own if own else self.x_oth)[j * 128:(j + 1) * 128, :]
            dst = self.hT if own else self.hTo
            Td = (self.ThT if own else self.ThTo)[j // 4]
            self.norm_transpose(src, pn, Tpn, dst, Td, slice(j * 128, (j + 1) * 128), i, xst, Txst, xbb, Txbb)

        sm, cols = self.sm, self.cols
        tmp = self.sb("lamtmp", [1, 256], F32)
        Ttmp = T("lamtmp")
        self.tt("dve", tmp[:, 0:64], sm[:, 0:64], sm[:, 64:128], ALU.mult, [self.Tsm], [Ttmp])
        self.tt("dve", tmp[:, 64:128], sm[:, 128:192], sm[:, 192:256], ALU.mult, [self.Tsm], [Ttmp])
        S.op("dve", lambda e: e.reduce_sum(out=tmp[:, 128:130], in_=tmp[:, 0:128].rearrange("p (a b) -> p a b", a=2),
                                           axis=AX.X), reads=[Ttmp], writes=[Ttmp])
        self.act(tmp[:, 130:132], tmp[:, 128:130], AF.Exp, [Ttmp], [Ttmp])
        self.tt("dve", tmp[:, 132:133], tmp[:, 131:132], tmp[:, 130:131], ALU.subtract, [Ttmp], [Ttmp])
        self.ts("dve", tmp[:, 133:134], tmp[:, 132:133], -LAM_INIT, None, ALU.add, None, [Ttmp], [Ttmp])
        onesrow = self.cf[0:1, C_ONE:C_ONE + 128]
        self.mm([(self.ps[:, 0, 0:1], onesrow, tmp[:, 133:134], True, True),
                 (self.ps[:, 0, 1:2], sm[:, 256:384], self.cf[0:1, C_ONE:C_ONE + 1], True, True),
                 (self.ps[:, 0, 2:3], sm[:, 384:512], self.cf[0:1, C_ONE:C_ONE + 1], True, True)],
                [Ttmp, self.Tsm, self.Tcf], [self.Tps[0]])
        self.copy("dve", cols[:, 0:3], self.ps[:, 0, 0:3], [self.Tps[0]], [self.Tcols])

        rb = self.sb("rb", [32, 8], F32)
        ohs = self.sb("ohs", [32, WX], F32)
        ohf = self.sb("ohf", [32, 256], F32)
        Trb = T("rb")
        S.dma(lambda e: e.dma_start(out=rb[:], in_=self.relb[:, :]), "cst", writes=[Trb])
        S.dma(lambda e: e.dma_start(out=ohs[:], in_=self.oh[:, :]), "cst", writes=[Trb])
        S.dma(lambda e: e.dma_start(out=ohf[:], in_=self.ohfar[:, :]), "cst", writes=[Trb])
        wv = self.sb("wv", [8, WX], F32)
        Twv = T("wv")
        self.mm([(self.ps[0:8, 1, 0:512], rb[:], ohs[:, 0:512], True, True),
                 (self.ps[0:8, 2, 0:512], rb[:], ohs[:, 512:1024], True, True),
                 (self.ps[0:8, 3, 0:256], rb[:], ohs[:, 1024:1280], True, True),
                 (self.ps[:, 4, 0:8], ohf[:, 0:128], rb[:], True, True),
                 (self.ps[:, 4, 8:16], ohf[:, 128:256], rb[:], True, True)],
                [Trb], [self.Tps[1], self.Tps[2], self.Tps[3], self.Tps[4]])
        self.ts("dve", wv[:, 0:512], self.ps[0:8, 1, 0:512], 8.0, None, ALU.mult, None, [self.Tps[1]], [Twv])
        self.ts("dve", wv[:, 512:1024], self.ps[0:8, 2, 0:512], 8.0, None, ALU.mult, None, [self.Tps[2]], [Twv])
        self.ts("dve", wv[:, 1024:1280], self.ps[0:8, 3, 0:256], 8.0, None, ALU.mult, None, [self.Tps[3]], [Twv])
        self.copy("dve", self.cfar[:], self.ps[:, 4, 0:16], [self.Tps[4]], [self.Tcfar])
        self.Twscr = T("wscr")
        S.dma(lambda e: e.dma_start(out=self.wscr[:, :], in_=wv[:]), "cst", reads=[Twv], writes=[self.Twscr])
        self.end_phase()

    def sbA(self, name, shape, dtype):
        if not hasattr(self, "_mid"):
            self._mid = []
        assert not self._ph, "allocate mid tensors first"
        return self._alloc(self._mid, "sb", name, shape, dtype)

    def end_phaseA_keep(self):
        self.end_phase()

    def free_mid(self):
        for cm in reversed(self._mid):
            cm.__exit__(None, None, None)
        self._mid = []

    def proj_fm(self, wbf, Tw, c0, src, Tsrc_blk, tok0, ntok, pbank, extra_reads=()):
        mms = []
        for c in range(8):
            mms.append((self.ps[:, pbank, 0:ntok], wbf[:, c, c0:c0 + 128], src[:, c, tok0:tok0 + ntok], c == 0, c == 7))
        return self.mm(mms, [Tw, Tsrc_blk] + list(extra_reads), [self.Tps[pbank]])

    def proj_tm(self, wbf, Tw, c0, ncol, src, Tsrc_blk, tok0, pout, Tpout):
        mms = []
        for c in range(8):
            mms.append((pout, src[:, c, tok0:tok0 + 128], wbf[:, c, c0:c0 + ncol], c == 0, c == 7))
        return self.mm(mms, [Tw, Tsrc_blk], Tpout)

    def branch_merge(self, X, g, Tg, first):
        S = self.S
        gate0 = 11264 + X * 1024
        tmpf = [self.sb("bm_t%d" % i, [128, 512], F32) for i in range(2)]
        Ttmpf = [T("bm_t%d" % i) for i in range(2)]
        tmpb = [self.sb("bm_b%d" % i, [128, 512], BF16) for i in range(2)]
        Ttmpb = [T("bm_b%d" % i) for i in range(2)]
        it = 0
        for dc in range(8):
            wb, Twb = self.load_w(self.w_br[X, :, dc * 128:(dc + 1) * 128])
            wg, Twg = self.load_w(self.w_in[:, gate0 + dc * 128:gate0 + (dc + 1) * 128])
            for J in range(4):
                pby, pbg = (0, 1) if it % 2 == 0 else (2, 3)
                sl = it % 2
                it += 1
                mms = []
                for fc in range(8):
                    mms.append((self.ps[:, pby, :], wb[:, fc, 0:128], g[:, fc, J * 512:(J + 1) * 512], fc == 0, fc == 7))
                self.mm(mms, [Twb, Tg[J]], [self.Tps[pby]])
                self.proj_fm(wg, Twg, 0, self.hT, self.ThT[J], J * 512, 512, pbg)
                tf, Ttf = tmpf[sl], Ttmpf[sl]
                tb, Ttb = tmpb[sl], Ttmpb[sl]
                self.act(tf[:], self.ps[:, pbg, :], AF.Exp, [self.Tps[pbg]], [Ttf], scale=-1.0)
                self.act(tf[:], tf[:], AF.Ln, [Ttf], [Ttf], bias=1.0)
                self.act(tf[:], tf[:], AF.Exp, [Ttf], [Ttf], scale=-1.0)
                mg = self.merged[:, dc, J * 512:(J + 1) * 512]
                if first:
                    self.tt("dve", mg, self.ps[:, pby, :], tf[:], ALU.mult, [self.Tps[pby], Ttf], [self.Tmg[dc][J]])
                else:
                    self.tt("dve", tb[:], self.ps[:, pby, :], tf[:], ALU.mult, [self.Tps[pby], Ttf], [Ttb])
                    self.tt("pool", mg, mg, tb[:], ALU.add, [Ttb, self.Tmg[dc][J]], [self.Tmg[dc][J]])

    def dump_dbg(self, X, g, Tg):
        if not self.debug:
            return
        self.S.dma(lambda e: e.dma_start(out=self.dbg[X], in_=g[:].rearrange("p c t -> p (c t)").bitcast(F32)),
                   "dbg", reads=list(Tg))

    def phaseA(self):
        S = self.S
        gA, TgA = self.gA, self.TgA
        KT = self.sb("KT", [128, 4096], BF16)
        TKT = T("KT")
        QT = self.sb("QT", [128, NT], BF16)
        TQT = T("QT")
        V = self.sb("V", [128, 32, 128], BF16)
        TV = T("V")
        Wf = self.sb("Wf", [128, WW], F32)
        TWf = T("Wf")
        Wb = self.sb("Wb", [128, WW], BF16)
        TWb = T("Wb")
        NE = 3
        E = [self.sb("E%d" % i, [128, 2, 512], BF16) for i in range(NE)]
        TE = [T("E%d" % i) for i in range(NE)]
        acc = self.sb("acc", [128, 4, 512], F32)
        Tacc = [T("acc%d" % i) for i in range(4)]
        Zs2 = [self.sb("Zs%d" % i, [128, 2, 512], F32) for i in range(2)]
        TZs2 = [[T("Zs%d_0" % i), T("Zs%d_1" % i)] for i in range(2)]
        sqb = self.sb("sqb", [128, 512], BF16)
        Tsqb = T("sqb")
        wzA = [self.sb("wzA%d" % i, [128, 8, 128], BF16) for i in range(2)]
        TwzA = [T("wzA%d" % i) for i in range(2)]
        fin_q = []
        pend = []
        zero_c = self.sb("zero_c", [128, 1], F32)
        Tz = T("zero_c")
        S.op("pool", lambda e: e.memset(zero_c[:], 0.0), writes=[Tz])
        ln08 = math.log(1.0 - LAM_INIT)
        unit = 0
        for h in range(int(os.environ.get('K_AHEADS', '8'))):
            base = h * 512

            def load_head(hh):
                b_ = hh * 512
                r = [self.load_w(self.w_in[:, b_ + k * 128:b_ + (k + 1) * 128]) for k in range(3)]
                self.load_w(self.w_in[:, b_ + 384:b_ + 512], dst=wzA[hh % 2][:], Tdst=TwzA[hh % 2])
                return r
            pre_w = load_head(h)
            (wq_, Twq_), (wk_, Twk_), (w23, Tw23) = pre_w
            for J in range(4):
                pb = J % 2
                self.proj_fm(wq_, Twq_, 0, self.hT, self.ThT[J], J * 512, 512, pb)
                self.copy("act" if J % 2 == 0 else "dve", QT[:, J * 512:(J + 1) * 512], self.ps[:, pb, :], [self.Tps[pb]], [TQT])
            for Jk in range(8):
                pb = Jk % 2
                src, Ts = (self.hT, self.ThT[Jk]) if Jk < 4 else (self.hTo, self.ThTo[Jk - 4])
                self.proj_fm(wk_, Twk_, 0, src, Ts, (Jk % 4) * 512, 512, pb)
                self.copy("act" if Jk % 2 == 0 else "dve", KT[:, Jk * 512:(Jk + 1) * 512], self.ps[:, pb, :], [self.Tps[pb]], [TKT])
            for i4 in range(8):
                pb = i4 % 2
                mms = []
                for ii in range(4):
                    i = i4 * 4 + ii
                    src, Ts = (self.hT, self.ThT[i // 4]) if i < 16 else (self.hTo, self.ThTo[(i - 16) // 4])
                    for c in range(8):
                        mms.append((self.ps[:, pb, ii * 128:(ii + 1) * 128], src[:, c, (i % 16) * 128:(i % 16 + 1) * 128],
                                    w23[:, c, 0:128], c == 0, c == 7))
                Ts_all = self.ThT[i4] if i4 < 4 else self.ThTo[i4 - 4]
                self.mm(mms, [Tw23, Ts_all], [self.Tps[pb]])
                self.copy("act" if i4 % 2 == 0 else "dve", V[:, i4 * 4:(i4 + 1) * 4, :],
                          self.ps[:, pb, :].rearrange("p (a b) -> p a b", a=4), [self.Tps[pb]], [TV])
            hsrc = AP(tensor=self.wscr_h, offset=h * WX, ap=[[1, 128], [1, WW]])
            S.dma(lambda e, hsrc=hsrc: e.dma_start(out=Wf[:], in_=hsrc), "wf", reads=[self.Twscr], writes=[TWf])
            self.copy("pool", Wb[:], Wf[:], [TWf], [TWb])

            for J in range(4):
                def qk(i, slot):
                    b0 = slot * 2
                    mms = [(self.ps[:, b0, :], KT[0:64, i * 128:(i + 1) * 128], QT[0:64, J * 512:(J + 1) * 512], True, False),
                           (self.ps[:, b0 + 1, :], KT[64:128, i * 128:(i + 1) * 128], QT[64:128, J * 512:(J + 1) * 512], True, False)]
                    near = (4 * J - 1) <= i <= (4 * J + 4)
                    rd = [TKT, TQT]
                    if near:
                        off = 639 - ((i - 4 * J) * 128 + 127)
                        mms.append((self.ps[:, b0, :], self.ajb, Wb[:, off:off + 512], False, True))
                        mms.append((self.ps[:, b0 + 1, :], self.ajb, Wb[:, off:off + 512], False, True))
                        rd += [TWb, self.Tcb]
                    else:
                        mms[0] = mms[0][:4] + (True,)
                        mms[1] = mms[1][:4] + (True,)
                    self.mm(mms, rd, [self.Tps[b0], self.Tps[b0 + 1]])
                    return near

                def ex(i, slot, near, u):
                    b0 = slot * 2
                    e_, Te_ = E[u % NE], TE[u % NE]
                    if near:
                        bias = zero_c[:, 0:1]
                        rd = [Tz]
                    else:
                        side = 0 if i < 4 * J else 1
                        bias = self.cfar[:, side * 8 + h:side * 8 + h + 1]
                        rd = [self.Tcfar]
                    self.act(e_[:].rearrange("p a b -> p (a b)"), self.ps[:, b0:b0 + 2, :].rearrange("p a b -> p (a b)"),
                             AF.Exp, [self.Tps[b0], self.Tps[b0 + 1]] + rd, [Te_], bias=bias, scale=0.125)

                def pv(i, u):
                    e_, Te_ = E[u % NE], TE[u % NE]
                    st, sp = (i == 0), (i == 31)
                    mms = [(self.ps[:, 6, :], V[:, i, :], e_[:, 0, :], st, sp),
                           (self.ps[:, 7, :], V[:, i, :], e_[:, 1, :], st, sp)]
                    self.mm(mms, [TV, Te_], [self.Tps[6], self.Tps[7]])
                    zf = Zs[:].rearrange("p a b -> p (a b)")
                    ef = e_[:].rearrange("p a b -> p (a b)")
                    SPL = 896
                    if i == 0:
                        self.copy("dve", zf[:, 0:SPL], ef[:, 0:SPL], [Te_], [TZs[0]])
                        self.copy("pool", zf[:, SPL:1024], ef[:, SPL:1024], [Te_], [TZs[1]])
                    else:
                        self.tt("dve", zf[:, 0:SPL], zf[:, 0:SPL], ef[:, 0:SPL], ALU.add, [Te_, TZs[0]], [TZs[0]])
                        self.tt("pool", zf[:, SPL:1024], zf[:, SPL:1024], ef[:, SPL:1024], ALU.add, [Te_, TZs[1]], [TZs[1]])

                Zs, TZs = Zs2[(unit // 32) % 2], TZs2[(unit // 32) % 2]
                nears = {}
                nears[0] = qk(0, 0)
                for i in range(32):
                    if i + 1 < 32:
                        nears[i + 1] = qk(i + 1, (i + 1) % 2)
                    ex(i, i % 2, nears[i], unit + i)
                    if i == 0 and pend:
                        imm_, fo_ = pend.pop(0)
                        imm_()
                        fin_q.extend(fo_())
                    pv(i, unit + i)
                    if fin_q and i >= 1:
                        f_ = fin_q.pop(0)
                        if f_ is not None:
                            f_()
                assert not fin_q
                unit += 32
                onesf = self.cf[:, C_ONE:C_ONE + 128]
                A0, A1, A2, A3 = acc[:, 0, :], acc[:, 1, :], acc[:, 2, :], acc[:, 3, :]
                wz_, Twz_ = wzA[h % 2], TwzA[h % 2]

                def imm(Zs=Zs, TZs=TZs):
                    self.copy("dve", A0, self.ps[:, 6, :], [self.Tps[6]], [Tacc[0]])
                    self.copy("dve", A1, self.ps[:, 7, :], [self.Tps[7]], [Tacc[1]])
                    self.mm([(self.ps[:, 4, :], onesf, Zs[:, 0, :], True, True),
                             (self.ps[:, 5, :], onesf, Zs[:, 1, :], True, True)], [TZs[0], TZs[1], self.Tcf],
                            [self.Tps[4], self.Tps[5]])

                def fin_ops(h=h, J=J, wz_=wz_, Twz_=Twz_):
                    ops = []
                    ops.append(lambda: self.act(A2, self.ps[:, 4, :], AF.Ln, [self.Tps[4]], [Tacc[2]]))
                    ops.append(lambda: self.act(A3, self.ps[:, 5, :], AF.Ln, [self.Tps[5]], [Tacc[3]]))
                    ops.append(lambda: self.act(A2, A2, AF.Exp, [Tacc[2]], [Tacc[2]], scale=-1.0))
                    ops.append(lambda: self.act(A3, A3, AF.Exp, [Tacc[3]], [Tacc[3]], scale=-1.0))
                    ops.append(lambda: self.tt("dve", A0, A0, A2, ALU.mult, [Tacc[0], Tacc[2]], [Tacc[0]]))
                    ops.append(lambda: self.tt("dve", A1, A1, A3, ALU.mult, [Tacc[1], Tacc[3]], [Tacc[1]]))
                    ops.append(lambda: self.stt("dve", A2, A1, self.cols[:, 0:1], A0, ALU.mult, ALU.add,
                                                [Tacc[0], Tacc[1], self.Tcols], [Tacc[2]]))
                    ops.append(None)
                    ops.append(lambda: self.act(sqb[:], A2, AF.Square, [Tacc[2]], [Tsqb]))
                    ops.append(lambda: self.mm([(self.ps[:, 4, :], self.ones128[:], sqb[:], True, True)], [Tsqb, self.Tcb], [self.Tps[4]]))
                    ops.append(lambda: self.proj_fm(wz_, Twz_, 0, self.hT, self.ThT[J], J * 512, 512, 5))
                    ops.append(None)
                    ops.append(lambda: self.act(A0, self.ps[:, 4, :], AF.Ln, [self.Tps[4]], [Tacc[0]], bias=EPS))
                    ops.append(lambda: self.act(A1, self.ps[:, 5, :], AF.Exp, [self.Tps[5]], [Tacc[1]], scale=-1.0))
                    ops.append(lambda: self.act(A1, A1, AF.Ln, [Tacc[1]], [Tacc[1]], bias=1.0))
                    ops.append(lambda: self.stt("dve", A0, A0, 0.5, A1, ALU.mult, ALU.add, [Tacc[0], Tacc[1]], [Tacc[0]]))
                    ops.append(None)
                    ops.append(lambda: self.act(A0, A0, AF.Exp, [Tacc[0]], [Tacc[0]], scale=-1.0, bias=ln08))
                    ops.append(lambda: self.stt("dve", A3, A2, self.cols[:, 1:2], self.ps[:, 5, :], ALU.mult, ALU.mult,
                                                [Tacc[2], self.Tcols, self.Tps[5]], [Tacc[3]]))
                    ops.append(lambda: self.tt("dve", gA[:, h, J * 512:(J + 1) * 512], A3, A0, ALU.mult, [Tacc[3], Tacc[0]], [TgA[J]]))
                    return ops
                pend.append((imm, fin_ops))
        while pend:
            imm_, fo_ = pend.pop(0)
            imm_()
            fin_q.extend(fo_())
        while fin_q:
            f_ = fin_q.pop(0)
            if f_ is not None:
                f_()
        self.dump_dbg(0, gA, TgA)

    def phaseC(self):
        S = self.S
        gC, TgC = self.gC, self.TgC
        mT = self.sb("mT", [128, 8, 256], BF16)
        TmT = T("mT")
        mn = self.sb("mn_bc", [128, D], F32)
        Tmn = T("mn")
        xst = [self.sb("cxst%d" % i, [128, D], F32) for i in range(2)]
        Txst = [T("cxst%d" % i) for i in range(2)]
        xbb = [self.sb("cxbb%d" % i, [128, D], BF16) for i in range(2)]
        Txbb = [T("cxbb%d" % i) for i in range(2)]
        S.dma(lambda e: e.dma_start(out=mn[:], in_=self.vecs[2, :].partition_broadcast(128)), "cst", writes=[Tmn])
        for i in range(2):
            self.norm_transpose(self.mem[i * 128:(i + 1) * 128, :], mn, Tmn, mT, TmT, slice(i * 128, (i + 1) * 128),
                                48 + i, xst, Txst, xbb, Txbb)
        mkT = self.sb("mkT", [128, 8, 256], BF16)
        TmkT = T("mkT")
        mv = self.sb("mv", [128, 2, 1024], BF16)
        Tmv = T("mv")
        for blk in range(16):
            w, Tw = self.load_w(self.w_kv[:, blk * 128:(blk + 1) * 128])
            pb = blk % 2
            if blk < 8:
                self.proj_fm(w, Tw, 0, mT, TmT, 0, 256, pb)
                self.copy("dve", mkT[:, blk, :], self.ps[:, pb, 0:256], [self.Tps[pb]], [TmkT])
            else:
                for mt in range(2):
                    self.proj_tm(w, Tw, 0, 128, mT, TmT, mt * 128, self.ps[:, pb, mt * 128:(mt + 1) * 128], [self.Tps[pb]])
                self.copy("dve", mv[:, :, (blk - 8) * 128:(blk - 7) * 128],
                          self.ps[:, pb, 0:256].rearrange("p (a b) -> p a b", a=2), [self.Tps[pb]], [Tmv])
        wq = self.sb("cwq", [128, 8, 256], BF16)
        Twq = T("cwq")
        wz = self.sb("cwz", [128, 8, 256], BF16)
        Twz = T("cwz")
        cqT = [self.sb("cqT%d" % i, [128, 2, 512], BF16) for i in range(2)]
        TcqT = [T("cqT%d" % i) for i in range(2)]
        Ec = [self.sb("Ec%d" % i, [128, 2, 512], BF16) for i in range(2)]
        TEc = [T("Ec%d" % i) for i in range(2)]
        fL = [self.sb("fL%d" % i, [128, 512], F32) for i in range(2)]
        TfL = [T("fL%d" % i) for i in range(2)]
        f1 = [self.sb("cf1%d" % i, [128, 512], F32) for i in range(2)]
        Tf1 = [T("cf1%d" % i) for i in range(2)]
        f2 = [self.sb("cf2%d" % i, [128, 512], F32) for i in range(2)]
        Tf2 = [T("cf2%d" % i) for i in range(2)]

        def unit_gen(hc, J, si):
            ba, bb, bc, bd = [4 * si + x for x in range(4)]
            oc = (ba, bb)
            for ch in range(2):
                self.proj_fm(wq, Twq, ch * 128, self.hT, self.ThT[J], J * 512, 512, oc[ch])
                yield
                self.copy("act" if ch == 0 else "dve", cqT[si][:, ch, :], self.ps[:, oc[ch], :], [self.Tps[oc[ch]]], [TcqT[si]])
                yield
            mms = []
            for mt in range(2):
                for ch in range(2):
                    mms.append((self.ps[:, bc + mt, :], mkT[:, hc * 2 + ch, mt * 128:(mt + 1) * 128], cqT[si][:, ch, :], ch == 0, ch == 1))
            self.mm(mms, [TmkT, TcqT[si]], [self.Tps[bc], self.Tps[bd]])
            yield
            self.act(Ec[si][:].rearrange("p a b -> p (a b)"), self.ps[:, bc:bc + 2, :].rearrange("p a b -> p (a b)"), AF.Exp,
                     [self.Tps[bc], self.Tps[bd]], [TEc[si]], scale=1.0 / 16)
            yield
            mms = []
            for ch in range(2):
                for mt in range(2):
                    mms.append((self.ps[:, oc[ch], :], mv[:, mt, hc * 256 + ch * 128:hc * 256 + (ch + 1) * 128], Ec[si][:, mt, :], mt == 0, mt == 1))
            for mt in range(2):
                mms.append((self.ps[:, bc, :], self.oneb, Ec[si][:, mt, :], mt == 0, mt == 1))
            self.mm(mms, [Tmv, TEc[si], self.Tcb], [self.Tps[ba], self.Tps[bb], self.Tps[bc]])
            yield
            self.act(fL[si][:], self.ps[:, bc, :], AF.Ln, [self.Tps[bc]], [TfL[si]])
            yield
            for ch in range(2):
                self.proj_fm(wz, Twz, ch * 128, self.hT, self.ThT[J], J * 512, 512, bd)
                yield
                self.act(f1[si][:], self.ps[:, bd, :], AF.Exp, [self.Tps[bd]], [Tf1[si]], scale=-1.0)
                yield
                self.act(f1[si][:], f1[si][:], AF.Ln, [Tf1[si]], [Tf1[si]], bias=1.0)
                yield
                self.tt("pool", f1[si][:], f1[si][:], fL[si][:], ALU.add, [Tf1[si], TfL[si]], [Tf1[si]])
                yield
                self.act(f1[si][:], f1[si][:], AF.Exp, [Tf1[si]], [Tf1[si]], scale=-1.0)
                yield
                self.tt("dve", f2[si][:], self.ps[:, bd, :], f1[si][:], ALU.mult, [self.Tps[bd], Tf1[si]], [Tf2[si]])
                yield
                self.tt("dve", gC[:, hc * 2 + ch, J * 512:(J + 1) * 512], self.ps[:, oc[ch], :], f2[si][:], ALU.mult,
                        [self.Tps[oc[ch]], Tf2[si]], [TgC[J]])
                yield

        for hc in range(4):
            for k in range(2):
                self.load_w(self.w_in[:, 9216 + hc * 256 + k * 128:9216 + hc * 256 + (k + 1) * 128],
                            dst=wq[:, :, k * 128:(k + 1) * 128], Tdst=Twq)
                self.load_w(self.w_in[:, 10240 + hc * 256 + k * 128:10240 + hc * 256 + (k + 1) * 128],
                            dst=wz[:, :, k * 128:(k + 1) * 128], Tdst=Twz)
            pending = list(range(4))
            active = []
            while pending or active:
                while pending and len(active) < 2:
                    used = [e[1] for e in active]
                    si = 0 if 0 not in used else 1
                    active.append([unit_gen(hc, pending.pop(0), si), si])
                for ent in list(active):
                    try:
                        next(ent[0])
                    except StopIteration:
                        active.remove(ent)
        self.dump_dbg(2, gC, TgC)

    def phaseB(self):
        S = self.S
        gB, TgB = self.gB, self.TgB
        cf = self.cf
        if self.debug:
            for j in range(4):
                S.op("pool", lambda e, j=j: e.memset(gB[:, :, j * 512:(j + 1) * 512], 0.0), writes=[TgB[j]])
        OF = self.sb("bOF", [128, NT], F32)
        TOF = [T("bOF%d" % j) for j in range(4)]
        lrows = OF[0:8, 0:512].rearrange("p (r k) -> p r k", r=4)
        Tlrows = TOF[0]
        for r in range(4):
            S.dma(lambda e, r=r: e.dma_start(out=lrows[:, r, :], in_=self.vecs[3 + r, :].rearrange("(h k) -> h k", k=128)),
                  "cst", writes=[Tlrows])
        lbc = self.sb("lbc", [128, 32], F32)
        Tlbc = T("lbc")
        self.mm([(self.ps[:, 0, r * 8:(r + 1) * 8], lrows[:, r, :], cf[0:8, C_ID:C_ID + 8], True, True) for r in range(4)],
                [Tlrows, self.Tcf], [self.Tps[0]])
        p4 = self.ps[:, 0, 0:32].rearrange("p (r h) -> p r h", r=4)
        l4 = lbc[:, 0:32].rearrange("p (r h) -> p r h", r=4)
        self.copy("dve", lbc[:, 0:32], self.ps[:, 0, 0:32], [self.Tps[0]], [Tlbc])
        for d in range(2):
            self.tt("dve", lbc[:, d * 16:d * 16 + 8], lbc[:, d * 16 + 8:d * 16 + 16], lbc[:, d * 16:d * 16 + 8], ALU.subtract,
                    [Tlbc], [Tlbc])
            self.act(lbc[:, d * 16:d * 16 + 8], lbc[:, d * 16:d * 16 + 8], AF.Exp, [Tlbc], [Tlbc])
            self.act(lbc[:, d * 16:d * 16 + 8], lbc[:, d * 16:d * 16 + 8], AF.Ln, [Tlbc], [Tlbc], bias=1.0)
            self.act(lbc[:, d * 16:d * 16 + 8], lbc[:, d * 16:d * 16 + 8], AF.Exp, [Tlbc], [Tlbc], scale=-1.0)
        cmask = self.sb("cmask", [128, 512], F32)
        Tcmask = T("cmask")
        S.op("pool", lambda e: e.memset(cmask[:], 1.0), writes=[Tcmask])
        S.op("pool", lambda e: e.memset(cmask[:].rearrange("p (c s) -> p c s", s=64)[:, :, 0:1], 0.0), writes=[Tcmask])
        wH2 = [self.sb("wH%d" % i, [128, 8, 640], BF16) for i in range(2)]
        TwH2 = [T("wH%d" % i) for i in range(2)]

        class WS:
            pass
        sets = []
        for i in range(2):
            w = WS()
            def mk(nm, shape, dt, i=i, w=w):
                setattr(w, nm, self.sb("b%s%d" % (nm, i), shape, dt))
                setattr(w, "T" + nm, T("b%s%d" % (nm, i)))
            mk("A1", [128, 512], F32); mk("A2", [128, 512], F32); mk("G", [128, 512], F32); mk("P", [128, 512], F32)
            mk("CR", [128, 512], F32); mk("EI", [128, 512], F32); mk("KKT", [128, 512], BF16); mk("KBT", [128, 512], BF16)
            mk("Vb", [128, 4, 128], BF16); mk("DEC", [128, 8], F32)
            mk("KB", [128, 4, 128], BF16); mk("QB", [128, 512], BF16); mk("KI", [128, 512], BF16)
            mk("AM", [128, 4, 128], BF16); mk("SN", [128, 8, 128], F32); mk("SNb", [128, 8, 128], BF16)
            sets.append(w)
        OS = self.sb("bOS", [128, 512], F32); TOS = T("bOS")
        SQ = self.sb("bSQ", [128, 512], BF16); TSQ = T("bSQ")
        F1 = self.sb("bF1", [128, 512], F32); TF1 = T("bF1")
        F2 = self.sb("bF2", [128, 512], F32); TF2 = T("bF2")
        Sp = [self.sb("bSp%d" % d, [128, 128], F32) for d in range(2)]
        TSp = [T("bSp%d" % d) for d in range(2)]
        dn = {0: "f", 1: "b"}
        nblk = 0
        for hb in range(int(os.environ.get('K_BHEADS', '8'))):
            wH, TwH = wH2[hb % 2], TwH2[hb % 2]
            nheads = int(os.environ.get('K_BHEADS', '8'))
            for hl in ([0, 1] if hb == 0 else [hb + 1]):
                if hl < nheads:
                    base = 4096 + hl * 640
                    for k in range(5):
                        self.load_w(self.w_in[:, base + k * 128:base + (k + 1) * 128],
                                    dst=wH2[hl % 2][:, :, k * 128:(k + 1) * 128], Tdst=TwH2[hl % 2])
            for d in range(2):
                S.op("pool", lambda e, d=d: e.memset(Sp[d][:], 0.0), writes=[TSp[d]])
            fw = [(0, jb, True, self.hT, self.ThT) for jb in range(4)]
            bo = [(1, jb, False, self.hTo, self.ThTo) for jb in (3, 2, 1, 0)]
            bw = [(1, jb, True, self.hT, self.ThT) for jb in (3, 2, 1, 0)]
            sweeps = []
            for i in range(4):
                sweeps += [fw[i], bo[i]]
            sweeps += bw
            def block_gen(hb, d, jb, wout, src, Tsrc, si, wH=wH, TwH=TwH):
                w = sets[si]
                ba, bb, bc, bd = [4 * si + x for x in range(4)]
                dk = dn[d]
                tok0 = jb * 512
                fcol = 128 * (1 + d)
                self.proj_fm(wH, TwH, fcol, src, Tsrc[jb], tok0, 512, ba)
                yield
                mV = []
                for t in range(4):
                    for c in range(8):
                        mV.append((self.ps[:, bb, t * 128:(t + 1) * 128], src[:, c, tok0 + t * 128:tok0 + (t + 1) * 128],
                                   wH[:, c, 384:512], c == 0, c == 7))
                self.mm(mV, [TwH, Tsrc[jb]], [self.Tps[bb]])
                yield
                lbcol = lbc[:, d * 16 + hb:d * 16 + hb + 1]
                self.act(w.A1[:], self.ps[:, ba, :], AF.Exp, [self.Tps[ba]], [w.TA1], scale=-1.0)
                yield
                self.copy("dve", w.Vb[:], self.ps[:, bb, :].rearrange("p (a b) -> p a b", a=4), [self.Tps[bb]], [w.TVb])
                yield
                self.act(w.A2[:], w.A1[:], AF.Ln, [w.TA1], [w.TA2], bias=1.0)
                yield
                self.act(w.A1[:], w.A1[:], AF.Ln, [w.TA1, Tlbc], [w.TA1], bias=1.0, scale=lbcol)
                yield
                self.tt("dve", w.G[:], w.A1[:], w.A2[:], ALU.subtract, [w.TA1, w.TA2], [w.TG])
                yield
                S.op("dve", lambda e, w=w: e.tensor_tensor_scan(out=w.P[:], data0=cmask[:], data1=w.G[:], initial=0.0,
                                                                op0=ALU.mult, op1=ALU.add),
                     reads=[w.TG, Tcmask], writes=[w.TP])
                yield
                self.act(w.A2[:], w.G[:], AF.Exp, [w.TG], [w.TA2])
                yield
                self.ts("pool", w.KKT[:], w.A2[:], -1.0, 1.0, ALU.mult, ALU.add, [w.TA2], [w.TKKT])
                yield
                P3 = w.P[:].rearrange("p (c s) -> p c s", s=64)
                tot = P3[:, :, 63:64]
                totb = tot.to_broadcast([128, 8, 64])
                CR3 = w.CR[:].rearrange("p (c s) -> p c s", s=64)
                if d == 0:
                    self.tt("dve", CR3, totb, P3, ALU.subtract, [w.TP], [w.TCR])
                    yield
                    bsrc, Tb = w.P, w.TP
                else:
                    self.tt("dve", w.CR[:], w.P[:], w.G[:], ALU.subtract, [w.TP, w.TG], [w.TCR])
                    yield
                    self.tt("dve", w.A1[:].rearrange("p (c s) -> p c s", s=64), totb, CR3, ALU.subtract, [w.TP, w.TCR], [w.TA1])
                    yield
                    bsrc, Tb = w.A1, w.TA1
                self.act(w.DEC[:], P3[:, :, 63], AF.Exp, [w.TP], [w.TDEC])
                yield
                if wout:
                    self.act(w.EI[:], bsrc[:], AF.Exp, [Tb], [w.TEI], scale=-1.0)
                    yield
                    self.act(w.A2[:], bsrc[:], AF.Exp, [Tb], [w.TA2])
                    yield
                self.act(w.CR[:], w.CR[:], AF.Exp, [w.TCR], [w.TCR])
                yield
                self.tt("pool", w.KBT[:], w.KKT[:], w.CR[:], ALU.mult, [w.TKKT, w.TCR], [w.TKBT])
                yield
                pT = self.ps[:, bd, :].bitcast(BF16)

                def tr(e, pT=pT, w=w):
                    ins = None
                    for t in range(4):
                        ins = e.transpose(out=pT[:, t * 128:(t + 1) * 128], in_=w.KBT[:, t * 128:(t + 1) * 128], identity=self.idb)
                    return ins
                S.op("pe", tr, reads=[w.TKBT, self.Tcb], writes=[self.Tps[bd]])
                yield
                self.copy("act", w.KB[:], pT[:, 0:512].rearrange("p (a b) -> p a b", a=4), [self.Tps[bd]], [w.TKB])
                yield
                if wout:
                    self.proj_fm(wH, TwH, 0, src, Tsrc[jb], tok0, 512, bc)
                    yield
                    self.tt("dve", w.QB[:], self.ps[:, bc, :], w.A2[:], ALU.mult, [self.Tps[bc], w.TA2], [w.TQB])
                    yield
                    self.tt("pool", w.KI[:], w.KKT[:], w.EI[:], ALU.mult, [w.TKKT, w.TEI], [w.TKI])
                    yield
                torder = range(4) if d == 0 else range(3, -1, -1)
                corder = (0, 1) if d == 0 else (1, 0)
                chunks = [(t, ch) for t in torder for ch in corder]
                mU = []
                for n, (t, ch) in enumerate(chunks):
                    pr = slice(ch * 64, (ch + 1) * 64)
                    ub = ba if ch == 0 else bb
                    mU.append((self.ps[:, ub, t * 128:(t + 1) * 128], w.KB[pr, t, :], w.Vb[pr, t, :], True, True))
                self.mm(mU, [w.TKB, w.TVb], [self.Tps[ba], self.Tps[bb]])
                yield
                yield ("state", d)
                self.copy("pool", w.SN[:, 0, :], Sp[d][:], [TSp[d]], [w.TSN])
                yield
                for n, (t, ch) in enumerate(chunks):
                    ub = ba if ch == 0 else bb
                    ups = self.ps[:, ub, t * 128:(t + 1) * 128]
                    if n < 7:
                        self.stt("dve", w.SN[:, n + 1, :], w.SN[:, n, :], w.DEC[:, t * 2 + ch:t * 2 + ch + 1], ups, ALU.mult, ALU.add,
                                 [w.TSN, w.TDEC, self.Tps[ub]], [w.TSN])
                        yield
                    else:
                        self.stt("dve", Sp[d][:], w.SN[:, n, :], w.DEC[:, t * 2 + ch:t * 2 + ch + 1], ups, ALU.mult, ALU.add,
                                 [w.TSN, w.TDEC, self.Tps[ub]], [TSp[d]])
                        yield
                if wout:
                    self.copy("act", w.SNb[:], w.SN[:], [w.TSN], [w.TSNb])
                    yield
                    mA = []
                    for t in range(4):
                        mA.append((self.ps[:, bc, t * 128:(t + 1) * 128], w.KI[:, t * 128:(t + 1) * 128], w.QB[:, t * 128:(t + 1) * 128], True, True))
                    self.mm(mA, [w.TKI, w.TQB], [self.Tps[bc]])
                    yield
                    mk_ = cf[:, C_MASK[dk]:C_MASK[dk] + 128].unsqueeze(1).to_broadcast([128, 4, 128])
                    self.tt("dve", w.AM[:], self.ps[:, bc, :].rearrange("p (a b) -> p a b", a=4), mk_, ALU.mult,
                            [self.Tps[bc], self.Tcf], [w.TAM])
                    yield
                    mI = []
                    for t in range(4):
                        mI.append((self.ps[:, bd, t * 128:(t + 1) * 128], w.Vb[:, t, :], w.AM[:, t, :], True, True))
                    self.mm(mI, [w.TVb, w.TAM], [self.Tps[bd]])
                    yield
                    mN = []
                    for n, (t, ch) in enumerate(chunks):
                        c0 = t * 128 + ch * 64
                        mN.append((self.ps[:, bc, c0:c0 + 64], w.SNb[:, n, :], w.QB[:, c0:c0 + 64], True, True))
                    self.mm(mN, [w.TSNb, w.TQB], [self.Tps[bc]])
                    yield
                if wout and d == 0:
                    self.copy("act", w.A1[:], self.ps[:, bc, :], [self.Tps[bc]], [w.TA1])
                    yield
                    self.tt("dve", OF[:, tok0:tok0 + 512], self.ps[:, bd, :], w.A1[:], ALU.add, [self.Tps[bd], w.TA1], [TOF[jb]])
                    yield
                elif wout:
                    self.copy("act", w.A1[:], self.ps[:, bc, :], [self.Tps[bc]], [w.TA1])
                    yield
                    self.tt("dve", w.P[:], self.ps[:, bd, :], OF[:, tok0:tok0 + 512], ALU.add, [self.Tps[bd], TOF[jb]], [w.TP])
                    yield
                    self.tt("pool", w.P[:], w.P[:], w.A1[:], ALU.add, [w.TP, w.TA1], [w.TP])
                    yield
                    self.act(w.KKT[:], w.P[:], AF.Square, [w.TP], [w.TKKT])
                    yield
                    self.mm([(self.ps[:, ba, :], self.ones128[:], w.KKT[:], True, True)], [w.TKKT, self.Tcb], [self.Tps[ba]])
                    yield
                    self.proj_fm(wH, TwH, 512, self.hT, self.ThT[jb], tok0, 512, bb)
                    yield
                    self.act(w.A1[:], self.ps[:, ba, :], AF.Ln, [self.Tps[ba]], [w.TA1], bias=EPS)
                    yield
                    self.act(w.G[:], self.ps[:, bb, :], AF.Exp, [self.Tps[bb]], [w.TG], scale=-1.0)
                    yield
                    self.act(w.G[:], w.G[:], AF.Ln, [w.TG], [w.TG], bias=1.0)
                    yield
                    self.stt("dve", w.A1[:], w.A1[:], 0.5, w.G[:], ALU.mult, ALU.add, [w.TA1, w.TG], [w.TA1])
                    yield
                    self.act(w.A1[:], w.A1[:], AF.Exp, [w.TA1], [w.TA1], scale=-1.0)
                    yield
                    self.stt("dve", w.G[:], w.P[:], self.cols[:, 2:3], self.ps[:, bb, :], ALU.mult, ALU.mult,
                             [w.TP, self.Tcols, self.Tps[bb]], [w.TG])
                    yield
                    self.tt("dve", gB[:, hb, tok0:tok0 + 512], w.G[:], w.A1[:], ALU.mult, [w.TG, w.TA1], [TgB[jb]])
                    yield

            active = []
            pending = [(d, jb, wout, src, Tsrc) for (d, jb, wout, src, Tsrc) in sweeps]
            done_dirs = {0: 0, 1: 0}
            started_dirs = {0: 0, 1: 0}
            nstart = 0
            while pending or active:
                while pending and len(active) < int(os.environ.get("K_BWIN", "2")):
                    (d, jb, wout, src, Tsrc) = pending.pop(0)
                    used = [e[4] for e in active]
                    si = 0 if 0 not in used else 1
                    g = block_gen(hb, d, jb, wout, src, Tsrc, si)
                    active.append([g, d, started_dirs[d], False, si])
                    started_dirs[d] += 1
                    nstart += 1
                progressed = False
                for ent in list(active):
                    g, d, order, blocked, _si = ent
                    if blocked:
                        if done_dirs[d] < order:
                            continue
                        ent[3] = False
                    try:
                        r = next(g)
                        progressed = True
                        if isinstance(r, tuple) and r[0] == "state" and done_dirs[d] < order:
                            ent[3] = True
                    except StopIteration:
                        active.remove(ent)
                        done_dirs[d] += 1
                        progressed = True
                assert progressed
        self.dump_dbg(1, gB, TgB)

    def phaseF(self):
        S = self.S
        Wo = self.sb("Wo", [128, 8, D], BF16)
        TWo = T("Wo")
        for k in range(8):
            self.load_w(self.w_out[:, k * 128:(k + 1) * 128], dst=Wo[:, :, k * 128:(k + 1) * 128], Tdst=TWo)
        pn = self.sb("post_bc", [128, D], F32)
        Tpn = T("postn")
        S.dma(lambda e: e.dma_start(out=pn[:], in_=self.vecs[1, :].partition_broadcast(128)), "cst", writes=[Tpn])
        xs = [self.sb("xres%d" % i, [128, D], F32) for i in range(2)]
        Txs = [T("xres%d" % i) for i in range(2)]
        ob = [self.sb("ob%d" % i, [128, D], F32) for i in range(2)]
        Tob = [T("ob%d" % i) for i in range(2)]
        junk = self.sb("junk", [128, D], BF16)
        Tj = T("junk")
        stat = self.stat
        for t in range(16):
            sl = t % 2
            b0 = 0 if sl == 0 else 2
            S.dma(lambda e, t=t, sl=sl: e.dma_start(out=xs[sl][:], in_=self.x_own[t * 128:(t + 1) * 128, :]),
                  "xres%d" % sl, writes=[Txs[sl]])
            mms = []
            for half in range(2):
                for dc in range(8):
                    mms.append((self.ps[:, b0 + half, :], self.merged[:, dc, t * 128:(t + 1) * 128],
                                Wo[:, dc, half * 512:(half + 1) * 512], dc == 0, dc == 7))
            self.mm(mms, [TWo] + [self.Tmg[dc][t // 4] for dc in range(8)], [self.Tps[b0], self.Tps[b0 + 1]])
            c0 = 128 + t * 4
            yps = self.ps[:, b0:b0 + 2, :].rearrange("p a b -> p (a b)")
            self.act(junk[:], yps, AF.Square, [self.Tps[b0], self.Tps[b0 + 1]], [Tj, self.Tstat], accum_out=stat[:, c0:c0 + 1])
            self.act(stat[:, c0 + 1:c0 + 2], stat[:, c0:c0 + 1], AF.Ln, [self.Tstat], [self.Tstat], bias=EPS, scale=1.0 / D)
            self.act(stat[:, c0 + 2:c0 + 3], stat[:, c0 + 1:c0 + 2], AF.Exp, [self.Tstat], [self.Tstat], scale=-0.5)
            self.stt("dve", ob[sl][:], yps, stat[:, c0 + 2:c0 + 3], pn[:], ALU.mult, ALU.mult,
                     [self.Tps[b0], self.Tps[b0 + 1], self.Tstat, Tpn], [Tob[sl]])
            self.tt("dve", ob[sl][:], ob[sl][:], xs[sl][:], ALU.add, [Tob[sl], Txs[sl]], [Tob[sl]])
            tok = S.dma(lambda e, t=t, sl=sl: e.dma_start(out=self.y[t * 128:(t + 1) * 128, :], in_=ob[sl][:]),
                        "yst%d" % sl, reads=[Tob[sl]])
            self.out_toks.append(tok)
        for tok in self.out_toks[-2:]:
            S.wait_tok("sp", tok)
        if self.debug and S.dma_cnt.get("dbg", 0):
            S.wait_tok("sp", ("dbg", 16 * S.dma_cnt["dbg"]))

    def zero_merged(self):
        for dc in range(8):
            self.S.op("pool", lambda e, dc=dc: e.memset(self.merged[:, dc, :], 0.0), writes=self.Tmg[dc])

    def build(self):
        self.setup_globals()
        self.phase0()
        self._mid = []
        self.gB = self._alloc(self._mid, "sb", "gB", [128, 8, NT], BF16)
        self.TgB = [T("gB%d" % j) for j in range(4)]
        self.cast_eng = "pool"
        if "B" in self.phases:
            self.phaseB()
            self.end_phase()
        self.gA = self._alloc(self._mid, "sb", "gA", [128, 8, NT], BF16)
        self.TgA = [T("gA%d" % j) for j in range(4)]
        if "A" in self.phases:
            self.phaseA()
            self.end_phase()
        self.S.barrier()
        self.cast_eng = "dve"
        first = True
        for X, ph, g, Tg in ((0, "A", self.gA, self.TgA), (1, "B", self.gB, self.TgB)):
            if ph in self.phases:
                self.branch_merge(X, g, Tg, first)
                first = False
                self.end_phase()
        self.free_mid()
        if "C" in self.phases:
            self.gC = self.sb("gC", [128, 8, NT], BF16)
            self.TgC = [T("gC%d" % j) for j in range(4)]
            self.phaseC()
            self.branch_merge(2, self.gC, self.TgC, first)
            first = False
            self.end_phase()
        if "F" in self.phases:
            if first:
                self.zero_merged()
            self.phaseF()
        elif self.debug and self.S.dma_cnt.get("dbg", 0):
            self.S.wait_tok("sp", ("dbg", 16 * self.S.dma_cnt["dbg"]))
        self.end_phase()
        return self.nc


def _prep_shared(inputs):
    f = lambda a: np.ascontiguousarray(np.asarray(a, dtype=np.float32))
    w_in = f(inputs["w_in"])[0]
    aq, ak, av, az = w_in[:, 0:1024], w_in[:, 1024:2048], w_in[:, 2048:3072], w_in[:, 3072:4096]
    rest = w_in[:, 4096:]
    blocks = []
    for h in range(8):
        q = np.concatenate([aq[:, h * 64:(h + 1) * 64], aq[:, 512 + h * 64:512 + (h + 1) * 64]], axis=1)
        k = np.concatenate([ak[:, h * 64:(h + 1) * 64], ak[:, 512 + h * 64:512 + (h + 1) * 64]], axis=1)
        blocks += [q, k, av[:, h * 128:(h + 1) * 128], az[:, h * 128:(h + 1) * 128]]
    wA = np.concatenate(blocks, axis=1)
    def wB(swap):
        fo, bo = (2048, 1024) if swap else (1024, 2048)
        bl = []
        for hb in range(8):
            sl = slice(hb * 128, (hb + 1) * 128)
            bl += [rest[:, 0:1024][:, sl], rest[:, fo:fo + 1024][:, sl], rest[:, bo:bo + 1024][:, sl],
                   rest[:, 3072:4096][:, sl], rest[:, 4096:5120][:, sl]]
        return np.concatenate(bl, axis=1)
    tail = rest[:, 5120:]
    w_even = np.ascontiguousarray(np.concatenate([wA, wB(False), tail], axis=1))
    w_odd = np.ascontiguousarray(np.concatenate([wA, wB(True), tail], axis=1))
    lbl = f(inputs["lb_logits"])
    vec_even = np.zeros((8, 1024), np.float32)
    vec_even[0] = f(inputs["pre_norm"])[0]
    vec_even[1] = f(inputs["post_norm"])[0]
    vec_even[2] = f(inputs["mem_norm"])[0]
    vec_even[3], vec_even[4] = lbl[0, 0], lbl[0, 1]
    vec_even[5], vec_even[6] = lbl[1, 0], lbl[1, 1]
    vec_odd = vec_even.copy()
    vec_odd[3], vec_odd[4] = lbl[1, 0], lbl[1, 1]
    vec_odd[5], vec_odd[6] = lbl[0, 0], lbl[0, 1]
    small = np.zeros((1, 512), np.float32)
    small[0, 0:64] = f(inputs["lambda_q1"])[0]
    small[0, 64:128] = f(inputs["lambda_k1"])[0]
    small[0, 128:192] = f(inputs["lambda_q2"])[0]
    small[0, 192:256] = f(inputs["lambda_k2"])[0]
    small[0, 256:384] = f(inputs["diff_subln"])[0]
    small[0, 384:512] = f(inputs["hgrn_norm"])[0]
    return dict(w_even=w_even, w_odd=w_odd, vec_even=vec_even, vec_odd=vec_odd, small=small,
                w_kv=f(inputs["w_mem_kv"])[0], w_br=f(inputs["w_branch"])[0], w_out=f(inputs["w_out"])[0],
                relb=f(inputs["rel_bias"]))


def _in_maps(inputs):
    sh = _prep_shared(inputs)
    x = np.asarray(inputs["x"], dtype=np.float32)
    mem = np.asarray(inputs["mem"], dtype=np.float32)
    cst = _consts()
    tabs = [_oh_tables(False), _oh_tables(True)]
    maps = []
    for c in range(8):
        b, half = c // 2, c % 2
        xb = x[b] if half == 0 else x[b, ::-1]
        maps.append({
            "x_own": np.ascontiguousarray(xb[0:NT]), "x_oth": np.ascontiguousarray(xb[NT:2 * NT]),
            "mem": np.ascontiguousarray(mem[b]),
            "w_in": sh["w_even"] if half == 0 else sh["w_odd"],
            "w_kv": sh["w_kv"], "w_br": sh["w_br"], "w_out": sh["w_out"],
            "vecs": sh["vec_even"] if half == 0 else sh["vec_odd"],
            "small": sh["small"], "relb": sh["relb"], "cst": cst,
            "oh": tabs[half][0], "ohfar": tabs[half][1],
        })
    return maps


def _run(inputs, phases="0ABCF", debug=False, cores=None):
    prog = Prog(phases=phases, debug=debug)
    nc = prog.build()
    maps = _in_maps(inputs)
    ids = list(range(8)) if cores is None else cores
    res = run_bass_kernel_spmd(nc, [maps[c] for c in ids], core_ids=list(range(len(ids))))
    return res, ids


def kernel(**inputs):
    res, ids = _run(inputs, phases=os.environ.get("K_PHASES", "0ABCF"))
    out = np.zeros((4, 4096, D), np.float32)
    for r, c in zip(res.results, ids):
        b, half = c // 2, c % 2
        y = np.asarray(r["y"], dtype=np.float32)
        if half == 0:
            out[b, 0:NT] = y
        else:
            out[b, NT:2 * NT] = y[::-1]
    return out
```

```python
import math
import os
import numpy as np
import concourse.bass as bass
import concourse.mybir as mybir
from concourse.ap import AP
from concourse.bass_utils import run_bass_kernel_spmd

F32 = mybir.dt.float32
BF16 = mybir.dt.bfloat16
AF = mybir.ActivationFunctionType
ALU = mybir.AluOpType
AX = mybir.AxisListType

ENGS = ("pe", "act", "dve", "pool", "sp")
NT = 2048
D = 1024
EPS = 1e-6
LAM_INIT = 0.2
WX = 1280
WW = 1152


class T:
    __slots__ = ("name", "w", "r", "excl")

    def __init__(self, name, excl=False):
        self.name = name
        self.w = None
        self.r = {}
        self.excl = excl


class Sched:
    def __init__(self, nc):
        self.nc = nc
        self.streams = {e: [] for e in ENGS}
        self.count = {e: 0 for e in ENGS}
        self.seen = {e: {} for e in ENGS}
        self.sems = {}
        self.dma_cnt = {}
        self._ctx = []

    def sem(self, key):
        if key not in self.sems:
            cm = self.nc.semaphore("s_" + str(key))
            self.sems[key] = cm.__enter__()
            self._ctx.append(cm)
        return self.sems[key]

    def _waits(self, eng, reads, writes):
        need = {}

        def add(tok):
            if tok is None:
                return
            k, v = tok
            if need.get(k, 0) < v:
                need[k] = v
        for t in reads:
            add(t.w)
            if t.excl:
                for k, v in t.r.items():
                    if k != eng:
                        add((k, v))
        for t in writes:
            add(t.w)
            for k, v in t.r.items():
                add((k, v))
        if eng == "pe":
            need.pop("pe", None)
        out = []
        seen = self.seen[eng]
        for k, v in need.items():
            if seen.get(k, 0) < v:
                seen[k] = v
                out.append((k, v))
        return out

    def _commit(self, tok, reads, writes):
        for t in writes:
            t.w = tok
            t.r = {}
        for t in reads:
            if t.r.get(tok[0], 0) < tok[1]:
                t.r[tok[0]] = tok[1]

    def op(self, eng, fn, reads=(), writes=(), single=False):
        waits = self._waits(eng, reads, writes)
        self.count[eng] += 1
        tok = (eng, self.count[eng])
        self.streams[eng].append((waits, fn, tok, single))
        self._commit(tok, reads, writes)
        return tok

    def dma(self, fn, semkey, reads=(), writes=(), q="sp"):
        if semkey == "cst":
            self._ncst = getattr(self, "_ncst", 0) + 1
            semkey = "cst%d" % self._ncst
        waits = self._waits(q, reads, writes)
        self.dma_cnt[semkey] = self.dma_cnt.get(semkey, 0) + 1
        tok = (semkey, 16 * self.dma_cnt[semkey])
        self.sem(semkey)
        self.streams[q].append((waits, fn, tok, False))
        self._commit(tok, reads, writes)
        return tok

    def barrier(self):
        for e in ENGS:
            for k in list(self.sems.keys()) + [x for x in ENGS if x not in self.sems]:
                v = self.count[k] if k in self.count else 16 * self.dma_cnt.get(k, 0)
                if v > 0 and k != e:
                    self.wait_tok(e, (k, v))

    def wait_tok(self, eng, tok):
        if self.seen[eng].get(tok[0], 0) < tok[1]:
            self.seen[eng][tok[0]] = tok[1]
            self.streams[eng].append(([tok], None, None, False))

    def emit(self, block):
        engmap = {"pe": "tensor", "act": "scalar", "dve": "vector", "pool": "gpsimd", "sp": "sync"}
        for e in ENGS:
            self.sem(e)
        sched = self

        def make(e):
            items = sched.streams[e]
            sched.streams[e] = []

            def body(engine):
                embed = not os.environ.get("K_NOEMBED")
                for waits, fn, tok, single in items:
                    emb = None
                    if single and waits and embed:
                        emb = waits[-1]
                        waits = waits[:-1]
                    for k, v in waits:
                        engine.wait_ge(sched.sems[k], v)
                    if fn is None:
                        continue
                    ins = fn(engine)
                    first = ins
                    if isinstance(ins, tuple):
                        first, ins = ins
                    if emb is not None:
                        first._wait_ge(sched.sems[emb[0]], emb[1])
                    if tok[0] == e:
                        ins.then_inc(sched.sems[e], 1)
                    else:
                        ins.then_inc(sched.sems[tok[0]], 16)
            return body
        for e in ENGS:
            getattr(block, engmap[e])(make(e))

    def close(self):
        for cm in reversed(self._ctx):
            cm.__exit__(None, None, None)


def _t5_bucket_np(rel):
    rel = np.asarray(rel, dtype=np.int64)
    nb = 16
    max_exact = 8
    ret = np.where(rel > 0, nb, 0)
    n = np.abs(rel)
    nf = np.maximum(n, 1).astype(np.float32)
    large = max_exact + (np.log(nf / np.float32(max_exact)) / np.float32(math.log(128 / max_exact))
                         * np.float32(nb - max_exact)).astype(np.int32)
    large = np.minimum(large, nb - 1)
    return ret + np.where(n < max_exact, n, large)


C_ID, C_AJ, C_ONE = 0, 128, 256
C_TRI = {"f": 384, "b": 512}
C_SREV = {"f": 640, "b": 768}
C_MASK = {"f": 896, "b": 1024}
C_CIND = 1152
C_N = 1160


def _consts():
    c = np.zeros((128, C_N), np.float32)
    i = np.arange(128)
    c[:, C_ID:C_ID + 128] = np.eye(128)
    c[:, C_AJ:C_AJ + 128] = np.eye(128)[::-1]
    c[:, C_ONE:C_ONE + 128] = 1.0
    s = i[:, None]
    cc = i[None, :]
    same = (s // 64) == (cc // 64)
    c[:, C_TRI["f"]:C_TRI["f"] + 128] = (same & (s <= cc))
    c[:, C_TRI["b"]:C_TRI["b"] + 128] = (same & (s >= cc))
    c[:, C_SREV["f"]:C_SREV["f"] + 128] = (same & (s > cc))
    c[:, C_SREV["b"]:C_SREV["b"] + 128] = (same & (s < cc))
    c[:, C_MASK["f"]:C_MASK["f"] + 128] = (same & (s <= cc))
    c[:, C_MASK["b"]:C_MASK["b"] + 128] = (same & (s >= cc))
    c[:, C_CIND] = (i < 64)
    c[:, C_CIND + 1] = (i >= 64)
    return c


def _oh_tables(mirror):
    x = np.arange(WX)
    delta = 639 - x
    if mirror:
        delta = -delta
    bk = _t5_bucket_np(delta)
    oh = np.zeros((32, WX), np.float32)
    oh[bk, x] = 1.0
    ohfar = np.zeros((32, 256), np.float32)
    bm = int(_t5_bucket_np(np.array([1000 if mirror else -1000]))[0])
    bp = int(_t5_bucket_np(np.array([-1000 if mirror else 1000]))[0])
    ohfar[bm, 0:128] = 1.0
    ohfar[bp, 128:256] = 1.0
    return oh, ohfar


class Prog:
    def __init__(self, phases="0ABCF", debug=False):
        self.phases = phases
        self.debug = debug
        nc = bass.Bass("TRN2", target_bir_lowering=False)
        self.nc = nc
        self.S = Sched(nc)
        self._glob = []
        self._ph = []
        dt = nc.dram_tensor
        self.x_own = dt("x_own", [NT, D], F32, kind="ExternalInput").ap()
        self.x_oth = dt("x_oth", [NT, D], F32, kind="ExternalInput").ap()
        self.mem = dt("mem", [256, D], F32, kind="ExternalInput").ap()
        self.w_in = dt("w_in", [D, 14336], F32, kind="ExternalInput").ap()
        self.w_kv = dt("w_kv", [D, 2048], F32, kind="ExternalInput").ap()
        self.w_br = dt("w_br", [3, D, D], F32, kind="ExternalInput").ap()
        self.w_out = dt("w_out", [D, D], F32, kind="ExternalInput").ap()
        self.vecs = dt("vecs", [8, D], F32, kind="ExternalInput").ap()
        self.small = dt("small", [1, 512], F32, kind="ExternalInput").ap()
        self.relb = dt("relb", [32, 8], F32, kind="ExternalInput").ap()
        self.cst = dt("cst", [128, C_N], F32, kind="ExternalInput").ap()
        self.oh = dt("oh", [32, WX], F32, kind="ExternalInput").ap()
        self.ohfar = dt("ohfar", [32, 256], F32, kind="ExternalInput").ap()
        self.y = dt("y", [NT, D], F32, kind="ExternalOutput").ap()
        self.wscr_h = dt("wscr", [8, WX], F32)
        self.wscr = self.wscr_h.ap()
        if debug:
            self.dbg = dt("dbg", [3, 128, 4 * NT], F32, kind="ExternalOutput").ap()
        self.nwl = 0
        self.out_toks = []

    def _alloc(self, lst, kind, name, shape, dtype):
        self._nalloc = getattr(self, "_nalloc", 0) + 1
        name = "%s_%d" % (name, self._nalloc)
        cm = (self.nc.sbuf_tensor if kind == "sb" else self.nc.psum_tensor)(name, shape, dtype)
        h = cm.__enter__()
        lst.append(cm)
        return h

    def gsb(self, name, shape, dtype):
        return self._alloc(self._glob, "sb", name, shape, dtype)

    def sb(self, name, shape, dtype):
        return self._alloc(self._ph, "sb", name, shape, dtype)

    def end_phase(self):
        if not os.environ.get("K_NOBAR"):
            self.S.barrier()
        with self.nc.Block() as block:
            self.S.emit(block)
        for cm in reversed(self._ph):
            cm.__exit__(None, None, None)
        self._ph = []

    def act(self, out, in_, func, reads, writes, bias=0.0, scale=1.0, accum_out=None):
        kw = {}
        if accum_out is not None:
            kw["accum_out"] = accum_out
        return self.S.op("act", lambda e: e.activation(out=out, in_=in_, func=func, bias=bias, scale=scale, **kw),
                         reads=reads, writes=writes, single=True)

    def tt(self, eng, out, in0, in1, op, reads, writes):
        return self.S.op(eng, lambda e: e.tensor_tensor(out=out, in0=in0, in1=in1, op=op), reads=reads, writes=writes, single=True)

    def ts(self, eng, out, in0, s1, s2, op0, op1, reads, writes):
        if s2 is None:
            return self.S.op(eng, lambda e: e.tensor_scalar(out=out, in0=in0, scalar1=s1, scalar2=None, op0=op0),
                             reads=reads, writes=writes, single=True)
        return self.S.op(eng, lambda e: e.tensor_scalar(out=out, in0=in0, scalar1=s1, scalar2=s2, op0=op0, op1=op1),
                         reads=reads, writes=writes, single=True)

    def stt(self, eng, out, in0, scalar, in1, op0, op1, reads, writes):
        return self.S.op(eng, lambda e: e.scalar_tensor_tensor(out=out, in0=in0, scalar=scalar, in1=in1, op0=op0, op1=op1),
                         reads=reads, writes=writes, single=True)

    def copy(self, eng, out, in_, reads, writes):
        if eng == "act":
            return self.S.op("act", lambda e: e.copy(out=out, in_=in_), reads=reads, writes=writes, single=True)
        return self.S.op(eng, lambda e: e.tensor_copy(out=out, in_=in_), reads=reads, writes=writes, single=True)

    def mm(self, mms, reads, writes):
        def fn(e):
            ins = None
            first = None
            for (o, l, r, st, sp) in mms:
                ins = e.matmul(o, lhsT=l, rhs=r, start=st, stop=sp)
                if first is None:
                    first = ins
            return (first, ins)
        return self.S.op("pe", fn, reads=reads, writes=writes, single=True)

    def setup_globals(self):
        nc, S = self.nc, self.S
        self.ps = self._alloc(self._glob, "ps", "psall", [128, 8, 512], F32)
        self.Tps = [T("ps%d" % i, excl=True) for i in range(8)]
        self.cf = self.gsb("cf", [128, C_N], F32)
        self.Tcf = T("cf")
        self.cb = self.gsb("cb", [128, 384], BF16)
        self.Tcb = T("cb")
        self.ones128 = self.gsb("ones128", [128, 128], BF16)
        self.hT = self.gsb("hT", [128, 8, NT], BF16)
        self.ThT = [T("hT%d" % j) for j in range(4)]
        self.merged = self.gsb("merged", [128, 8, NT], BF16)
        self.hTo = self.merged
        self.ThTo = [T("hTo%d" % j) for j in range(4)]
        self.Tmg = [[T("mg%d_%d" % (dc, j)) for j in range(4)] for dc in range(8)]
        self.wst = [self.gsb("wst%d" % i, [128, 8, 128], F32) for i in range(2)]
        self.Twst = [T("wst%d" % i) for i in range(2)]
        self.wbf = [self.gsb("wbf%d" % i, [128, 8, 128], BF16) for i in range(3)]
        self.Twbf = [T("wbf%d" % i) for i in range(3)]
        self.stat = self.gsb("stat", [128, 256], F32)
        self.Tstat = T("stat")
        self.sm = self.gsb("sm", [1, 512], F32)
        self.Tsm = T("sm")
        self.cols = self.gsb("cols", [128, 8], F32)
        self.Tcols = T("cols")
        self.cfar = self.gsb("cfar", [128, 16], F32)
        self.Tcfar = T("cfar")

        S.dma(lambda e: e.dma_start(out=self.cf[:], in_=self.cst[:, :]), "cst", writes=[self.Tcf])
        S.dma(lambda e: e.dma_start(out=self.sm[:], in_=self.small[:, :]), "cst", writes=[self.Tsm])
        self.copy("dve", self.cb[:], self.cf[:, 0:384], [self.Tcf], [self.Tcb])
        self.ts("dve", self.ones128[:], self.cf[:, C_ONE:C_ONE + 128], 1.0 / 128, None, ALU.mult, None,
                [self.Tcf], [self.Tcb])
        S.op("pool", lambda e: e.memset(self.stat[:], 0.0), writes=[self.Tstat])
        self.idb = self.cb[:, 0:128]
        self.ajb = self.cb[:, 128:256]
        self.oneb = self.cb[:, 256:384]

    def psb(self, i):
        return self.ps[:, i, :]

    def load_w(self, src2d, ncol=128, dst=None, Tdst=None):
        S = self.S
        k = self.nwl
        self.nwl += 1
        st, Tst = self.wst[k % 2], self.Twst[k % 2]
        if dst is None:
            self.nwb = getattr(self, "nwb", 0) + 1
            bf, Tbf = self.wbf[self.nwb % 3], self.Twbf[self.nwb % 3]
        else:
            bf, Tbf = dst, Tdst
        src = src2d.rearrange("(c p) n -> p c n", p=128)
        q = "sp"
        S.dma(lambda e: e.dma_start(out=st[:, :, 0:ncol], in_=src), "wst%d" % (k % 2), writes=[Tst], q=q)
        ce = getattr(self, "cast_eng", "pool")
        if dst is None:
            S.op(ce, lambda e: e.tensor_copy(out=bf[:, :, 0:ncol], in_=st[:, :, 0:ncol]), reads=[Tst], writes=[Tbf], single=True)
        else:
            S.op(ce, lambda e: e.tensor_copy(out=bf, in_=st[:, :, 0:ncol]), reads=[Tst], writes=[Tbf], single=True)
        return bf, Tbf

    def norm_transpose(self, src_rows, wbc, Twbc, dst, Tdst, dst_cols, idx, xst, Txst, xbb, Txbb):
        S = self.S
        sl = idx % 2
        xs, Txs = xst[sl], Txst[sl]
        xb, Txb = xbb[sl], Txbb[sl]
        c0 = (idx % 64) * 4
        stat = self.stat
        S.dma(lambda e: e.dma_start(out=xs[:], in_=src_rows), "xld%d" % sl, writes=[Txs])
        self.act(xb[:], xs[:], AF.Square, [Txs], [Txb, self.Tstat], accum_out=stat[:, c0:c0 + 1])
        self.act(stat[:, c0 + 1:c0 + 2], stat[:, c0:c0 + 1], AF.Ln, [self.Tstat], [self.Tstat], bias=EPS, scale=1.0 / D)
        self.act(stat[:, c0 + 2:c0 + 3], stat[:, c0 + 1:c0 + 2], AF.Exp, [self.Tstat], [self.Tstat], scale=-0.5)
        self.stt("dve", xb[:], xs[:], stat[:, c0 + 2:c0 + 3], wbc[:], ALU.mult, ALU.mult,
                 [Txs, self.Tstat, Twbc], [Txb])
        pb = 6 + sl
        pT = self.ps[:, pb, :].bitcast(BF16)

        def tr(e):
            ins = None
            for c in range(8):
                ins = e.transpose(out=pT[:, c * 128:(c + 1) * 128], in_=xb[:, c * 128:(c + 1) * 128], identity=self.idb)
            return ins
        S.op("pe", tr, reads=[Txb, self.Tcb], writes=[self.Tps[pb]])
        eng = "act" if idx % 2 == 0 else "dve"
        self.copy(eng, dst[:, :, dst_cols], pT.rearrange("p (c t) -> p c t", c=8), [self.Tps[pb]], [Tdst])

    def phase0(self):
        S = self.S
        xst = [self.sb("xst%d" % i, [128, D], F32) for i in range(2)]
        Txst = [T("xst%d" % i) for i in range(2)]
        xbb = [self.sb("xbb%d" % i, [128, D], BF16) for i in range(2)]
        Txbb = [T("xbb%d" % i) for i in range(2)]
        pn = self.sb("pn_bc", [128, D], F32)
        Tpn = T("pn")
        S.dma(lambda e: e.dma_start(out=pn[:], in_=self.vecs[0, :].partition_broadcast(128)), "cst", writes=[Tpn])
        for i in range(32):
            own = i < 16
            j = i % 16
            src = (self.x_own if own else self.x_oth)[j * 128:(j + 1) * 128, :]
            dst = self.hT if own else self.hTo
            Td = (self.ThT if own else self.ThTo)[j // 4]
            self.norm_transpose(src, pn, Tpn, dst, Td, slice(j * 128, (j + 1) * 128), i, xst, Txst, xbb, Txbb)

        sm, cols = self.sm, self.cols
        tmp = self.sb("lamtmp", [1, 256], F32)
        Ttmp = T("lamtmp")
        self.tt("dve", tmp[:, 0:64], sm[:, 0:64], sm[:, 64:128], ALU.mult, [self.Tsm], [Ttmp])
        self.tt("dve", tmp[:, 64:128], sm[:, 128:192], sm[:, 192:256], ALU.mult, [self.Tsm], [Ttmp])
        S.op("dve", lambda e: e.reduce_sum(out=tmp[:, 128:130], in_=tmp[:, 0:128].rearrange("p (a b) -> p a b", a=2),
                                           axis=AX.X), reads=[Ttmp], writes=[Ttmp])
        self.act(tmp[:, 130:132], tmp[:, 128:130], AF.Exp, [Ttmp], [Ttmp])
        self.tt("dve", tmp[:, 132:133], tmp[:, 131:132], tmp[:, 130:131], ALU.subtract, [Ttmp], [Ttmp])
        self.ts("dve", tmp[:, 133:134], tmp[:, 132:133], -LAM_INIT, None, ALU.add, None, [Ttmp], [Ttmp])
        onesrow = self.cf[0:1, C_ONE:C_ONE + 128]
        self.mm([(self.ps[:, 0, 0:1], onesrow, tmp[:, 133:134], True, True),
                 (self.ps[:, 0, 1:2], sm[:, 256:384], self.cf[0:1, C_ONE:C_ONE + 1], True, True),
                 (self.ps[:, 0, 2:3], sm[:, 384:512], self.cf[0:1, C_ONE:C_ONE + 1], True, True)],
                [Ttmp, self.Tsm, self.Tcf], [self.Tps[0]])
        self.copy("dve", cols[:, 0:3], self.ps[:, 0, 0:3], [self.Tps[0]], [self.Tcols])

        rb = self.sb("rb", [32, 8], F32)
        ohs = self.sb("ohs", [32, WX], F32)
        ohf = self.sb("ohf", [32, 256], F32)
        Trb = T("rb")
        S.dma(lambda e: e.dma_start(out=rb[:], in_=self.relb[:, :]), "cst", writes=[Trb])
        S.dma(lambda e: e.dma_start(out=ohs[:], in_=self.oh[:, :]), "cst", writes=[Trb])
        S.dma(lambda e: e.dma_start(out=ohf[:], in_=self.ohfar[:, :]), "cst", writes=[Trb])
        wv = self.sb("wv", [8, WX], F32)
        Twv = T("wv")
        self.mm([(self.ps[0:8, 1, 0:512], rb[:], ohs[:, 0:512], True, True),
                 (self.ps[0:8, 2, 0:512], rb[:], ohs[:, 512:1024], True, True),
                 (self.ps[0:8, 3, 0:256], rb[:], ohs[:, 1024:1280], True, True),
                 (self.ps[:, 4, 0:8], ohf[:, 0:128], rb[:], True, True),
                 (self.ps[:, 4, 8:16], ohf[:, 128:256], rb[:], True, True)],
                [Trb], [self.Tps[1], self.Tps[2], self.Tps[3], self.Tps[4]])
        self.ts("dve", wv[:, 0:512], self.ps[0:8, 1, 0:512], 8.0, None, ALU.mult, None, [self.Tps[1]], [Twv])
        self.ts("dve", wv[:, 512:1024], self.ps[0:8, 2, 0:512], 8.0, None, ALU.mult, None, [self.Tps[2]], [Twv])
        self.ts("dve", wv[:, 1024:1280], self.ps[0:8, 3, 0:256], 8.0, None, ALU.mult, None, [self.Tps[3]], [Twv])
        self.copy("dve", self.cfar[:], self.ps[:, 4, 0:16], [self.Tps[4]], [self.Tcfar])
        self.Twscr = T("wscr")
        S.dma(lambda e: e.dma_start(out=self.wscr[:, :], in_=wv[:]), "cst", reads=[Twv], writes=[self.Twscr])
        self.end_phase()

    def sbA(self, name, shape, dtype):
        if not hasattr(self, "_mid"):
            self._mid = []
        assert not self._ph, "allocate mid tensors first"
        return self._alloc(self._mid, "sb", name, shape, dtype)

    def end_phaseA_keep(self):
        self.end_phase()

    def free_mid(self):
        for cm in reversed(self._mid):
            cm.__exit__(None, None, None)
        self._mid = []

    def proj_fm(self, wbf, Tw, c0, src, Tsrc_blk, tok0, ntok, pbank, extra_reads=()):
        mms = []
        for c in range(8):
            mms.append((self.ps[:, pbank, 0:ntok], wbf[:, c, c0:c0 + 128], src[:, c, tok0:tok0 + ntok], c == 0, c == 7))
        return self.mm(mms, [Tw, Tsrc_blk] + list(extra_reads), [self.Tps[pbank]])

    def proj_tm(self, wbf, Tw, c0, ncol, src, Tsrc_blk, tok0, pout, Tpout):
        mms = []
        for c in range(8):
            mms.append((pout, src[:, c, tok0:tok0 + 128], wbf[:, c, c0:c0 + ncol], c == 0, c == 7))
        return self.mm(mms, [Tw, Tsrc_blk], Tpout)

    def branch_merge(self, X, g, Tg, first):
        S = self.S
        gate0 = 11264 + X * 1024
        tmpf = [self.sb("bm_t%d" % i, [128, 512], F32) for i in range(2)]
        Ttmpf = [T("bm_t%d" % i) for i in range(2)]
        tmpb = [self.sb("bm_b%d" % i, [128, 512], BF16) for i in range(2)]
        Ttmpb = [T("bm_b%d" % i) for i in range(2)]
        it = 0
        for dc in range(8):
            wb, Twb = self.load_w(self.w_br[X, :, dc * 128:(dc + 1) * 128])
            wg, Twg = self.load_w(self.w_in[:, gate0 + dc * 128:gate0 + (dc + 1) * 128])
            for J in range(4):
                pby, pbg = (0, 1) if it % 2 == 0 else (2, 3)
                sl = it % 2
                it += 1
                mms = []
                for fc in range(8):
                    mms.append((self.ps[:, pby, :], wb[:, fc, 0:128], g[:, fc, J * 512:(J + 1) * 512], fc == 0, fc == 7))
                self.mm(mms, [Twb, Tg[J]], [self.Tps[pby]])
                self.proj_fm(wg, Twg, 0, self.hT, self.ThT[J], J * 512, 512, pbg)
                tf, Ttf = tmpf[sl], Ttmpf[sl]
                tb, Ttb = tmpb[sl], Ttmpb[sl]
                self.act(tf[:], self.ps[:, pbg, :], AF.Exp, [self.Tps[pbg]], [Ttf], scale=-1.0)
                self.act(tf[:], tf[:], AF.Ln, [Ttf], [Ttf], bias=1.0)
                self.act(tf[:], tf[:], AF.Exp, [Ttf], [Ttf], scale=-1.0)
                mg = self.merged[:, dc, J * 512:(J + 1) * 512]
                if first:
                    self.tt("dve", mg, self.ps[:, pby, :], tf[:], ALU.mult, [self.Tps[pby], Ttf], [self.Tmg[dc][J]])
                else:
                    self.tt("dve", tb[:], self.ps[:, pby, :], tf[:], ALU.mult, [self.Tps[pby], Ttf], [Ttb])
                    self.tt("pool", mg, mg, tb[:], ALU.add, [Ttb, self.Tmg[dc][J]], [self.Tmg[dc][J]])

    def dump_dbg(self, X, g, Tg):
        if not self.debug:
            return
        self.S.dma(lambda e: e.dma_start(out=self.dbg[X], in_=g[:].rearrange("p c t -> p (c t)").bitcast(F32)),
                   "dbg", reads=list(Tg))

    def phaseA(self):
        S = self.S
        gA, TgA = self.gA, self.TgA
        KT = self.sb("KT", [128, 4096], BF16)
        TKT = T("KT")
        QT = self.sb("QT", [128, NT], BF16)
        TQT = T("QT")
        V = self.sb("V", [128, 32, 128], BF16)
        TV = T("V")
        Wf = self.sb("Wf", [128, WW], F32)
        TWf = T("Wf")
        Wb = self.sb("Wb", [128, WW], BF16)
        TWb = T("Wb")
        NE = 3
        E = [self.sb("E%d" % i, [128, 2, 512], BF16) for i in range(NE)]
        TE = [T("E%d" % i) for i in range(NE)]
        acc = self.sb("acc", [128, 4, 512], F32)
        Tacc = [T("acc%d" % i) for i in range(4)]
        Zs2 = [self.sb("Zs%d" % i, [128, 2, 512], F32) for i in range(2)]
        TZs2 = [[T("Zs%d_0" % i), T("Zs%d_1" % i)] for i in range(2)]
        sqb = self.sb("sqb", [128, 512], BF16)
        Tsqb = T("sqb")
        wzA = [self.sb("wzA%d" % i, [128, 8, 128], BF16) for i in range(2)]
        TwzA = [T("wzA%d" % i) for i in range(2)]
        fin_q = []
        pend = []
        zero_c = self.sb("zero_c", [128, 1], F32)
        Tz = T("zero_c")
        S.op("pool", lambda e: e.memset(zero_c[:], 0.0), writes=[Tz])
        ln08 = math.log(1.0 - LAM_INIT)
        unit = 0
        for h in range(int(os.environ.get('K_AHEADS', '8'))):
            base = h * 512

            def load_head(hh):
                b_ = hh * 512
                r = [self.load_w(self.w_in[:, b_ + k * 128:b_ + (k + 1) * 128]) for k in range(3)]
                self.load_w(self.w_in[:, b_ + 384:b_ + 512], dst=wzA[hh % 2][:], Tdst=TwzA[hh % 2])
                return r
            pre_w = load_head(h)
            (wq_, Twq_), (wk_, Twk_), (w23, Tw23) = pre_w
            for J in range(4):
                pb = J % 2
                self.proj_fm(wq_, Twq_, 0, self.hT, self.ThT[J], J * 512, 512, pb)
                self.copy("act" if J % 2 == 0 else "dve", QT[:, J * 512:(J + 1) * 512], self.ps[:, pb, :], [self.Tps[pb]], [TQT])
            for Jk in range(8):
                pb = Jk % 2
                src, Ts = (self.hT, self.ThT[Jk]) if Jk < 4 else (self.hTo, self.ThTo[Jk - 4])
                self.proj_fm(wk_, Twk_, 0, src, Ts, (Jk % 4) * 512, 512, pb)
                self.copy("act" if Jk % 2 == 0 else "dve", KT[:, Jk * 512:(Jk + 1) * 512], self.ps[:, pb, :], [self.Tps[pb]], [TKT])
            for i4 in range(8):
                pb = i4 % 2
                mms = []
                for ii in range(4):
                    i = i4 * 4 + ii
                    src, Ts = (self.hT, self.ThT[i // 4]) if i < 16 else (self.hTo, self.ThTo[(i - 16) // 4])
                    for c in range(8):
                        mms.append((self.ps[:, pb, ii * 128:(ii + 1) * 128], src[:, c, (i % 16) * 128:(i % 16 + 1) * 128],
                                    w23[:, c, 0:128], c == 0, c == 7))
                Ts_all = self.ThT[i4] if i4 < 4 else self.ThTo[i4 - 4]
                self.mm(mms, [Tw23, Ts_all], [self.Tps[pb]])
                self.copy("act" if i4 % 2 == 0 else "dve", V[:, i4 * 4:(i4 + 1) * 4, :],
                          self.ps[:, pb, :].rearrange("p (a b) -> p a b", a=4), [self.Tps[pb]], [TV])
            hsrc = AP(tensor=self.wscr_h, offset=h * WX, ap=[[1, 128], [1, WW]])
            S.dma(lambda e, hsrc=hsrc: e.dma_start(out=Wf[:], in_=hsrc), "wf", reads=[self.Twscr], writes=[TWf])
            self.copy("pool", Wb[:], Wf[:], [TWf], [TWb])

            for J in range(4):
                def qk(i, slot):
                    b0 = slot * 2
                    mms = [(self.ps[:, b0, :], KT[0:64, i * 128:(i + 1) * 128], QT[0:64, J * 512:(J + 1) * 512], True, False),
                           (self.ps[:, b0 + 1, :], KT[64:128, i * 128:(i + 1) * 128], QT[64:128, J * 512:(J + 1) * 512], True, False)]
                    near = (4 * J - 1) <= i <= (4 * J + 4)
                    rd = [TKT, TQT]
                    if near:
                        off = 639 - ((i - 4 * J) * 128 + 127)
                        mms.append((self.ps[:, b0, :], self.ajb, Wb[:, off:off + 512], False, True))
                        mms.append((self.ps[:, b0 + 1, :], self.ajb, Wb[:, off:off + 512], False, True))
                        rd += [TWb, self.Tcb]
                    else:
                        mms[0] = mms[0][:4] + (True,)
                        mms[1] = mms[1][:4] + (True,)
                    self.mm(mms, rd, [self.Tps[b0], self.Tps[b0 + 1]])
                    return near

                def ex(i, slot, near, u):
                    b0 = slot * 2
                    e_, Te_ = E[u % NE], TE[u % NE]
                    if near:
                        bias = zero_c[:, 0:1]
                        rd = [Tz]
                    else:
                        side = 0 if i < 4 * J else 1
                        bias = self.cfar[:, side * 8 + h:side * 8 + h + 1]
                        rd = [self.Tcfar]
                    self.act(e_[:].rearrange("p a b -> p (a b)"), self.ps[:, b0:b0 + 2, :].rearrange("p a b -> p (a b)"),
                             AF.Exp, [self.Tps[b0], self.Tps[b0 + 1]] + rd, [Te_], bias=bias, scale=0.125)

                def pv(i, u):
                    e_, Te_ = E[u % NE], TE[u % NE]
                    st, sp = (i == 0), (i == 31)
                    mms = [(self.ps[:, 6, :], V[:, i, :], e_[:, 0, :], st, sp),
                           (self.ps[:, 7, :], V[:, i, :], e_[:, 1, :], st, sp)]
                    self.mm(mms, [TV, Te_], [self.Tps[6], self.Tps[7]])
                    zf = Zs[:].rearrange("p a b -> p (a b)")
                    ef = e_[:].rearrange("p a b -> p (a b)")
                    SPL = 896
                    if i == 0:
                        self.copy("dve", zf[:, 0:SPL], ef[:, 0:SPL], [Te_], [TZs[0]])
                        self.copy("pool", zf[:, SPL:1024], ef[:, SPL:1024], [Te_], [TZs[1]])
                    else:
                        self.tt("dve", zf[:, 0:SPL], zf[:, 0:SPL], ef[:, 0:SPL], ALU.add, [Te_, TZs[0]], [TZs[0]])
                        self.tt("pool", zf[:, SPL:1024], zf[:, SPL:1024], ef[:, SPL:1024], ALU.add, [Te_, TZs[1]], [TZs[1]])

                Zs, TZs = Zs2[(unit // 32) % 2], TZs2[(unit // 32) % 2]
                nears = {}
                nears[0] = qk(0, 0)
                for i in range(32):
                    if i + 1 < 32:
                        nears[i + 1] = qk(i + 1, (i + 1) % 2)
                    ex(i, i % 2, nears[i], unit + i)
                    if i == 0 and pend:
                        imm_, fo_ = pend.pop(0)
                        imm_()
                        fin_q.extend(fo_())
                    pv(i, unit + i)
                    if fin_q and i >= 1:
                        f_ = fin_q.pop(0)
                        if f_ is not None:
                            f_()
                assert not fin_q
                unit += 32
                onesf = self.cf[:, C_ONE:C_ONE + 128]
                A0, A1, A2, A3 = acc[:, 0, :], acc[:, 1, :], acc[:, 2, :], acc[:, 3, :]
                wz_, Twz_ = wzA[h % 2], TwzA[h % 2]

                def imm(Zs=Zs, TZs=TZs):
                    self.copy("dve", A0, self.ps[:, 6, :], [self.Tps[6]], [Tacc[0]])
                    self.copy("dve", A1, self.ps[:, 7, :], [self.Tps[7]], [Tacc[1]])
                    self.mm([(self.ps[:, 4, :], onesf, Zs[:, 0, :], True, True),
                             (self.ps[:, 5, :], onesf, Zs[:, 1, :], True, True)], [TZs[0], TZs[1], self.Tcf],
                            [self.Tps[4], self.Tps[5]])

                def fin_ops(h=h, J=J, wz_=wz_, Twz_=Twz_):
                    ops = []
                    ops.append(lambda: self.act(A2, self.ps[:, 4, :], AF.Ln, [self.Tps[4]], [Tacc[2]]))
                    ops.append(lambda: self.act(A3, self.ps[:, 5, :], AF.Ln, [self.Tps[5]], [Tacc[3]]))
                    ops.append(lambda: self.act(A2, A2, AF.Exp, [Tacc[2]], [Tacc[2]], scale=-1.0))
                    ops.append(lambda: self.act(A3, A3, AF.Exp, [Tacc[3]], [Tacc[3]], scale=-1.0))
                    ops.append(lambda: self.tt("dve", A0, A0, A2, ALU.mult, [Tacc[0], Tacc[2]], [Tacc[0]]))
                    ops.append(lambda: self.tt("dve", A1, A1, A3, ALU.mult, [Tacc[1], Tacc[3]], [Tacc[1]]))
                    ops.append(lambda: self.stt("dve", A2, A1, self.cols[:, 0:1], A0, ALU.mult, ALU.add,
                                                [Tacc[0], Tacc[1], self.Tcols], [Tacc[2]]))
                    ops.append(None)
                    ops.append(lambda: self.act(sqb[:], A2, AF.Square, [Tacc[2]], [Tsqb]))
                    ops.append(lambda: self.mm([(self.ps[:, 4, :], self.ones128[:], sqb[:], True, True)], [Tsqb, self.Tcb], [self.Tps[4]]))
                    ops.append(lambda: self.proj_fm(wz_, Twz_, 0, self.hT, self.ThT[J], J * 512, 512, 5))
                    ops.append(None)
                    ops.append(lambda: self.act(A0, self.ps[:, 4, :], AF.Ln, [self.Tps[4]], [Tacc[0]], bias=EPS))
                    ops.append(lambda: self.act(A1, self.ps[:, 5, :], AF.Exp, [self.Tps[5]], [Tacc[1]], scale=-1.0))
                    ops.append(lambda: self.act(A1, A1, AF.Ln, [Tacc[1]], [Tacc[1]], bias=1.0))
                    ops.append(lambda: self.stt("dve", A0, A0, 0.5, A1, ALU.mult, ALU.add, [Tacc[0], Tacc[1]], [Tacc[0]]))
                    ops.append(None)
                    ops.append(lambda: self.act(A0, A0, AF.Exp, [Tacc[0]], [Tacc[0]], scale=-1.0, bias=ln08))
                    ops.append(lambda: self.stt("dve", A3, A2, self.cols[:, 1:2], self.ps[:, 5, :], ALU.mult, ALU.mult,
                                                [Tacc[2], self.Tcols, self.Tps[5]], [Tacc[3]]))
                    ops.append(lambda: self.tt("dve", gA[:, h, J * 512:(J + 1) * 512], A3, A0, ALU.mult, [Tacc[3], Tacc[0]], [TgA[J]]))
                    return ops
                pend.append((imm, fin_ops))
        while pend:
            imm_, fo_ = pend.pop(0)
            imm_()
            fin_q.extend(fo_())
        while fin_q:
            f_ = fin_q.pop(0)
            if f_ is not None:
                f_()
        self.dump_dbg(0, gA, TgA)

    def phaseC(self):
        S = self.S
        gC, TgC = self.gC, self.TgC
        mT = self.sb("mT", [128, 8, 256], BF16)
        TmT = T("mT")
        mn = self.sb("mn_bc", [128, D], F32)
        Tmn = T("mn")
        xst = [self.sb("cxst%d" % i, [128, D], F32) for i in range(2)]
        Txst = [T("cxst%d" % i) for i in range(2)]
        xbb = [self.sb("cxbb%d" % i, [128, D], BF16) for i in range(2)]
        Txbb = [T("cxbb%d" % i) for i in range(2)]
        S.dma(lambda e: e.dma_start(out=mn[:], in_=self.vecs[2, :].partition_broadcast(128)), "cst", writes=[Tmn])
        for i in range(2):
            self.norm_transpose(self.mem[i * 128:(i + 1) * 128, :], mn, Tmn, mT, TmT, slice(i * 128, (i + 1) * 128),
                                48 + i, xst, Txst, xbb, Txbb)
        mkT = self.sb("mkT", [128, 8, 256], BF16)
        TmkT = T("mkT")
        mv = self.sb("mv", [128, 2, 1024], BF16)
        Tmv = T("mv")
        for blk in range(16):
            w, Tw = self.load_w(self.w_kv[:, blk * 128:(blk + 1) * 128])
            pb = blk % 2
            if blk < 8:
                self.proj_fm(w, Tw, 0, mT, TmT, 0, 256, pb)
                self.copy("dve", mkT[:, blk, :], self.ps[:, pb, 0:256], [self.Tps[pb]], [TmkT])
            else:
                for mt in range(2):
                    self.proj_tm(w, Tw, 0, 128, mT, TmT, mt * 128, self.ps[:, pb, mt * 128:(mt + 1) * 128], [self.Tps[pb]])
                self.copy("dve", mv[:, :, (blk - 8) * 128:(blk - 7) * 128],
                          self.ps[:, pb, 0:256].rearrange("p (a b) -> p a b", a=2), [self.Tps[pb]], [Tmv])
        wq = self.sb("cwq", [128, 8, 256], BF16)
        Twq = T("cwq")
        wz = self.sb("cwz", [128, 8, 256], BF16)
        Twz = T("cwz")
        cqT = [self.sb("cqT%d" % i, [128, 2, 512], BF16) for i in range(2)]
        TcqT = [T("cqT%d" % i) for i in range(2)]
        Ec = [self.sb("Ec%d" % i, [128, 2, 512], BF16) for i in range(2)]
        TEc = [T("Ec%d" % i) for i in range(2)]
        fL = [self.sb("fL%d" % i, [128, 512], F32) for i in range(2)]
        TfL = [T("fL%d" % i) for i in range(2)]
        f1 = [self.sb("cf1%d" % i, [128, 512], F32) for i in range(2)]
        Tf1 = [T("cf1%d" % i) for i in range(2)]
        f2 = [self.sb("cf2%d" % i, [128, 512], F32) for i in range(2)]
        Tf2 = [T("cf2%d" % i) for i in range(2)]

        def unit_gen(hc, J, si):
            ba, bb, bc, bd = [4 * si + x for x in range(4)]
            oc = (ba, bb)
            for ch in range(2):
                self.proj_fm(wq, Twq, ch * 128, self.hT, self.ThT[J], J * 512, 512, oc[ch])
                yield
                self.copy("act" if ch == 0 else "dve", cqT[si][:, ch, :], self.ps[:, oc[ch], :], [self.Tps[oc[ch]]], [TcqT[si]])
                yield
            mms = []
            for mt in range(2):
                for ch in range(2):
                    mms.append((self.ps[:, bc + mt, :], mkT[:, hc * 2 + ch, mt * 128:(mt + 1) * 128], cqT[si][:, ch, :], ch == 0, ch == 1))
            self.mm(mms, [TmkT, TcqT[si]], [self.Tps[bc], self.Tps[bd]])
            yield
            self.act(Ec[si][:].rearrange("p a b -> p (a b)"), self.ps[:, bc:bc + 2, :].rearrange("p a b -> p (a b)"), AF.Exp,
                     [self.Tps[bc], self.Tps[bd]], [TEc[si]], scale=1.0 / 16)
            yield
            mms = []
            for ch in range(2):
                for mt in range(2):
                    mms.append((self.ps[:, oc[ch], :], mv[:, mt, hc * 256 + ch * 128:hc * 256 + (ch + 1) * 128], Ec[si][:, mt, :], mt == 0, mt == 1))
            for mt in range(2):
                mms.append((self.ps[:, bc, :], self.oneb, Ec[si][:, mt, :], mt == 0, mt == 1))
            self.mm(mms, [Tmv, TEc[si], self.Tcb], [self.Tps[ba], self.Tps[bb], self.Tps[bc]])
            yield
            self.act(fL[si][:], self.ps[:, bc, :], AF.Ln, [self.Tps[bc]], [TfL[si]])
            yield
            for ch in range(2):
                self.proj_fm(wz, Twz, ch * 128, self.hT, self.ThT[J], J * 512, 512, bd)
                yield
                self.act(f1[si][:], self.ps[:, bd, :], AF.Exp, [self.Tps[bd]], [Tf1[si]], scale=-1.0)
                yield
                self.act(f1[si][:], f1[si][:], AF.Ln, [Tf1[si]], [Tf1[si]], bias=1.0)
                yield
                self.tt("pool", f1[si][:], f1[si][:], fL[si][:], ALU.add, [Tf1[si], TfL[si]], [Tf1[si]])
                yield
                self.act(f1[si][:], f1[si][:], AF.Exp, [Tf1[si]], [Tf1[si]], scale=-1.0)
                yield
                self.tt("dve", f2[si][:], self.ps[:, bd, :], f1[si][:], ALU.mult, [self.Tps[bd], Tf1[si]], [Tf2[si]])
                yield
                self.tt("dve", gC[:, hc * 2 + ch, J * 512:(J + 1) * 512], self.ps[:, oc[ch], :], f2[si][:], ALU.mult,
                        [self.Tps[oc[ch]], Tf2[si]], [TgC[J]])
                yield

        for hc in range(4):
            for k in range(2):
                self.load_w(self.w_in[:, 9216 + hc * 256 + k * 128:9216 + hc * 256 + (k + 1) * 128],
                            dst=wq[:, :, k * 128:(k + 1) * 128], Tdst=Twq)
                self.load_w(self.w_in[:, 10240 + hc * 256 + k * 128:10240 + hc * 256 + (k + 1) * 128],
                            dst=wz[:, :, k * 128:(k + 1) * 128], Tdst=Twz)
            pending = list(range(4))
            active = []
            while pending or active:
                while pending and len(active) < 2:
                    used = [e[1] for e in active]
                    si = 0 if 0 not in used else 1
                    active.append([unit_gen(hc, pending.pop(0), si), si])
                for ent in list(active):
                    try:
                        next(ent[0])
                    except StopIteration:
                        active.remove(ent)
        self.dump_dbg(2, gC, TgC)

    def phaseB(self):
        S = self.S
        gB, TgB = self.gB, self.TgB
        cf = self.cf
        if self.debug:
            for j in range(4):
                S.op("pool", lambda e, j=j: e.memset(gB[:, :, j * 512:(j + 1) * 512], 0.0), writes=[TgB[j]])
        OF = self.sb("bOF", [128, NT], F32)
        TOF = [T("bOF%d" % j) for j in range(4)]
        lrows = OF[0:8, 0:512].rearrange("p (r k) -> p r k", r=4)
        Tlrows = TOF[0]
        for r in range(4):
            S.dma(lambda e, r=r: e.dma_start(out=lrows[:, r, :], in_=self.vecs[3 + r, :].rearrange("(h k) -> h k", k=128)),
                  "cst", writes=[Tlrows])
        lbc = self.sb("lbc", [128, 32], F32)
        Tlbc = T("lbc")
        self.mm([(self.ps[:, 0, r * 8:(r + 1) * 8], lrows[:, r, :], cf[0:8, C_ID:C_ID + 8], True, True) for r in range(4)],
                [Tlrows, self.Tcf], [self.Tps[0]])
        p4 = self.ps[:, 0, 0:32].rearrange("p (r h) -> p r h", r=4)
        l4 = lbc[:, 0:32].rearrange("p (r h) -> p r h", r=4)
        self.copy("dve", lbc[:, 0:32], self.ps[:, 0, 0:32], [self.Tps[0]], [Tlbc])
        for d in range(2):
            self.tt("dve", lbc[:, d * 16:d * 16 + 8], lbc[:, d * 16 + 8:d * 16 + 16], lbc[:, d * 16:d * 16 + 8], ALU.subtract,
                    [Tlbc], [Tlbc])
            self.act(lbc[:, d * 16:d * 16 + 8], lbc[:, d * 16:d * 16 + 8], AF.Exp, [Tlbc], [Tlbc])
            self.act(lbc[:, d * 16:d * 16 + 8], lbc[:, d * 16:d * 16 + 8], AF.Ln, [Tlbc], [Tlbc], bias=1.0)
            self.act(lbc[:, d * 16:d * 16 + 8], lbc[:, d * 16:d * 16 + 8], AF.Exp, [Tlbc], [Tlbc], scale=-1.0)
        cmask = self.sb("cmask", [128, 512], F32)
        Tcmask = T("cmask")
        S.op("pool", lambda e: e.memset(cmask[:], 1.0), writes=[Tcmask])
        S.op("pool", lambda e: e.memset(cmask[:].rearrange("p (c s) -> p c s", s=64)[:, :, 0:1], 0.0), writes=[Tcmask])
        wH2 = [self.sb("wH%d" % i, [128, 8, 640], BF16) for i in range(2)]
        TwH2 = [T("wH%d" % i) for i in range(2)]

        class WS:
            pass
        sets = []
        for i in range(2):
            w = WS()
            def mk(nm, shape, dt, i=i, w=w):
                setattr(w, nm, self.sb("b%s%d" % (nm, i), shape, dt))
                setattr(w, "T" + nm, T("b%s%d" % (nm, i)))
            mk("A1", [128, 512], F32); mk("A2", [128, 512], F32); mk("G", [128, 512], F32); mk("P", [128, 512], F32)
            mk("CR", [128, 512], F32); mk("EI", [128, 512], F32); mk("KKT", [128, 512], BF16); mk("KBT", [128, 512], BF16)
            mk("Vb", [128, 4, 128], BF16); mk("DEC", [128, 8], F32)
            mk("KB", [128, 4, 128], BF16); mk("QB", [128, 512], BF16); mk("KI", [128, 512], BF16)
            mk("AM", [128, 4, 128], BF16); mk("SN", [128, 8, 128], F32); mk("SNb", [128, 8, 128], BF16)
            sets.append(w)
        OS = self.sb("bOS", [128, 512], F32); TOS = T("bOS")
        SQ = self.sb("bSQ", [128, 512], BF16); TSQ = T("bSQ")
        F1 = self.sb("bF1", [128, 512], F32); TF1 = T("bF1")
        F2 = self.sb("bF2", [128, 512], F32); TF2 = T("bF2")
        Sp = [self.sb("bSp%d" % d, [128, 128], F32) for d in range(2)]
        TSp = [T("bSp%d" % d) for d in range(2)]
        dn = {0: "f", 1: "b"}
        nblk = 0
        for hb in range(int(os.environ.get('K_BHEADS', '8'))):
            wH, TwH = wH2[hb % 2], TwH2[hb % 2]
            nheads = int(os.environ.get('K_BHEADS', '8'))
            for hl in ([0, 1] if hb == 0 else [hb + 1]):
                if hl < nheads:
                    base = 4096 + hl * 640
                    for k in range(5):
                        self.load_w(self.w_in[:, base + k * 128:base + (k + 1) * 128],
                                    dst=wH2[hl % 2][:, :, k * 128:(k + 1) * 128], Tdst=TwH2[hl % 2])
            for d in range(2):
                S.op("pool", lambda e, d=d: e.memset(Sp[d][:], 0.0), writes=[TSp[d]])
            fw = [(0, jb, True, self.hT, self.ThT) for jb in range(4)]
            bo = [(1, jb, False, self.hTo, self.ThTo) for jb in (3, 2, 1, 0)]
            bw = [(1, jb, True, self.hT, self.ThT) for jb in (3, 2, 1, 0)]
            sweeps = []
            for i in range(4):
                sweeps += [fw[i], bo[i]]
            sweeps += bw
            def block_gen(hb, d, jb, wout, src, Tsrc, si, wH=wH, TwH=TwH):
                w = sets[si]
                ba, bb, bc, bd = [4 * si + x for x in range(4)]
                dk = dn[d]
                tok0 = jb * 512
                fcol = 128 * (1 + d)
                self.proj_fm(wH, TwH, fcol, src, Tsrc[jb], tok0, 512, ba)
                yield
                mV = []
                for t in range(4):
                    for c in range(8):
                        mV.append((self.ps[:, bb, t * 128:(t + 1) * 128], src[:, c, tok0 + t * 128:tok0 + (t + 1) * 128],
                                   wH[:, c, 384:512], c == 0, c == 7))
                self.mm(mV, [TwH, Tsrc[jb]], [self.Tps[bb]])
                yield
                lbcol = lbc[:, d * 16 + hb:d * 16 + hb + 1]
                self.act(w.A1[:], self.ps[:, ba, :], AF.Exp, [self.Tps[ba]], [w.TA1], scale=-1.0)
                yield
                self.copy("dve", w.Vb[:], self.ps[:, bb, :].rearrange("p (a b) -> p a b", a=4), [self.Tps[bb]], [w.TVb])
                yield
                self.act(w.A2[:], w.A1[:], AF.Ln, [w.TA1], [w.TA2], bias=1.0)
                yield
                self.act(w.A1[:], w.A1[:], AF.Ln, [w.TA1, Tlbc], [w.TA1], bias=1.0, scale=lbcol)
                yield
                self.tt("dve", w.G[:], w.A1[:], w.A2[:], ALU.subtract, [w.TA1, w.TA2], [w.TG])
                yield
                S.op("dve", lambda e, w=w: e.tensor_tensor_scan(out=w.P[:], data0=cmask[:], data1=w.G[:], initial=0.0,
                                                                op0=ALU.mult, op1=ALU.add),
                     reads=[w.TG, Tcmask], writes=[w.TP])
                yield
                self.act(w.A2[:], w.G[:], AF.Exp, [w.TG], [w.TA2])
                yield
                self.ts("pool", w.KKT[:], w.A2[:], -1.0, 1.0, ALU.mult, ALU.add, [w.TA2], [w.TKKT])
                yield
                P3 = w.P[:].rearrange("p (c s) -> p c s", s=64)
                tot = P3[:, :, 63:64]
                totb = tot.to_broadcast([128, 8, 64])
                CR3 = w.CR[:].rearrange("p (c s) -> p c s", s=64)
                if d == 0:
                    self.tt("dve", CR3, totb, P3, ALU.subtract, [w.TP], [w.TCR])
                    yield
                    bsrc, Tb = w.P, w.TP
                else:
                    self.tt("dve", w.CR[:], w.P[:], w.G[:], ALU.subtract, [w.TP, w.TG], [w.TCR])
                    yield
                    self.tt("dve", w.A1[:].rearrange("p (c s) -> p c s", s=64), totb, CR3, ALU.subtract, [w.TP, w.TCR], [w.TA1])
                    yield
                    bsrc, Tb = w.A1, w.TA1
                self.act(w.DEC[:], P3[:, :, 63], AF.Exp, [w.TP], [w.TDEC])
                yield
                if wout:
                    self.act(w.EI[:], bsrc[:], AF.Exp, [Tb], [w.TEI], scale=-1.0)
                    yield
                    self.act(w.A2[:], bsrc[:], AF.Exp, [Tb], [w.TA2])
                    yield
                self.act(w.CR[:], w.CR[:], AF.Exp, [w.TCR], [w.TCR])
                yield
                self.tt("pool", w.KBT[:], w.KKT[:], w.CR[:], ALU.mult, [w.TKKT, w.TCR], [w.TKBT])
                yield
                pT = self.ps[:, bd, :].bitcast(BF16)

                def tr(e, pT=pT, w=w):
                    ins = None
                    for t in range(4):
                        ins = e.transpose(out=pT[:, t * 128:(t + 1) * 128], in_=w.KBT[:, t * 128:(t + 1) * 128], identity=self.idb)
                    return ins
                S.op("pe", tr, reads=[w.TKBT, self.Tcb], writes=[self.Tps[bd]])
                yield
                self.copy("act", w.KB[:], pT[:, 0:512].rearrange("p (a b) -> p a b", a=4), [self.Tps[bd]], [w.TKB])
                yield
                if wout:
                    self.proj_fm(wH, TwH, 0, src, Tsrc[jb], tok0, 512, bc)
                    yield
                    self.tt("dve", w.QB[:], self.ps[:, bc, :], w.A2[:], ALU.mult, [self.Tps[bc], w.TA2], [w.TQB])
                    yield
                    self.tt("pool", w.KI[:], w.KKT[:], w.EI[:], ALU.mult, [w.TKKT, w.TEI], [w.TKI])
                    yield
                torder = range(4) if d == 0 else range(3, -1, -1)
                corder = (0, 1) if d == 0 else (1, 0)
                chunks = [(t, ch) for t in torder for ch in corder]
                mU = []
                for n, (t, ch) in enumerate(chunks):
                    pr = slice(ch * 64, (ch + 1) * 64)
                    ub = ba if ch == 0 else bb
                    mU.append((self.ps[:, ub, t * 128:(t + 1) * 128], w.KB[pr, t, :], w.Vb[pr, t, :], True, True))
                self.mm(mU, [w.TKB, w.TVb], [self.Tps[ba], self.Tps[bb]])
                yield
                self.copy("act", w.G[:], self.ps[:, ba, :], [self.Tps[ba]], [w.TG])
                yield
                self.copy("act", w.P[:], self.ps[:, bb, :], [self.Tps[bb]], [w.TP])
                yield
                yield ("state", d)
                self.copy("pool", w.SN[:, 0, :], Sp[d][:], [TSp[d]], [w.TSN])
                yield
                for n, (t, ch) in enumerate(chunks):
                    ubuf, Tub = (w.G, w.TG) if ch == 0 else (w.P, w.TP)
                    ups = ubuf[:, t * 128:(t + 1) * 128]
                    if n < 7:
                        self.stt("dve", w.SN[:, n + 1, :], w.SN[:, n, :], w.DEC[:, t * 2 + ch:t * 2 + ch + 1], ups, ALU.mult, ALU.add,
                                 [w.TSN, w.TDEC, Tub], [w.TSN])
                        yield
                    else:
                        self.stt("dve", Sp[d][:], w.SN[:, n, :], w.DEC[:, t * 2 + ch:t * 2 + ch + 1], ups, ALU.mult, ALU.add,
                                 [w.TSN, w.TDEC, Tub], [TSp[d]])
                        yield
                if wout:
                    self.copy("act", w.SNb[:], w.SN[:], [w.TSN], [w.TSNb])
                    yield
                    mA = []
                    for t in range(4):
                        mA.append((self.ps[:, bc, t * 128:(t + 1) * 128], w.KI[:, t * 128:(t + 1) * 128], w.QB[:, t * 128:(t + 1) * 128], True, True))
                    self.mm(mA, [w.TKI, w.TQB], [self.Tps[bc]])
                    yield
                    mk_ = cf[:, C_MASK[dk]:C_MASK[dk] + 128].unsqueeze(1).to_broadcast([128, 4, 128])
                    self.tt("dve", w.AM[:], self.ps[:, bc, :].rearrange("p (a b) -> p a b", a=4), mk_, ALU.mult,
                            [self.Tps[bc], self.Tcf], [w.TAM])
                    yield
                    mI = []
                    for t in range(4):
                        mI.append((self.ps[:, bd, t * 128:(t + 1) * 128], w.Vb[:, t, :], w.AM[:, t, :], True, True))
                    self.mm(mI, [w.TVb, w.TAM], [self.Tps[bd]])
                    yield
                    mN = []
                    for n, (t, ch) in enumerate(chunks):
                        c0 = t * 128 + ch * 64
                        mN.append((self.ps[:, bc, c0:c0 + 64], w.SNb[:, n, :], w.QB[:, c0:c0 + 64], True, True))
                    self.mm(mN, [w.TSNb, w.TQB], [self.Tps[bc]])
                    yield
                if wout and d == 0:
                    self.copy("act", w.A1[:], self.ps[:, bc, :], [self.Tps[bc]], [w.TA1])
                    yield
                    self.tt("dve", OF[:, tok0:tok0 + 512], self.ps[:, bd, :], w.A1[:], ALU.add, [self.Tps[bd], w.TA1], [TOF[jb]])
                    yield
                elif wout:
                    self.copy("act", w.A1[:], self.ps[:, bc, :], [self.Tps[bc]], [w.TA1])
                    yield
                    self.tt("dve", w.P[:], self.ps[:, bd, :], OF[:, tok0:tok0 + 512], ALU.add, [self.Tps[bd], TOF[jb]], [w.TP])
                    yield
                    self.tt("pool", w.P[:], w.P[:], w.A1[:], ALU.add, [w.TP, w.TA1], [w.TP])
                    yield
                    self.act(w.KKT[:], w.P[:], AF.Square, [w.TP], [w.TKKT])
                    yield
                    self.mm([(self.ps[:, ba, :], self.ones128[:], w.KKT[:], True, True)], [w.TKKT, self.Tcb], [self.Tps[ba]])
                    yield
                    self.proj_fm(wH, TwH, 512, self.hT, self.ThT[jb], tok0, 512, bb)
                    yield
                    self.act(w.A1[:], self.ps[:, ba, :], AF.Ln, [self.Tps[ba]], [w.TA1], bias=EPS)
                    yield
                    self.act(w.G[:], self.ps[:, bb, :], AF.Exp, [self.Tps[bb]], [w.TG], scale=-1.0)
                    yield
                    self.act(w.G[:], w.G[:], AF.Ln, [w.TG], [w.TG], bias=1.0)
                    yield
                    self.stt("dve", w.A1[:], w.A1[:], 0.5, w.G[:], ALU.mult, ALU.add, [w.TA1, w.TG], [w.TA1])
                    yield
                    self.act(w.A1[:], w.A1[:], AF.Exp, [w.TA1], [w.TA1], scale=-1.0)
                    yield
                    self.stt("dve", w.G[:], w.P[:], self.cols[:, 2:3], self.ps[:, bb, :], ALU.mult, ALU.mult,
                             [w.TP, self.Tcols, self.Tps[bb]], [w.TG])
                    yield
                    self.tt("dve", gB[:, hb, tok0:tok0 + 512], w.G[:], w.A1[:], ALU.mult, [w.TG, w.TA1], [TgB[jb]])
                    yield

            active = []
            pending = [(d, jb, wout, src, Tsrc) for (d, jb, wout, src, Tsrc) in sweeps]
            done_dirs = {0: 0, 1: 0}
            started_dirs = {0: 0, 1: 0}
            nstart = 0
            while pending or active:
                while pending and len(active) < int(os.environ.get("K_BWIN", "2")):
                    (d, jb, wout, src, Tsrc) = pending.pop(0)
                    used = [e[4] for e in active]
                    si = 0 if 0 not in used else 1
                    g = block_gen(hb, d, jb, wout, src, Tsrc, si)
                    active.append([g, d, started_dirs[d], False, si])
                    started_dirs[d] += 1
                    nstart += 1
                progressed = False
                for ent in list(active):
                    g, d, order, blocked, _si = ent
                    if blocked:
                        if done_dirs[d] < order:
                            continue
                        ent[3] = False
                    try:
                        r = next(g)
                        progressed = True
                        if isinstance(r, tuple) and r[0] == "state" and done_dirs[d] < order:
                            ent[3] = True
                    except StopIteration:
                        active.remove(ent)
                        done_dirs[d] += 1
                        progressed = True
                assert progressed
        self.dump_dbg(1, gB, TgB)

    def phaseF(self):
        S = self.S
        Wo = self.sb("Wo", [128, 8, D], BF16)
        TWo = T("Wo")
        for k in range(8):
            self.load_w(self.w_out[:, k * 128:(k + 1) * 128], dst=Wo[:, :, k * 128:(k + 1) * 128], Tdst=TWo)
        pn = self.sb("post_bc", [128, D], F32)
        Tpn = T("postn")
        S.dma(lambda e: e.dma_start(out=pn[:], in_=self.vecs[1, :].partition_broadcast(128)), "cst", writes=[Tpn])
        xs = [self.sb("xres%d" % i, [128, D], F32) for i in range(2)]
        Txs = [T("xres%d" % i) for i in range(2)]
        ob = [self.sb("ob%d" % i, [128, D], F32) for i in range(2)]
        Tob = [T("ob%d" % i) for i in range(2)]
        junk = self.sb("junk", [128, D], BF16)
        Tj = T("junk")
        stat = self.stat
        for t in range(16):
            sl = t % 2
            b0 = 0 if sl == 0 else 2
            S.dma(lambda e, t=t, sl=sl: e.dma_start(out=xs[sl][:], in_=self.x_own[t * 128:(t + 1) * 128, :]),
                  "xres%d" % sl, writes=[Txs[sl]])
            mms = []
            for half in range(2):
                for dc in range(8):
                    mms.append((self.ps[:, b0 + half, :], self.merged[:, dc, t * 128:(t + 1) * 128],
                                Wo[:, dc, half * 512:(half + 1) * 512], dc == 0, dc == 7))
            self.mm(mms, [TWo] + [self.Tmg[dc][t // 4] for dc in range(8)], [self.Tps[b0], self.Tps[b0 + 1]])
            c0 = 128 + t * 4
            yps = self.ps[:, b0:b0 + 2, :].rearrange("p a b -> p (a b)")
            self.act(junk[:], yps, AF.Square, [self.Tps[b0], self.Tps[b0 + 1]], [Tj, self.Tstat], accum_out=stat[:, c0:c0 + 1])
            self.act(stat[:, c0 + 1:c0 + 2], stat[:, c0:c0 + 1], AF.Ln, [self.Tstat], [self.Tstat], bias=EPS, scale=1.0 / D)
            self.act(stat[:, c0 + 2:c0 + 3], stat[:, c0 + 1:c0 + 2], AF.Exp, [self.Tstat], [self.Tstat], scale=-0.5)
            self.stt("dve", ob[sl][:], yps, stat[:, c0 + 2:c0 + 3], pn[:], ALU.mult, ALU.mult,
                     [self.Tps[b0], self.Tps[b0 + 1], self.Tstat, Tpn], [Tob[sl]])
            self.tt("dve", ob[sl][:], ob[sl][:], xs[sl][:], ALU.add, [Tob[sl], Txs[sl]], [Tob[sl]])
            tok = S.dma(lambda e, t=t, sl=sl: e.dma_start(out=self.y[t * 128:(t + 1) * 128, :], in_=ob[sl][:]),
                        "yst%d" % sl, reads=[Tob[sl]])
            self.out_toks.append(tok)
        for tok in self.out_toks[-2:]:
            S.wait_tok("sp", tok)
        if self.debug and S.dma_cnt.get("dbg", 0):
            S.wait_tok("sp", ("dbg", 16 * S.dma_cnt["dbg"]))

    def zero_merged(self):
        for dc in range(8):
            self.S.op("pool", lambda e, dc=dc: e.memset(self.merged[:, dc, :], 0.0), writes=self.Tmg[dc])

    def build(self):
        self.setup_globals()
        self.phase0()
        self._mid = []
        self.gB = self._alloc(self._mid, "sb", "gB", [128, 8, NT], BF16)
        self.TgB = [T("gB%d" % j) for j in range(4)]
        self.cast_eng = "pool"
        if "B" in self.phases:
            self.phaseB()
            self.end_phase()
        self.gA = self._alloc(self._mid, "sb", "gA", [128, 8, NT], BF16)
        self.TgA = [T("gA%d" % j) for j in range(4)]
        if "A" in self.phases:
            self.phaseA()
            self.end_phase()
        self.S.barrier()
        self.cast_eng = "dve"
        first = True
        for X, ph, g, Tg in ((0, "A", self.gA, self.TgA), (1, "B", self.gB, self.TgB)):
            if ph in self.phases:
                self.branch_merge(X, g, Tg, first)
                first = False
                self.end_phase()
        self.free_mid()
        if "C" in self.phases:
            self.gC = self.sb("gC", [128, 8, NT], BF16)
            self.TgC = [T("gC%d" % j) for j in range(4)]
            self.phaseC()
            self.branch_merge(2, self.gC, self.TgC, first)
            first = False
            self.end_phase()
        if "F" in self.phases:
            if first:
                self.zero_merged()
            self.phaseF()
        elif self.debug and self.S.dma_cnt.get("dbg", 0):
            self.S.wait_tok("sp", ("dbg", 16 * self.S.dma_cnt["dbg"]))
        self.end_phase()
        return self.nc


def _prep_shared(inputs):
    f = lambda a: np.ascontiguousarray(np.asarray(a, dtype=np.float32))
    w_in = f(inputs["w_in"])[0]
    aq, ak, av, az = w_in[:, 0:1024], w_in[:, 1024:2048], w_in[:, 2048:3072], w_in[:, 3072:4096]
    rest = w_in[:, 4096:]
    blocks = []
    for h in range(8):
        q = np.concatenate([aq[:, h * 64:(h + 1) * 64], aq[:, 512 + h * 64:512 + (h + 1) * 64]], axis=1)
        k = np.concatenate([ak[:, h * 64:(h + 1) * 64], ak[:, 512 + h * 64:512 + (h + 1) * 64]], axis=1)
        blocks += [q, k, av[:, h * 128:(h + 1) * 128], az[:, h * 128:(h + 1) * 128]]
    wA = np.concatenate(blocks, axis=1)
    def wB(swap):
        fo, bo = (2048, 1024) if swap else (1024, 2048)
        bl = []
        for hb in range(8):
            sl = slice(hb * 128, (hb + 1) * 128)
            bl += [rest[:, 0:1024][:, sl], rest[:, fo:fo + 1024][:, sl], rest[:, bo:bo + 1024][:, sl],
                   rest[:, 3072:4096][:, sl], rest[:, 4096:5120][:, sl]]
        return np.concatenate(bl, axis=1)
    tail = rest[:, 5120:]
    w_even = np.ascontiguousarray(np.concatenate([wA, wB(False), tail], axis=1))
    w_odd = np.ascontiguousarray(np.concatenate([wA, wB(True), tail], axis=1))
    lbl = f(inputs["lb_logits"])
    vec_even = np.zeros((8, 1024), np.float32)
    vec_even[0] = f(inputs["pre_norm"])[0]
    vec_even[1] = f(inputs["post_norm"])[0]
    vec_even[2] = f(inputs["mem_norm"])[0]
    vec_even[3], vec_even[4] = lbl[0, 0], lbl[0, 1]
    vec_even[5], vec_even[6] = lbl[1, 0], lbl[1, 1]
    vec_odd = vec_even.copy()
    vec_odd[3], vec_odd[4] = lbl[1, 0], lbl[1, 1]
    vec_odd[5], vec_odd[6] = lbl[0, 0], lbl[0, 1]
    small = np.zeros((1, 512), np.float32)
    small[0, 0:64] = f(inputs["lambda_q1"])[0]
    small[0, 64:128] = f(inputs["lambda_k1"])[0]
    small[0, 128:192] = f(inputs["lambda_q2"])[0]
    small[0, 192:256] = f(inputs["lambda_k2"])[0]
    small[0, 256:384] = f(inputs["diff_subln"])[0]
    small[0, 384:512] = f(inputs["hgrn_norm"])[0]
    return dict(w_even=w_even, w_odd=w_odd, vec_even=vec_even, vec_odd=vec_odd, small=small,
                w_kv=f(inputs["w_mem_kv"])[0], w_br=f(inputs["w_branch"])[0], w_out=f(inputs["w_out"])[0],
                relb=f(inputs["rel_bias"]))


def _in_maps(inputs):
    sh = _prep_shared(inputs)
    x = np.asarray(inputs["x"], dtype=np.float32)
    mem = np.asarray(inputs["mem"], dtype=np.float32)
    cst = _consts()
    tabs = [_oh_tables(False), _oh_tables(True)]
    maps = []
    for c in range(8):
        b, half = c // 2, c % 2
        xb = x[b] if half == 0 else x[b, ::-1]
        maps.append({
            "x_own": np.ascontiguousarray(xb[0:NT]), "x_oth": np.ascontiguousarray(xb[NT:2 * NT]),
            "mem": np.ascontiguousarray(mem[b]),
            "w_in": sh["w_even"] if half == 0 else sh["w_odd"],
            "w_kv": sh["w_kv"], "w_br": sh["w_br"], "w_out": sh["w_out"],
            "vecs": sh["vec_even"] if half == 0 else sh["vec_odd"],
            "small": sh["small"], "relb": sh["relb"], "cst": cst,
            "oh": tabs[half][0], "ohfar": tabs[half][1],
        })
    return maps


def _run(inputs, phases="0ABCF", debug=False, cores=None):
    prog = Prog(phases=phases, debug=debug)
    nc = prog.build()
    maps = _in_maps(inputs)
    ids = list(range(8)) if cores is None else cores
    res = run_bass_kernel_spmd(nc, [maps[c] for c in ids], core_ids=list(range(len(ids))))
    return res, ids


def kernel(**inputs):
    res, ids = _run(inputs, phases=os.environ.get("K_PHASES", "0ABCF"))
    out = np.zeros((4, 4096, D), np.float32)
    for r, c in zip(res.results, ids):
        b, half = c // 2, c % 2
        y = np.asarray(r["y"], dtype=np.float32)
        if half == 0:
            out[b, 0:NT] = y
        else:
            out[b, NT:2 * NT] = y[::-1]
    return out
```

```python
import math
import os
import numpy as np
import concourse.bass as bass
import concourse.mybir as mybir
from concourse.ap import AP
from concourse.bass_utils import run_bass_kernel_spmd

F32 = mybir.dt.float32
BF16 = mybir.dt.bfloat16
AF = mybir.ActivationFunctionType
ALU = mybir.AluOpType
AX = mybir.AxisListType

ENGS = ("pe", "act", "dve", "pool", "sp")
NT = 2048
D = 1024
EPS = 1e-6
LAM_INIT = 0.2
WX = 1280
WW = 1152


class T:
    __slots__ = ("name", "w", "r", "excl")

    def __init__(self, name, excl=False):
        self.name = name
        self.w = None
        self.r = {}
        self.excl = excl


class Sched:
    def __init__(self, nc):
        self.nc = nc
        self.streams = {e: [] for e in ENGS}
        self.count = {e: 0 for e in ENGS}
        self.seen = {e: {} for e in ENGS}
        self.sems = {}
        self.dma_cnt = {}
        self._ctx = []

    def sem(self, key):
        if key not in self.sems:
            cm = self.nc.semaphore("s_" + str(key))
            self.sems[key] = cm.__enter__()
            self._ctx.append(cm)
        return self.sems[key]

    def _waits(self, eng, reads, writes):
        need = {}

        def add(tok):
            if tok is None:
                return
            k, v = tok
            if need.get(k, 0) < v:
                need[k] = v
        for t in reads:
            add(t.w)
            if t.excl:
                for k, v in t.r.items():
                    if k != eng:
                        add((k, v))
        for t in writes:
            add(t.w)
            for k, v in t.r.items():
                add((k, v))
        if eng == "pe":
            need.pop("pe", None)
        out = []
        seen = self.seen[eng]
        for k, v in need.items():
            if seen.get(k, 0) < v:
                seen[k] = v
                out.append((k, v))
        return out

    def _commit(self, tok, reads, writes):
        for t in writes:
            t.w = tok
            t.r = {}
        for t in reads:
            if t.r.get(tok[0], 0) < tok[1]:
                t.r[tok[0]] = tok[1]

    def op(self, eng, fn, reads=(), writes=(), single=False):
        waits = self._waits(eng, reads, writes)
        self.count[eng] += 1
        tok = (eng, self.count[eng])
        self.streams[eng].append((waits, fn, tok, single))
        self._commit(tok, reads, writes)
        return tok

    def dma(self, fn, semkey, reads=(), writes=(), q="sp"):
        if semkey == "cst":
            self._ncst = getattr(self, "_ncst", 0) + 1
            semkey = "cst%d" % self._ncst
        waits = self._waits(q, reads, writes)
        self.dma_cnt[semkey] = self.dma_cnt.get(semkey, 0) + 1
        tok = (semkey, 16 * self.dma_cnt[semkey])
        self.sem(semkey)
        self.streams[q].append((waits, fn, tok, False))
        self._commit(tok, reads, writes)
        return tok

    def barrier(self):
        for e in ENGS:
            for k in list(self.sems.keys()) + [x for x in ENGS if x not in self.sems]:
                v = self.count[k] if k in self.count else 16 * self.dma_cnt.get(k, 0)
                if v > 0 and k != e:
                    self.wait_tok(e, (k, v))

    def wait_tok(self, eng, tok):
        if self.seen[eng].get(tok[0], 0) < tok[1]:
            self.seen[eng][tok[0]] = tok[1]
            self.streams[eng].append(([tok], None, None, False))

    def emit(self, block):
        engmap = {"pe": "tensor", "act": "scalar", "dve": "vector", "pool": "gpsimd", "sp": "sync"}
        for e in ENGS:
            self.sem(e)
        sched = self

        def make(e):
            items = sched.streams[e]
            sched.streams[e] = []

            def body(engine):
                embed = not os.environ.get("K_NOEMBED")
                for waits, fn, tok, single in items:
                    emb = None
                    if single and waits and embed:
                        emb = waits[-1]
                        waits = waits[:-1]
                    for k, v in waits:
                        engine.wait_ge(sched.sems[k], v)
                    if fn is None:
                        continue
                    ins = fn(engine)
                    first = ins
                    if isinstance(ins, tuple):
                        first, ins = ins
                    if emb is not None:
                        first._wait_ge(sched.sems[emb[0]], emb[1])
                    if tok[0] == e:
                        ins.then_inc(sched.sems[e], 1)
                    else:
                        ins.then_inc(sched.sems[tok[0]], 16)
            return body
        for e in ENGS:
            getattr(block, engmap[e])(make(e))

    def close(self):
        for cm in reversed(self._ctx):
            cm.__exit__(None, None, None)


def _t5_bucket_np(rel):
    rel = np.asarray(rel, dtype=np.int64)
    nb = 16
    max_exact = 8
    ret = np.where(rel > 0, nb, 0)
    n = np.abs(rel)
    nf = np.maximum(n, 1).astype(np.float32)
    large = max_exact + (np.log(nf / np.float32(max_exact)) / np.float32(math.log(128 / max_exact))
                         * np.float32(nb - max_exact)).astype(np.int32)
    large = np.minimum(large, nb - 1)
    return ret + np.where(n < max_exact, n, large)


C_ID, C_AJ, C_ONE = 0, 128, 256
C_TRI = {"f": 384, "b": 512}
C_SREV = {"f": 640, "b": 768}
C_MASK = {"f": 896, "b": 1024}
C_CIND = 1152
C_N = 1160


def _consts():
    c = np.zeros((128, C_N), np.float32)
    i = np.arange(128)
    c[:, C_ID:C_ID + 128] = np.eye(128)
    c[:, C_AJ:C_AJ + 128] = np.eye(128)[::-1]
    c[:, C_ONE:C_ONE + 128] = 1.0
    s = i[:, None]
    cc = i[None, :]
    same = (s // 64) == (cc // 64)
    c[:, C_TRI["f"]:C_TRI["f"] + 128] = (same & (s <= cc))
    c[:, C_TRI["b"]:C_TRI["b"] + 128] = (same & (s >= cc))
    c[:, C_SREV["f"]:C_SREV["f"] + 128] = (same & (s > cc))
    c[:, C_SREV["b"]:C_SREV["b"] + 128] = (same & (s < cc))
    c[:, C_MASK["f"]:C_MASK["f"] + 128] = (same & (s <= cc))
    c[:, C_MASK["b"]:C_MASK["b"] + 128] = (same & (s >= cc))
    c[:, C_CIND] = (i < 64)
    c[:, C_CIND + 1] = (i >= 64)
    return c


def _oh_tables(mirror):
    x = np.arange(WX)
    delta = 639 - x
    if mirror:
        delta = -delta
    bk = _t5_bucket_np(delta)
    oh = np.zeros((32, WX), np.float32)
    oh[bk, x] = 1.0
    ohfar = np.zeros((32, 256), np.float32)
    bm = int(_t5_bucket_np(np.array([1000 if mirror else -1000]))[0])
    bp = int(_t5_bucket_np(np.array([-1000 if mirror else 1000]))[0])
    ohfar[bm, 0:128] = 1.0
    ohfar[bp, 128:256] = 1.0
    return oh, ohfar


class Prog:
    def __init__(self, phases="0ABCF", debug=False):
        self.phases = phases
        self.debug = debug
        nc = bass.Bass("TRN2", target_bir_lowering=False)
        self.nc = nc
        self.S = Sched(nc)
        self._glob = []
        self._ph = []
        dt = nc.dram_tensor
        self.x_own = dt("x_own", [NT, D], F32, kind="ExternalInput").ap()
        self.x_oth = dt("x_oth", [NT, D], F32, kind="ExternalInput").ap()
        self.mem = dt("mem", [256, D], F32, kind="ExternalInput").ap()
        self.w_in = dt("w_in", [D, 14336], F32, kind="ExternalInput").ap()
        self.w_kv = dt("w_kv", [D, 2048], F32, kind="ExternalInput").ap()
        self.w_br = dt("w_br", [3, D, D], F32, kind="ExternalInput").ap()
        self.w_out = dt("w_out", [D, D], F32, kind="ExternalInput").ap()
        self.vecs = dt("vecs", [8, D], F32, kind="ExternalInput").ap()
        self.small = dt("small", [1, 512], F32, kind="ExternalInput").ap()
        self.relb = dt("relb", [32, 8], F32, kind="ExternalInput").ap()
        self.cst = dt("cst", [128, C_N], F32, kind="ExternalInput").ap()
        self.oh = dt("oh", [32, WX], F32, kind="ExternalInput").ap()
        self.ohfar = dt("ohfar", [32, 256], F32, kind="ExternalInput").ap()
        self.y = dt("y", [NT, D], F32, kind="ExternalOutput").ap()
        self.wscr_h = dt("wscr", [8, WX], F32)
        self.wscr = self.wscr_h.ap()
        if debug:
            self.dbg = dt("dbg", [3, 128, 4 * NT], F32, kind="ExternalOutput").ap()
        self.nwl = 0
        self.out_toks = []

    def _alloc(self, lst, kind, name, shape, dtype):
        self._nalloc = getattr(self, "_nalloc", 0) + 1
        name = "%s_%d" % (name, self._nalloc)
        cm = (self.nc.sbuf_tensor if kind == "sb" else self.nc.psum_tensor)(name, shape, dtype)
        h = cm.__enter__()
        lst.append(cm)
        return h

    def gsb(self, name, shape, dtype):
        return self._alloc(self._glob, "sb", name, shape, dtype)

    def sb(self, name, shape, dtype):
        return self._alloc(self._ph, "sb", name, shape, dtype)

    def end_phase(self):
        if not os.environ.get("K_NOBAR"):
            self.S.barrier()
        with self.nc.Block() as block:
            self.S.emit(block)
        for cm in reversed(self._ph):
            cm.__exit__(None, None, None)
        self._ph = []

    def act(self, out, in_, func, reads, writes, bias=0.0, scale=1.0, accum_out=None):
        kw = {}
        if accum_out is not None:
            kw["accum_out"] = accum_out
        return self.S.op("act", lambda e: e.activation(out=out, in_=in_, func=func, bias=bias, scale=scale, **kw),
                         reads=reads, writes=writes, single=True)

    def tt(self, eng, out, in0, in1, op, reads, writes):
        return self.S.op(eng, lambda e: e.tensor_tensor(out=out, in0=in0, in1=in1, op=op), reads=reads, writes=writes, single=True)

    def ts(self, eng, out, in0, s1, s2, op0, op1, reads, writes):
        if s2 is None:
            return self.S.op(eng, lambda e: e.tensor_scalar(out=out, in0=in0, scalar1=s1, scalar2=None, op0=op0),
                             reads=reads, writes=writes, single=True)
        return self.S.op(eng, lambda e: e.tensor_scalar(out=out, in0=in0, scalar1=s1, scalar2=s2, op0=op0, op1=op1),
                         reads=reads, writes=writes, single=True)

    def stt(self, eng, out, in0, scalar, in1, op0, op1, reads, writes):
        return self.S.op(eng, lambda e: e.scalar_tensor_tensor(out=out, in0=in0, scalar=scalar, in1=in1, op0=op0, op1=op1),
                         reads=reads, writes=writes, single=True)

    def copy(self, eng, out, in_, reads, writes):
        if eng == "act":
            return self.S.op("act", lambda e: e.copy(out=out, in_=in_), reads=reads, writes=writes, single=True)
        return self.S.op(eng, lambda e: e.tensor_copy(out=out, in_=in_), reads=reads, writes=writes, single=True)

    def mm(self, mms, reads, writes):
        def fn(e):
            ins = None
            first = None
            for (o, l, r, st, sp) in mms:
                ins = e.matmul(o, lhsT=l, rhs=r, start=st, stop=sp)
                if first is None:
                    first = ins
            return (first, ins)
        return self.S.op("pe", fn, reads=reads, writes=writes, single=True)

    def setup_globals(self):
        nc, S = self.nc, self.S
        self.ps = self._alloc(self._glob, "ps", "psall", [128, 8, 512], F32)
        self.Tps = [T("ps%d" % i, excl=True) for i in range(8)]
        self.cf = self.gsb("cf", [128, C_N], F32)
        self.Tcf = T("cf")
        self.cb = self.gsb("cb", [128, 384], BF16)
        self.Tcb = T("cb")
        self.ones128 = self.gsb("ones128", [128, 128], BF16)
        self.hT = self.gsb("hT", [128, 8, NT], BF16)
        self.ThT = [T("hT%d" % j) for j in range(4)]
        self.merged = self.gsb("merged", [128, 8, NT], BF16)
        self.hTo = self.merged
        self.ThTo = [T("hTo%d" % j) for j in range(4)]
        self.Tmg = [[T("mg%d_%d" % (dc, j)) for j in range(4)] for dc in range(8)]
        self.wst = [self.gsb("wst%d" % i, [128, 8, 128], F32) for i in range(2)]
        self.Twst = [T("wst%d" % i) for i in range(2)]
        self.wbf = [self.gsb("wbf%d" % i, [128, 8, 128], BF16) for i in range(3)]
        self.Twbf = [T("wbf%d" % i) for i in range(3)]
        self.stat = self.gsb("stat", [128, 256], F32)
        self.Tstat = T("stat")
        self.sm = self.gsb("sm", [1, 512], F32)
        self.Tsm = T("sm")
        self.cols = self.gsb("cols", [128, 8], F32)
        self.Tcols = T("cols")
        self.cfar = self.gsb("cfar", [128, 16], F32)
        self.Tcfar = T("cfar")

        S.dma(lambda e: e.dma_start(out=self.cf[:], in_=self.cst[:, :]), "cst", writes=[self.Tcf])
        S.dma(lambda e: e.dma_start(out=self.sm[:], in_=self.small[:, :]), "cst", writes=[self.Tsm])
        self.copy("dve", self.cb[:], self.cf[:, 0:384], [self.Tcf], [self.Tcb])
        self.ts("dve", self.ones128[:], self.cf[:, C_ONE:C_ONE + 128], 1.0 / 128, None, ALU.mult, None,
                [self.Tcf], [self.Tcb])
        S.op("pool", lambda e: e.memset(self.stat[:], 0.0), writes=[self.Tstat])
        self.idb = self.cb[:, 0:128]
        self.ajb = self.cb[:, 128:256]
        self.oneb = self.cb[:, 256:384]

    def psb(self, i):
        return self.ps[:, i, :]

    def load_w(self, src2d, ncol=128, dst=None, Tdst=None):
        S = self.S
        k = self.nwl
        self.nwl += 1
        st, Tst = self.wst[k % 2], self.Twst[k % 2]
        if dst is None:
            self.nwb = getattr(self, "nwb", 0) + 1
            bf, Tbf = self.wbf[self.nwb % 3], self.Twbf[self.nwb % 3]
        else:
            bf, Tbf = dst, Tdst
        src = src2d.rearrange("(c p) n -> p c n", p=128)
        q = "sp"
        S.dma(lambda e: e.dma_start(out=st[:, :, 0:ncol], in_=src), "wst%d" % (k % 2), writes=[Tst], q=q)
        ce = getattr(self, "cast_eng", "pool")
        if dst is None:
            S.op(ce, lambda e: e.tensor_copy(out=bf[:, :, 0:ncol], in_=st[:, :, 0:ncol]), reads=[Tst], writes=[Tbf], single=True)
        else:
            S.op(ce, lambda e: e.tensor_copy(out=bf, in_=st[:, :, 0:ncol]), reads=[Tst], writes=[Tbf], single=True)
        return bf, Tbf

    def norm_transpose(self, src_rows, wbc, Twbc, dst, Tdst, dst_cols, idx, xst, Txst, xbb, Txbb):
        S = self.S
        sl = idx % 2
        xs, Txs = xst[sl], Txst[sl]
        xb, Txb = xbb[sl], Txbb[sl]
        c0 = (idx % 64) * 4
        stat = self.stat
        S.dma(lambda e: e.dma_start(out=xs[:], in_=src_rows), "xld%d" % sl, writes=[Txs])
        self.act(xb[:], xs[:], AF.Square, [Txs], [Txb, self.Tstat], accum_out=stat[:, c0:c0 + 1])
        self.act(stat[:, c0 + 1:c0 + 2], stat[:, c0:c0 + 1], AF.Ln, [self.Tstat], [self.Tstat], bias=EPS, scale=1.0 / D)
        self.act(stat[:, c0 + 2:c0 + 3], stat[:, c0 + 1:c0 + 2], AF.Exp, [self.Tstat], [self.Tstat], scale=-0.5)
        self.stt("dve", xb[:], xs[:], stat[:, c0 + 2:c0 + 3], wbc[:], ALU.mult, ALU.mult,
                 [Txs, self.Tstat, Twbc], [Txb])
        pb = 6 + sl
        pT = self.ps[:, pb, :].bitcast(BF16)

        def tr(e):
            ins = None
            for c in range(8):
                ins = e.transpose(out=pT[:, c * 128:(c + 1) * 128], in_=xb[:, c * 128:(c + 1) * 128], identity=self.idb)
            return ins
        S.op("pe", tr, reads=[Txb, self.Tcb], writes=[self.Tps[pb]])
        eng = "act" if idx % 2 == 0 else "dve"
        self.copy(eng, dst[:, :, dst_cols], pT.rearrange("p (c t) -> p c t", c=8), [self.Tps[pb]], [Tdst])

    def phase0(self):
        S = self.S
        xst = [self.sb("xst%d" % i, [128, D], F32) for i in range(2)]
        Txst = [T("xst%d" % i) for i in range(2)]
        xbb = [self.sb("xbb%d" % i, [128, D], BF16) for i in range(2)]
        Txbb = [T("xbb%d" % i) for i in range(2)]
        pn = self.sb("pn_bc", [128, D], F32)
        Tpn = T("pn")
        S.dma(lambda e: e.dma_start(out=pn[:], in_=self.vecs[0, :].partition_broadcast(128)), "cst", writes=[Tpn])
        for i in range(32):
            own = i < 16
            j = i % 16
            src = (self.x_own if own else self.x_oth)[j * 128:(j + 1) * 128, :]
            dst = self.hT if own else self.hTo
            Td = (self.ThT if own else self.ThTo)[j // 4]
            self.norm_transpose(src, pn, Tpn, dst, Td, slice(j * 128, (j + 1) * 128), i, xst, Txst, xbb, Txbb)

        sm, cols = self.sm, self.cols
        tmp = self.sb("lamtmp", [1, 256], F32)
        Ttmp = T("lamtmp")
        self.tt("dve", tmp[:, 0:64], sm[:, 0:64], sm[:, 64:128], ALU.mult, [self.Tsm], [Ttmp])
        self.tt("dve", tmp[:, 64:128], sm[:, 128:192], sm[:, 192:256], ALU.mult, [self.Tsm], [Ttmp])
        S.op("dve", lambda e: e.reduce_sum(out=tmp[:, 128:130], in_=tmp[:, 0:128].rearrange("p (a b) -> p a b", a=2),
                                           axis=AX.X), reads=[Ttmp], writes=[Ttmp])
        self.act(tmp[:, 130:132], tmp[:, 128:130], AF.Exp, [Ttmp], [Ttmp])
        self.tt("dve", tmp[:, 132:133], tmp[:, 131:132], tmp[:, 130:131], ALU.subtract, [Ttmp], [Ttmp])
        self.ts("dve", tmp[:, 133:134], tmp[:, 132:133], -LAM_INIT, None, ALU.add, None, [Ttmp], [Ttmp])
        onesrow = self.cf[0:1, C_ONE:C_ONE + 128]
        self.mm([(self.ps[:, 0, 0:1], onesrow, tmp[:, 133:134], True, True),
                 (self.ps[:, 0, 1:2], sm[:, 256:384], self.cf[0:1, C_ONE:C_ONE + 1], True, True),
                 (self.ps[:, 0, 2:3], sm[:, 384:512], self.cf[0:1, C_ONE:C_ONE + 1], True, True)],
                [Ttmp, self.Tsm, self.Tcf], [self.Tps[0]])
        self.copy("dve", cols[:, 0:3], self.ps[:, 0, 0:3], [self.Tps[0]], [self.Tcols])

        rb = self.sb("rb", [32, 8], F32)
        ohs = self.sb("ohs", [32, WX], F32)
        ohf = self.sb("ohf", [32, 256], F32)
        Trb = T("rb")
        S.dma(lambda e: e.dma_start(out=rb[:], in_=self.relb[:, :]), "cst", writes=[Trb])
        S.dma(lambda e: e.dma_start(out=ohs[:], in_=self.oh[:, :]), "cst", writes=[Trb])
        S.dma(lambda e: e.dma_start(out=ohf[:], in_=self.ohfar[:, :]), "cst", writes=[Trb])
        wv = self.sb("wv", [8, WX], F32)
        Twv = T("wv")
        self.mm([(self.ps[0:8, 1, 0:512], rb[:], ohs[:, 0:512], True, True),
                 (self.ps[0:8, 2, 0:512], rb[:], ohs[:, 512:1024], True, True),
                 (self.ps[0:8, 3, 0:256], rb[:], ohs[:, 1024:1280], True, True),
                 (self.ps[:, 4, 0:8], ohf[:, 0:128], rb[:], True, True),
                 (self.ps[:, 4, 8:16], ohf[:, 128:256], rb[:], True, True)],
                [Trb], [self.Tps[1], self.Tps[2], self.Tps[3], self.Tps[4]])
        self.ts("dve", wv[:, 0:512], self.ps[0:8, 1, 0:512], 8.0, None, ALU.mult, None, [self.Tps[1]], [Twv])
        self.ts("dve", wv[:, 512:1024], self.ps[0:8, 2, 0:512], 8.0, None, ALU.mult, None, [self.Tps[2]], [Twv])
        self.ts("dve", wv[:, 1024:1280], self.ps[0:8, 3, 0:256], 8.0, None, ALU.mult, None, [self.Tps[3]], [Twv])
        self.copy("dve", self.cfar[:], self.ps[:, 4, 0:16], [self.Tps[4]], [self.Tcfar])
        self.Twscr = T("wscr")
        S.dma(lambda e: e.dma_start(out=self.wscr[:, :], in_=wv[:]), "cst", reads=[Twv], writes=[self.Twscr])
        self.end_phase()

    def sbA(self, name, shape, dtype):
        if not hasattr(self, "_mid"):
            self._mid = []
        assert not self._ph, "allocate mid tensors first"
        return self._alloc(self._mid, "sb", name, shape, dtype)

    def end_phaseA_keep(self):
        self.end_phase()

    def free_mid(self):
        for cm in reversed(self._mid):
            cm.__exit__(None, None, None)
        self._mid = []

    def proj_fm(self, wbf, Tw, c0, src, Tsrc_blk, tok0, ntok, pbank, extra_reads=()):
        mms = []
        for c in range(8):
            mms.append((self.ps[:, pbank, 0:ntok], wbf[:, c, c0:c0 + 128], src[:, c, tok0:tok0 + ntok], c == 0, c == 7))
        return self.mm(mms, [Tw, Tsrc_blk] + list(extra_reads), [self.Tps[pbank]])

    def proj_tm(self, wbf, Tw, c0, ncol, src, Tsrc_blk, tok0, pout, Tpout):
        mms = []
        for c in range(8):
            mms.append((pout, src[:, c, tok0:tok0 + 128], wbf[:, c, c0:c0 + ncol], c == 0, c == 7))
        return self.mm(mms, [Tw, Tsrc_blk], Tpout)

    def branch_merge(self, X, g, Tg, first):
        S = self.S
        gate0 = 11264 + X * 1024
        tmpf = [self.sb("bm_t%d" % i, [128, 512], F32) for i in range(2)]
        Ttmpf = [T("bm_t%d" % i) for i in range(2)]
        tmpb = [self.sb("bm_b%d" % i, [128, 512], BF16) for i in range(2)]
        Ttmpb = [T("bm_b%d" % i) for i in range(2)]
        it = 0
        for dc in range(8):
            wb, Twb = self.load_w(self.w_br[X, :, dc * 128:(dc + 1) * 128])
            wg, Twg = self.load_w(self.w_in[:, gate0 + dc * 128:gate0 + (dc + 1) * 128])
            for J in range(4):
                pby, pbg = (0, 1) if it % 2 == 0 else (2, 3)
                sl = it % 2
                it += 1
                mms = []
                for fc in range(8):
                    mms.append((self.ps[:, pby, :], wb[:, fc, 0:128], g[:, fc, J * 512:(J + 1) * 512], fc == 0, fc == 7))
                self.mm(mms, [Twb, Tg[J]], [self.Tps[pby]])
                self.proj_fm(wg, Twg, 0, self.hT, self.ThT[J], J * 512, 512, pbg)
                tf, Ttf = tmpf[sl], Ttmpf[sl]
                tb, Ttb = tmpb[sl], Ttmpb[sl]
                self.act(tf[:], self.ps[:, pbg, :], AF.Exp, [self.Tps[pbg]], [Ttf], scale=-1.0)
                self.act(tf[:], tf[:], AF.Ln, [Ttf], [Ttf], bias=1.0)
                self.act(tf[:], tf[:], AF.Exp, [Ttf], [Ttf], scale=-1.0)
                mg = self.merged[:, dc, J * 512:(J + 1) * 512]
                if first:
                    self.tt("dve", mg, self.ps[:, pby, :], tf[:], ALU.mult, [self.Tps[pby], Ttf], [self.Tmg[dc][J]])
                else:
                    self.tt("dve", tb[:], self.ps[:, pby, :], tf[:], ALU.mult, [self.Tps[pby], Ttf], [Ttb])
                    self.tt("pool", mg, mg, tb[:], ALU.add, [Ttb, self.Tmg[dc][J]], [self.Tmg[dc][J]])

    def dump_dbg(self, X, g, Tg):
        if not self.debug:
            return
        self.S.dma(lambda e: e.dma_start(out=self.dbg[X], in_=g[:].rearrange("p c t -> p (c t)").bitcast(F32)),
                   "dbg", reads=list(Tg))

    def phaseA(self):
        S = self.S
        gA, TgA = self.gA, self.TgA
        KT = self.sb("KT", [128, 4096], BF16)
        TKT = T("KT")
        QT = self.sb("QT", [128, NT], BF16)
        TQT = T("QT")
        V = self.sb("V", [128, 32, 128], BF16)
        TV = T("V")
        Wf = self.sb("Wf", [128, WW], F32)
        TWf = T("Wf")
        Wb = self.sb("Wb", [128, WW], BF16)
        TWb = T("Wb")
        NE = 3
        E = [self.sb("E%d" % i, [128, 2, 512], BF16) for i in range(NE)]
        TE = [T("E%d" % i) for i in range(NE)]
        acc = self.sb("acc", [128, 4, 512], F32)
        Tacc = [T("acc%d" % i) for i in range(4)]
        Zs2 = [self.sb("Zs%d" % i, [128, 2, 512], F32) for i in range(2)]
        TZs2 = [[T("Zs%d_0" % i), T("Zs%d_1" % i)] for i in range(2)]
        sqb = self.sb("sqb", [128, 512], BF16)
        Tsqb = T("sqb")
        wzA = [self.sb("wzA%d" % i, [128, 8, 128], BF16) for i in range(2)]
        TwzA = [T("wzA%d" % i) for i in range(2)]
        fin_q = []
        pend = []
        zero_c = self.sb("zero_c", [128, 1], F32)
        Tz = T("zero_c")
        S.op("pool", lambda e: e.memset(zero_c[:], 0.0), writes=[Tz])
        ln08 = math.log(1.0 - LAM_INIT)
        unit = 0
        for h in range(int(os.environ.get('K_AHEADS', '8'))):
            base = h * 512

            def load_head(hh):
                b_ = hh * 512
                r = [self.load_w(self.w_in[:, b_ + k * 128:b_ + (k + 1) * 128]) for k in range(3)]
                self.load_w(self.w_in[:, b_ + 384:b_ + 512], dst=wzA[hh % 2][:], Tdst=TwzA[hh % 2])
                return r
            pre_w = load_head(h)
            (wq_, Twq_), (wk_, Twk_), (w23, Tw23) = pre_w
            for J in range(4):
                pb = J % 2
                self.proj_fm(wq_, Twq_, 0, self.hT, self.ThT[J], J * 512, 512, pb)
                self.copy("act" if J % 2 == 0 else "dve", QT[:, J * 512:(J + 1) * 512], self.ps[:, pb, :], [self.Tps[pb]], [TQT])
            for Jk in range(8):
                pb = Jk % 2
                src, Ts = (self.hT, self.ThT[Jk]) if Jk < 4 else (self.hTo, self.ThTo[Jk - 4])
                self.proj_fm(wk_, Twk_, 0, src, Ts, (Jk % 4) * 512, 512, pb)
                self.copy("act" if Jk % 2 == 0 else "dve", KT[:, Jk * 512:(Jk + 1) * 512], self.ps[:, pb, :], [self.Tps[pb]], [TKT])
            for i4 in range(8):
                pb = i4 % 2
                mms = []
                for ii in range(4):
                    i = i4 * 4 + ii
                    src, Ts = (self.hT, self.ThT[i // 4]) if i < 16 else (self.hTo, self.ThTo[(i - 16) // 4])
                    for c in range(8):
                        mms.append((self.ps[:, pb, ii * 128:(ii + 1) * 128], src[:, c, (i % 16) * 128:(i % 16 + 1) * 128],
                                    w23[:, c, 0:128], c == 0, c == 7))
                Ts_all = self.ThT[i4] if i4 < 4 else self.ThTo[i4 - 4]
                self.mm(mms, [Tw23, Ts_all], [self.Tps[pb]])
                self.copy("act" if i4 % 2 == 0 else "dve", V[:, i4 * 4:(i4 + 1) * 4, :],
                          self.ps[:, pb, :].rearrange("p (a b) -> p a b", a=4), [self.Tps[pb]], [TV])
            hsrc = AP(tensor=self.wscr_h, offset=h * WX, ap=[[1, 128], [1, WW]])
            S.dma(lambda e, hsrc=hsrc: e.dma_start(out=Wf[:], in_=hsrc), "wf", reads=[self.Twscr], writes=[TWf])
            self.copy("pool", Wb[:], Wf[:], [TWf], [TWb])

            for J in range(4):
                def qk(i, slot):
                    b0 = slot * 2
                    mms = [(self.ps[:, b0, :], KT[0:64, i * 128:(i + 1) * 128], QT[0:64, J * 512:(J + 1) * 512], True, False),
                           (self.ps[:, b0 + 1, :], KT[64:128, i * 128:(i + 1) * 128], QT[64:128, J * 512:(J + 1) * 512], True, False)]
                    near = (4 * J - 1) <= i <= (4 * J + 4)
                    rd = [TKT, TQT]
                    if near:
                        off = 639 - ((i - 4 * J) * 128 + 127)
                        mms.append((self.ps[:, b0, :], self.ajb, Wb[:, off:off + 512], False, True))
                        mms.append((self.ps[:, b0 + 1, :], self.ajb, Wb[:, off:off + 512], False, True))
                        rd += [TWb, self.Tcb]
                    else:
                        mms[0] = mms[0][:4] + (True,)
                        mms[1] = mms[1][:4] + (True,)
                    self.mm(mms, rd, [self.Tps[b0], self.Tps[b0 + 1]])
                    return near

                def ex(i, slot, near, u):
                    b0 = slot * 2
                    e_, Te_ = E[u % NE], TE[u % NE]
                    if near:
                        bias = zero_c[:, 0:1]
                        rd = [Tz]
                    else:
                        side = 0 if i < 4 * J else 1
                        bias = self.cfar[:, side * 8 + h:side * 8 + h + 1]
                        rd = [self.Tcfar]
                    self.act(e_[:].rearrange("p a b -> p (a b)"), self.ps[:, b0:b0 + 2, :].rearrange("p a b -> p (a b)"),
                             AF.Exp, [self.Tps[b0], self.Tps[b0 + 1]] + rd, [Te_], bias=bias, scale=0.125)

                def pv(i, u):
                    e_, Te_ = E[u % NE], TE[u % NE]
                    st, sp = (i == 0), (i == 31)
                    mms = [(self.ps[:, 6, :], V[:, i, :], e_[:, 0, :], st, sp),
                           (self.ps[:, 7, :], V[:, i, :], e_[:, 1, :], st, sp)]
                    self.mm(mms, [TV, Te_], [self.Tps[6], self.Tps[7]])
                    zf = Zs[:].rearrange("p a b -> p (a b)")
                    ef = e_[:].rearrange("p a b -> p (a b)")
                    SPL = int(os.environ.get("K_SPL", "960"))
                    if i == 0:
                        self.copy("dve", zf[:, 0:SPL], ef[:, 0:SPL], [Te_], [TZs[0]])
                        if SPL < 1024:
                            self.copy("pool", zf[:, SPL:1024], ef[:, SPL:1024], [Te_], [TZs[1]])
                    else:
                        self.tt("dve", zf[:, 0:SPL], zf[:, 0:SPL], ef[:, 0:SPL], ALU.add, [Te_, TZs[0]], [TZs[0]])
                        if SPL < 1024:
                            self.tt("pool", zf[:, SPL:1024], zf[:, SPL:1024], ef[:, SPL:1024], ALU.add, [Te_, TZs[1]], [TZs[1]])

                Zs, TZs = Zs2[(unit // 32) % 2], TZs2[(unit // 32) % 2]
                nears = {}
                nears[0] = qk(0, 0)
                for i in range(32):
                    if i + 1 < 32:
                        nears[i + 1] = qk(i + 1, (i + 1) % 2)
                    ex(i, i % 2, nears[i], unit + i)
                    if i == 0 and pend:
                        imm_, fo_ = pend.pop(0)
                        imm_()
                        fin_q.extend(fo_())
                    pv(i, unit + i)
                    if fin_q and i >= 1:
                        f_ = fin_q.pop(0)
                        if f_ is not None:
                            f_()
                assert not fin_q
                unit += 32
                onesf = self.cf[:, C_ONE:C_ONE + 128]
                A0, A1, A2, A3 = acc[:, 0, :], acc[:, 1, :], acc[:, 2, :], acc[:, 3, :]
                wz_, Twz_ = wzA[h % 2], TwzA[h % 2]

                def imm(Zs=Zs, TZs=TZs):
                    self.copy("dve", A0, self.ps[:, 6, :], [self.Tps[6]], [Tacc[0]])
                    self.copy("dve", A1, self.ps[:, 7, :], [self.Tps[7]], [Tacc[1]])
                    self.mm([(self.ps[:, 4, :], onesf, Zs[:, 0, :], True, True),
                             (self.ps[:, 5, :], onesf, Zs[:, 1, :], True, True)], [TZs[0], TZs[1], self.Tcf],
                            [self.Tps[4], self.Tps[5]])

                def fin_ops(h=h, J=J, wz_=wz_, Twz_=Twz_):
                    ops = []
                    ops.append(lambda: self.act(A2, self.ps[:, 4, :], AF.Ln, [self.Tps[4]], [Tacc[2]]))
                    ops.append(lambda: self.act(A3, self.ps[:, 5, :], AF.Ln, [self.Tps[5]], [Tacc[3]]))
                    ops.append(lambda: self.act(A2, A2, AF.Exp, [Tacc[2]], [Tacc[2]], scale=-1.0))
                    ops.append(lambda: self.act(A3, A3, AF.Exp, [Tacc[3]], [Tacc[3]], scale=-1.0))
                    ops.append(lambda: self.tt("dve", A0, A0, A2, ALU.mult, [Tacc[0], Tacc[2]], [Tacc[0]]))
                    ops.append(lambda: self.tt("dve", A1, A1, A3, ALU.mult, [Tacc[1], Tacc[3]], [Tacc[1]]))
                    ops.append(lambda: self.stt("dve", A2, A1, self.cols[:, 0:1], A0, ALU.mult, ALU.add,
                                                [Tacc[0], Tacc[1], self.Tcols], [Tacc[2]]))
                    ops.append(None)
                    ops.append(lambda: self.act(sqb[:], A2, AF.Square, [Tacc[2]], [Tsqb]))
                    ops.append(lambda: self.mm([(self.ps[:, 4, :], self.ones128[:], sqb[:], True, True)], [Tsqb, self.Tcb], [self.Tps[4]]))
                    ops.append(lambda: self.proj_fm(wz_, Twz_, 0, self.hT, self.ThT[J], J * 512, 512, 5))
                    ops.append(None)
                    ops.append(lambda: self.act(A0, self.ps[:, 4, :], AF.Ln, [self.Tps[4]], [Tacc[0]], bias=EPS))
                    ops.append(lambda: self.act(A1, self.ps[:, 5, :], AF.Exp, [self.Tps[5]], [Tacc[1]], scale=-1.0))
                    ops.append(lambda: self.act(A1, A1, AF.Ln, [Tacc[1]], [Tacc[1]], bias=1.0))
                    ops.append(lambda: self.stt("dve", A0, A0, 0.5, A1, ALU.mult, ALU.add, [Tacc[0], Tacc[1]], [Tacc[0]]))
                    ops.append(None)
                    ops.append(lambda: self.act(A0, A0, AF.Exp, [Tacc[0]], [Tacc[0]], scale=-1.0, bias=ln08))
                    ops.append(lambda: self.stt("dve", A3, A2, self.cols[:, 1:2], self.ps[:, 5, :], ALU.mult, ALU.mult,
                                                [Tacc[2], self.Tcols, self.Tps[5]], [Tacc[3]]))
                    ops.append(lambda: self.tt("dve", gA[:, h, J * 512:(J + 1) * 512], A3, A0, ALU.mult, [Tacc[3], Tacc[0]], [TgA[J]]))
                    return ops
                pend.append((imm, fin_ops))
        while pend:
            imm_, fo_ = pend.pop(0)
            imm_()
            fin_q.extend(fo_())
        while fin_q:
            f_ = fin_q.pop(0)
            if f_ is not None:
                f_()
        self.dump_dbg(0, gA, TgA)

    def phaseC(self):
        S = self.S
        gC, TgC = self.gC, self.TgC
        mT = self.sb("mT", [128, 8, 256], BF16)
        TmT = T("mT")
        mn = self.sb("mn_bc", [128, D], F32)
        Tmn = T("mn")
        xst = [self.sb("cxst%d" % i, [128, D], F32) for i in range(2)]
        Txst = [T("cxst%d" % i) for i in range(2)]
        xbb = [self.sb("cxbb%d" % i, [128, D], BF16) for i in range(2)]
        Txbb = [T("cxbb%d" % i) for i in range(2)]
        S.dma(lambda e: e.dma_start(out=mn[:], in_=self.vecs[2, :].partition_broadcast(128)), "cst", writes=[Tmn])
        for i in range(2):
            self.norm_transpose(self.mem[i * 128:(i + 1) * 128, :], mn, Tmn, mT, TmT, slice(i * 128, (i + 1) * 128),
                                48 + i, xst, Txst, xbb, Txbb)
        mkT = self.sb("mkT", [128, 8, 256], BF16)
        TmkT = T("mkT")
        mv = self.sb("mv", [128, 2, 1024], BF16)
        Tmv = T("mv")
        for blk in range(16):
            w, Tw = self.load_w(self.w_kv[:, blk * 128:(blk + 1) * 128])
            pb = blk % 2
            if blk < 8:
                self.proj_fm(w, Tw, 0, mT, TmT, 0, 256, pb)
                self.copy("dve", mkT[:, blk, :], self.ps[:, pb, 0:256], [self.Tps[pb]], [TmkT])
            else:
                for mt in range(2):
                    self.proj_tm(w, Tw, 0, 128, mT, TmT, mt * 128, self.ps[:, pb, mt * 128:(mt + 1) * 128], [self.Tps[pb]])
                self.copy("dve", mv[:, :, (blk - 8) * 128:(blk - 7) * 128],
                          self.ps[:, pb, 0:256].rearrange("p (a b) -> p a b", a=2), [self.Tps[pb]], [Tmv])
        wq = self.sb("cwq", [128, 8, 256], BF16)
        Twq = T("cwq")
        wz = self.sb("cwz", [128, 8, 256], BF16)
        Twz = T("cwz")
        cqT = [self.sb("cqT%d" % i, [128, 2, 512], BF16) for i in range(2)]
        TcqT = [T("cqT%d" % i) for i in range(2)]
        Ec = [self.sb("Ec%d" % i, [128, 2, 512], BF16) for i in range(2)]
        TEc = [T("Ec%d" % i) for i in range(2)]
        fL = [self.sb("fL%d" % i, [128, 512], F32) for i in range(2)]
        TfL = [T("fL%d" % i) for i in range(2)]
        f1 = [self.sb("cf1%d" % i, [128, 512], F32) for i in range(2)]
        Tf1 = [T("cf1%d" % i) for i in range(2)]
        f2 = [self.sb("cf2%d" % i, [128, 512], F32) for i in range(2)]
        Tf2 = [T("cf2%d" % i) for i in range(2)]

        def unit_gen(hc, J, si):
            ba, bb, bc, bd = [4 * si + x for x in range(4)]
            oc = (ba, bb)
            for ch in range(2):
                self.proj_fm(wq, Twq, ch * 128, self.hT, self.ThT[J], J * 512, 512, oc[ch])
                yield
                self.copy("act" if ch == 0 else "dve", cqT[si][:, ch, :], self.ps[:, oc[ch], :], [self.Tps[oc[ch]]], [TcqT[si]])
                yield
            mms = []
            for mt in range(2):
                for ch in range(2):
                    mms.append((self.ps[:, bc + mt, :], mkT[:, hc * 2 + ch, mt * 128:(mt + 1) * 128], cqT[si][:, ch, :], ch == 0, ch == 1))
            self.mm(mms, [TmkT, TcqT[si]], [self.Tps[bc], self.Tps[bd]])
            yield
            self.act(Ec[si][:].rearrange("p a b -> p (a b)"), self.ps[:, bc:bc + 2, :].rearrange("p a b -> p (a b)"), AF.Exp,
                     [self.Tps[bc], self.Tps[bd]], [TEc[si]], scale=1.0 / 16)
            yield
            mms = []
            for ch in range(2):
                for mt in range(2):
                    mms.append((self.ps[:, oc[ch], :], mv[:, mt, hc * 256 + ch * 128:hc * 256 + (ch + 1) * 128], Ec[si][:, mt, :], mt == 0, mt == 1))
            for mt in range(2):
                mms.append((self.ps[:, bc, :], self.oneb, Ec[si][:, mt, :], mt == 0, mt == 1))
            self.mm(mms, [Tmv, TEc[si], self.Tcb], [self.Tps[ba], self.Tps[bb], self.Tps[bc]])
            yield
            self.act(fL[si][:], self.ps[:, bc, :], AF.Ln, [self.Tps[bc]], [TfL[si]])
            yield
            for ch in range(2):
                self.proj_fm(wz, Twz, ch * 128, self.hT, self.ThT[J], J * 512, 512, bd)
                yield
                self.act(f1[si][:], self.ps[:, bd, :], AF.Exp, [self.Tps[bd]], [Tf1[si]], scale=-1.0)
                yield
                self.act(f1[si][:], f1[si][:], AF.Ln, [Tf1[si]], [Tf1[si]], bias=1.0)
                yield
                self.tt("pool", f1[si][:], f1[si][:], fL[si][:], ALU.add, [Tf1[si], TfL[si]], [Tf1[si]])
                yield
                self.act(f1[si][:], f1[si][:], AF.Exp, [Tf1[si]], [Tf1[si]], scale=-1.0)
                yield
                self.tt("dve", f2[si][:], self.ps[:, bd, :], f1[si][:], ALU.mult, [self.Tps[bd], Tf1[si]], [Tf2[si]])
                yield
                self.tt("dve", gC[:, hc * 2 + ch, J * 512:(J + 1) * 512], self.ps[:, oc[ch], :], f2[si][:], ALU.mult,
                        [self.Tps[oc[ch]], Tf2[si]], [TgC[J]])
                yield

        for hc in range(4):
            for k in range(2):
                self.load_w(self.w_in[:, 9216 + hc * 256 + k * 128:9216 + hc * 256 + (k + 1) * 128],
                            dst=wq[:, :, k * 128:(k + 1) * 128], Tdst=Twq)
                self.load_w(self.w_in[:, 10240 + hc * 256 + k * 128:10240 + hc * 256 + (k + 1) * 128],
                            dst=wz[:, :, k * 128:(k + 1) * 128], Tdst=Twz)
            pending = list(range(4))
            active = []
            while pending or active:
                while pending and len(active) < 2:
                    used = [e[1] for e in active]
                    si = 0 if 0 not in used else 1
                    active.append([unit_gen(hc, pending.pop(0), si), si])
                for ent in list(active):
                    try:
                        next(ent[0])
                    except StopIteration:
                        active.remove(ent)
        self.dump_dbg(2, gC, TgC)

    def phaseB(self):
        S = self.S
        gB, TgB = self.gB, self.TgB
        cf = self.cf
        if self.debug:
            for j in range(4):
                S.op("pool", lambda e, j=j: e.memset(gB[:, :, j * 512:(j + 1) * 512], 0.0), writes=[TgB[j]])
        OF = self.sb("bOF", [128, NT], F32)
        TOF = [T("bOF%d" % j) for j in range(4)]
        lrows = OF[0:8, 0:512].rearrange("p (r k) -> p r k", r=4)
        Tlrows = TOF[0]
        for r in range(4):
            S.dma(lambda e, r=r: e.dma_start(out=lrows[:, r, :], in_=self.vecs[3 + r, :].rearrange("(h k) -> h k", k=128)),
                  "cst", writes=[Tlrows])
        lbc = self.sb("lbc", [128, 32], F32)
        Tlbc = T("lbc")
        self.mm([(self.ps[:, 0, r * 8:(r + 1) * 8], lrows[:, r, :], cf[0:8, C_ID:C_ID + 8], True, True) for r in range(4)],
                [Tlrows, self.Tcf], [self.Tps[0]])
        p4 = self.ps[:, 0, 0:32].rearrange("p (r h) -> p r h", r=4)
        l4 = lbc[:, 0:32].rearrange("p (r h) -> p r h", r=4)
        self.copy("dve", lbc[:, 0:32], self.ps[:, 0, 0:32], [self.Tps[0]], [Tlbc])
        for d in range(2):
            self.tt("dve", lbc[:, d * 16:d * 16 + 8], lbc[:, d * 16 + 8:d * 16 + 16], lbc[:, d * 16:d * 16 + 8], ALU.subtract,
                    [Tlbc], [Tlbc])
            self.act(lbc[:, d * 16:d * 16 + 8], lbc[:, d * 16:d * 16 + 8], AF.Exp, [Tlbc], [Tlbc])
            self.act(lbc[:, d * 16:d * 16 + 8], lbc[:, d * 16:d * 16 + 8], AF.Ln, [Tlbc], [Tlbc], bias=1.0)
            self.act(lbc[:, d * 16:d * 16 + 8], lbc[:, d * 16:d * 16 + 8], AF.Exp, [Tlbc], [Tlbc], scale=-1.0)
        cmask = self.sb("cmask", [128, 512], F32)
        Tcmask = T("cmask")
        S.op("pool", lambda e: e.memset(cmask[:], 1.0), writes=[Tcmask])
        S.op("pool", lambda e: e.memset(cmask[:].rearrange("p (c s) -> p c s", s=64)[:, :, 0:1], 0.0), writes=[Tcmask])
        wH2 = [self.sb("wH%d" % i, [128, 8, 640], BF16) for i in range(2)]
        TwH2 = [T("wH%d" % i) for i in range(2)]

        class WS:
            pass
        sets = []
        for i in range(2):
            w = WS()
            def mk(nm, shape, dt, i=i, w=w):
                setattr(w, nm, self.sb("b%s%d" % (nm, i), shape, dt))
                setattr(w, "T" + nm, T("b%s%d" % (nm, i)))
            mk("A1", [128, 512], F32); mk("A2", [128, 512], F32); mk("G", [128, 512], F32); mk("P", [128, 512], F32)
            mk("CR", [128, 512], F32); mk("EI", [128, 512], F32); mk("KKT", [128, 512], BF16); mk("KBT", [128, 512], BF16)
            mk("Vb", [128, 4, 128], BF16); mk("DEC", [128, 8], F32)
            mk("KB", [128, 4, 128], BF16); mk("QB", [128, 512], BF16); mk("KI", [128, 512], BF16)
            mk("AM", [128, 4, 128], BF16); mk("SN", [128, 8, 128], F32); mk("SNb", [128, 8, 128], BF16)
            sets.append(w)
        OS = self.sb("bOS", [128, 512], F32); TOS = T("bOS")
        SQ = self.sb("bSQ", [128, 512], BF16); TSQ = T("bSQ")
        F1 = self.sb("bF1", [128, 512], F32); TF1 = T("bF1")
        F2 = self.sb("bF2", [128, 512], F32); TF2 = T("bF2")
        Sp = [self.sb("bSp%d" % d, [128, 128], F32) for d in range(2)]
        TSp = [T("bSp%d" % d) for d in range(2)]
        dn = {0: "f", 1: "b"}
        nblk = 0
        for hb in range(int(os.environ.get('K_BHEADS', '8'))):
            wH, TwH = wH2[hb % 2], TwH2[hb % 2]
            nheads = int(os.environ.get('K_BHEADS', '8'))
            for hl in ([0, 1] if hb == 0 else [hb + 1]):
                if hl < nheads:
                    base = 4096 + hl * 640
                    for k in range(5):
                        self.load_w(self.w_in[:, base + k * 128:base + (k + 1) * 128],
                                    dst=wH2[hl % 2][:, :, k * 128:(k + 1) * 128], Tdst=TwH2[hl % 2])
            for d in range(2):
                S.op("pool", lambda e, d=d: e.memset(Sp[d][:], 0.0), writes=[TSp[d]])
            fw = [(0, jb, True, self.hT, self.ThT) for jb in range(4)]
            bo = [(1, jb, False, self.hTo, self.ThTo) for jb in (3, 2, 1, 0)]
            bw = [(1, jb, True, self.hT, self.ThT) for jb in (3, 2, 1, 0)]
            sweeps = []
            for i in range(4):
                sweeps += [fw[i], bo[i]]
            sweeps += bw
            def block_gen(hb, d, jb, wout, src, Tsrc, si, wH=wH, TwH=TwH):
                w = sets[si]
                ba, bb, bc, bd = [4 * si + x for x in range(4)]
                dk = dn[d]
                tok0 = jb * 512
                fcol = 128 * (1 + d)
                self.proj_fm(wH, TwH, fcol, src, Tsrc[jb], tok0, 512, ba)
                yield
                mV = []
                for t in range(4):
                    for c in range(8):
                        mV.append((self.ps[:, bb, t * 128:(t + 1) * 128], src[:, c, tok0 + t * 128:tok0 + (t + 1) * 128],
                                   wH[:, c, 384:512], c == 0, c == 7))
                self.mm(mV, [TwH, Tsrc[jb]], [self.Tps[bb]])
                yield
                lbcol = lbc[:, d * 16 + hb:d * 16 + hb + 1]
                self.act(w.A1[:], self.ps[:, ba, :], AF.Exp, [self.Tps[ba]], [w.TA1], scale=-1.0)
                yield
                self.copy("dve", w.Vb[:], self.ps[:, bb, :].rearrange("p (a b) -> p a b", a=4), [self.Tps[bb]], [w.TVb])
                yield
                self.act(w.A2[:], w.A1[:], AF.Ln, [w.TA1], [w.TA2], bias=1.0)
                yield
                self.act(w.A1[:], w.A1[:], AF.Ln, [w.TA1, Tlbc], [w.TA1], bias=1.0, scale=lbcol)
                yield
                self.tt("dve", w.G[:], w.A1[:], w.A2[:], ALU.subtract, [w.TA1, w.TA2], [w.TG])
                yield
                S.op("dve", lambda e, w=w: e.tensor_tensor_scan(out=w.P[:], data0=cmask[:], data1=w.G[:], initial=0.0,
                                                                op0=ALU.mult, op1=ALU.add),
                     reads=[w.TG, Tcmask], writes=[w.TP])
                yield
                self.act(w.A2[:], w.G[:], AF.Exp, [w.TG], [w.TA2])
                yield
                self.ts("pool", w.KKT[:], w.A2[:], -1.0, 1.0, ALU.mult, ALU.add, [w.TA2], [w.TKKT])
                yield
                P3 = w.P[:].rearrange("p (c s) -> p c s", s=64)
                tot = P3[:, :, 63:64]
                totb = tot.to_broadcast([128, 8, 64])
                CR3 = w.CR[:].rearrange("p (c s) -> p c s", s=64)
                if d == 0:
                    self.tt("dve", CR3, totb, P3, ALU.subtract, [w.TP], [w.TCR])
                    yield
                    bsrc, Tb = w.P, w.TP
                else:
                    self.tt("dve", w.CR[:], w.P[:], w.G[:], ALU.subtract, [w.TP, w.TG], [w.TCR])
                    yield
                    self.tt("dve", w.A1[:].rearrange("p (c s) -> p c s", s=64), totb, CR3, ALU.subtract, [w.TP, w.TCR], [w.TA1])
                    yield
                    bsrc, Tb = w.A1, w.TA1
                self.act(w.DEC[:], P3[:, :, 63], AF.Exp, [w.TP], [w.TDEC])
                yield
                if wout:
                    self.act(w.EI[:], bsrc[:], AF.Exp, [Tb], [w.TEI], scale=-1.0)
                    yield
                    self.act(w.A2[:], bsrc[:], AF.Exp, [Tb], [w.TA2])
                    yield
                self.act(w.CR[:], w.CR[:], AF.Exp, [w.TCR], [w.TCR])
                yield
                self.tt("pool", w.KBT[:], w.KKT[:], w.CR[:], ALU.mult, [w.TKKT, w.TCR], [w.TKBT])
                yield
                pT = self.ps[:, bd, :].bitcast(BF16)

                def tr(e, pT=pT, w=w):
                    ins = None
                    for t in range(4):
                        ins = e.transpose(out=pT[:, t * 128:(t + 1) * 128], in_=w.KBT[:, t * 128:(t + 1) * 128], identity=self.idb)
                    return ins
                S.op("pe", tr, reads=[w.TKBT, self.Tcb], writes=[self.Tps[bd]])
                yield
                self.copy("act", w.KB[:], pT[:, 0:512].rearrange("p (a b) -> p a b", a=4), [self.Tps[bd]], [w.TKB])
                yield
                if wout:
                    self.proj_fm(wH, TwH, 0, src, Tsrc[jb], tok0, 512, bc)
                    yield
                    self.tt("dve", w.QB[:], self.ps[:, bc, :], w.A2[:], ALU.mult, [self.Tps[bc], w.TA2], [w.TQB])
                    yield
                    self.tt("pool", w.KI[:], w.KKT[:], w.EI[:], ALU.mult, [w.TKKT, w.TEI], [w.TKI])
                    yield
                torder = range(4) if d == 0 else range(3, -1, -1)
                corder = (0, 1) if d == 0 else (1, 0)
                chunks = [(t, ch) for t in torder for ch in corder]
                mU = []
                for n, (t, ch) in enumerate(chunks):
                    pr = slice(ch * 64, (ch + 1) * 64)
                    ub = ba if ch == 0 else bb
                    mU.append((self.ps[:, ub, t * 128:(t + 1) * 128], w.KB[pr, t, :], w.Vb[pr, t, :], True, True))
                self.mm(mU, [w.TKB, w.TVb], [self.Tps[ba], self.Tps[bb]])
                yield
                self.copy("act", w.G[:], self.ps[:, ba, :], [self.Tps[ba]], [w.TG])
                yield
                self.copy("act", w.P[:], self.ps[:, bb, :], [self.Tps[bb]], [w.TP])
                yield
                yield ("state", d)
                self.copy("pool", w.SN[:, 0, :], Sp[d][:], [TSp[d]], [w.TSN])
                yield
                for n, (t, ch) in enumerate(chunks):
                    ubuf, Tub = (w.G, w.TG) if ch == 0 else (w.P, w.TP)
                    ups = ubuf[:, t * 128:(t + 1) * 128]
                    if n < 7:
                        self.stt("dve", w.SN[:, n + 1, :], w.SN[:, n, :], w.DEC[:, t * 2 + ch:t * 2 + ch + 1], ups, ALU.mult, ALU.add,
                                 [w.TSN, w.TDEC, Tub], [w.TSN])
                        yield
                    else:
                        self.stt("dve", Sp[d][:], w.SN[:, n, :], w.DEC[:, t * 2 + ch:t * 2 + ch + 1], ups, ALU.mult, ALU.add,
                                 [w.TSN, w.TDEC, Tub], [TSp[d]])
                        yield
                if wout:
                    self.copy("act", w.SNb[:], w.SN[:], [w.TSN], [w.TSNb])
                    yield
                    mA = []
                    for t in range(4):
                        mA.append((self.ps[:, bc, t * 128:(t + 1) * 128], w.KI[:, t * 128:(t + 1) * 128], w.QB[:, t * 128:(t + 1) * 128], True, True))
                    self.mm(mA, [w.TKI, w.TQB], [self.Tps[bc]])
                    yield
                    mk_ = cf[:, C_MASK[dk]:C_MASK[dk] + 128].unsqueeze(1).to_broadcast([128, 4, 128])
                    self.tt("dve", w.AM[:], self.ps[:, bc, :].rearrange("p (a b) -> p a b", a=4), mk_, ALU.mult,
                            [self.Tps[bc], self.Tcf], [w.TAM])
                    yield
                    mI = []
                    for t in range(4):
                        mI.append((self.ps[:, bd, t * 128:(t + 1) * 128], w.Vb[:, t, :], w.AM[:, t, :], True, True))
                    self.mm(mI, [w.TVb, w.TAM], [self.Tps[bd]])
                    yield
                    mN = []
                    for n, (t, ch) in enumerate(chunks):
                        c0 = t * 128 + ch * 64
                        mN.append((self.ps[:, bc, c0:c0 + 64], w.SNb[:, n, :], w.QB[:, c0:c0 + 64], True, True))
                    self.mm(mN, [w.TSNb, w.TQB], [self.Tps[bc]])
                    yield
                if wout and d == 0:
                    self.copy("act", w.A1[:], self.ps[:, bc, :], [self.Tps[bc]], [w.TA1])
                    yield
                    self.tt("dve", OF[:, tok0:tok0 + 512], self.ps[:, bd, :], w.A1[:], ALU.add, [self.Tps[bd], w.TA1], [TOF[jb]])
                    yield
                elif wout:
                    self.copy("act", w.A1[:], self.ps[:, bc, :], [self.Tps[bc]], [w.TA1])
                    yield
                    self.tt("dve", w.P[:], self.ps[:, bd, :], OF[:, tok0:tok0 + 512], ALU.add, [self.Tps[bd], TOF[jb]], [w.TP])
                    yield
                    self.tt("pool", w.P[:], w.P[:], w.A1[:], ALU.add, [w.TP, w.TA1], [w.TP])
                    yield
                    self.act(w.KKT[:], w.P[:], AF.Square, [w.TP], [w.TKKT])
                    yield
                    self.mm([(self.ps[:, ba, :], self.ones128[:], w.KKT[:], True, True)], [w.TKKT, self.Tcb], [self.Tps[ba]])
                    yield
                    self.proj_fm(wH, TwH, 512, self.hT, self.ThT[jb], tok0, 512, bb)
                    yield
                    self.act(w.A1[:], self.ps[:, ba, :], AF.Ln, [self.Tps[ba]], [w.TA1], bias=EPS)
                    yield
                    self.act(w.G[:], self.ps[:, bb, :], AF.Exp, [self.Tps[bb]], [w.TG], scale=-1.0)
                    yield
                    self.act(w.G[:], w.G[:], AF.Ln, [w.TG], [w.TG], bias=1.0)
                    yield
                    self.stt("dve", w.A1[:], w.A1[:], 0.5, w.G[:], ALU.mult, ALU.add, [w.TA1, w.TG], [w.TA1])
                    yield
                    self.act(w.A1[:], w.A1[:], AF.Exp, [w.TA1], [w.TA1], scale=-1.0)
                    yield
                    self.stt("dve", w.G[:], w.P[:], self.cols[:, 2:3], self.ps[:, bb, :], ALU.mult, ALU.mult,
                             [w.TP, self.Tcols, self.Tps[bb]], [w.TG])
                    yield
                    self.tt("dve", gB[:, hb, tok0:tok0 + 512], w.G[:], w.A1[:], ALU.mult, [w.TG, w.TA1], [TgB[jb]])
                    yield

            active = []
            pending = [(d, jb, wout, src, Tsrc) for (d, jb, wout, src, Tsrc) in sweeps]
            done_dirs = {0: 0, 1: 0}
            started_dirs = {0: 0, 1: 0}
            nstart = 0
            while pending or active:
                while pending and len(active) < int(os.environ.get("K_BWIN", "2")):
                    (d, jb, wout, src, Tsrc) = pending.pop(0)
                    used = [e[4] for e in active]
                    si = 0 if 0 not in used else 1
                    g = block_gen(hb, d, jb, wout, src, Tsrc, si)
                    active.append([g, d, started_dirs[d], False, si])
                    started_dirs[d] += 1
                    nstart += 1
                progressed = False
                for ent in list(active):
                    g, d, order, blocked, _si = ent
                    if blocked:
                        if done_dirs[d] < order:
                            continue
                        ent[3] = False
                    try:
                        r = next(g)
                        progressed = True
                        if isinstance(r, tuple) and r[0] == "state" and done_dirs[d] < order:
                            ent[3] = True
                    except StopIteration:
                        active.remove(ent)
                        done_dirs[d] += 1
                        progressed = True
                assert progressed
        self.dump_dbg(1, gB, TgB)

    def phaseF(self):
        S = self.S
        Wo = self.sb("Wo", [128, 8, D], BF16)
        TWo = T("Wo")
        for k in range(8):
            self.load_w(self.w_out[:, k * 128:(k + 1) * 128], dst=Wo[:, :, k * 128:(k + 1) * 128], Tdst=TWo)
        pn = self.sb("post_bc", [128, D], F32)
        Tpn = T("postn")
        S.dma(lambda e: e.dma_start(out=pn[:], in_=self.vecs[1, :].partition_broadcast(128)), "cst", writes=[Tpn])
        xs = [self.sb("xres%d" % i, [128, D], F32) for i in range(2)]
        Txs = [T("xres%d" % i) for i in range(2)]
        ob = [self.sb("ob%d" % i, [128, D], F32) for i in range(2)]
        Tob = [T("ob%d" % i) for i in range(2)]
        junk = self.sb("junk", [128, D], BF16)
        Tj = T("junk")
        stat = self.stat
        for t in range(16):
            sl = t % 2
            b0 = 0 if sl == 0 else 2
            S.dma(lambda e, t=t, sl=sl: e.dma_start(out=xs[sl][:], in_=self.x_own[t * 128:(t + 1) * 128, :]),
                  "xres%d" % sl, writes=[Txs[sl]])
            mms = []
            for half in range(2):
                for dc in range(8):
                    mms.append((self.ps[:, b0 + half, :], self.merged[:, dc, t * 128:(t + 1) * 128],
                                Wo[:, dc, half * 512:(half + 1) * 512], dc == 0, dc == 7))
            self.mm(mms, [TWo] + [self.Tmg[dc][t // 4] for dc in range(8)], [self.Tps[b0], self.Tps[b0 + 1]])
            c0 = 128 + t * 4
            yps = self.ps[:, b0:b0 + 2, :].rearrange("p a b -> p (a b)")
            self.act(junk[:], yps, AF.Square, [self.Tps[b0], self.Tps[b0 + 1]], [Tj, self.Tstat], accum_out=stat[:, c0:c0 + 1])
            self.act(stat[:, c0 + 1:c0 + 2], stat[:, c0:c0 + 1], AF.Ln, [self.Tstat], [self.Tstat], bias=EPS, scale=1.0 / D)
            self.act(stat[:, c0 + 2:c0 + 3], stat[:, c0 + 1:c0 + 2], AF.Exp, [self.Tstat], [self.Tstat], scale=-0.5)
            self.stt("dve", ob[sl][:], yps, stat[:, c0 + 2:c0 + 3], pn[:], ALU.mult, ALU.mult,
                     [self.Tps[b0], self.Tps[b0 + 1], self.Tstat, Tpn], [Tob[sl]])
            self.tt("dve", ob[sl][:], ob[sl][:], xs[sl][:], ALU.add, [Tob[sl], Txs[sl]], [Tob[sl]])
            tok = S.dma(lambda e, t=t, sl=sl: e.dma_start(out=self.y[t * 128:(t + 1) * 128, :], in_=ob[sl][:]),
                        "yst%d" % sl, reads=[Tob[sl]])
            self.out_toks.append(tok)
        for tok in self.out_toks[-2:]:
            S.wait_tok("sp", tok)
        if self.debug and S.dma_cnt.get("dbg", 0):
            S.wait_tok("sp", ("dbg", 16 * S.dma_cnt["dbg"]))

    def zero_merged(self):
        for dc in range(8):
            self.S.op("pool", lambda e, dc=dc: e.memset(self.merged[:, dc, :], 0.0), writes=self.Tmg[dc])

    def build(self):
        self.setup_globals()
        self.phase0()
        self._mid = []
        self.gB = self._alloc(self._mid, "sb", "gB", [128, 8, NT], BF16)
        self.TgB = [T("gB%d" % j) for j in range(4)]
        self.cast_eng = "pool"
        if "B" in self.phases:
            self.phaseB()
            self.end_phase()
        self.gA = self._alloc(self._mid, "sb", "gA", [128, 8, NT], BF16)
        self.TgA = [T("gA%d" % j) for j in range(4)]
        if "A" in self.phases:
            self.phaseA()
            self.end_phase()
        self.S.barrier()
        self.cast_eng = "dve"
        first = True
        for X, ph, g, Tg in ((0, "A", self.gA, self.TgA), (1, "B", self.gB, self.TgB)):
            if ph in self.phases:
                self.branch_merge(X, g, Tg, first)
                first = False
                self.end_phase()
        self.free_mid()
        if "C" in self.phases:
            self.gC = self.sb("gC", [128, 8, NT], BF16)
            self.TgC = [T("gC%d" % j) for j in range(4)]
            self.phaseC()
            self.branch_merge(2, self.gC, self.TgC, first)
            first = False
            self.end_phase()
        if "F" in self.phases:
            if first:
                self.zero_merged()
            self.phaseF()
        elif self.debug and self.S.dma_cnt.get("dbg", 0):
            self.S.wait_tok("sp", ("dbg", 16 * self.S.dma_cnt["dbg"]))
        self.end_phase()
        return self.nc


def _prep_shared(inputs):
    f = lambda a: np.ascontiguousarray(np.asarray(a, dtype=np.float32))
    w_in = f(inputs["w_in"])[0]
    aq, ak, av, az = w_in[:, 0:1024], w_in[:, 1024:2048], w_in[:, 2048:3072], w_in[:, 3072:4096]
    rest = w_in[:, 4096:]
    blocks = []
    for h in range(8):
        q = np.concatenate([aq[:, h * 64:(h + 1) * 64], aq[:, 512 + h * 64:512 + (h + 1) * 64]], axis=1)
        k = np.concatenate([ak[:, h * 64:(h + 1) * 64], ak[:, 512 + h * 64:512 + (h + 1) * 64]], axis=1)
        blocks += [q, k, av[:, h * 128:(h + 1) * 128], az[:, h * 128:(h + 1) * 128]]
    wA = np.concatenate(blocks, axis=1)
    def wB(swap):
        fo, bo = (2048, 1024) if swap else (1024, 2048)
        bl = []
        for hb in range(8):
            sl = slice(hb * 128, (hb + 1) * 128)
            bl += [rest[:, 0:1024][:, sl], rest[:, fo:fo + 1024][:, sl], rest[:, bo:bo + 1024][:, sl],
                   rest[:, 3072:4096][:, sl], rest[:, 4096:5120][:, sl]]
        return np.concatenate(bl, axis=1)
    tail = rest[:, 5120:]
    w_even = np.ascontiguousarray(np.concatenate([wA, wB(False), tail], axis=1))
    w_odd = np.ascontiguousarray(np.concatenate([wA, wB(True), tail], axis=1))
    lbl = f(inputs["lb_logits"])
    vec_even = np.zeros((8, 1024), np.float32)
    vec_even[0] = f(inputs["pre_norm"])[0]
    vec_even[1] = f(inputs["post_norm"])[0]
    vec_even[2] = f(inputs["mem_norm"])[0]
    vec_even[3], vec_even[4] = lbl[0, 0], lbl[0, 1]
    vec_even[5], vec_even[6] = lbl[1, 0], lbl[1, 1]
    vec_odd = vec_even.copy()
    vec_odd[3], vec_odd[4] = lbl[1, 0], lbl[1, 1]
    vec_odd[5], vec_odd[6] = lbl[0, 0], lbl[0, 1]
    small = np.zeros((1, 512), np.float32)
    small[0, 0:64] = f(inputs["lambda_q1"])[0]
    small[0, 64:128] = f(inputs["lambda_k1"])[0]
    small[0, 128:192] = f(inputs["lambda_q2"])[0]
    small[0, 192:256] = f(inputs["lambda_k2"])[0]
    small[0, 256:384] = f(inputs["diff_subln"])[0]
    small[0, 384:512] = f(inputs["hgrn_norm"])[0]
    return dict(w_even=w_even, w_odd=w_odd, vec_even=vec_even, vec_odd=vec_odd, small=small,
                w_kv=f(inputs["w_mem_kv"])[0], w_br=f(inputs["w_branch"])[0], w_out=f(inputs["w_out"])[0],
                relb=f(inputs["rel_bias"]))


def _in_maps(inputs):
    sh = _prep_shared(inputs)
    x = np.asarray(inputs["x"], dtype=np.float32)
    mem = np.asarray(inputs["mem"], dtype=np.float32)
    cst = _consts()
    tabs = [_oh_tables(False), _oh_tables(True)]
    maps = []
    for c in range(8):
        b, half = c // 2, c % 2
        xb = x[b] if half == 0 else x[b, ::-1]
        maps.append({
            "x_own": np.ascontiguousarray(xb[0:NT]), "x_oth": np.ascontiguousarray(xb[NT:2 * NT]),
            "mem": np.ascontiguousarray(mem[b]),
            "w_in": sh["w_even"] if half == 0 else sh["w_odd"],
            "w_kv": sh["w_kv"], "w_br": sh["w_br"], "w_out": sh["w_out"],
            "vecs": sh["vec_even"] if half == 0 else sh["vec_odd"],
            "small": sh["small"], "relb": sh["relb"], "cst": cst,
            "oh": tabs[half][0], "ohfar": tabs[half][1],
        })
    return maps


def _run(inputs, phases="0ABCF", debug=False, cores=None):
    prog = Prog(phases=phases, debug=debug)
    nc = prog.build()
    maps = _in_maps(inputs)
    ids = list(range(8)) if cores is None else cores
    res = run_bass_kernel_spmd(nc, [maps[c] for c in ids], core_ids=list(range(len(ids))))
    return res, ids


def kernel(**inputs):
    res, ids = _run(inputs, phases=os.environ.get("K_PHASES", "0ABCF"))
    out = np.zeros((4, 4096, D), np.float32)
    for r, c in zip(res.results, ids):
        b, half = c // 2, c % 2
        y = np.asarray(r["y"], dtype=np.float32)
        if half == 0:
            out[b, 0:NT] = y
        else:
            out[b, NT:2 * NT] = y[::-1]
    return out
```
